# Optimizing a Trainium2 kernel written in Bass

```python
import jax
import jax.numpy as jnp
from jax import lax
import numpy as np

D_MODEL = 2048
BATCH = 8
SEQ = 2048
DEPTH = 1

RWKV_WIDTH = 1024
RWKV_HEAD_DIM = 64
RWKV_HEADS = RWKV_WIDTH // RWKV_HEAD_DIM
DECAY_LORA = 64
ICLR_LORA = 64
GN_EPS = 64e-5

MOBA_WIDTH = 1024
MOBA_HEAD_DIM = 64
MOBA_HEADS = MOBA_WIDTH // MOBA_HEAD_DIM
MOBA_BLOCK = 256
MOBA_TOPK = 3
QUERY_CHUNK = 16

RMS_EPS = 1e-6
NEG_INF = -1e30

IN_SPLITS = (RWKV_WIDTH, RWKV_WIDTH, RWKV_WIDTH, RWKV_WIDTH, DECAY_LORA, ICLR_LORA,
             MOBA_WIDTH, MOBA_WIDTH, MOBA_WIDTH, MOBA_WIDTH, D_MODEL, D_MODEL)
IN_COLS = 4 * RWKV_WIDTH + DECAY_LORA + ICLR_LORA + 4 * MOBA_WIDTH + 2 * D_MODEL

kernel_name = "hybrid_rwkv7_moba_gated_block"


def rms_norm(x, g):
    xf = x.astype(jnp.float32)
    xf = xf * lax.rsqrt(jnp.mean(xf * xf, axis=-1, keepdims=True) + RMS_EPS)
    return (xf * g.astype(jnp.float32)).astype(x.dtype)


def token_shift(p, mu):
    prev = jnp.pad(p, ((0, 0), (1, 0), (0, 0)))[:, :-1]
    return p + (prev - p) * mu


def rwkv7_mix(p_r, p_k, p_v, p_wd, p_ad, mu_r, mu_k, mu_v, mu_w, mu_a, w0, w_decay_up,
              a0, w_iclr_up, k_k, k_a, r_k, gn_w, gn_b):
    B, S, _ = p_r.shape
    H, N = RWKV_HEADS, RWKV_HEAD_DIM
    f32 = jnp.float32
    r = token_shift(p_r, mu_r)
    k = token_shift(p_k, mu_k)
    v = token_shift(p_v, mu_v)
    xw = token_shift(p_wd, mu_w)
    xa = token_shift(p_ad, mu_a)
    w = -jax.nn.softplus(-(w0 + jnp.tanh(xw) @ w_decay_up)) - 0.5
    decay = jnp.exp(-jnp.exp(w.astype(f32)))
    a = jax.nn.sigmoid(a0 + xa @ w_iclr_up)
    kk = (k * k_k).reshape(B, S, H, N).astype(f32)
    kk = kk / jnp.maximum(jnp.sqrt(jnp.sum(kk * kk, axis=-1, keepdims=True)), 1e-12)
    k = k * (1.0 + (a - 1.0) * k_a)

    def heads(t):
        return t.reshape(B, S, H, N).astype(f32)

    rh, kh, vh, wh, ah = heads(r), heads(k), heads(v), heads(decay), heads(a)
    bh = kk * ah
    xs = tuple(jnp.moveaxis(t, 1, 0) for t in (rh, wh, kh, vh, kk, bh))

    def step(state, inp):
        r_t, w_t, k_t, v_t, kk_t, b_t = inp
        sa = jnp.einsum('bhvk,bhk->bhv', state, -kk_t)
        state = (state * w_t[:, :, None, :]
                 + sa[..., None] * b_t[:, :, None, :]
                 + v_t[..., None] * k_t[:, :, None, :])
        y_t = jnp.einsum('bhvk,bhk->bhv', state, r_t)
        return state, y_t

    state0 = jnp.zeros((B, H, N, N), f32)
    _, y = lax.scan(step, state0, xs)
    y = jnp.moveaxis(y, 0, 1)
    mean = jnp.mean(y, axis=-1, keepdims=True)
    var = jnp.mean(jnp.square(y - mean), axis=-1, keepdims=True)
    y = ((y - mean) * lax.rsqrt(var + GN_EPS)).reshape(B, S, H * N)
    y = y * gn_w.astype(f32) + gn_b.astype(f32)
    bonus = jnp.sum(rh * kh * r_k.reshape(H, N).astype(f32), axis=-1, keepdims=True) * vh
    y = y + bonus.reshape(B, S, H * N)
    return y.astype(p_r.dtype)


def moba_attention(p_q, p_k, p_v, q_norm_w, k_norm_w):
    B, S, _ = p_q.shape
    H, Dh, BLK, C = MOBA_HEADS, MOBA_HEAD_DIM, MOBA_BLOCK, QUERY_CHUNK
    nb = -(-S // BLK)
    s_pad = nb * BLK
    n_sel = min(MOBA_TOPK, nb)
    scale = Dh ** -0.5
    q = rms_norm(p_q.reshape(B, S, H, Dh), q_norm_w).transpose(0, 2, 1, 3)
    k = rms_norm(p_k.reshape(B, S, H, Dh), k_norm_w).transpose(0, 2, 1, 3)
    v = p_v.reshape(B, S, H, Dh).transpose(0, 2, 1, 3)
    pad = ((0, 0), (0, 0), (0, s_pad - S), (0, 0))
    kb = jnp.pad(k, pad).reshape(B, H, nb, BLK, Dh)
    vb = jnp.pad(v, pad).reshape(B, H, nb, BLK, Dh)

    k_mean = jnp.mean(kb, axis=3)
    q_blk = jnp.arange(S) // BLK
    gate = jnp.einsum('bhsd,bhnd->bhsn', q, k_mean).astype(jnp.float32)
    past = jnp.arange(nb)[None, :] < q_blk[:, None]
    gate = jnp.where(past, gate, NEG_INF)
    _, sel = lax.top_k(gate, n_sel)
    sel_valid = jnp.arange(n_sel)[None, :] < q_blk[:, None]

    b_ix = jnp.arange(B)[:, None, None, None]
    h_ix = jnp.arange(H)[None, :, None, None]
    key_off = jnp.arange(BLK)

    def chunk(c):
        t0 = c * C
        blk = t0 // BLK
        q_c = lax.dynamic_slice_in_dim(q, t0, C, axis=2)
        sel_c = lax.dynamic_slice_in_dim(sel, t0, C, axis=2)
        valid_c = lax.dynamic_slice_in_dim(sel_valid, t0, C, axis=0)
        k_own = lax.dynamic_index_in_dim(kb, blk, axis=2, keepdims=False)
        v_own = lax.dynamic_index_in_dim(vb, blk, axis=2, keepdims=False)
        s_own = jnp.einsum('bhcd,bhjd->bhcj', q_c, k_own).astype(jnp.float32) * scale
        q_pos = t0 + jnp.arange(C)
        k_pos = blk * BLK + key_off
        s_own = jnp.where(k_pos[None, :] <= q_pos[:, None], s_own, NEG_INF)
        k_g = kb[b_ix, h_ix, sel_c]
        v_g = vb[b_ix, h_ix, sel_c]
        s_g = jnp.einsum('bhcd,bhckjd->bhckj', q_c, k_g).astype(jnp.float32) * scale
        s_g = jnp.where(valid_c[:, :, None], s_g, NEG_INF)
        logits = jnp.concatenate([s_own, s_g.reshape(B, H, C, n_sel * BLK)], axis=-1)
        probs = jax.nn.softmax(logits, axis=-1).astype(v.dtype)
        p_own = probs[..., :BLK]
        p_g = probs[..., BLK:].reshape(B, H, C, n_sel, BLK)
        return (jnp.einsum('bhcj,bhjd->bhcd', p_own, v_own)
                + jnp.einsum('bhckj,bhckjd->bhcd', p_g, v_g))

    o = lax.map(chunk, jnp.arange(S // C))
    return o.transpose(1, 0, 3, 2, 4).reshape(B, S, H * Dh)


def setup_inputs(seed: int = 0) -> dict:
    key = jax.random.key(seed)
    ks = jax.random.split(key, 24)
    f32 = jnp.float32
    L = DEPTH

    def nrm(k, shape, scale):
        return jax.random.normal(k, shape, f32) * scale

    n = jnp.arange(RWKV_WIDTH, dtype=f32) / (RWKV_WIDTH - 1)
    decay_speed = -7.0 + 5.0 * n ** 0.85
    return {
        "x": nrm(ks[0], (BATCH, SEQ, D_MODEL), 1.0),
        "norm_w": 1.0 + nrm(ks[1], (L, D_MODEL), 0.02),
        "w_in": nrm(ks[2], (L, D_MODEL, IN_COLS), D_MODEL ** -0.5),
        "mu_r": jax.random.uniform(ks[3], (L, RWKV_WIDTH), f32, 0.1, 0.9),
        "mu_k": jax.random.uniform(ks[4], (L, RWKV_WIDTH), f32, 0.1, 0.9),
        "mu_v": jax.random.uniform(ks[5], (L, RWKV_WIDTH), f32, 0.1, 0.9),
        "mu_w": jax.random.uniform(ks[6], (L, DECAY_LORA), f32, 0.1, 0.9),
        "mu_a": jax.random.uniform(ks[7], (L, ICLR_LORA), f32, 0.1, 0.9),
        "w0": decay_speed[None, :] + 0.5 + nrm(ks[8], (L, RWKV_WIDTH), 0.05),
        "w_decay_up": nrm(ks[9], (L, DECAY_LORA, RWKV_WIDTH), 0.5 * DECAY_LORA ** -0.5),
        "a0": nrm(ks[10], (L, RWKV_WIDTH), 0.1),
        "w_iclr_up": nrm(ks[11], (L, ICLR_LORA, RWKV_WIDTH), 0.5 * ICLR_LORA ** -0.5),
        "k_k": 0.85 + nrm(ks[12], (L, RWKV_WIDTH), 0.02),
        "k_a": 1.0 + nrm(ks[13], (L, RWKV_WIDTH), 0.02),
        "r_k": nrm(ks[14], (L, RWKV_WIDTH), 0.1),
        "gn_w": 1.0 + nrm(ks[15], (L, RWKV_WIDTH), 0.02),
        "gn_b": nrm(ks[16], (L, RWKV_WIDTH), 0.02),
        "q_norm_w": 1.0 + nrm(ks[17], (L, MOBA_HEAD_DIM), 0.02),
        "k_norm_w": 1.0 + nrm(ks[18], (L, MOBA_HEAD_DIM), 0.02),
        "w_proj_rwkv": nrm(ks[19], (L, RWKV_WIDTH, D_MODEL), RWKV_WIDTH ** -0.5),
        "w_proj_moba": nrm(ks[20], (L, MOBA_WIDTH, D_MODEL), MOBA_WIDTH ** -0.5),
        "w_out": nrm(ks[21], (L, D_MODEL, D_MODEL), D_MODEL ** -0.5),
    }


def reference(x, norm_w, w_in, mu_r, mu_k, mu_v, mu_w, mu_a, w0, w_decay_up, a0, w_iclr_up,
              k_k, k_a, r_k, gn_w, gn_b, q_norm_w, k_norm_w, w_proj_rwkv, w_proj_moba, w_out):
    split_at = [int(i) for i in np.cumsum(IN_SPLITS)[:-1]]
    B, S, _ = x.shape
    for layer in range(DEPTH):
        h = rms_norm(x, norm_w[layer])
        p = h @ w_in[layer]
        (p_r, p_k, p_v, z_a, p_wd, p_ad,
         p_q, p_kq, p_vq, z_b, g_a, g_b) = jnp.split(p, split_at, axis=-1)
        y_a = rwkv7_mix(p_r, p_k, p_v, p_wd, p_ad, mu_r[layer], mu_k[layer], mu_v[layer],
                        mu_w[layer], mu_a[layer], w0[layer], w_decay_up[layer], a0[layer],
                        w_iclr_up[layer], k_k[layer], k_a[layer], r_k[layer], gn_w[layer],
                        gn_b[layer])
        y_a = y_a * jax.nn.silu(z_a)
        y_b = moba_attention(p_q, p_kq, p_vq, q_norm_w[layer], k_norm_w[layer])
        y_b = y_b * jax.nn.silu(z_b)
        merged = (jax.nn.sigmoid(g_a) * (y_a @ w_proj_rwkv[layer])
                  + jax.nn.sigmoid(g_b) * (y_b @ w_proj_moba[layer]))
        x = x + merged @ w_out[layer]
    return x
```

```python
import contextlib
import numpy as np
import ml_dtypes
import concourse.bass as bass
import concourse.mybir as mybir
from concourse.bass_utils import run_bass_kernel_spmd

F32 = mybir.dt.float32
BF16 = mybir.dt.bfloat16
AF = mybir.ActivationFunctionType
ALU = mybir.AluOpType
AX = mybir.AxisListType

S_TOK = 2048
D = 2048
KC = 16
IN_COLS = 12416
COL = dict(r=0, k=1024, v=2048, za=3072, wd=4096, q=4224, kq=5248, vq=6272, zb=7296, ga=8320, gb=10368)
CDEC = 0.6065306597126334
NEG = -1.0e30
STOP = 99
NOMERGE = False
STRICT = True

PP = dict(mu_r=0, mu_k=8, mu_v=16, w0=24, a0=32, k_k=40, k_a=48, r_k=56, gn_w=64, gn_b=72, norm_w=80,
          mu_wa=96, qnw=97, knw=98)
NPP = 100
CB = dict(ident=0, blk2=128, m_su=256, m_iu=384, m_sl=512, onesA=640, onesB=768, cmask=896)
NCB = 896 + 4 * 512
CF = dict(scanmask=0, pastneg=512, ownpos=768, swapP=1024)
NCF = 1152


_DTSZ = {}


def _dtsize(dt):
    s = _DTSZ.get(dt)
    if s is None:
        name = str(dt)
        s = 4 if '32' in name else 2 if '16' in name else 1 if '8' in name else 8
        _DTSZ[dt] = s
    return s


def box_of(ap):
    dims = ap.ap
    sz = _dtsize(ap.dtype)
    pstep, pcnt = dims[0]
    off = int(ap.offset)
    if pstep == 0:
        pstep = 1 << 40
    p0 = off // pstep
    f0 = off % pstep
    ext = 0
    for st, cn in dims[1:]:
        ext += abs(st) * (cn - 1)
    f1 = f0 + ext + 1
    if 'PSUM' in str(ap.space).upper():
        b0 = (f0 * sz) // 2048
        b1 = ((f1 * sz) - 1) // 2048
        return ('PS', ap.name, 0, 128, b0 * 2048, (b1 + 1) * 2048, True)
    return ('SB', ap.name, p0, p0 + pcnt, f0 * sz, f1 * sz, False)


class Op:
    __slots__ = ('idx', 'eng', 'fn', 'deps', 'signal', 'count', 'is_dma', 'dsem', 'dcount', 'prev_slot')

    def __init__(self, idx, eng, fn, is_dma):
        self.idx = idx
        self.eng = eng
        self.fn = fn
        self.deps = {}
        self.signal = False
        self.count = 0
        self.is_dma = is_dma
        self.dsem = None
        self.dcount = 0
        self.prev_slot = None


class Sched:
    ENGS = ('pe', 'act', 'dve', 'pool', 'sp')

    def __init__(self, nc, n_dma_slots=10):
        self.nc = nc
        self.ops = []
        self.recs = {}
        self.n_dma_slots = n_dma_slots

    def _touch(self, op, box, is_write):
        kind, name, p0, p1, f0, f1, excl = box
        lst = self.recs.setdefault(name, [])
        found = None
        for r in lst:
            if r[0] < p1 and p0 < r[1] and r[2] < f1 and f0 < r[3]:
                if r[4] is not None:
                    if (not is_write) or excl:
                        op.deps[r[4]] = True
                    else:
                        op.deps.setdefault(r[4], False)
                if is_write or excl:
                    for e, o in r[5].items():
                        op.deps.setdefault(o, False)
            if r[0] == p0 and r[1] == p1 and r[2] == f0 and r[3] == f1:
                found = r
        if found is None:
            found = [p0, p1, f0, f1, None, {}]
            lst.append(found)
        if is_write or excl:
            found[4] = op.idx
            found[5] = {}
        else:
            found[5][op.eng] = op.idx

    capture = None

    def add(self, eng, fn, reads=(), writes=(), dma=False, cost=0.5):
        if self.capture is not None:
            self.capture.append((eng, fn, list(reads), list(writes), dma, cost))
            return -1
        op = Op(len(self.ops), eng, fn, dma)
        self.ops.append(op)
        for ap in reads:
            if ap is not None and not isinstance(ap, (int, float)):
                self._touch(op, ap if isinstance(ap, tuple) else box_of(ap), False)
        for ap in writes:
            if ap is not None:
                self._touch(op, ap if isinstance(ap, tuple) else box_of(ap), True)
        op.deps.pop(op.idx, None)
        return op.idx

    def commit(self, lst):
        for it in lst:
            self.add(*it[:5])

    def merge_streams(self, streams):
        if NOMERGE:
            for st in streams:
                self.commit(st)
            return
        pos = [0] * len(streams)
        ready = [0.0] * len(streams)
        free = {e: 0.0 for e in self.ENGS}
        while True:
            best = None
            for si, st in enumerate(streams):
                if pos[si] >= len(st):
                    continue
                it = st[pos[si]]
                t = max(free[it[0]], ready[si])
                if best is None or t < best[0] - 1e-9:
                    best = (t, si)
            if best is None:
                break
            t, si = best
            it = streams[si][pos[si]]
            pos[si] += 1
            self.add(*it[:5])
            if it[4]:
                free[it[0]] = t + 0.06
                ready[si] = t + 0.06
            else:
                free[it[0]] = t + it[5]
                ready[si] = t + it[5] + 0.12

    def merge(self, main, bg):
        return self.merge_streams([main, bg])
        nb = len(bg)
        nm = max(len(main), 1)
        j = 0
        for i, it in enumerate(main):
            self.add(*it[:5])
            tgt = (i + 1) * nb // nm
            while j < tgt:
                self.add(*bg[j][:5])
                j += 1
        while j < nb:
            self.add(*bg[j])
            j += 1

    def emit(self, final_wait_ops=()):
        nc = self.nc
        ops = self.ops
        for op in ops:
            for d, raw in op.deps.items():
                dop = ops[d]
                if dop.is_dma:
                    continue
                if (not op.is_dma) and dop.eng == op.eng and (op.eng == 'pe' or (not raw and not STRICT)):
                    continue
                dop.signal = True
        for d in final_wait_ops:
            if not ops[d].is_dma:
                ops[d].signal = True
        cnt = {e: 0 for e in self.ENGS}
        for op in ops:
            if op.is_dma:
                continue
            if op.signal:
                cnt[op.eng] += 1
            op.count = cnt[op.eng]
        slot_state = {}
        for op in ops:
            if not op.is_dma:
                continue
            st = slot_state.setdefault(op.eng, {'next': 0, 'counts': [0] * self.n_dma_slots,
                                                'last': [None] * self.n_dma_slots})
            s = st['next']
            st['next'] = (s + 1) % self.n_dma_slots
            op.prev_slot = st['last'][s]
            st['counts'][s] += 16
            op.dsem = (op.eng, s)
            op.dcount = st['counts'][s]
            st['last'][s] = op.idx
        used = [e for e in self.ENGS if any(o.eng == e for o in ops)]
        if 'sp' not in used:
            used.append('sp')
        with contextlib.ExitStack() as es:
            sems = {e: es.enter_context(nc.semaphore('s_' + e)) for e in used}
            dsems = {}
            for e in slot_state:
                for s in range(self.n_dma_slots):
                    dsems[(e, s)] = es.enter_context(nc.semaphore('d_%s_%d' % (e, s)))
            block = es.enter_context(nc.Block())

            def run_engine(ename, eng):
                waited = {e: 0 for e in self.ENGS}
                dwaited = {}

                def wait_on(dop):
                    if dop.is_dma:
                        if dwaited.get(dop.dsem, 0) < dop.dcount:
                            eng.wait_ge(dsems[dop.dsem], dop.dcount)
                            dwaited[dop.dsem] = dop.dcount
                    elif dop.count > waited[dop.eng]:
                        eng.wait_ge(sems[dop.eng], dop.count)
                        waited[dop.eng] = dop.count

                for op in ops:
                    if op.eng != ename:
                        continue
                    for d in sorted(op.deps):
                        dop = ops[d]
                        raw = op.deps[d]
                        if (not dop.is_dma) and (not op.is_dma) and dop.eng == ename and (ename == 'pe' or (not raw and not STRICT)):
                            continue
                        wait_on(dop)
                    if op.is_dma and op.prev_slot is not None:
                        wait_on(ops[op.prev_slot])
                    ins = op.fn(eng)
                    if op.is_dma:
                        ins.then_inc(dsems[op.dsem], 16)
                    elif op.signal:
                        ins.then_inc(sems[ename], 1)
                if ename == 'sp':
                    for d in final_wait_ops:
                        wait_on(ops[d])

            @block.tensor
            def _(eng):
                run_engine('pe', eng)

            @block.scalar
            def _(eng):
                run_engine('act', eng)

            @block.vector
            def _(eng):
                run_engine('dve', eng)

            @block.gpsimd
            def _(eng):
                run_engine('pool', eng)

            @block.sync
            def _(eng):
                run_engine('sp', eng)
        return cnt


def build(n_hp_r=8, n_hp_m=8, do_final=True, dbg=False):
    nc = bass.Bass("TRN2", target_bir_lowering=False)
    x_d = nc.dram_tensor("x", [S_TOK, D], F32, kind="ExternalInput").ap()
    win_d = nc.dram_tensor("w_in", [D, IN_COLS], F32, kind="ExternalInput").ap()
    wpa_d = nc.dram_tensor("w_pa", [1024, D], F32, kind="ExternalInput").ap()
    wpb_d = nc.dram_tensor("w_pb", [1024, D], F32, kind="ExternalInput").ap()
    wo_d = nc.dram_tensor("w_out", [D, D], F32, kind="ExternalInput").ap()
    lora_d = nc.dram_tensor("lora", [128, 1024], F32, kind="ExternalInput").ap()
    pp_d = nc.dram_tensor("pp", [128, NPP], F32, kind="ExternalInput").ap()
    cb_d = nc.dram_tensor("cb", [128, NCB], BF16, kind="ExternalInput").ap()
    cf_d = nc.dram_tensor("cf", [128, NCF], F32, kind="ExternalInput").ap()
    ind_d = nc.dram_tensor("ind", [8, S_TOK], BF16, kind="ExternalInput").ap()
    out_d = nc.dram_tensor("out", [S_TOK, D], F32, kind="ExternalOutput").ap()
    scr_kind = "ExternalOutput" if dbg else "Internal"
    ya_d = nc.dram_tensor("ya_scr", [8, 128, S_TOK], BF16, kind=scr_kind).ap()
    yb_d = nc.dram_tensor("yb_scr", [8, 128, S_TOK], BF16, kind=scr_kind).ap()

    with contextlib.ExitStack() as es:
        def sb(name, shape, dt):
            return es.enter_context(nc.sbuf_tensor(name, shape, dt))

        hT = sb("hT", [128, KC, S_TOK], BF16)
        NW = 4
        wpool = [sb("wp%d" % i, [128, KC, 128], BF16) for i in range(NW)]
        cb = sb("cb_s", [128, NCB], BF16)
        cf = sb("cf_s", [128, NCF], F32)
        pp = sb("pp_s", [128, NPP], F32)
        loraD = sb("loraD_s", [128, 1024], BF16)
        loraI = sb("loraI_s", [128, 1024], BF16)
        TL = sb("TL", [128, S_TOK], BF16)
        RAWT = [sb("rawt%d" % s, [128, 4 * S_TOK], BF16) for s in range(2)]
        RAW = [[RAWT[s][:, j * S_TOK:(j + 1) * S_TOK] for j in range(4)] for s in range(2)]
        ARENA_N = 38 * 1024
        arena_t = sb("arena", [128, ARENA_N], BF16)
        PSA = es.enter_context(nc.psum_tensor("PSA", [128, 2048], F32))
        PSB = es.enter_context(nc.psum_tensor("PSB", [128, 2048], F32))

        S = Sched(nc)
        out_dmas = []

        class Arena:
            def __init__(self):
                self.off = 0

            def reset(self):
                self.off = 0

            def a(self, n, dt=BF16):
                nb = n * (2 if dt == F32 else 1)
                nb = (nb + 1) // 2 * 2
                assert self.off + nb <= ARENA_N, (self.off, nb)
                v = arena_t[:, self.off:self.off + nb]
                self.off += nb
                if dt == F32:
                    v = v.bitcast(F32)
                return v

        AR_ = Arena()

        def rd(*aps):
            return [a for a in aps if a is not None and not isinstance(a, (int, float))]

        def fsz(ap):
            n = 1
            for st, cn in ap.ap[1:]:
                n *= cn
            return n

        def ACT(out, in_, func, bias=None, scale=None, accum=None):
            kw = {}
            if bias is not None:
                kw['bias'] = bias
            if scale is not None:
                kw['scale'] = scale
            if accum is not None:
                kw['accum_out'] = accum
            return S.add('act', lambda e: e.activation(out=out, in_=in_, func=func, **kw),
                         reads=rd(in_, bias, scale), writes=[out, accum], cost=0.22 + fsz(out) / 1200.0)

        def TT(eng, out, in0, in1, op):
            return S.add(eng, lambda e: e.tensor_tensor(out=out, in0=in0, in1=in1, op=op),
                         reads=rd(in0, in1), writes=[out], cost=0.1 + fsz(out) / 960.0)

        def TS(eng, out, in0, s1, s2, op0, op1=None):
            if op1 is None:
                return S.add(eng, lambda e: e.tensor_scalar(out=out, in0=in0, scalar1=s1, scalar2=None, op0=op0),
                             reads=rd(in0, s1), writes=[out], cost=0.1 + fsz(out) / 960.0)
            return S.add(eng, lambda e: e.tensor_scalar(out=out, in0=in0, scalar1=s1, scalar2=s2, op0=op0, op1=op1),
                         reads=rd(in0, s1, s2), writes=[out], cost=0.1 + fsz(out) / 960.0)

        def STT(eng, out, in0, scalar, in1, op0, op1):
            eng = 'dve'
            return S.add(eng, lambda e: e.scalar_tensor_tensor(out=out, in0=in0, scalar=scalar, in1=in1,
                                                                op0=op0, op1=op1),
                         reads=rd(in0, scalar, in1), writes=[out], cost=0.1 + fsz(out) / 960.0)

        def CP(eng, out, in_):
            if eng == 'act':
                return S.add('act', lambda e: e.copy(out=out, in_=in_), reads=[in_], writes=[out],
                             cost=0.22 + fsz(out) / 1200.0)
            return S.add(eng, lambda e: e.tensor_copy(out=out, in_=in_), reads=[in_], writes=[out],
                         cost=0.1 + fsz(out) / 960.0)

        def MEMSET(eng, out, val):
            return S.add(eng, lambda e: e.memset(out, val), writes=[out], cost=1.0)

        def MM(items, reads, writes):
            def fn(e):
                ins = None
                for (o, l, r, st, sp) in items:
                    ins = e.matmul(o, lhsT=l, rhs=r, start=st, stop=sp)
                return ins
            c = 0.05
            for (o, l, r, st, sp) in items:
                c += max(0.065, fsz(o) / (600.0 if l.dtype == F32 else 2400.0))
            return S.add('pe', fn, reads=reads, writes=writes, cost=c)

        def TRS(items, reads, writes):
            def fn(e):
                ins = None
                for (o, i, idn) in items:
                    ins = e.transpose(o, i, idn)
                return ins
            return S.add('pe', fn, reads=reads, writes=writes, cost=0.05 + 0.11 * len(items))

        def DMA(q, out, in_, reads=(), writes=()):
            return S.add(q, lambda e: e.dma_start(out=out, in_=in_), reads=reads, writes=writes, dma=True)

        def ppc(name, j=0):
            c = PP[name] + j
            return pp[:, c:c + 1]

        ident = cb[:, CB['ident']:CB['ident'] + 128]
        blk2 = cb[:, CB['blk2']:CB['blk2'] + 128]
        m_su = cb[:, CB['m_su']:CB['m_su'] + 128]
        m_iu = cb[:, CB['m_iu']:CB['m_iu'] + 128]
        m_sl = cb[:, CB['m_sl']:CB['m_sl'] + 128]
        onesA = cb[:, CB['onesA']:CB['onesA'] + 128]
        onesB = cb[:, CB['onesB']:CB['onesB'] + 128]
        cmask = cb[:, CB['cmask']:CB['cmask'] + 2048].rearrange("p (a b) -> p a b", b=512)
        scanmask = cf[:, CF['scanmask']:CF['scanmask'] + 512]
        pastneg = cf[:, CF['pastneg']:CF['pastneg'] + 256]
        ownpos = cf[:, CF['ownpos']:CF['ownpos'] + 256]
        swapP = cf[:, CF['swapP']:CF['swapP'] + 128]

        def psb_f32(bank, n=512, off=0):
            return PSB[:, bank * 512 + off: bank * 512 + off + n]

        def psb_bf(bank, n=1024, off=0):
            v = PSB[:, bank * 512:(bank + 1) * 512].bitcast(BF16)
            return v[:, off:off + n]

        def psb2_bf(bank):
            return PSB[:, bank * 512:(bank + 2) * 512].bitcast(BF16)

        DMA('sp', cb[:], cb_d, writes=[cb[:]])
        DMA('sp', cf[:], cf_d, writes=[cf[:]])
        DMA('sp', pp[:], pp_d, writes=[pp[:]])
        S.add('pool', lambda e: e.memset(loraD[:], 0.0), writes=[loraD[:]])
        S.add('pool', lambda e: e.memset(loraI[:], 0.0), writes=[loraI[:]])
        DMA('pool', loraD[0:64, :], lora_d[0:64, :], writes=[loraD[0:64, :]])
        DMA('pool', loraI[64:128, :], lora_d[64:128, :], writes=[loraI[64:128, :]])

        wstate = {'n': 0}

        def wload_in(col):
            b = wpool[wstate['n'] % NW]
            wstate['n'] += 1
            src = win_d[:, col:col + 128].rearrange("(kc p) m -> p kc m", p=128)
            DMA('pool', b[:], src, writes=[b[:]])
            return b

        def wload_proj(wd, col):
            b = wpool[wstate['n'] % NW]
            wstate['n'] += 1
            src = wd[:, col:col + 128].rearrange("(kc p) m -> p kc m", p=128)
            DMA('pool', b[:, 0:8, :], src, writes=[b[:, 0:8, :]])
            return b

        def inproj(wbuf, nk, rhs_src, evac):
            for half in range(2):
                acc = PSA[:, 0:1024]
                items = []
                for k in range(nk):
                    for tt in range(2):
                        t0 = half * 1024 + tt * 512
                        items.append((acc[:, tt * 512:(tt + 1) * 512], wbuf[:, k, :], rhs_src[:, k, t0:t0 + 512],
                                      k == 0, k == nk - 1))
                for i0 in range(0, len(items), 8):
                    MM(items[i0:i0 + 8], reads=[wbuf[:, 0:nk, :], rhs_src[:, 0:nk, half * 1024:(half + 1) * 1024]], writes=[acc])
                evac(acc, half)

        AR_.reset()
        xt = [AR_.a(2048, F32) for _ in range(2)]
        hb = [AR_.a(2048) for _ in range(2)]
        junk = AR_.a(2048)
        ssb = AR_.a(16, F32)
        rsb = AR_.a(16, F32)
        nwb = pp[:, PP['norm_w']:PP['norm_w'] + 16].unsqueeze(2).to_broadcast([128, 16, 128])
        for tt in range(16):
            xtile = xt[tt % 2]
            DMA('sp', xtile, x_d[tt * 128:(tt + 1) * 128, :], writes=[xtile])
            ACT(junk, xtile, AF.Square, accum=ssb[:, tt:tt + 1])
            TS('dve', rsb[:, tt:tt + 1], ssb[:, tt:tt + 1], 1.0 / D, 1e-6, ALU.mult, ALU.add)
            ACT(rsb[:, tt:tt + 1], rsb[:, tt:tt + 1], AF.Sqrt)
            S.add('dve', (lambda o: (lambda e: e.reciprocal(out=o, in_=o)))(rsb[:, tt:tt + 1]),
                  reads=[rsb[:, tt:tt + 1]], writes=[rsb[:, tt:tt + 1]])
            if tt % 2:
                TS('dve', hb[tt % 2], xtile, rsb[:, tt:tt + 1], None, ALU.mult)
            else:
                ACT(hb[tt % 2], xtile, AF.Identity, scale=rsb[:, tt:tt + 1])
            pst = psb2_bf((tt % 2) * 2)
            TRS([(pst[:, k * 128:(k + 1) * 128], hb[tt % 2][:, k * 128:(k + 1) * 128], ident) for k in range(16)],
                reads=[hb[tt % 2], ident], writes=[pst])
            TT('dve', hT[:, :, tt * 128:(tt + 1) * 128], pst.rearrange("p (a b) -> p a b", b=128), nwb, ALU.mult)

        jobs = []
        jobs_seq = [('R', hp) for hp in range(n_hp_r)] + [('M', hp) for hp in range(n_hp_m)]
        for hp in range(n_hp_r):
            for nm in ('r', 'k', 'v', 'za'):
                jobs.append(('R', hp, nm, COL[nm] + hp * 128))
            if hp == 0:
                jobs.append(('R', 0, 'wd', COL['wd']))
        for hp in range(n_hp_m):
            for nm in ('q', 'kq', 'vq', 'zb'):
                jobs.append(('M', hp, nm, COL[nm] + hp * 128))
        PREF = NW - 1
        wq = []
        jstate = {'issued': 0}

        def next_w():
            while jstate['issued'] < len(jobs) and len(wq) < PREF + 1:
                wq.append(wload_in(jobs[jstate['issued']][3]))
                jstate['issued'] += 1
            return wq.pop(0)

        def ev_copy(dst, eng):
            def f(acc, half):
                CP(eng, dst[:, half * 1024:(half + 1) * 1024], acc)
            return f

        def ev_silu(dst):
            def f(acc, half):
                ACT(dst[:, half * 1024:(half + 1) * 1024], acc, AF.Silu)
            return f

        def inproj_job(ji):
            kind, hp = jobs_seq[ji]
            R0, R1, R2, R3 = RAW[ji % 2]
            inproj(next_w(), KC, hT, ev_copy(R0, 'act'))
            inproj(next_w(), KC, hT, ev_copy(R1, 'dve'))
            inproj(next_w(), KC, hT, ev_copy(R2, 'act'))
            inproj(next_w(), KC, hT, ev_silu(R3))

        def lora_prep():
            AR_.reset()
            lraw = AR_.a(2048)
            ld = AR_.a(2048)
            inproj(next_w(), KC, hT, ev_copy(lraw, 'dve'))
            TT('dve', ld[:, 1:2048], lraw[:, 0:2047], lraw[:, 1:2048], ALU.subtract)
            TS('dve', ld[:, 0:1], lraw[:, 0:1], -1.0, None, ALU.mult)
            STT('dve', TL[:], ld, ppc('mu_wa'), lraw, ALU.mult, ALU.add)
            ACT(TL[0:64, :], TL[0:64, :], AF.Tanh)

        def split_parts(lst, fracs):
            tot = float(sum(fracs))
            out = []
            acc = 0.0
            i0 = 0
            for f in fracs:
                acc += f
                i1 = int(round(len(lst) * acc / tot))
                out.append(lst[i0:i1])
                i0 = i1
            out[-1] = out[-1] + lst[i0:]
            return out

        def cap(fn, *args):
            S.capture = []
            fn(*args)
            lst = S.capture
            S.capture = None
            return lst

        def rwkv_headpair(hp, ji, nxt):
            Rr, Rk, Rv, SZ = RAW[ji % 2]
            if STOP <= 1:
                return
            AR_.reset()
            A = AR_.a
            GT = 256
            NG = S_TOK // GT
            Sf = [A(128, F32) for _ in range(2)]
            Sbf = A(128)
            Ssc = A(128, F32)
            Stmp = A(128, F32)
            Zsb = A(128)
            Usb = A(128)
            tinyb_t = A(2, F32)
            YAg = [A(GT) for _ in range(2)]
            ARfs = [A(2 * GT, F32) for _ in range(2)]
            KTfs = [A(GT, F32) for _ in range(2)]
            BTfs = [A(GT, F32) for _ in range(2)]
            KTbs = [A(GT) for _ in range(2)]
            BTbs = [A(GT) for _ in range(2)]
            vgs = [A(GT) for _ in range(2)]
            WCs = [A(4, F32) for _ in range(4)]
            BONs = [A(GT) for _ in range(4)]
            ARb = [A(2 * GT) for _ in range(2)]
            TM = [A(3 * GT) for _ in range(2)]
            AKs = [A(2 * GT) for _ in range(2)]
            RKs = [A(2 * GT) for _ in range(2)]
            RBs = [A(2 * GT) for _ in range(2)]
            TTs = [A(2 * GT) for _ in range(2)]
            Ytms = [A(GT, F32) for _ in range(2)]
            mean = A(4, F32); var = A(4, F32)
            YC = A(GT, F32); YQ = A(GT, F32); YN = A(GT)
            Dm = A(GT); SQ = A(GT); PROD = A(GT)
            rg = A(GT, F32); kg = A(GT, F32); SG = A(GT, F32); CS = A(GT, F32); ag = A(GT, F32)
            E1 = A(GT, F32); E2 = A(GT, F32); E3 = A(GT, F32)
            KK = A(GT, F32); RI = A(GT, F32); Tt = A(GT, F32); K2 = A(GT, F32); Bb = A(GT, F32)
            N0s = A(2 * GT); X0s = A(2 * GT)
            Ns = [A(2 * GT) for _ in range(2)]
            Xs = [A(2 * GT) for _ in range(2)]
            Ps = [A(2 * GT) for _ in range(2)]
            MEMSET('pool', tinyb_t, 1e-12)
            tinyb = tinyb_t[:, 0:1]
            MEMSET('pool', Sf[0], 0.0)
            MEMSET('pool', Sbf, 0.0)
            NT = GT // 128
            NM = NT * 2
            PA2 = PSA[:, 1024:1536]
            PA3 = PSA[:, 1536:2048]

            def v3(ap):
                return ap.rearrange("p (j t) -> p j t", t=128)

            def emit_A(g):
                c0 = g * GT
                par = g % 2
                ARf = ARfs[par]; KTf = KTfs[par]; BTf = BTfs[par]; vg = vgs[par]
                ARfv = ARf.rearrange("p (j w t) -> p j w t", w=2, t=128)
                for (raw, mu, dst) in ((Rr, 'mu_r', rg), (Rk, 'mu_k', kg), (Rv, 'mu_v', vg)):
                    if g == 0:
                        TT('dve', Dm[:, 1:GT], raw[:, 0:GT - 1], raw[:, 1:GT], ALU.subtract)
                        TS('dve', Dm[:, 0:1], raw[:, 0:1], -1.0, None, ALU.mult)
                    else:
                        TT('dve', Dm, raw[:, c0 - 1:c0 + GT - 1], raw[:, c0:c0 + GT], ALU.subtract)
                    STT('dve', dst, Dm, ppc(mu, hp), raw[:, c0:c0 + GT], ALU.mult, ALU.add)
                pu = PA2[:, 0:GT]
                pa_ = PA2[:, GT:2 * GT]
                MM([(pu, loraD[:, hp * 128:(hp + 1) * 128], TL[:, c0:c0 + GT], True, True)],
                   reads=[loraD[:, hp * 128:(hp + 1) * 128], TL[:, c0:c0 + GT]], writes=[pu])
                MM([(pa_, loraI[:, hp * 128:(hp + 1) * 128], TL[:, c0:c0 + GT], True, True)],
                   reads=[loraI[:, hp * 128:(hp + 1) * 128], TL[:, c0:c0 + GT]], writes=[pa_])
                ACT(SG, pu, AF.Sigmoid, bias=ppc('w0', hp))
                ACT(ag, pa_, AF.Sigmoid, bias=ppc('a0', hp))
                S.add('dve', (lambda o, m, d1: (lambda e: e.tensor_tensor_scan(out=o, data0=m, data1=d1, initial=0.0,
                                                                                op0=ALU.mult, op1=ALU.add)))(CS, scanmask[:, 0:GT], SG),
                      reads=[scanmask[:, 0:GT], SG], writes=[CS], cost=0.1 + 2 * GT / 960.0)
                ACT(E1, CS, AF.Exp, scale=-CDEC)
                ACT(WCs[g % 4], CS.rearrange("p (c t) -> p c t", t=64)[:, :, 63], AF.Exp, scale=-CDEC)
                ACT(E3, CS, AF.Exp, scale=CDEC)
                TT('dve', SG, CS, SG, ALU.subtract)
                ACT(E2, SG, AF.Exp, scale=-CDEC)
                ACT(KK, kg, AF.Identity, scale=ppc('k_k', hp))
                ACT(SQ, kg, AF.Square, scale=ppc('k_k', hp))
                pk = PA2[:, 0:GT]
                MM([(pk, blk2, SQ, True, True)], reads=[blk2, SQ], writes=[pk])
                ACT(RI, pk, AF.Ln, bias=tinyb)
                ACT(RI, RI, AF.Exp, scale=-0.5)
                TT('dve', KK, KK, RI, ALU.mult)
                TS('dve', Tt, ag, -1.0, ppc('k_a', hp), ALU.add, ALU.mult)
                STT('dve', K2, Tt, 1.0, kg, ALU.add, ALU.mult)
                TT('dve', Bb, KK, ag, ALU.mult)
                STT('dve', PROD, rg, ppc('r_k', hp), K2, ALU.mult, ALU.mult)
                pb_ = PA2[:, GT:2 * GT]
                MM([(pb_, blk2, PROD, True, True)], reads=[blk2, PROD], writes=[pb_])
                TT('dve', BONs[g % 4], pb_, vg, ALU.mult)
                TT('dve', ARfv[:, :, 1, :], v3(rg), v3(E1), ALU.mult)
                STT('dve', ARfv[:, :, 0, :], v3(KK), -1.0, v3(E2), ALU.mult, ALU.mult)
                TT('dve', KTf, K2, E3, ALU.mult)
                TT('dve', BTf, Bb, E3, ALU.mult)
                CP('act', KTbs[par], KTf)
                CP('act', BTbs[par], BTf)

            def emit_B(g):
                par = g % 2
                ARf = ARfs[par]; KTf = KTfs[par]; BTf = BTfs[par]; vg = vgs[par]
                KTb = KTbs[par]; BTb = BTbs[par]
                CP('act', ARb[par], ARf)
                ptm = PA3.bitcast(BF16)[:, 0:3 * GT]
                items = []
                for q, src in enumerate((vg, KTb, BTb)):
                    for j in range(NT):
                        items.append((ptm[:, (q * NT + j) * 128:(q * NT + j + 1) * 128], src[:, j * 128:(j + 1) * 128], ident))
                TRS(items, reads=[vg, KTb, BTb, ident], writes=[ptm])
                CP('act', TM[par], ptm)
                for j in range(NT):
                    items = []
                    for h in range(2):
                        hr = slice(h * 64, h * 64 + 64)
                        rhs_ar = ARf[hr, j * 256:(j + 1) * 256]
                        bh = psb_f32(h)
                        bzh = PA3[:, h * 128:(h + 1) * 128]
                        items.append((bh[:, 0:256], KTf[hr, j * 128:(j + 1) * 128], rhs_ar, True, True))
                        items.append((bh[:, 256:512], BTf[hr, j * 128:(j + 1) * 128], rhs_ar, True, True))
                        items.append((bzh, ARf[hr, j * 256:j * 256 + 128], BTf[hr, j * 128:(j + 1) * 128], True, True))
                    MM(items, reads=[KTf, BTf, ARf], writes=[PSB[:, 0:1024], PA3])
                    bxy = PSB[:, 0:1024].rearrange("p (h w t) -> p h w t", h=2, w=4)
                    bzv = PA3[:, 0:256].rearrange("p (h t) -> p h t", h=2)
                    msu_b = m_su.unsqueeze(1).to_broadcast([128, 2, 128])
                    miu_b = m_iu.unsqueeze(1).to_broadcast([128, 2, 128])
                    msl_b = m_sl.unsqueeze(1).to_broadcast([128, 2, 128])

                    def dst(t):
                        return t.rearrange("p (j h t) -> p j h t", h=2, t=128)[:, j, :, :]
                    TT('dve', dst(AKs[par]), bxy[:, :, 0, :], msu_b, ALU.mult)
                    TT('dve', dst(RKs[par]), bxy[:, :, 1, :], miu_b, ALU.mult)
                    TT('dve', dst(N0s), bxy[:, :, 2, :], msu_b, ALU.mult)
                    TT('dve', dst(RBs[par]), bxy[:, :, 3, :], miu_b, ALU.mult)
                    TT('dve', dst(X0s), bzv, msl_b, ALU.mult)

                def m8(t):
                    return t.rearrange("p (m t) -> p m t", t=128)
                W = NM * 128
                TT('dve', m8(Ps[0]), m8(N0s), ident.unsqueeze(1).to_broadcast([128, NM, 128]), ALU.add)
                Ncur, Xcur, Pcur = N0s, X0s, Ps[0]
                for lvl in range(1, 6):
                    pX = PSB[:, 0:W]
                    MM([(pX[:, m * 128:(m + 1) * 128], Ncur[:, m * 128:(m + 1) * 128], Xcur[:, m * 128:(m + 1) * 128], True, True)
                        for m in range(NM)], reads=[Ncur, Xcur], writes=[pX])
                    Xn = Xs[lvl % 2]
                    if lvl < 5:
                        pN = PSB[:, 512:512 + W]
                        MM([(pN[:, m * 128:(m + 1) * 128], Xcur[:, m * 128:(m + 1) * 128], Ncur[:, m * 128:(m + 1) * 128], True, True)
                            for m in range(NM)], reads=[Ncur, Xcur], writes=[pN])
                    CP('act', Xn, pX)
                    if lvl < 5:
                        Nn = Ns[lvl % 2]
                        CP('dve', Nn, pN)
                    pP = PA3[:, 0:W]
                    MM([(pP[:, m * 128:(m + 1) * 128], Xn[:, m * 128:(m + 1) * 128], Pcur[:, m * 128:(m + 1) * 128], True, True)
                        for m in range(NM)], reads=[Xn, Pcur], writes=[pP])
                    Pn = TTs[par] if lvl == 5 else Ps[lvl % 2]
                    TT('dve', Pn, pP, Pcur, ALU.add)
                    Pcur = Pn
                    Xcur = Xn
                    if lvl < 5:
                        Ncur = Nn

            def emit_chain(g):
                par = g % 2
                ARbv = ARb[par].rearrange("p (j w t) -> p j w t", w=2, t=128)
                TMv = TM[par].rearrange("p (q j f) -> p q j f", q=3, f=128)
                TTg = TTs[par].rearrange("p (j h t) -> p j h t", h=2, t=128)
                AKv = AKs[par].rearrange("p (j h t) -> p j h t", h=2, t=128)
                RKv = RKs[par].rearrange("p (j h t) -> p j h t", h=2, t=128)
                RBv = RBs[par].rearrange("p (j h t) -> p j h t", h=2, t=128)
                Ytv = Ytms[par].rearrange("p (j f) -> p j f", f=128)
                for cl in range(2 * NT):
                    c = g * 2 * NT + cl
                    j = cl // 2
                    pr = slice((cl % 2) * 64, (cl % 2) * 64 + 64)
                    Scur = Sf[c % 2]
                    Snxt = Sf[(c + 1) % 2]
                    wc = WCs[g % 4][:, cl:cl + 1]
                    Zp = PSB[:, 1536:1664]
                    Up = PSB[:, 1664:1792]
                    Yp = PSB[:, 1792:1920]
                    Sp = PSB[:, 1920:2048]
                    ACT(Ssc, Scur, AF.Identity, scale=wc)
                    MM([(Zp[:, 0:64], AKv[pr, j, 0, :], TMv[pr, 0, j, 0:64], True, False),
                        (Zp[:, 64:128], AKv[pr, j, 1, :], TMv[pr, 0, j, 64:128], False, False),
                        (Zp[:, 0:128], ARbv[:, j, 0, :], Sbf, False, True)],
                       reads=[AKs[par], TM[par], ARb[par], Sbf], writes=[Zp])
                    CP('act', Zsb[pr, :], Zp[pr, :])
                    MM([(Up[:, 0:64], TTg[pr, j, 0, :], Zsb[pr, 0:64], True, False),
                        (Up[:, 64:128], TTg[pr, j, 1, :], Zsb[pr, 64:128], False, True)],
                       reads=[TTs[par], Zsb[pr, :]], writes=[Up])
                    CP('dve', Usb[pr, :], Up[pr, :])
                    MM([(Yp[:, 0:128], ARbv[:, j, 1, :], Sbf, True, False),
                        (Yp[:, 0:64], RBv[pr, j, 0, :], Usb[pr, 0:64], False, False),
                        (Yp[:, 0:64], RKv[pr, j, 0, :], TMv[pr, 0, j, 0:64], False, False),
                        (Yp[:, 64:128], RBv[pr, j, 1, :], Usb[pr, 64:128], False, False),
                        (Yp[:, 64:128], RKv[pr, j, 1, :], TMv[pr, 0, j, 64:128], False, True)],
                       reads=[ARb[par], Sbf, RBs[par], RKs[par], Usb[pr, :], TM[par]], writes=[Yp])
                    CP('act', Ytv[pr, j, :], Yp[pr, :])
                    MM([(Sp, TMv[pr, 2, j, :], Usb[pr, :], True, False),
                        (Sp, TMv[pr, 1, j, :], TMv[pr, 0, j, :], False, True)],
                       reads=[TM[par], Usb[pr, :]], writes=[Sp])
                    TT('dve', Stmp, Sp, blk2, ALU.mult)
                    STT('dve', Sbf, Stmp, wc, Ssc, ALU.mult, ALU.add)
                    STT('dve', Snxt, Stmp, wc, Ssc, ALU.mult, ALU.add)

            def emit_post(g):
                par = g % 2
                c0 = g * GT
                Ytm = Ytms[par]
                Y4 = Ytm.rearrange("p (m f) -> p m f", f=64)
                YC4 = YC.rearrange("p (m f) -> p m f", f=64)
                YQ4 = YQ.rearrange("p (m f) -> p m f", f=64)
                nm_ = 2 * NT
                S.add('dve', (lambda o, i: (lambda e: e.tensor_reduce(out=o, in_=i, axis=AX.X, op=ALU.add)))(mean, Y4),
                      reads=[Ytm], writes=[mean])
                TS('dve', mean, mean, 1.0 / 64, None, ALU.mult)
                TT('dve', YC4, Y4, mean.unsqueeze(2).to_broadcast([128, nm_, 64]), ALU.subtract)
                ACT(YQ, YC, AF.Square)
                S.add('dve', (lambda o, i: (lambda e: e.tensor_reduce(out=o, in_=i, axis=AX.X, op=ALU.add)))(var, YQ4),
                      reads=[YQ], writes=[var])
                TS('dve', var, var, 1.0 / 64, 64e-5, ALU.mult, ALU.add)
                ACT(var, var, AF.Ln)
                ACT(var, var, AF.Exp, scale=-0.5)
                TT('dve', YN.rearrange("p (m f) -> p m f", f=64), YC4, var.unsqueeze(2).to_broadcast([128, nm_, 64]), ALU.mult)
                pyt = psb_bf(2, GT)
                TRS([(pyt[:, jj * 128:(jj + 1) * 128], YN[:, jj * 128:(jj + 1) * 128], ident) for jj in range(NT)],
                    reads=[YN, ident], writes=[pyt])
                YF = YC
                ACT(YF, pyt, AF.Identity, bias=ppc('gn_b', hp), scale=ppc('gn_w', hp))
                TT('dve', YF, YF, BONs[g % 4], ALU.add)
                TT('dve', YAg[par], YF, SZ[:, c0:c0 + GT], ALU.mult)
                DMA('sp', ya_d[hp, :, c0:c0 + GT], YAg[par], reads=[YAg[par]], writes=[('DR', 'ya', hp, hp + 1, 0, 1, False)])

            nparts = split_parts(nxt, [1.0] * (NG + 3))
            for it in range(-2, NG + 1):
                streams = []
                if 0 <= it < NG:
                    streams.append(cap(emit_chain, it))
                if 0 <= it + 1 < NG:
                    streams.append(cap(emit_B, it + 1))
                if 0 <= it - 1 < NG:
                    streams.append(cap(emit_post, it - 1))
                if 0 <= it + 2 < NG:
                    streams.append(cap(emit_A, it + 2))
                streams.append(nparts[it + 2])
                S.merge_streams(streams)

        def nxt_stream(ji):
            return cap(inproj_job, ji + 1) if ji + 1 < len(jobs_seq) else []

        if jobs_seq:
            inproj_job(0)
        if n_hp_r > 0:
            lora_prep()
        for hp in range(n_hp_r):
            rwkv_headpair(hp, hp, nxt_stream(hp))

        AR_.reset()
        if n_hp_m > 0:
            QA = AR_.a(2048); QB = AR_.a(2048); KA = AR_.a(2048); KB = AR_.a(2048)
            VA = AR_.a(2048); VB = AR_.a(2048)
            for t in (QA, QB, KA, KB):
                MEMSET('pool', t, 0.0)
            MEMSET('pool', VA, 1.0)
            MEMSET('pool', VB, 1.0)
            DMA('sp', KA[64:72, :], ind_d, writes=[KA[64:72, :]])
            DMA('sp', KB[0:8, :], ind_d, writes=[KB[0:8, :]])
        m_base = AR_.off

        def moba_headpair(hp, ji, nxt):
            Rq, Rkq, Rvq, SZ = RAW[ji % 2]
            AR_.off = m_base
            A = AR_.a
            nparts = split_parts(nxt, [2.0, 1.0, 2.0, 3.0, 4.0])
            S.capture = []
            SQm = A(512); RIm = A(512, F32)
            Qf = A(2048, F32); Kf = A(2048, F32)
            for (raw, wname, dA, dB, Ff) in ((Rq, 'qnw', QA, QB, Qf), (Rkq, 'knw', KA, KB, Kf)):
                for g in range(4):
                    c0 = g * 512
                    ACT(SQm, raw[:, c0:c0 + 512], AF.Square)
                    pk = psb_f32(g % 2)
                    MM([(pk, blk2, SQm, True, True)], reads=[blk2, SQm], writes=[pk])
                    ACT(RIm, pk, AF.Ln, bias=ppc_eps, scale=1.0 / 64)
                    ACT(RIm, RIm, AF.Exp, scale=-0.5)
                    STT('dve', Ff[:, c0:c0 + 512], raw[:, c0:c0 + 512], ppc(wname), RIm, ALU.mult, ALU.mult)
                    CP('act', dA[0:64, c0:c0 + 512], Ff[0:64, c0:c0 + 512])
                    CP('dve', dB[64:128, c0:c0 + 512], Ff[64:128, c0:c0 + 512])
            pv = psb2_bf(2)
            TRS([(pv[:, t * 128:(t + 1) * 128], Rvq[:, t * 128:(t + 1) * 128], ident) for t in range(16)],
                reads=[Rvq, ident], writes=[pv])
            pv3 = pv.rearrange("p (t f) -> p t f", f=128)
            CP('act', VA.rearrange("p (t f) -> p t f", f=128)[:, :, 0:64], pv3[:, :, 0:64])
            CP('dve', VB.rearrange("p (t f) -> p t f", f=128)[:, :, 64:128], pv3[:, :, 64:128])
            kmp = A(16, F32)
            MEMSET('pool', kmp, 0.0)
            S.add('dve', (lambda o, i: (lambda e: e.tensor_reduce(out=o, in_=i, axis=AX.X, op=ALU.add)))(
                kmp[0:64, 0:8], Kf[0:64, :].rearrange("p (n t) -> p n t", t=256)), reads=[Kf[0:64, :]], writes=[kmp[0:64, 0:8]])
            S.add('dve', (lambda o, i: (lambda e: e.tensor_reduce(out=o, in_=i, axis=AX.X, op=ALU.add)))(
                kmp[64:128, 8:16], Kf[64:128, :].rearrange("p (n t) -> p n t", t=256)), reads=[Kf[64:128, :]], writes=[kmp[64:128, 8:16]])
            pg = psb_f32(0, 256)
            MM([(pg[:, qt * 16:(qt + 1) * 16], Qf[:, qt * 128:(qt + 1) * 128], kmp, True, True) for qt in range(16)],
               reads=[Qf, kmp], writes=[pg])
            GM = A(256, F32); G2 = A(256, F32); EQ = A(256, F32); mx = A(32, F32); BI = A(256)
            g3 = lambda t: t.rearrange("p (m n) -> p m n", n=8)
            mxb = mx.unsqueeze(2).to_broadcast([128, 32, 8])

            def rmax(o, i):
                S.add('dve', (lambda o_, i_: (lambda e: e.tensor_reduce(out=o_, in_=i_, axis=AX.X, op=ALU.max)))(o, g3(i)),
                      reads=[i], writes=[o])
            TT('dve', GM, pg, pastneg, ALU.add)
            rmax(mx, GM)
            TT('dve', g3(EQ), g3(GM), mxb, ALU.is_ge)
            STT('dve', G2, EQ, NEG, GM, ALU.mult, ALU.add)
            rmax(mx, G2)
            TT('dve', g3(EQ), g3(G2), mxb, ALU.is_ge)
            STT('dve', G2, EQ, NEG, G2, ALU.mult, ALU.add)
            rmax(mx, G2)
            TT('dve', g3(EQ), g3(GM), mxb, ALU.is_ge)
            TT('dve', EQ, EQ, ownpos, ALU.max)
            TS('dve', BI, EQ, -1.0, -NEG, ALU.add, ALU.mult)
            pbt = psb_bf(1, 1024)
            pbt2 = psb_bf(2, 1024)
            TRS([((pbt if qt < 8 else pbt2)[0:16, (qt % 8) * 128:(qt % 8 + 1) * 128], BI[:, qt * 16:(qt + 1) * 16], ident)
                 for qt in range(16)], reads=[BI, ident], writes=[pbt, pbt2])
            BT_ = A(2048)
            CP('act', BT_[0:16, 0:1024], pbt[0:16, :])
            CP('act', BT_[0:16, 1024:2048], pbt2[0:16, :])
            DMA('sp', QA[64:72, :], BT_[0:8, :], reads=[BT_[0:8, :]], writes=[QA[64:72, :]])
            DMA('sp', QB[0:8, :], BT_[8:16, :], reads=[BT_[8:16, :]], writes=[QB[0:8, :]])
            pro = S.capture
            S.capture = None
            S.merge_streams([pro, nparts[0]])
            PT = [A(512) for _ in range(3)]
            RS = A(512, F32)
            RW = A(512, F32)
            YO = A(512, F32)
            YB = A(2048)
            VA3 = VA.rearrange("p (t f) -> p t f", f=128)
            VB3 = VB.rearrange("p (t f) -> p t f", f=128)
            OpA = psb_f32(2)
            OpB = psb_f32(3)
            for QT in range(4):
                S.capture = []
                q0 = QT * 512
                units = [(h, kt) for kt in range(4 * QT + 4) for h in range(2)]
                nkt = 4 * QT + 4

                def qk(ui):
                    h, kt = units[ui]
                    Kh = KA if h == 0 else KB
                    Qh = QA if h == 0 else QB
                    sp_ = psb_f32(ui % 2)
                    diag = kt >= 4 * QT
                    items = [(sp_, Kh[:, kt * 128:(kt + 1) * 128], Qh[:, q0:q0 + 512], True, not diag)]
                    rds = [Kh[:, kt * 128:(kt + 1) * 128], Qh[:, q0:q0 + 512]]
                    if diag:
                        items.append((sp_, ident, cmask[:, kt - 4 * QT, :], False, True))
                        rds += [ident, cmask[:, kt - 4 * QT, :]]
                    MM(items, reads=rds, writes=[sp_])
                qk(0)
                for ui, (h, kt) in enumerate(units):
                    if ui + 1 < len(units):
                        qk(ui + 1)
                    sp_ = psb_f32(ui % 2)
                    pt = PT[ui % 3]
                    ACT(pt, sp_, AF.Exp, scale=0.125)
                    Vh = VA3 if h == 0 else VB3
                    Oh = OpA if h == 0 else OpB
                    MM([(Oh, Vh[:, kt, :], pt, kt == 0, kt == nkt - 1)], reads=[Vh[:, kt, :], pt], writes=[Oh])
                ACT(RS[64:128, :], OpA[64:128, :], AF.Ln)
                ACT(RS[0:64, :], OpB[0:64, :], AF.Ln)
                ACT(RS, RS, AF.Exp, scale=-1.0)
                pw = psb_f32(0)
                MM([(pw, swapP, RS, True, True)], reads=[swapP, RS], writes=[pw])
                CP('act', RW, pw)
                TT('dve', YO[0:64, :], OpA[0:64, :], RW[0:64, :], ALU.mult)
                TT('dve', YO[64:128, :], OpB[64:128, :], RW[64:128, :], ALU.mult)
                TT('dve', YB[:, q0:q0 + 512], YO, SZ[:, q0:q0 + 512], ALU.mult)
                att = S.capture
                S.capture = None
                S.merge_streams([att, nparts[QT + 1]])
            out_dmas.append(DMA('sp', yb_d[hp], YB, reads=[YB], writes=[('DR', 'yb', hp, hp + 1, 0, 1, False)]))

        if n_hp_m > 0:
            epsb = AR_.a(2, F32)
            MEMSET('pool', epsb, 1e-6)
            ppc_eps = epsb[:, 0:1]
            m_base = AR_.off
        for hp in range(n_hp_m):
            moba_headpair(hp, n_hp_r + hp, nxt_stream(n_hp_r + hp))

        if do_final:
            for th in range(2):
                AR_.reset()
                t0 = th * 1024
                YAh = RAWT[0][:, :]; YBh = RAWT[1][:, :]
                YA3 = YAh.rearrange("p (k t) -> p k t", t=1024)
                YB3 = YBh.rearrange("p (k t) -> p k t", t=1024)
                for k in range(8):
                    DMA('sp', YA3[:, k, :], ya_d[k, :, t0:t0 + 1024], reads=[('DR', 'ya', k, k + 1, 0, 1, False)], writes=[YA3[:, k, :]])
                    DMA('sp', YB3[:, k, :], yb_d[k, :, t0:t0 + 1024], reads=[('DR', 'yb', k, k + 1, 0, 1, False)], writes=[YB3[:, k, :]])
                MG = AR_.a(16 * 1024)
                MG3 = MG.rearrange("p (k t) -> p k t", t=1024)
                sga = AR_.a(1024); sgb = AR_.a(1024); m1 = AR_.a(1024, F32)
                fj = []
                for c in range(16):
                    fj.append(('in', COL['ga'] + c * 128))
                    fj.append(('pa', c * 128))
                    fj.append(('in', COL['gb'] + c * 128))
                    fj.append(('pb', c * 128))
                fq = []
                fst = {'i': 0}

                def next_fw():
                    while fst['i'] < len(fj) and len(fq) < NW:
                        kind, col = fj[fst['i']]
                        fst['i'] += 1
                        if kind == 'in':
                            fq.append(wload_in(col))
                        elif kind == 'pa':
                            fq.append(wload_proj(wpa_d, col))
                        else:
                            fq.append(wload_proj(wpb_d, col))
                    return fq.pop(0)

                def run_half_proj(wbuf, nk, src3, evac):
                    acc = PSA[:, (run_half_proj.n % 2) * 1024:(run_half_proj.n % 2 + 1) * 1024]
                    run_half_proj.n += 1
                    items = []
                    for k in range(nk):
                        for tt in range(2):
                            items.append((acc[:, tt * 512:(tt + 1) * 512], wbuf[:, k, :], src3(k, tt), k == 0, k == nk - 1))
                    MM(items, reads=[wbuf[:, 0:nk, :]] + run_half_proj.rd, writes=[acc])
                    evac(acc)
                run_half_proj.n = 0
                for c in range(16):
                    run_half_proj.rd = [hT[:, :, t0:t0 + 1024]]
                    run_half_proj(next_fw(), KC, lambda k, tt: hT[:, k, t0 + tt * 512:t0 + (tt + 1) * 512],
                                  lambda acc: ACT(sga, acc, AF.Sigmoid))
                    run_half_proj.rd = [YAh]
                    run_half_proj(next_fw(), 8, lambda k, tt: YA3[:, k, tt * 512:(tt + 1) * 512],
                                  lambda acc: TT('dve', m1, acc, sga, ALU.mult))
                    run_half_proj.rd = [hT[:, :, t0:t0 + 1024]]
                    run_half_proj(next_fw(), KC, lambda k, tt: hT[:, k, t0 + tt * 512:t0 + (tt + 1) * 512],
                                  lambda acc: ACT(sgb, acc, AF.Sigmoid))
                    run_half_proj.rd = [YBh]

                    def ev_b(acc, c=c):
                        TT('dve', sgb, acc, sgb, ALU.mult)
                        TT('dve', MG3[:, c, :], m1, sgb, ALU.add)
                    run_half_proj(next_fw(), 8, lambda k, tt: YB3[:, k, tt * 512:(tt + 1) * 512], ev_b)
                WO = [AR_.a(16 * 256) for _ in range(2)]
                XR = [AR_.a(256, F32) for _ in range(2)]
                OT = [AR_.a(256, F32) for _ in range(2)]
                n_o = 0
                for c8 in range(8):
                    wo = WO[c8 % 2]
                    wo3 = wo.rearrange("p (k m) -> p k m", m=256)
                    DMA('pool', wo3, wo_d[:, c8 * 256:(c8 + 1) * 256].rearrange("(kc p) m -> p kc m", p=128), writes=[wo])
                    for tl in range(8):
                        tok0 = t0 + tl * 128
                        xr = XR[n_o % 2]
                        ot = OT[n_o % 2]
                        acc = PSB[:, (n_o % 4) * 512:(n_o % 4) * 512 + 256]
                        n_o += 1
                        DMA('sp', xr, x_d[tok0:tok0 + 128, c8 * 256:(c8 + 1) * 256], writes=[xr])
                        MM([(acc, MG3[:, k, tl * 128:(tl + 1) * 128], wo3[:, k, :], k == 0, k == 15) for k in range(16)],
                           reads=[MG, wo], writes=[acc])
                        TT('dve', ot, acc, xr, ALU.add)
                        out_dmas.append(DMA('sp', out_d[tok0:tok0 + 128, c8 * 256:(c8 + 1) * 256], ot, reads=[ot]))

        cnt = S.emit(final_wait_ops=out_dmas)
    return nc, cnt, len(S.ops)


def _consts():
    bf = ml_dtypes.bfloat16
    cbv = np.zeros((128, NCB), np.float32)
    idx = np.arange(128)
    cbv[:, CB['ident']:CB['ident'] + 128] = np.eye(128)
    same = (idx[:, None] // 64) == (idx[None, :] // 64)
    cbv[:, CB['blk2']:CB['blk2'] + 128] = same
    cbv[:, CB['m_su']:CB['m_su'] + 128] = same & (idx[:, None] < idx[None, :])
    cbv[:, CB['m_iu']:CB['m_iu'] + 128] = same & (idx[:, None] <= idx[None, :])
    cbv[:, CB['m_sl']:CB['m_sl'] + 128] = same & (idx[:, None] > idx[None, :])
    cbv[:, CB['onesA']:CB['onesA'] + 64] = 1.0
    cbv[:, CB['onesB'] + 64:CB['onesB'] + 128] = 1.0
    cm = np.zeros((128, 4, 512), np.float32)
    for ktl in range(4):
        kpos = ktl * 128 + idx[:, None]
        qpos = np.arange(512)[None, :]
        kb = kpos // 256
        qb = qpos // 256
        ok = np.where(kb == qb, kpos <= qpos, kb < qb)
        cm[:, ktl, :] = np.where(ok, 0.0, NEG)
    cbv[:, CB['cmask']:CB['cmask'] + 2048] = cm.reshape(128, 2048)
    cfv = np.zeros((128, NCF), np.float32)
    sm = np.ones(512, np.float32)
    sm[::64] = 0.0
    cfv[:, CF['scanmask']:CF['scanmask'] + 512] = sm[None, :]
    pn = np.zeros((16, 2, 8), np.float32)
    op = np.zeros((16, 2, 8), np.float32)
    for qt in range(16):
        qb = qt // 2
        for n in range(8):
            pn[qt, :, n] = 0.0 if n < qb else NEG
            op[qt, :, n] = 1.0 if n >= qb else 0.0
    cfv[:, CF['pastneg']:CF['pastneg'] + 256] = pn.reshape(1, 256)
    cfv[:, CF['ownpos']:CF['ownpos'] + 256] = op.reshape(1, 256)
    cfv[:, CF['swapP']:CF['swapP'] + 128] = np.roll(np.eye(128, dtype=np.float32), 64, axis=1)
    ind = np.zeros((8, S_TOK), np.float32)
    for n in range(8):
        ind[n, n * 256:(n + 1) * 256] = 1.0
    return cbv.astype(bf), cfv, ind.astype(bf)


def _pack_params(i):
    ppv = np.zeros((128, NPP), np.float32)

    def fm(v):
        return np.ascontiguousarray(v.reshape(-1, 128).T)
    for nm in ('mu_r', 'mu_k', 'mu_v', 'w0', 'a0', 'k_k', 'k_a', 'r_k', 'gn_w', 'gn_b'):
        ppv[:, PP[nm]:PP[nm] + 8] = fm(i[nm][0])
    ppv[:, PP['norm_w']:PP['norm_w'] + 16] = fm(i['norm_w'][0])
    ppv[0:64, PP['mu_wa']] = i['mu_w'][0]
    ppv[64:128, PP['mu_wa']] = i['mu_a'][0]
    ppv[0:64, PP['qnw']] = i['q_norm_w'][0]
    ppv[64:128, PP['qnw']] = i['q_norm_w'][0]
    ppv[0:64, PP['knw']] = i['k_norm_w'][0]
    ppv[64:128, PP['knw']] = i['k_norm_w'][0]
    lora = np.concatenate([i['w_decay_up'][0], i['w_iclr_up'][0]], axis=0).astype(np.float32)
    return ppv, np.ascontiguousarray(lora)


_CACHE = {}


def make_in_maps(inputs, n_cores=8):
    i = {k: np.asarray(v) for k, v in inputs.items()}
    cbv, cfv, ind = _consts()
    ppv, lora = _pack_params(i)
    shared = dict(w_in=np.ascontiguousarray(i['w_in'][0]), w_pa=np.ascontiguousarray(i['w_proj_rwkv'][0]),
                  w_pb=np.ascontiguousarray(i['w_proj_moba'][0]), w_out=np.ascontiguousarray(i['w_out'][0]),
                  lora=lora, pp=ppv, cb=cbv, cf=cfv, ind=ind)
    maps = []
    for c in range(n_cores):
        m = dict(shared)
        m['x'] = np.ascontiguousarray(i['x'][c])
        maps.append(m)
    return maps


def kernel(**inputs):
    if 'nc' not in _CACHE:
        _CACHE['nc'] = build()[0]
    nc = _CACHE['nc']
    maps = make_in_maps(inputs, 8)
    res = run_bass_kernel_spmd(nc, maps, core_ids=list(range(8)))
    out = np.stack([np.asarray(r['out']) for r in res.results], axis=0)
    return out.astype(np.float32)
```

```python
import contextlib
import numpy as np
import ml_dtypes
import concourse.bass as bass
import concourse.mybir as mybir
from concourse.bass_utils import run_bass_kernel_spmd

F32 = mybir.dt.float32
BF16 = mybir.dt.bfloat16
AF = mybir.ActivationFunctionType
ALU = mybir.AluOpType
AX = mybir.AxisListType

S_TOK = 2048
D = 2048
KC = 16
IN_COLS = 12416
COL = dict(r=0, k=1024, v=2048, za=3072, wd=4096, q=4224, kq=5248, vq=6272, zb=7296, ga=8320, gb=10368)
CDEC = 0.6065306597126334
NEG = -1.0e30
STOP = 99
NOMERGE = False
STRICT = True

PP = dict(mu_r=0, mu_k=8, mu_v=16, w0=24, a0=32, k_k=40, k_a=48, r_k=56, gn_w=64, gn_b=72, norm_w=80,
          mu_wa=96, qnw=97, knw=98)
NPP = 100
CB = dict(ident=0, blk2=128, m_su=256, m_iu=384, m_sl=512, onesA=640, onesB=768, cmask=896)
NCB = 896 + 4 * 512
CF = dict(scanmask=0, pastneg=512, ownpos=768, swapP=1024)
NCF = 1152


_DTSZ = {}


def _dtsize(dt):
    s = _DTSZ.get(dt)
    if s is None:
        name = str(dt)
        s = 4 if '32' in name else 2 if '16' in name else 1 if '8' in name else 8
        _DTSZ[dt] = s
    return s


def box_of(ap):
    dims = ap.ap
    sz = _dtsize(ap.dtype)
    pstep, pcnt = dims[0]
    off = int(ap.offset)
    if pstep == 0:
        pstep = 1 << 40
    p0 = off // pstep
    f0 = off % pstep
    ext = 0
    for st, cn in dims[1:]:
        ext += abs(st) * (cn - 1)
    f1 = f0 + ext + 1
    if 'PSUM' in str(ap.space).upper():
        b0 = (f0 * sz) // 2048
        b1 = ((f1 * sz) - 1) // 2048
        return ('PS', ap.name, 0, 128, b0 * 2048, (b1 + 1) * 2048, True)
    return ('SB', ap.name, p0, p0 + pcnt, f0 * sz, f1 * sz, False)


class Op:
    __slots__ = ('idx', 'eng', 'fn', 'deps', 'signal', 'count', 'is_dma', 'dsem', 'dcount', 'prev_slot')

    def __init__(self, idx, eng, fn, is_dma):
        self.idx = idx
        self.eng = eng
        self.fn = fn
        self.deps = {}
        self.signal = False
        self.count = 0
        self.is_dma = is_dma
        self.dsem = None
        self.dcount = 0
        self.prev_slot = None


class Sched:
    ENGS = ('pe', 'act', 'dve', 'pool', 'sp')

    def __init__(self, nc, n_dma_slots=10):
        self.nc = nc
        self.ops = []
        self.recs = {}
        self.n_dma_slots = n_dma_slots

    def _touch(self, op, box, is_write):
        kind, name, p0, p1, f0, f1, excl = box
        lst = self.recs.setdefault(name, [])
        found = None
        for r in lst:
            if r[0] < p1 and p0 < r[1] and r[2] < f1 and f0 < r[3]:
                if r[4] is not None:
                    if (not is_write) or excl:
                        op.deps[r[4]] = True
                    else:
                        op.deps.setdefault(r[4], False)
                if is_write or excl:
                    for e, o in r[5].items():
                        op.deps.setdefault(o, False)
            if r[0] == p0 and r[1] == p1 and r[2] == f0 and r[3] == f1:
                found = r
        if found is None:
            found = [p0, p1, f0, f1, None, {}]
            lst.append(found)
        if is_write or excl:
            found[4] = op.idx
            found[5] = {}
        else:
            found[5][op.eng] = op.idx

    capture = None

    def add(self, eng, fn, reads=(), writes=(), dma=False, cost=0.5):
        if self.capture is not None:
            self.capture.append((eng, fn, list(reads), list(writes), dma, cost))
            return -1
        op = Op(len(self.ops), eng, fn, dma)
        self.ops.append(op)
        for ap in reads:
            if ap is not None and not isinstance(ap, (int, float)):
                self._touch(op, ap if isinstance(ap, tuple) else box_of(ap), False)
        for ap in writes:
            if ap is not None:
                self._touch(op, ap if isinstance(ap, tuple) else box_of(ap), True)
        op.deps.pop(op.idx, None)
        return op.idx

    def commit(self, lst):
        for it in lst:
            self.add(*it[:5])

    def merge_streams(self, streams):
        if NOMERGE:
            for st in streams:
                self.commit(st)
            return
        pos = [0] * len(streams)
        ready = [0.0] * len(streams)
        free = {e: 0.0 for e in self.ENGS}
        while True:
            best = None
            for si, st in enumerate(streams):
                if pos[si] >= len(st):
                    continue
                it = st[pos[si]]
                t = max(free[it[0]], ready[si])
                if best is None or t < best[0] - 1e-9:
                    best = (t, si)
            if best is None:
                break
            t, si = best
            it = streams[si][pos[si]]
            pos[si] += 1
            self.add(*it[:5])
            if it[4]:
                free[it[0]] = t + 0.06
                ready[si] = t + 0.06
            else:
                free[it[0]] = t + it[5]
                ready[si] = t + it[5] + 0.12

    def merge(self, main, bg):
        return self.merge_streams([main, bg])
        nb = len(bg)
        nm = max(len(main), 1)
        j = 0
        for i, it in enumerate(main):
            self.add(*it[:5])
            tgt = (i + 1) * nb // nm
            while j < tgt:
                self.add(*bg[j][:5])
                j += 1
        while j < nb:
            self.add(*bg[j])
            j += 1

    def emit(self, final_wait_ops=()):
        nc = self.nc
        ops = self.ops
        for op in ops:
            for d, raw in op.deps.items():
                dop = ops[d]
                if dop.is_dma:
                    continue
                if (not op.is_dma) and dop.eng == op.eng and (op.eng == 'pe' or (not raw and not STRICT)):
                    continue
                dop.signal = True
        for d in final_wait_ops:
            if not ops[d].is_dma:
                ops[d].signal = True
        cnt = {e: 0 for e in self.ENGS}
        for op in ops:
            if op.is_dma:
                continue
            if op.signal:
                cnt[op.eng] += 1
            op.count = cnt[op.eng]
        slot_state = {}
        for op in ops:
            if not op.is_dma:
                continue
            st = slot_state.setdefault(op.eng, {'next': 0, 'counts': [0] * self.n_dma_slots,
                                                'last': [None] * self.n_dma_slots})
            s = st['next']
            st['next'] = (s + 1) % self.n_dma_slots
            op.prev_slot = st['last'][s]
            st['counts'][s] += 16
            op.dsem = (op.eng, s)
            op.dcount = st['counts'][s]
            st['last'][s] = op.idx
        used = [e for e in self.ENGS if any(o.eng == e for o in ops)]
        if 'sp' not in used:
            used.append('sp')
        with contextlib.ExitStack() as es:
            sems = {e: es.enter_context(nc.semaphore('s_' + e)) for e in used}
            dsems = {}
            for e in slot_state:
                for s in range(self.n_dma_slots):
                    dsems[(e, s)] = es.enter_context(nc.semaphore('d_%s_%d' % (e, s)))
            block = es.enter_context(nc.Block())

            def run_engine(ename, eng):
                waited = {e: 0 for e in self.ENGS}
                dwaited = {}

                def wait_on(dop):
                    if dop.is_dma:
                        if dwaited.get(dop.dsem, 0) < dop.dcount:
                            eng.wait_ge(dsems[dop.dsem], dop.dcount)
                            dwaited[dop.dsem] = dop.dcount
                    elif dop.count > waited[dop.eng]:
                        eng.wait_ge(sems[dop.eng], dop.count)
                        waited[dop.eng] = dop.count

                for op in ops:
                    if op.eng != ename:
                        continue
                    for d in sorted(op.deps):
                        dop = ops[d]
                        raw = op.deps[d]
                        if (not dop.is_dma) and (not op.is_dma) and dop.eng == ename and (ename == 'pe' or (not raw and not STRICT)):
                            continue
                        wait_on(dop)
                    if op.is_dma and op.prev_slot is not None:
                        wait_on(ops[op.prev_slot])
                    ins = op.fn(eng)
                    if op.is_dma:
                        ins.then_inc(dsems[op.dsem], 16)
                    elif op.signal:
                        ins.then_inc(sems[ename], 1)
                if ename == 'sp':
                    for d in final_wait_ops:
                        wait_on(ops[d])

            @block.tensor
            def _(eng):
                run_engine('pe', eng)

            @block.scalar
            def _(eng):
                run_engine('act', eng)

            @block.vector
            def _(eng):
                run_engine('dve', eng)

            @block.gpsimd
            def _(eng):
                run_engine('pool', eng)

            @block.sync
            def _(eng):
                run_engine('sp', eng)
        return cnt


def build(n_hp_r=8, n_hp_m=8, do_final=True, dbg=False):
    nc = bass.Bass("TRN2", target_bir_lowering=False)
    x_d = nc.dram_tensor("x", [S_TOK, D], F32, kind="ExternalInput").ap()
    win_d = nc.dram_tensor("w_in", [D, IN_COLS], F32, kind="ExternalInput").ap()
    wpa_d = nc.dram_tensor("w_pa", [1024, D], F32, kind="ExternalInput").ap()
    wpb_d = nc.dram_tensor("w_pb", [1024, D], F32, kind="ExternalInput").ap()
    wo_d = nc.dram_tensor("w_out", [D, D], F32, kind="ExternalInput").ap()
    lora_d = nc.dram_tensor("lora", [128, 1024], F32, kind="ExternalInput").ap()
    pp_d = nc.dram_tensor("pp", [128, NPP], F32, kind="ExternalInput").ap()
    cb_d = nc.dram_tensor("cb", [128, NCB], BF16, kind="ExternalInput").ap()
    cf_d = nc.dram_tensor("cf", [128, NCF], F32, kind="ExternalInput").ap()
    ind_d = nc.dram_tensor("ind", [8, S_TOK], BF16, kind="ExternalInput").ap()
    out_d = nc.dram_tensor("out", [S_TOK, D], F32, kind="ExternalOutput").ap()
    scr_kind = "ExternalOutput" if dbg else "Internal"
    ya_d = nc.dram_tensor("ya_scr", [8, 128, S_TOK], BF16, kind=scr_kind).ap()
    yb_d = nc.dram_tensor("yb_scr", [8, 128, S_TOK], BF16, kind=scr_kind).ap()

    with contextlib.ExitStack() as es:
        def sb(name, shape, dt):
            return es.enter_context(nc.sbuf_tensor(name, shape, dt))

        hT = sb("hT", [128, KC, S_TOK], BF16)
        NW = 4
        wpool = [sb("wp%d" % i, [128, KC, 128], BF16) for i in range(NW)]
        cb = sb("cb_s", [128, NCB], BF16)
        cf = sb("cf_s", [128, NCF], F32)
        pp = sb("pp_s", [128, NPP], F32)
        loraD = sb("loraD_s", [128, 1024], BF16)
        loraI = sb("loraI_s", [128, 1024], BF16)
        TL = sb("TL", [128, S_TOK], BF16)
        RAWT = [sb("rawt%d" % s, [128, 4 * S_TOK], BF16) for s in range(2)]
        RAW = [[RAWT[s][:, j * S_TOK:(j + 1) * S_TOK] for j in range(4)] for s in range(2)]
        ARENA_N = 38 * 1024
        arena_t = sb("arena", [128, ARENA_N], BF16)
        PSA = es.enter_context(nc.psum_tensor("PSA", [128, 2048], F32))
        PSB = es.enter_context(nc.psum_tensor("PSB", [128, 2048], F32))

        S = Sched(nc)
        out_dmas = []

        class Arena:
            def __init__(self):
                self.off = 0

            def reset(self):
                self.off = 0

            def a(self, n, dt=BF16):
                nb = n * (2 if dt == F32 else 1)
                nb = (nb + 1) // 2 * 2
                assert self.off + nb <= ARENA_N, (self.off, nb)
                v = arena_t[:, self.off:self.off + nb]
                self.off += nb
                if dt == F32:
                    v = v.bitcast(F32)
                return v

        AR_ = Arena()

        def rd(*aps):
            return [a for a in aps if a is not None and not isinstance(a, (int, float))]

        def fsz(ap):
            n = 1
            for st, cn in ap.ap[1:]:
                n *= cn
            return n

        def ACT(out, in_, func, bias=None, scale=None, accum=None):
            kw = {}
            if bias is not None:
                kw['bias'] = bias
            if scale is not None:
                kw['scale'] = scale
            if accum is not None:
                kw['accum_out'] = accum
            return S.add('act', lambda e: e.activation(out=out, in_=in_, func=func, **kw),
                         reads=rd(in_, bias, scale), writes=[out, accum], cost=0.22 + fsz(out) / 1200.0)

        def TT(eng, out, in0, in1, op):
            return S.add(eng, lambda e: e.tensor_tensor(out=out, in0=in0, in1=in1, op=op),
                         reads=rd(in0, in1), writes=[out], cost=0.1 + fsz(out) / 960.0)

        def TS(eng, out, in0, s1, s2, op0, op1=None):
            if op1 is None:
                return S.add(eng, lambda e: e.tensor_scalar(out=out, in0=in0, scalar1=s1, scalar2=None, op0=op0),
                             reads=rd(in0, s1), writes=[out], cost=0.1 + fsz(out) / 960.0)
            return S.add(eng, lambda e: e.tensor_scalar(out=out, in0=in0, scalar1=s1, scalar2=s2, op0=op0, op1=op1),
                         reads=rd(in0, s1, s2), writes=[out], cost=0.1 + fsz(out) / 960.0)

        def STT(eng, out, in0, scalar, in1, op0, op1):
            eng = 'dve'
            return S.add(eng, lambda e: e.scalar_tensor_tensor(out=out, in0=in0, scalar=scalar, in1=in1,
                                                                op0=op0, op1=op1),
                         reads=rd(in0, scalar, in1), writes=[out], cost=0.1 + fsz(out) / 960.0)

        def CP(eng, out, in_):
            if eng == 'act':
                return S.add('act', lambda e: e.copy(out=out, in_=in_), reads=[in_], writes=[out],
                             cost=0.22 + fsz(out) / 1200.0)
            return S.add(eng, lambda e: e.tensor_copy(out=out, in_=in_), reads=[in_], writes=[out],
                         cost=0.1 + fsz(out) / 960.0)

        def MEMSET(eng, out, val):
            return S.add(eng, lambda e: e.memset(out, val), writes=[out], cost=1.0)

        def MM(items, reads, writes):
            def fn(e):
                ins = None
                for (o, l, r, st, sp) in items:
                    ins = e.matmul(o, lhsT=l, rhs=r, start=st, stop=sp)
                return ins
            c = 0.05
            for (o, l, r, st, sp) in items:
                c += max(0.065, fsz(o) / (600.0 if l.dtype == F32 else 2400.0))
            return S.add('pe', fn, reads=reads, writes=writes, cost=c)

        def TRS(items, reads, writes):
            def fn(e):
                ins = None
                for (o, i, idn) in items:
                    ins = e.transpose(o, i, idn)
                return ins
            return S.add('pe', fn, reads=reads, writes=writes, cost=0.05 + 0.11 * len(items))

        def DMA(q, out, in_, reads=(), writes=()):
            return S.add(q, lambda e: e.dma_start(out=out, in_=in_), reads=reads, writes=writes, dma=True)

        def ppc(name, j=0):
            c = PP[name] + j
            return pp[:, c:c + 1]

        ident = cb[:, CB['ident']:CB['ident'] + 128]
        blk2 = cb[:, CB['blk2']:CB['blk2'] + 128]
        m_su = cb[:, CB['m_su']:CB['m_su'] + 128]
        m_iu = cb[:, CB['m_iu']:CB['m_iu'] + 128]
        m_sl = cb[:, CB['m_sl']:CB['m_sl'] + 128]
        onesA = cb[:, CB['onesA']:CB['onesA'] + 128]
        onesB = cb[:, CB['onesB']:CB['onesB'] + 128]
        cmask = cb[:, CB['cmask']:CB['cmask'] + 2048].rearrange("p (a b) -> p a b", b=512)
        scanmask = cf[:, CF['scanmask']:CF['scanmask'] + 512]
        pastneg = cf[:, CF['pastneg']:CF['pastneg'] + 256]
        ownpos = cf[:, CF['ownpos']:CF['ownpos'] + 256]
        swapP = cf[:, CF['swapP']:CF['swapP'] + 128]

        def psb_f32(bank, n=512, off=0):
            return PSB[:, bank * 512 + off: bank * 512 + off + n]

        def psb_bf(bank, n=1024, off=0):
            v = PSB[:, bank * 512:(bank + 1) * 512].bitcast(BF16)
            return v[:, off:off + n]

        def psb2_bf(bank):
            return PSB[:, bank * 512:(bank + 2) * 512].bitcast(BF16)

        DMA('sp', cb[:], cb_d, writes=[cb[:]])
        DMA('sp', cf[:], cf_d, writes=[cf[:]])
        DMA('sp', pp[:], pp_d, writes=[pp[:]])
        S.add('pool', lambda e: e.memset(loraD[:], 0.0), writes=[loraD[:]])
        S.add('pool', lambda e: e.memset(loraI[:], 0.0), writes=[loraI[:]])
        DMA('pool', loraD[0:64, :], lora_d[0:64, :], writes=[loraD[0:64, :]])
        DMA('pool', loraI[64:128, :], lora_d[64:128, :], writes=[loraI[64:128, :]])

        wstate = {'n': 0}

        def wload_in(col):
            b = wpool[wstate['n'] % NW]
            wstate['n'] += 1
            src = win_d[:, col:col + 128].rearrange("(kc p) m -> p kc m", p=128)
            DMA('pool', b[:], src, writes=[b[:]])
            return b

        def wload_proj(wd, col):
            b = wpool[wstate['n'] % NW]
            wstate['n'] += 1
            src = wd[:, col:col + 128].rearrange("(kc p) m -> p kc m", p=128)
            DMA('pool', b[:, 0:8, :], src, writes=[b[:, 0:8, :]])
            return b

        def inproj(wbuf, nk, rhs_src, evac):
            for half in range(2):
                acc = PSA[:, 0:1024]
                items = []
                for k in range(nk):
                    for tt in range(2):
                        t0 = half * 1024 + tt * 512
                        items.append((acc[:, tt * 512:(tt + 1) * 512], wbuf[:, k, :], rhs_src[:, k, t0:t0 + 512],
                                      k == 0, k == nk - 1))
                for i0 in range(0, len(items), 8):
                    MM(items[i0:i0 + 8], reads=[wbuf[:, 0:nk, :], rhs_src[:, 0:nk, half * 1024:(half + 1) * 1024]], writes=[acc])
                evac(acc, half)

        AR_.reset()
        xt = [AR_.a(2048, F32) for _ in range(2)]
        hb = [AR_.a(2048) for _ in range(2)]
        junk = AR_.a(2048)
        ssb = AR_.a(16, F32)
        rsb = AR_.a(16, F32)
        nwb = pp[:, PP['norm_w']:PP['norm_w'] + 16].unsqueeze(2).to_broadcast([128, 16, 128])
        for tt in range(16):
            xtile = xt[tt % 2]
            DMA('sp', xtile, x_d[tt * 128:(tt + 1) * 128, :], writes=[xtile])
            ACT(junk, xtile, AF.Square, accum=ssb[:, tt:tt + 1])
            TS('dve', rsb[:, tt:tt + 1], ssb[:, tt:tt + 1], 1.0 / D, 1e-6, ALU.mult, ALU.add)
            ACT(rsb[:, tt:tt + 1], rsb[:, tt:tt + 1], AF.Sqrt)
            S.add('dve', (lambda o: (lambda e: e.reciprocal(out=o, in_=o)))(rsb[:, tt:tt + 1]),
                  reads=[rsb[:, tt:tt + 1]], writes=[rsb[:, tt:tt + 1]])
            if tt % 2:
                TS('dve', hb[tt % 2], xtile, rsb[:, tt:tt + 1], None, ALU.mult)
            else:
                ACT(hb[tt % 2], xtile, AF.Identity, scale=rsb[:, tt:tt + 1])
            pst = psb2_bf((tt % 2) * 2)
            TRS([(pst[:, k * 128:(k + 1) * 128], hb[tt % 2][:, k * 128:(k + 1) * 128], ident) for k in range(16)],
                reads=[hb[tt % 2], ident], writes=[pst])
            TT('dve', hT[:, :, tt * 128:(tt + 1) * 128], pst.rearrange("p (a b) -> p a b", b=128), nwb, ALU.mult)

        jobs = []
        jobs_seq = [('R', hp) for hp in range(n_hp_r)] + [('M', hp) for hp in range(n_hp_m)]
        for hp in range(n_hp_r):
            for nm in ('r', 'k', 'v', 'za'):
                jobs.append(('R', hp, nm, COL[nm] + hp * 128))
            if hp == 0:
                jobs.append(('R', 0, 'wd', COL['wd']))
        for hp in range(n_hp_m):
            for nm in ('q', 'kq', 'vq', 'zb'):
                jobs.append(('M', hp, nm, COL[nm] + hp * 128))
        PREF = NW - 1
        wq = []
        jstate = {'issued': 0}

        def next_w():
            while jstate['issued'] < len(jobs) and len(wq) < PREF + 1:
                wq.append(wload_in(jobs[jstate['issued']][3]))
                jstate['issued'] += 1
            return wq.pop(0)

        def ev_copy(dst, eng):
            def f(acc, half):
                CP(eng, dst[:, half * 1024:(half + 1) * 1024], acc)
            return f

        def ev_silu(dst):
            def f(acc, half):
                ACT(dst[:, half * 1024:(half + 1) * 1024], acc, AF.Silu)
            return f

        def inproj_job(ji):
            kind, hp = jobs_seq[ji]
            R0, R1, R2, R3 = RAW[ji % 2]
            inproj(next_w(), KC, hT, ev_copy(R0, 'act'))
            inproj(next_w(), KC, hT, ev_copy(R1, 'dve'))
            inproj(next_w(), KC, hT, ev_copy(R2, 'act'))
            inproj(next_w(), KC, hT, ev_silu(R3))

        def lora_prep():
            AR_.reset()
            lraw = AR_.a(2048)
            ld = AR_.a(2048)
            inproj(next_w(), KC, hT, ev_copy(lraw, 'dve'))
            TT('dve', ld[:, 1:2048], lraw[:, 0:2047], lraw[:, 1:2048], ALU.subtract)
            TS('dve', ld[:, 0:1], lraw[:, 0:1], -1.0, None, ALU.mult)
            STT('dve', TL[:], ld, ppc('mu_wa'), lraw, ALU.mult, ALU.add)
            ACT(TL[0:64, :], TL[0:64, :], AF.Tanh)

        def split_parts(lst, fracs):
            tot = float(sum(fracs))
            out = []
            acc = 0.0
            i0 = 0
            for f in fracs:
                acc += f
                i1 = int(round(len(lst) * acc / tot))
                out.append(lst[i0:i1])
                i0 = i1
            out[-1] = out[-1] + lst[i0:]
            return out

        def cap(fn, *args):
            S.capture = []
            fn(*args)
            lst = S.capture
            S.capture = None
            return lst

        def rwkv_headpair(hp, ji, nxt):
            Rr, Rk, Rv, SZ = RAW[ji % 2]
            if STOP <= 1:
                return
            AR_.reset()
            A = AR_.a
            GT = 256
            NG = S_TOK // GT
            Sf = [A(128, F32) for _ in range(2)]
            Sbf = A(128)
            Ssc = A(128, F32)
            Stmp = A(128, F32)
            Zsb = A(128)
            Usb = A(128)
            tinyb_t = A(2, F32)
            YAg = [A(GT) for _ in range(2)]
            ARfs = [A(2 * GT, F32) for _ in range(2)]
            KTfs = [A(GT, F32) for _ in range(2)]
            BTfs = [A(GT, F32) for _ in range(2)]
            KTbs = [A(GT) for _ in range(2)]
            BTbs = [A(GT) for _ in range(2)]
            vgs = [A(GT) for _ in range(2)]
            WCs = [A(4, F32) for _ in range(4)]
            BONs = [A(GT) for _ in range(4)]
            ARb = [A(2 * GT) for _ in range(2)]
            TM = [A(3 * GT) for _ in range(2)]
            AKs = [A(2 * GT) for _ in range(2)]
            RKs = [A(2 * GT) for _ in range(2)]
            RBs = [A(2 * GT) for _ in range(2)]
            TTs = [A(2 * GT) for _ in range(2)]
            Ytms = [A(GT, F32) for _ in range(2)]
            mean = A(4, F32); var = A(4, F32)
            YC = A(GT, F32); YQ = A(GT, F32); YN = A(GT)
            Dm = A(GT); SQ = A(GT); PROD = A(GT)
            rg = A(GT, F32); kg = A(GT, F32); SG = A(GT, F32); CS = A(GT, F32); ag = A(GT, F32)
            E1 = A(GT, F32); E2 = A(GT, F32); E3 = A(GT, F32)
            KK = A(GT, F32); RI = A(GT, F32); Tt = A(GT, F32); K2 = A(GT, F32); Bb = A(GT, F32)
            N0s = A(2 * GT); X0s = A(2 * GT)
            Ns = [A(2 * GT) for _ in range(2)]
            Xs = [A(2 * GT) for _ in range(2)]
            Ps = [A(2 * GT) for _ in range(2)]
            MEMSET('pool', tinyb_t, 1e-12)
            tinyb = tinyb_t[:, 0:1]
            MEMSET('pool', Sf[0], 0.0)
            MEMSET('pool', Sbf, 0.0)
            NT = GT // 128
            NM = NT * 2
            PA2 = PSA[:, 1024:1536]
            PA3 = PSA[:, 1536:2048]

            def v3(ap):
                return ap.rearrange("p (j t) -> p j t", t=128)

            def emit_A(g):
                c0 = g * GT
                par = g % 2
                ARf = ARfs[par]; KTf = KTfs[par]; BTf = BTfs[par]; vg = vgs[par]
                ARfv = ARf.rearrange("p (j w t) -> p j w t", w=2, t=128)
                for (raw, mu, dst) in ((Rr, 'mu_r', rg), (Rk, 'mu_k', kg), (Rv, 'mu_v', vg)):
                    if g == 0:
                        TT('dve', Dm[:, 1:GT], raw[:, 0:GT - 1], raw[:, 1:GT], ALU.subtract)
                        TS('dve', Dm[:, 0:1], raw[:, 0:1], -1.0, None, ALU.mult)
                    else:
                        TT('dve', Dm, raw[:, c0 - 1:c0 + GT - 1], raw[:, c0:c0 + GT], ALU.subtract)
                    STT('dve', dst, Dm, ppc(mu, hp), raw[:, c0:c0 + GT], ALU.mult, ALU.add)
                pu = PA2[:, 0:GT]
                pa_ = PA2[:, GT:2 * GT]
                MM([(pu, loraD[:, hp * 128:(hp + 1) * 128], TL[:, c0:c0 + GT], True, True)],
                   reads=[loraD[:, hp * 128:(hp + 1) * 128], TL[:, c0:c0 + GT]], writes=[pu])
                MM([(pa_, loraI[:, hp * 128:(hp + 1) * 128], TL[:, c0:c0 + GT], True, True)],
                   reads=[loraI[:, hp * 128:(hp + 1) * 128], TL[:, c0:c0 + GT]], writes=[pa_])
                ACT(SG, pu, AF.Sigmoid, bias=ppc('w0', hp))
                ACT(ag, pa_, AF.Sigmoid, bias=ppc('a0', hp))
                S.add('dve', (lambda o, m, d1: (lambda e: e.tensor_tensor_scan(out=o, data0=m, data1=d1, initial=0.0,
                                                                                op0=ALU.mult, op1=ALU.add)))(CS, scanmask[:, 0:GT], SG),
                      reads=[scanmask[:, 0:GT], SG], writes=[CS], cost=0.1 + 2 * GT / 960.0)
                ACT(E1, CS, AF.Exp, scale=-CDEC)
                ACT(WCs[g % 4], CS.rearrange("p (c t) -> p c t", t=64)[:, :, 63], AF.Exp, scale=-CDEC)
                ACT(E3, CS, AF.Exp, scale=CDEC)
                TT('dve', SG, CS, SG, ALU.subtract)
                ACT(E2, SG, AF.Exp, scale=-CDEC)
                ACT(KK, kg, AF.Identity, scale=ppc('k_k', hp))
                ACT(SQ, kg, AF.Square, scale=ppc('k_k', hp))
                pk = PA2[:, 0:GT]
                MM([(pk, blk2, SQ, True, True)], reads=[blk2, SQ], writes=[pk])
                ACT(RI, pk, AF.Ln, bias=tinyb)
                ACT(RI, RI, AF.Exp, scale=-0.5)
                TT('dve', KK, KK, RI, ALU.mult)
                TS('dve', Tt, ag, -1.0, ppc('k_a', hp), ALU.add, ALU.mult)
                STT('dve', K2, Tt, 1.0, kg, ALU.add, ALU.mult)
                TT('dve', Bb, KK, ag, ALU.mult)
                STT('dve', PROD, rg, ppc('r_k', hp), K2, ALU.mult, ALU.mult)
                pb_ = PA2[:, GT:2 * GT]
                MM([(pb_, blk2, PROD, True, True)], reads=[blk2, PROD], writes=[pb_])
                TT('dve', BONs[g % 4], pb_, vg, ALU.mult)
                TT('dve', ARfv[:, :, 1, :], v3(rg), v3(E1), ALU.mult)
                STT('dve', ARfv[:, :, 0, :], v3(KK), -1.0, v3(E2), ALU.mult, ALU.mult)
                TT('dve', KTf, K2, E3, ALU.mult)
                TT('dve', BTf, Bb, E3, ALU.mult)
                CP('act', KTbs[par], KTf)
                CP('act', BTbs[par], BTf)

            def emit_B(g):
                par = g % 2
                ARf = ARfs[par]; KTf = KTfs[par]; BTf = BTfs[par]; vg = vgs[par]
                KTb = KTbs[par]; BTb = BTbs[par]
                CP('act', ARb[par], ARf)
                ptm = PA3.bitcast(BF16)[:, 0:3 * GT]
                items = []
                for q, src in enumerate((vg, KTb, BTb)):
                    for j in range(NT):
                        items.append((ptm[:, (q * NT + j) * 128:(q * NT + j + 1) * 128], src[:, j * 128:(j + 1) * 128], ident))
                TRS(items, reads=[vg, KTb, BTb, ident], writes=[ptm])
                CP('act', TM[par], ptm)
                for j in range(NT):
                    items = []
                    for h in range(2):
                        hr = slice(h * 64, h * 64 + 64)
                        rhs_ar = ARf[hr, j * 256:(j + 1) * 256]
                        bh = psb_f32(h)
                        items.append((bh[:, 0:256], KTf[hr, j * 128:(j + 1) * 128], rhs_ar, True, True))
                        items.append((bh[:, 256:512], BTf[hr, j * 128:(j + 1) * 128], rhs_ar, True, True))
                    MM(items, reads=[KTf, BTf, ARf], writes=[PSB[:, 0:1024]])
                    bxy = PSB[:, 0:1024].rearrange("p (h w t) -> p h w t", h=2, w=4)
                    msu_b = m_su.unsqueeze(1).to_broadcast([128, 2, 128])
                    miu_b = m_iu.unsqueeze(1).to_broadcast([128, 2, 128])
                    msl_b = m_sl.unsqueeze(1).to_broadcast([128, 2, 128])

                    def dst(t):
                        return t.rearrange("p (j h t) -> p j h t", h=2, t=128)[:, j, :, :]
                    TT('dve', dst(AKs[par]), bxy[:, :, 0, :], msu_b, ALU.mult)
                    TT('dve', dst(RKs[par]), bxy[:, :, 1, :], miu_b, ALU.mult)
                    TT('dve', dst(N0s), bxy[:, :, 2, :], msu_b, ALU.mult)
                    TT('dve', dst(RBs[par]), bxy[:, :, 3, :], miu_b, ALU.mult)

                pxt = PA3.bitcast(BF16)[:, 0:NM * 128]
                TRS([(pxt[:, m * 128:(m + 1) * 128], N0s[:, m * 128:(m + 1) * 128], ident) for m in range(NM)],
                    reads=[N0s, ident], writes=[pxt])
                CP('act', X0s, pxt)

                def m8(t):
                    return t.rearrange("p (m t) -> p m t", t=128)
                W = NM * 128
                TT('dve', m8(Ps[0]), m8(N0s), ident.unsqueeze(1).to_broadcast([128, NM, 128]), ALU.add)
                Ncur, Xcur, Pcur = N0s, X0s, Ps[0]
                for lvl in range(1, 6):
                    pX = PSB[:, 0:W]
                    MM([(pX[:, m * 128:(m + 1) * 128], Ncur[:, m * 128:(m + 1) * 128], Xcur[:, m * 128:(m + 1) * 128], True, True)
                        for m in range(NM)], reads=[Ncur, Xcur], writes=[pX])
                    Xn = Xs[lvl % 2]
                    if lvl < 5:
                        pN = PSB[:, 512:512 + W]
                        MM([(pN[:, m * 128:(m + 1) * 128], Xcur[:, m * 128:(m + 1) * 128], Ncur[:, m * 128:(m + 1) * 128], True, True)
                            for m in range(NM)], reads=[Ncur, Xcur], writes=[pN])
                    CP('act', Xn, pX)
                    if lvl < 5:
                        Nn = Ns[lvl % 2]
                        CP('dve', Nn, pN)
                    pP = PA3[:, 0:W]
                    MM([(pP[:, m * 128:(m + 1) * 128], Xn[:, m * 128:(m + 1) * 128], Pcur[:, m * 128:(m + 1) * 128], True, True)
                        for m in range(NM)], reads=[Xn, Pcur], writes=[pP])
                    Pn = TTs[par] if lvl == 5 else Ps[lvl % 2]
                    TT('dve', Pn, pP, Pcur, ALU.add)
                    Pcur = Pn
                    Xcur = Xn
                    if lvl < 5:
                        Ncur = Nn

            def emit_chain(g):
                par = g % 2
                ARbv = ARb[par].rearrange("p (j w t) -> p j w t", w=2, t=128)
                TMv = TM[par].rearrange("p (q j f) -> p q j f", q=3, f=128)
                TTg = TTs[par].rearrange("p (j h t) -> p j h t", h=2, t=128)
                AKv = AKs[par].rearrange("p (j h t) -> p j h t", h=2, t=128)
                RKv = RKs[par].rearrange("p (j h t) -> p j h t", h=2, t=128)
                RBv = RBs[par].rearrange("p (j h t) -> p j h t", h=2, t=128)
                Ytv = Ytms[par].rearrange("p (j f) -> p j f", f=128)
                for cl in range(2 * NT):
                    c = g * 2 * NT + cl
                    j = cl // 2
                    pr = slice((cl % 2) * 64, (cl % 2) * 64 + 64)
                    Scur = Sf[c % 2]
                    Snxt = Sf[(c + 1) % 2]
                    wc = WCs[g % 4][:, cl:cl + 1]
                    Zp = PSB[:, 1536:1664]
                    Up = PSB[:, 1664:1792]
                    Yp = PSB[:, 1792:1920]
                    Sp = PSB[:, 1920:2048]
                    ACT(Ssc, Scur, AF.Identity, scale=wc)
                    MM([(Zp[:, 0:64], AKv[pr, j, 0, :], TMv[pr, 0, j, 0:64], True, False),
                        (Zp[:, 64:128], AKv[pr, j, 1, :], TMv[pr, 0, j, 64:128], False, False),
                        (Zp[:, 0:128], ARbv[:, j, 0, :], Sbf, False, True)],
                       reads=[AKs[par], TM[par], ARb[par], Sbf], writes=[Zp])
                    CP('act', Zsb[pr, :], Zp[pr, :])
                    MM([(Up[:, 0:64], TTg[pr, j, 0, :], Zsb[pr, 0:64], True, False),
                        (Up[:, 64:128], TTg[pr, j, 1, :], Zsb[pr, 64:128], False, True)],
                       reads=[TTs[par], Zsb[pr, :]], writes=[Up])
                    CP('dve', Usb[pr, :], Up[pr, :])
                    MM([(Yp[:, 0:128], ARbv[:, j, 1, :], Sbf, True, False),
                        (Yp[:, 0:64], RBv[pr, j, 0, :], Usb[pr, 0:64], False, False),
                        (Yp[:, 0:64], RKv[pr, j, 0, :], TMv[pr, 0, j, 0:64], False, False),
                        (Yp[:, 64:128], RBv[pr, j, 1, :], Usb[pr, 64:128], False, False),
                        (Yp[:, 64:128], RKv[pr, j, 1, :], TMv[pr, 0, j, 64:128], False, True)],
                       reads=[ARb[par], Sbf, RBs[par], RKs[par], Usb[pr, :], TM[par]], writes=[Yp])
                    CP('act', Ytv[pr, j, :], Yp[pr, :])
                    MM([(Sp, TMv[pr, 2, j, :], Usb[pr, :], True, False),
                        (Sp, TMv[pr, 1, j, :], TMv[pr, 0, j, :], False, True)],
                       reads=[TM[par], Usb[pr, :]], writes=[Sp])
                    TT('dve', Stmp, Sp, blk2, ALU.mult)
                    STT('dve', Sbf, Stmp, wc, Ssc, ALU.mult, ALU.add)
                    STT('dve', Snxt, Stmp, wc, Ssc, ALU.mult, ALU.add)

            def emit_post(g):
                par = g % 2
                c0 = g * GT
                Ytm = Ytms[par]
                Y4 = Ytm.rearrange("p (m f) -> p m f", f=64)
                YC4 = YC.rearrange("p (m f) -> p m f", f=64)
                YQ4 = YQ.rearrange("p (m f) -> p m f", f=64)
                nm_ = 2 * NT
                S.add('dve', (lambda o, i: (lambda e: e.tensor_reduce(out=o, in_=i, axis=AX.X, op=ALU.add)))(mean, Y4),
                      reads=[Ytm], writes=[mean])
                TS('dve', mean, mean, 1.0 / 64, None, ALU.mult)
                TT('dve', YC4, Y4, mean.unsqueeze(2).to_broadcast([128, nm_, 64]), ALU.subtract)
                ACT(YQ, YC, AF.Square)
                S.add('dve', (lambda o, i: (lambda e: e.tensor_reduce(out=o, in_=i, axis=AX.X, op=ALU.add)))(var, YQ4),
                      reads=[YQ], writes=[var])
                TS('dve', var, var, 1.0 / 64, 64e-5, ALU.mult, ALU.add)
                ACT(var, var, AF.Ln)
                ACT(var, var, AF.Exp, scale=-0.5)
                TT('dve', YN.rearrange("p (m f) -> p m f", f=64), YC4, var.unsqueeze(2).to_broadcast([128, nm_, 64]), ALU.mult)
                pyt = psb_bf(2, GT)
                TRS([(pyt[:, jj * 128:(jj + 1) * 128], YN[:, jj * 128:(jj + 1) * 128], ident) for jj in range(NT)],
                    reads=[YN, ident], writes=[pyt])
                YF = YC
                ACT(YF, pyt, AF.Identity, bias=ppc('gn_b', hp), scale=ppc('gn_w', hp))
                TT('dve', YF, YF, BONs[g % 4], ALU.add)
                TT('dve', YAg[par], YF, SZ[:, c0:c0 + GT], ALU.mult)
                DMA('sp', ya_d[hp, :, c0:c0 + GT], YAg[par], reads=[YAg[par]], writes=[('DR', 'ya', hp, hp + 1, 0, 1, False)])

            nparts = split_parts(nxt, [1.0] * (NG + 3))
            for it in range(-2, NG + 1):
                streams = []
                if 0 <= it < NG:
                    streams.append(cap(emit_chain, it))
                if 0 <= it + 1 < NG:
                    streams.append(cap(emit_B, it + 1))
                if 0 <= it - 1 < NG:
                    streams.append(cap(emit_post, it - 1))
                if 0 <= it + 2 < NG:
                    streams.append(cap(emit_A, it + 2))
                streams.append(nparts[it + 2])
                S.merge_streams(streams)

        def nxt_stream(ji):
            return cap(inproj_job, ji + 1) if ji + 1 < len(jobs_seq) else []

        if jobs_seq:
            inproj_job(0)
        if n_hp_r > 0:
            lora_prep()
        for hp in range(n_hp_r):
            rwkv_headpair(hp, hp, nxt_stream(hp))

        AR_.reset()
        if n_hp_m > 0:
            QA = AR_.a(2048); QB = AR_.a(2048); KA = AR_.a(2048); KB = AR_.a(2048)
            VA = AR_.a(2048); VB = AR_.a(2048)
            for t in (QA, QB, KA, KB):
                MEMSET('pool', t, 0.0)
            MEMSET('pool', VA, 1.0)
            MEMSET('pool', VB, 1.0)
            DMA('sp', KA[64:72, :], ind_d, writes=[KA[64:72, :]])
            DMA('sp', KB[0:8, :], ind_d, writes=[KB[0:8, :]])
        m_base = AR_.off

        def moba_headpair(hp, ji, nxt):
            Rq, Rkq, Rvq, SZ = RAW[ji % 2]
            AR_.off = m_base
            A = AR_.a
            nparts = split_parts(nxt, [2.0, 1.0, 2.0, 3.0, 4.0])
            S.capture = []
            SQm = A(512); RIm = A(512, F32)
            Qf = A(2048, F32); Kf = A(2048, F32)
            for (raw, wname, dA, dB, Ff) in ((Rq, 'qnw', QA, QB, Qf), (Rkq, 'knw', KA, KB, Kf)):
                for g in range(4):
                    c0 = g * 512
                    ACT(SQm, raw[:, c0:c0 + 512], AF.Square)
                    pk = psb_f32(g % 2)
                    MM([(pk, blk2, SQm, True, True)], reads=[blk2, SQm], writes=[pk])
                    ACT(RIm, pk, AF.Ln, bias=ppc_eps, scale=1.0 / 64)
                    ACT(RIm, RIm, AF.Exp, scale=-0.5)
                    STT('dve', Ff[:, c0:c0 + 512], raw[:, c0:c0 + 512], ppc(wname), RIm, ALU.mult, ALU.mult)
                    CP('act', dA[0:64, c0:c0 + 512], Ff[0:64, c0:c0 + 512])
                    CP('dve', dB[64:128, c0:c0 + 512], Ff[64:128, c0:c0 + 512])
            pv = psb2_bf(2)
            TRS([(pv[:, t * 128:(t + 1) * 128], Rvq[:, t * 128:(t + 1) * 128], ident) for t in range(16)],
                reads=[Rvq, ident], writes=[pv])
            pv3 = pv.rearrange("p (t f) -> p t f", f=128)
            CP('act', VA.rearrange("p (t f) -> p t f", f=128)[:, :, 0:64], pv3[:, :, 0:64])
            CP('dve', VB.rearrange("p (t f) -> p t f", f=128)[:, :, 64:128], pv3[:, :, 64:128])
            kmp = A(16, F32)
            MEMSET('pool', kmp, 0.0)
            S.add('dve', (lambda o, i: (lambda e: e.tensor_reduce(out=o, in_=i, axis=AX.X, op=ALU.add)))(
                kmp[0:64, 0:8], Kf[0:64, :].rearrange("p (n t) -> p n t", t=256)), reads=[Kf[0:64, :]], writes=[kmp[0:64, 0:8]])
            S.add('dve', (lambda o, i: (lambda e: e.tensor_reduce(out=o, in_=i, axis=AX.X, op=ALU.add)))(
                kmp[64:128, 8:16], Kf[64:128, :].rearrange("p (n t) -> p n t", t=256)), reads=[Kf[64:128, :]], writes=[kmp[64:128, 8:16]])
            pg = psb_f32(0, 256)
            MM([(pg[:, qt * 16:(qt + 1) * 16], Qf[:, qt * 128:(qt + 1) * 128], kmp, True, True) for qt in range(16)],
               reads=[Qf, kmp], writes=[pg])
            GM = A(256, F32); G2 = A(256, F32); EQ = A(256, F32); mx = A(32, F32); BI = A(256)
            g3 = lambda t: t.rearrange("p (m n) -> p m n", n=8)
            mxb = mx.unsqueeze(2).to_broadcast([128, 32, 8])

            def rmax(o, i):
                S.add('dve', (lambda o_, i_: (lambda e: e.tensor_reduce(out=o_, in_=i_, axis=AX.X, op=ALU.max)))(o, g3(i)),
                      reads=[i], writes=[o])
            TT('dve', GM, pg, pastneg, ALU.add)
            rmax(mx, GM)
            TT('dve', g3(EQ), g3(GM), mxb, ALU.is_ge)
            STT('dve', G2, EQ, NEG, GM, ALU.mult, ALU.add)
            rmax(mx, G2)
            TT('dve', g3(EQ), g3(G2), mxb, ALU.is_ge)
            STT('dve', G2, EQ, NEG, G2, ALU.mult, ALU.add)
            rmax(mx, G2)
            TT('dve', g3(EQ), g3(GM), mxb, ALU.is_ge)
            TT('dve', EQ, EQ, ownpos, ALU.max)
            TS('dve', BI, EQ, -1.0, -NEG, ALU.add, ALU.mult)
            pbt = psb_bf(1, 1024)
            pbt2 = psb_bf(2, 1024)
            TRS([((pbt if qt < 8 else pbt2)[0:16, (qt % 8) * 128:(qt % 8 + 1) * 128], BI[:, qt * 16:(qt + 1) * 16], ident)
                 for qt in range(16)], reads=[BI, ident], writes=[pbt, pbt2])
            BT_ = A(2048)
            CP('act', BT_[0:16, 0:1024], pbt[0:16, :])
            CP('act', BT_[0:16, 1024:2048], pbt2[0:16, :])
            DMA('sp', QA[64:72, :], BT_[0:8, :], reads=[BT_[0:8, :]], writes=[QA[64:72, :]])
            DMA('sp', QB[0:8, :], BT_[8:16, :], reads=[BT_[8:16, :]], writes=[QB[0:8, :]])
            pro = S.capture
            S.capture = None
            S.merge_streams([pro, nparts[0]])
            PT = [A(512) for _ in range(3)]
            RS = A(512, F32)
            RW = A(512, F32)
            YO = A(512, F32)
            YB = A(2048)
            VA3 = VA.rearrange("p (t f) -> p t f", f=128)
            VB3 = VB.rearrange("p (t f) -> p t f", f=128)
            OpA = psb_f32(2)
            OpB = psb_f32(3)
            for QT in range(4):
                S.capture = []
                q0 = QT * 512
                units = [(h, kt) for kt in range(4 * QT + 4) for h in range(2)]
                nkt = 4 * QT + 4

                def qk(ui):
                    h, kt = units[ui]
                    Kh = KA if h == 0 else KB
                    Qh = QA if h == 0 else QB
                    sp_ = psb_f32(ui % 2)
                    diag = kt >= 4 * QT
                    items = [(sp_, Kh[:, kt * 128:(kt + 1) * 128], Qh[:, q0:q0 + 512], True, not diag)]
                    rds = [Kh[:, kt * 128:(kt + 1) * 128], Qh[:, q0:q0 + 512]]
                    if diag:
                        items.append((sp_, ident, cmask[:, kt - 4 * QT, :], False, True))
                        rds += [ident, cmask[:, kt - 4 * QT, :]]
                    MM(items, reads=rds, writes=[sp_])
                qk(0)
                for ui, (h, kt) in enumerate(units):
                    if ui + 1 < len(units):
                        qk(ui + 1)
                    sp_ = psb_f32(ui % 2)
                    pt = PT[ui % 3]
                    ACT(pt, sp_, AF.Exp, scale=0.125)
                    Vh = VA3 if h == 0 else VB3
                    Oh = OpA if h == 0 else OpB
                    MM([(Oh, Vh[:, kt, :], pt, kt == 0, kt == nkt - 1)], reads=[Vh[:, kt, :], pt], writes=[Oh])
                ACT(RS[64:128, :], OpA[64:128, :], AF.Ln)
                ACT(RS[0:64, :], OpB[0:64, :], AF.Ln)
                ACT(RS, RS, AF.Exp, scale=-1.0)
                pw = psb_f32(0)
                MM([(pw, swapP, RS, True, True)], reads=[swapP, RS], writes=[pw])
                CP('act', RW, pw)
                TT('dve', YO[0:64, :], OpA[0:64, :], RW[0:64, :], ALU.mult)
                TT('dve', YO[64:128, :], OpB[64:128, :], RW[64:128, :], ALU.mult)
                TT('dve', YB[:, q0:q0 + 512], YO, SZ[:, q0:q0 + 512], ALU.mult)
                att = S.capture
                S.capture = None
                S.merge_streams([att, nparts[QT + 1]])
            out_dmas.append(DMA('sp', yb_d[hp], YB, reads=[YB], writes=[('DR', 'yb', hp, hp + 1, 0, 1, False)]))

        if n_hp_m > 0:
            epsb = AR_.a(2, F32)
            MEMSET('pool', epsb, 1e-6)
            ppc_eps = epsb[:, 0:1]
            m_base = AR_.off
        for hp in range(n_hp_m):
            moba_headpair(hp, n_hp_r + hp, nxt_stream(n_hp_r + hp))

        if do_final:
            for th in range(2):
                AR_.reset()
                t0 = th * 1024
                YAh = RAWT[0][:, :]; YBh = RAWT[1][:, :]
                YA3 = YAh.rearrange("p (k t) -> p k t", t=1024)
                YB3 = YBh.rearrange("p (k t) -> p k t", t=1024)
                for k in range(8):
                    DMA('sp', YA3[:, k, :], ya_d[k, :, t0:t0 + 1024], reads=[('DR', 'ya', k, k + 1, 0, 1, False)], writes=[YA3[:, k, :]])
                    DMA('sp', YB3[:, k, :], yb_d[k, :, t0:t0 + 1024], reads=[('DR', 'yb', k, k + 1, 0, 1, False)], writes=[YB3[:, k, :]])
                MG = AR_.a(16 * 1024)
                MG3 = MG.rearrange("p (k t) -> p k t", t=1024)
                sga = AR_.a(1024); sgb = AR_.a(1024); m1 = AR_.a(1024, F32)
                fj = []
                for c in range(16):
                    fj.append(('in', COL['ga'] + c * 128))
                    fj.append(('pa', c * 128))
                    fj.append(('in', COL['gb'] + c * 128))
                    fj.append(('pb', c * 128))
                fq = []
                fst = {'i': 0}

                def next_fw():
                    while fst['i'] < len(fj) and len(fq) < NW:
                        kind, col = fj[fst['i']]
                        fst['i'] += 1
                        if kind == 'in':
                            fq.append(wload_in(col))
                        elif kind == 'pa':
                            fq.append(wload_proj(wpa_d, col))
                        else:
                            fq.append(wload_proj(wpb_d, col))
                    return fq.pop(0)

                def run_half_proj(wbuf, nk, src3, evac):
                    acc = PSA[:, (run_half_proj.n % 2) * 1024:(run_half_proj.n % 2 + 1) * 1024]
                    run_half_proj.n += 1
                    items = []
                    for k in range(nk):
                        for tt in range(2):
                            items.append((acc[:, tt * 512:(tt + 1) * 512], wbuf[:, k, :], src3(k, tt), k == 0, k == nk - 1))
                    MM(items, reads=[wbuf[:, 0:nk, :]] + run_half_proj.rd, writes=[acc])
                    evac(acc)
                run_half_proj.n = 0
                for c in range(16):
                    run_half_proj.rd = [hT[:, :, t0:t0 + 1024]]
                    run_half_proj(next_fw(), KC, lambda k, tt: hT[:, k, t0 + tt * 512:t0 + (tt + 1) * 512],
                                  lambda acc: ACT(sga, acc, AF.Sigmoid))
                    run_half_proj.rd = [YAh]
                    run_half_proj(next_fw(), 8, lambda k, tt: YA3[:, k, tt * 512:(tt + 1) * 512],
                                  lambda acc: TT('dve', m1, acc, sga, ALU.mult))
                    run_half_proj.rd = [hT[:, :, t0:t0 + 1024]]
                    run_half_proj(next_fw(), KC, lambda k, tt: hT[:, k, t0 + tt * 512:t0 + (tt + 1) * 512],
                                  lambda acc: ACT(sgb, acc, AF.Sigmoid))
                    run_half_proj.rd = [YBh]

                    def ev_b(acc, c=c):
                        TT('dve', sgb, acc, sgb, ALU.mult)
                        TT('dve', MG3[:, c, :], m1, sgb, ALU.add)
                    run_half_proj(next_fw(), 8, lambda k, tt: YB3[:, k, tt * 512:(tt + 1) * 512], ev_b)
                WO = [AR_.a(16 * 256) for _ in range(2)]
                XR = [AR_.a(256, F32) for _ in range(2)]
                OT = [AR_.a(256, F32) for _ in range(2)]
                n_o = 0
                for c8 in range(8):
                    wo = WO[c8 % 2]
                    wo3 = wo.rearrange("p (k m) -> p k m", m=256)
                    DMA('pool', wo3, wo_d[:, c8 * 256:(c8 + 1) * 256].rearrange("(kc p) m -> p kc m", p=128), writes=[wo])
                    for tl in range(8):
                        tok0 = t0 + tl * 128
                        xr = XR[n_o % 2]
                        ot = OT[n_o % 2]
                        acc = PSB[:, (n_o % 4) * 512:(n_o % 4) * 512 + 256]
                        n_o += 1
                        DMA('sp', xr, x_d[tok0:tok0 + 128, c8 * 256:(c8 + 1) * 256], writes=[xr])
                        MM([(acc, MG3[:, k, tl * 128:(tl + 1) * 128], wo3[:, k, :], k == 0, k == 15) for k in range(16)],
                           reads=[MG, wo], writes=[acc])
                        TT('dve', ot, acc, xr, ALU.add)
                        out_dmas.append(DMA('sp', out_d[tok0:tok0 + 128, c8 * 256:(c8 + 1) * 256], ot, reads=[ot]))

        cnt = S.emit(final_wait_ops=out_dmas)
    return nc, cnt, len(S.ops)


def _consts():
    bf = ml_dtypes.bfloat16
    cbv = np.zeros((128, NCB), np.float32)
    idx = np.arange(128)
    cbv[:, CB['ident']:CB['ident'] + 128] = np.eye(128)
    same = (idx[:, None] // 64) == (idx[None, :] // 64)
    cbv[:, CB['blk2']:CB['blk2'] + 128] = same
    cbv[:, CB['m_su']:CB['m_su'] + 128] = same & (idx[:, None] < idx[None, :])
    cbv[:, CB['m_iu']:CB['m_iu'] + 128] = same & (idx[:, None] <= idx[None, :])
    cbv[:, CB['m_sl']:CB['m_sl'] + 128] = same & (idx[:, None] > idx[None, :])
    cbv[:, CB['onesA']:CB['onesA'] + 64] = 1.0
    cbv[:, CB['onesB'] + 64:CB['onesB'] + 128] = 1.0
    cm = np.zeros((128, 4, 512), np.float32)
    for ktl in range(4):
        kpos = ktl * 128 + idx[:, None]
        qpos = np.arange(512)[None, :]
        kb = kpos // 256
        qb = qpos // 256
        ok = np.where(kb == qb, kpos <= qpos, kb < qb)
        cm[:, ktl, :] = np.where(ok, 0.0, NEG)
    cbv[:, CB['cmask']:CB['cmask'] + 2048] = cm.reshape(128, 2048)
    cfv = np.zeros((128, NCF), np.float32)
    sm = np.ones(512, np.float32)
    sm[::64] = 0.0
    cfv[:, CF['scanmask']:CF['scanmask'] + 512] = sm[None, :]
    pn = np.zeros((16, 2, 8), np.float32)
    op = np.zeros((16, 2, 8), np.float32)
    for qt in range(16):
        qb = qt // 2
        for n in range(8):
            pn[qt, :, n] = 0.0 if n < qb else NEG
            op[qt, :, n] = 1.0 if n >= qb else 0.0
    cfv[:, CF['pastneg']:CF['pastneg'] + 256] = pn.reshape(1, 256)
    cfv[:, CF['ownpos']:CF['ownpos'] + 256] = op.reshape(1, 256)
    cfv[:, CF['swapP']:CF['swapP'] + 128] = np.roll(np.eye(128, dtype=np.float32), 64, axis=1)
    ind = np.zeros((8, S_TOK), np.float32)
    for n in range(8):
        ind[n, n * 256:(n + 1) * 256] = 1.0
    return cbv.astype(bf), cfv, ind.astype(bf)


def _pack_params(i):
    ppv = np.zeros((128, NPP), np.float32)

    def fm(v):
        return np.ascontiguousarray(v.reshape(-1, 128).T)
    for nm in ('mu_r', 'mu_k', 'mu_v', 'w0', 'a0', 'k_k', 'k_a', 'r_k', 'gn_w', 'gn_b'):
        ppv[:, PP[nm]:PP[nm] + 8] = fm(i[nm][0])
    ppv[:, PP['norm_w']:PP['norm_w'] + 16] = fm(i['norm_w'][0])
    ppv[0:64, PP['mu_wa']] = i['mu_w'][0]
    ppv[64:128, PP['mu_wa']] = i['mu_a'][0]
    ppv[0:64, PP['qnw']] = i['q_norm_w'][0]
    ppv[64:128, PP['qnw']] = i['q_norm_w'][0]
    ppv[0:64, PP['knw']] = i['k_norm_w'][0]
    ppv[64:128, PP['knw']] = i['k_norm_w'][0]
    lora = np.concatenate([i['w_decay_up'][0], i['w_iclr_up'][0]], axis=0).astype(np.float32)
    return ppv, np.ascontiguousarray(lora)


_CACHE = {}


def make_in_maps(inputs, n_cores=8):
    i = {k: np.asarray(v) for k, v in inputs.items()}
    cbv, cfv, ind = _consts()
    ppv, lora = _pack_params(i)
    shared = dict(w_in=np.ascontiguousarray(i['w_in'][0]), w_pa=np.ascontiguousarray(i['w_proj_rwkv'][0]),
                  w_pb=np.ascontiguousarray(i['w_proj_moba'][0]), w_out=np.ascontiguousarray(i['w_out'][0]),
                  lora=lora, pp=ppv, cb=cbv, cf=cfv, ind=ind)
    maps = []
    for c in range(n_cores):
        m = dict(shared)
        m['x'] = np.ascontiguousarray(i['x'][c])
        maps.append(m)
    return maps


def kernel(**inputs):
    if 'nc' not in _CACHE:
        _CACHE['nc'] = build()[0]
    nc = _CACHE['nc']
    maps = make_in_maps(inputs, 8)
    res = run_bass_kernel_spmd(nc, maps, core_ids=list(range(8)))
    out = np.stack([np.asarray(r['out']) for r in res.results], axis=0)
    return out.astype(np.float32)
```

```python
import contextlib
import numpy as np
import ml_dtypes
import concourse.bass as bass
import concourse.mybir as mybir
from concourse.bass_utils import run_bass_kernel_spmd

F32 = mybir.dt.float32
BF16 = mybir.dt.bfloat16
AF = mybir.ActivationFunctionType
ALU = mybir.AluOpType
AX = mybir.AxisListType

S_TOK = 2048
D = 2048
KC = 16
IN_COLS = 12416
COL = dict(r=0, k=1024, v=2048, za=3072, wd=4096, q=4224, kq=5248, vq=6272, zb=7296, ga=8320, gb=10368)
CDEC = 0.6065306597126334
NEG = -1.0e30
STOP = 99
NOMERGE = False
STRICT = True

PP = dict(mu_r=0, mu_k=8, mu_v=16, w0=24, a0=32, k_k=40, k_a=48, r_k=56, gn_w=64, gn_b=72, norm_w=80,
          mu_wa=96, qnw=97, knw=98)
NPP = 100
CB = dict(ident=0, blk2=128, m_su=256, m_iu=384, m_sl=512, onesA=640, onesB=768, cmask=896)
NCB = 896 + 4 * 512
CF = dict(scanmask=0, pastneg=512, ownpos=768, swapP=1024)
NCF = 1152


_DTSZ = {}


def _dtsize(dt):
    s = _DTSZ.get(dt)
    if s is None:
        name = str(dt)
        s = 4 if '32' in name else 2 if '16' in name else 1 if '8' in name else 8
        _DTSZ[dt] = s
    return s


def box_of(ap):
    dims = ap.ap
    sz = _dtsize(ap.dtype)
    pstep, pcnt = dims[0]
    off = int(ap.offset)
    if pstep == 0:
        pstep = 1 << 40
    p0 = off // pstep
    f0 = off % pstep
    ext = 0
    for st, cn in dims[1:]:
        ext += abs(st) * (cn - 1)
    f1 = f0 + ext + 1
    if 'PSUM' in str(ap.space).upper():
        b0 = (f0 * sz) // 2048
        b1 = ((f1 * sz) - 1) // 2048
        return ('PS', ap.name, 0, 128, b0 * 2048, (b1 + 1) * 2048, True)
    return ('SB', ap.name, p0, p0 + pcnt, f0 * sz, f1 * sz, False)


class Op:
    __slots__ = ('idx', 'eng', 'fn', 'deps', 'signal', 'count', 'is_dma', 'dsem', 'dcount', 'prev_slot')

    def __init__(self, idx, eng, fn, is_dma):
        self.idx = idx
        self.eng = eng
        self.fn = fn
        self.deps = {}
        self.signal = False
        self.count = 0
        self.is_dma = is_dma
        self.dsem = None
        self.dcount = 0
        self.prev_slot = None


class Sched:
    ENGS = ('pe', 'act', 'dve', 'pool', 'sp')

    def __init__(self, nc, n_dma_slots=10):
        self.nc = nc
        self.ops = []
        self.recs = {}
        self.n_dma_slots = n_dma_slots

    def _touch(self, op, box, is_write):
        kind, name, p0, p1, f0, f1, excl = box
        lst = self.recs.setdefault(name, [])
        found = None
        for r in lst:
            if r[0] < p1 and p0 < r[1] and r[2] < f1 and f0 < r[3]:
                if r[4] is not None:
                    if (not is_write) or excl:
                        op.deps[r[4]] = True
                    else:
                        op.deps.setdefault(r[4], False)
                if is_write or excl:
                    for e, o in r[5].items():
                        op.deps.setdefault(o, False)
            if r[0] == p0 and r[1] == p1 and r[2] == f0 and r[3] == f1:
                found = r
        if found is None:
            found = [p0, p1, f0, f1, None, {}]
            lst.append(found)
        if is_write or excl:
            found[4] = op.idx
            found[5] = {}
        else:
            found[5][op.eng] = op.idx

    capture = None

    def add(self, eng, fn, reads=(), writes=(), dma=False, cost=0.5):
        if self.capture is not None:
            self.capture.append((eng, fn, list(reads), list(writes), dma, cost))
            return -1
        op = Op(len(self.ops), eng, fn, dma)
        self.ops.append(op)
        for ap in reads:
            if ap is not None and not isinstance(ap, (int, float)):
                self._touch(op, ap if isinstance(ap, tuple) else box_of(ap), False)
        for ap in writes:
            if ap is not None:
                self._touch(op, ap if isinstance(ap, tuple) else box_of(ap), True)
        op.deps.pop(op.idx, None)
        return op.idx

    def commit(self, lst):
        for it in lst:
            self.add(*it[:5])

    def merge_streams(self, streams):
        if NOMERGE:
            for st in streams:
                self.commit(st)
            return
        pos = [0] * len(streams)
        ready = [0.0] * len(streams)
        free = {e: 0.0 for e in self.ENGS}
        while True:
            best = None
            for si, st in enumerate(streams):
                if pos[si] >= len(st):
                    continue
                it = st[pos[si]]
                t = max(free[it[0]], ready[si])
                if best is None or t < best[0] - 1e-9:
                    best = (t, si)
            if best is None:
                break
            t, si = best
            it = streams[si][pos[si]]
            pos[si] += 1
            self.add(*it[:5])
            if it[4]:
                free[it[0]] = t + 0.06
                ready[si] = t + 0.06
            else:
                free[it[0]] = t + it[5]
                ready[si] = t + it[5] + 0.12

    def merge(self, main, bg):
        return self.merge_streams([main, bg])
        nb = len(bg)
        nm = max(len(main), 1)
        j = 0
        for i, it in enumerate(main):
            self.add(*it[:5])
            tgt = (i + 1) * nb // nm
            while j < tgt:
                self.add(*bg[j][:5])
                j += 1
        while j < nb:
            self.add(*bg[j])
            j += 1

    def emit(self, final_wait_ops=()):
        nc = self.nc
        ops = self.ops
        for op in ops:
            for d, raw in op.deps.items():
                dop = ops[d]
                if dop.is_dma:
                    continue
                if (not op.is_dma) and dop.eng == op.eng and (op.eng == 'pe' or (not raw and not STRICT)):
                    continue
                dop.signal = True
        for d in final_wait_ops:
            if not ops[d].is_dma:
                ops[d].signal = True
        cnt = {e: 0 for e in self.ENGS}
        for op in ops:
            if op.is_dma:
                continue
            if op.signal:
                cnt[op.eng] += 1
            op.count = cnt[op.eng]
        slot_state = {}
        for op in ops:
            if not op.is_dma:
                continue
            st = slot_state.setdefault(op.eng, {'next': 0, 'counts': [0] * self.n_dma_slots,
                                                'last': [None] * self.n_dma_slots})
            s = st['next']
            st['next'] = (s + 1) % self.n_dma_slots
            op.prev_slot = st['last'][s]
            st['counts'][s] += 16
            op.dsem = (op.eng, s)
            op.dcount = st['counts'][s]
            st['last'][s] = op.idx
        used = [e for e in self.ENGS if any(o.eng == e for o in ops)]
        if 'sp' not in used:
            used.append('sp')
        with contextlib.ExitStack() as es:
            sems = {e: es.enter_context(nc.semaphore('s_' + e)) for e in used}
            dsems = {}
            for e in slot_state:
                for s in range(self.n_dma_slots):
                    dsems[(e, s)] = es.enter_context(nc.semaphore('d_%s_%d' % (e, s)))
            block = es.enter_context(nc.Block())

            def run_engine(ename, eng):
                waited = {e: 0 for e in self.ENGS}
                dwaited = {}

                def wait_on(dop):
                    if dop.is_dma:
                        if dwaited.get(dop.dsem, 0) < dop.dcount:
                            eng.wait_ge(dsems[dop.dsem], dop.dcount)
                            dwaited[dop.dsem] = dop.dcount
                    elif dop.count > waited[dop.eng]:
                        eng.wait_ge(sems[dop.eng], dop.count)
                        waited[dop.eng] = dop.count

                for op in ops:
                    if op.eng != ename:
                        continue
                    for d in sorted(op.deps):
                        dop = ops[d]
                        raw = op.deps[d]
                        if (not dop.is_dma) and (not op.is_dma) and dop.eng == ename and (ename == 'pe' or (not raw and not STRICT)):
                            continue
                        wait_on(dop)
                    if op.is_dma and op.prev_slot is not None:
                        wait_on(ops[op.prev_slot])
                    ins = op.fn(eng)
                    if op.is_dma:
                        ins.then_inc(dsems[op.dsem], 16)
                    elif op.signal:
                        ins.then_inc(sems[ename], 1)
                if ename == 'sp':
                    for d in final_wait_ops:
                        wait_on(ops[d])

            @block.tensor
            def _(eng):
                run_engine('pe', eng)

            @block.scalar
            def _(eng):
                run_engine('act', eng)

            @block.vector
            def _(eng):
                run_engine('dve', eng)

            @block.gpsimd
            def _(eng):
                run_engine('pool', eng)

            @block.sync
            def _(eng):
                run_engine('sp', eng)
        return cnt


def build(n_hp_r=8, n_hp_m=8, do_final=True, dbg=False):
    nc = bass.Bass("TRN2", target_bir_lowering=False)
    x_d = nc.dram_tensor("x", [S_TOK, D], F32, kind="ExternalInput").ap()
    win_d = nc.dram_tensor("w_in", [D, IN_COLS], F32, kind="ExternalInput").ap()
    wpa_d = nc.dram_tensor("w_pa", [1024, D], F32, kind="ExternalInput").ap()
    wpb_d = nc.dram_tensor("w_pb", [1024, D], F32, kind="ExternalInput").ap()
    wo_d = nc.dram_tensor("w_out", [D, D], F32, kind="ExternalInput").ap()
    lora_d = nc.dram_tensor("lora", [128, 1024], F32, kind="ExternalInput").ap()
    pp_d = nc.dram_tensor("pp", [128, NPP], F32, kind="ExternalInput").ap()
    cb_d = nc.dram_tensor("cb", [128, NCB], BF16, kind="ExternalInput").ap()
    cf_d = nc.dram_tensor("cf", [128, NCF], F32, kind="ExternalInput").ap()
    ind_d = nc.dram_tensor("ind", [8, S_TOK], BF16, kind="ExternalInput").ap()
    out_d = nc.dram_tensor("out", [S_TOK, D], F32, kind="ExternalOutput").ap()
    scr_kind = "ExternalOutput" if dbg else "Internal"
    ya_d = nc.dram_tensor("ya_scr", [8, 128, S_TOK], BF16, kind=scr_kind).ap()
    yb_d = nc.dram_tensor("yb_scr", [8, 128, S_TOK], BF16, kind=scr_kind).ap()

    with contextlib.ExitStack() as es:
        def sb(name, shape, dt):
            return es.enter_context(nc.sbuf_tensor(name, shape, dt))

        hT = sb("hT", [128, KC, S_TOK], BF16)
        NW = 4
        wpool = [sb("wp%d" % i, [128, KC, 128], BF16) for i in range(NW)]
        cb = sb("cb_s", [128, NCB], BF16)
        cf = sb("cf_s", [128, NCF], F32)
        pp = sb("pp_s", [128, NPP], F32)
        loraD = sb("loraD_s", [128, 1024], BF16)
        loraI = sb("loraI_s", [128, 1024], BF16)
        TL = sb("TL", [128, S_TOK], BF16)
        RAWT = [sb("rawt%d" % s, [128, 4 * S_TOK], BF16) for s in range(2)]
        RAW = [[RAWT[s][:, j * S_TOK:(j + 1) * S_TOK] for j in range(4)] for s in range(2)]
        ARENA_N = 39424
        arena_t = sb("arena", [128, ARENA_N], BF16)
        PSA = es.enter_context(nc.psum_tensor("PSA", [128, 2048], F32))
        PSB = es.enter_context(nc.psum_tensor("PSB", [128, 2048], F32))

        S = Sched(nc)
        out_dmas = []

        class Arena:
            def __init__(self):
                self.off = 0

            def reset(self):
                self.off = 0

            def a(self, n, dt=BF16):
                nb = n * (2 if dt == F32 else 1)
                nb = (nb + 1) // 2 * 2
                assert self.off + nb <= ARENA_N, (self.off, nb)
                v = arena_t[:, self.off:self.off + nb]
                self.off += nb
                if dt == F32:
                    v = v.bitcast(F32)
                return v

        AR_ = Arena()

        def rd(*aps):
            return [a for a in aps if a is not None and not isinstance(a, (int, float))]

        def fsz(ap):
            n = 1
            for st, cn in ap.ap[1:]:
                n *= cn
            return n

        def ACT(out, in_, func, bias=None, scale=None, accum=None):
            kw = {}
            if bias is not None:
                kw['bias'] = bias
            if scale is not None:
                kw['scale'] = scale
            if accum is not None:
                kw['accum_out'] = accum
            return S.add('act', lambda e: e.activation(out=out, in_=in_, func=func, **kw),
                         reads=rd(in_, bias, scale), writes=[out, accum], cost=0.22 + fsz(out) / 1200.0)

        def TT(eng, out, in0, in1, op):
            return S.add(eng, lambda e: e.tensor_tensor(out=out, in0=in0, in1=in1, op=op),
                         reads=rd(in0, in1), writes=[out], cost=0.1 + fsz(out) / 960.0)

        def TS(eng, out, in0, s1, s2, op0, op1=None):
            if op1 is None:
                return S.add(eng, lambda e: e.tensor_scalar(out=out, in0=in0, scalar1=s1, scalar2=None, op0=op0),
                             reads=rd(in0, s1), writes=[out], cost=0.1 + fsz(out) / 960.0)
            return S.add(eng, lambda e: e.tensor_scalar(out=out, in0=in0, scalar1=s1, scalar2=s2, op0=op0, op1=op1),
                         reads=rd(in0, s1, s2), writes=[out], cost=0.1 + fsz(out) / 960.0)

        def STT(eng, out, in0, scalar, in1, op0, op1):
            eng = 'dve'
            return S.add(eng, lambda e: e.scalar_tensor_tensor(out=out, in0=in0, scalar=scalar, in1=in1,
                                                                op0=op0, op1=op1),
                         reads=rd(in0, scalar, in1), writes=[out], cost=0.1 + fsz(out) / 960.0)

        def CP(eng, out, in_):
            if eng == 'act':
                return S.add('act', lambda e: e.copy(out=out, in_=in_), reads=[in_], writes=[out],
                             cost=0.22 + fsz(out) / 1200.0)
            return S.add(eng, lambda e: e.tensor_copy(out=out, in_=in_), reads=[in_], writes=[out],
                         cost=0.1 + fsz(out) / 960.0)

        def MEMSET(eng, out, val):
            return S.add(eng, lambda e: e.memset(out, val), writes=[out], cost=1.0)

        def MM(items, reads, writes):
            def fn(e):
                ins = None
                for (o, l, r, st, sp) in items:
                    ins = e.matmul(o, lhsT=l, rhs=r, start=st, stop=sp)
                return ins
            c = 0.05
            for (o, l, r, st, sp) in items:
                c += max(0.065, fsz(o) / (600.0 if l.dtype == F32 else 2400.0))
            return S.add('pe', fn, reads=reads, writes=writes, cost=c)

        def TRS(items, reads, writes):
            def fn(e):
                ins = None
                for (o, i, idn) in items:
                    ins = e.transpose(o, i, idn)
                return ins
            return S.add('pe', fn, reads=reads, writes=writes, cost=0.05 + 0.11 * len(items))

        def DMA(q, out, in_, reads=(), writes=()):
            return S.add(q, lambda e: e.dma_start(out=out, in_=in_), reads=reads, writes=writes, dma=True)

        def ppc(name, j=0):
            c = PP[name] + j
            return pp[:, c:c + 1]

        ident = cb[:, CB['ident']:CB['ident'] + 128]
        blk2 = cb[:, CB['blk2']:CB['blk2'] + 128]
        m_su = cb[:, CB['m_su']:CB['m_su'] + 128]
        m_iu = cb[:, CB['m_iu']:CB['m_iu'] + 128]
        m_sl = cb[:, CB['m_sl']:CB['m_sl'] + 128]
        onesA = cb[:, CB['onesA']:CB['onesA'] + 128]
        onesB = cb[:, CB['onesB']:CB['onesB'] + 128]
        cmask = cb[:, CB['cmask']:CB['cmask'] + 2048].rearrange("p (a b) -> p a b", b=512)
        scanmask = cf[:, CF['scanmask']:CF['scanmask'] + 512]
        pastneg = cf[:, CF['pastneg']:CF['pastneg'] + 256]
        ownpos = cf[:, CF['ownpos']:CF['ownpos'] + 256]
        swapP = cf[:, CF['swapP']:CF['swapP'] + 128]

        def psb_f32(bank, n=512, off=0):
            return PSB[:, bank * 512 + off: bank * 512 + off + n]

        def psb_bf(bank, n=1024, off=0):
            v = PSB[:, bank * 512:(bank + 1) * 512].bitcast(BF16)
            return v[:, off:off + n]

        def psb2_bf(bank):
            return PSB[:, bank * 512:(bank + 2) * 512].bitcast(BF16)

        DMA('sp', cb[:], cb_d, writes=[cb[:]])
        DMA('sp', cf[:], cf_d, writes=[cf[:]])
        DMA('sp', pp[:], pp_d, writes=[pp[:]])
        S.add('pool', lambda e: e.memset(loraD[:], 0.0), writes=[loraD[:]])
        S.add('pool', lambda e: e.memset(loraI[:], 0.0), writes=[loraI[:]])
        DMA('pool', loraD[0:64, :], lora_d[0:64, :], writes=[loraD[0:64, :]])
        DMA('pool', loraI[64:128, :], lora_d[64:128, :], writes=[loraI[64:128, :]])

        wstate = {'n': 0}

        def wload_in(col):
            b = wpool[wstate['n'] % NW]
            wstate['n'] += 1
            src = win_d[:, col:col + 128].rearrange("(kc p) m -> p kc m", p=128)
            DMA('pool', b[:], src, writes=[b[:]])
            return b

        def wload_proj(wd, col):
            b = wpool[wstate['n'] % NW]
            wstate['n'] += 1
            src = wd[:, col:col + 128].rearrange("(kc p) m -> p kc m", p=128)
            DMA('pool', b[:, 0:8, :], src, writes=[b[:, 0:8, :]])
            return b

        def inproj(wbuf, nk, rhs_src, evac):
            for half in range(2):
                acc = PSA[:, 0:1024]
                items = []
                for k in range(nk):
                    for tt in range(2):
                        t0 = half * 1024 + tt * 512
                        items.append((acc[:, tt * 512:(tt + 1) * 512], wbuf[:, k, :], rhs_src[:, k, t0:t0 + 512],
                                      k == 0, k == nk - 1))
                for i0 in range(0, len(items), 8):
                    MM(items[i0:i0 + 8], reads=[wbuf[:, 0:nk, :], rhs_src[:, 0:nk, half * 1024:(half + 1) * 1024]], writes=[acc])
                evac(acc, half)

        AR_.reset()
        xt = [AR_.a(2048, F32) for _ in range(2)]
        hb = [AR_.a(2048) for _ in range(2)]
        junk = AR_.a(2048)
        ssb = AR_.a(16, F32)
        rsb = AR_.a(16, F32)
        nwb = pp[:, PP['norm_w']:PP['norm_w'] + 16].unsqueeze(2).to_broadcast([128, 16, 128])
        for tt in range(16):
            xtile = xt[tt % 2]
            DMA('sp', xtile, x_d[tt * 128:(tt + 1) * 128, :], writes=[xtile])
            ACT(junk, xtile, AF.Square, accum=ssb[:, tt:tt + 1])
            TS('dve', rsb[:, tt:tt + 1], ssb[:, tt:tt + 1], 1.0 / D, 1e-6, ALU.mult, ALU.add)
            ACT(rsb[:, tt:tt + 1], rsb[:, tt:tt + 1], AF.Sqrt)
            S.add('dve', (lambda o: (lambda e: e.reciprocal(out=o, in_=o)))(rsb[:, tt:tt + 1]),
                  reads=[rsb[:, tt:tt + 1]], writes=[rsb[:, tt:tt + 1]])
            if tt % 2:
                TS('dve', hb[tt % 2], xtile, rsb[:, tt:tt + 1], None, ALU.mult)
            else:
                ACT(hb[tt % 2], xtile, AF.Identity, scale=rsb[:, tt:tt + 1])
            pst = psb2_bf((tt % 2) * 2)
            TRS([(pst[:, k * 128:(k + 1) * 128], hb[tt % 2][:, k * 128:(k + 1) * 128], ident) for k in range(16)],
                reads=[hb[tt % 2], ident], writes=[pst])
            TT('dve', hT[:, :, tt * 128:(tt + 1) * 128], pst.rearrange("p (a b) -> p a b", b=128), nwb, ALU.mult)

        jobs = []
        jobs_seq = [('R', hp) for hp in range(n_hp_r)] + [('M', hp) for hp in range(n_hp_m)]
        for hp in range(n_hp_r):
            for nm in ('r', 'k', 'v', 'za'):
                jobs.append(('R', hp, nm, COL[nm] + hp * 128))
            if hp == 0:
                jobs.append(('R', 0, 'wd', COL['wd']))
        for hp in range(n_hp_m):
            for nm in ('q', 'kq', 'vq', 'zb'):
                jobs.append(('M', hp, nm, COL[nm] + hp * 128))
        PREF = NW - 1
        wq = []
        jstate = {'issued': 0}

        def next_w():
            while jstate['issued'] < len(jobs) and len(wq) < PREF + 1:
                wq.append(wload_in(jobs[jstate['issued']][3]))
                jstate['issued'] += 1
            return wq.pop(0)

        def ev_copy(dst, eng):
            def f(acc, half):
                CP(eng, dst[:, half * 1024:(half + 1) * 1024], acc)
            return f

        def ev_silu(dst):
            def f(acc, half):
                ACT(dst[:, half * 1024:(half + 1) * 1024], acc, AF.Silu)
            return f

        def inproj_job(ji):
            kind, hp = jobs_seq[ji]
            R0, R1, R2, R3 = RAW[ji % 2]
            inproj(next_w(), KC, hT, ev_copy(R0, 'act'))
            inproj(next_w(), KC, hT, ev_copy(R1, 'dve'))
            inproj(next_w(), KC, hT, ev_copy(R2, 'act'))
            inproj(next_w(), KC, hT, ev_silu(R3))

        def lora_prep():
            AR_.reset()
            lraw = AR_.a(2048)
            ld = AR_.a(2048)
            inproj(next_w(), KC, hT, ev_copy(lraw, 'dve'))
            TT('dve', ld[:, 1:2048], lraw[:, 0:2047], lraw[:, 1:2048], ALU.subtract)
            TS('dve', ld[:, 0:1], lraw[:, 0:1], -1.0, None, ALU.mult)
            STT('dve', TL[:], ld, ppc('mu_wa'), lraw, ALU.mult, ALU.add)
            ACT(TL[0:64, :], TL[0:64, :], AF.Tanh)

        def split_parts(lst, fracs):
            tot = float(sum(fracs))
            out = []
            acc = 0.0
            i0 = 0
            for f in fracs:
                acc += f
                i1 = int(round(len(lst) * acc / tot))
                out.append(lst[i0:i1])
                i0 = i1
            out[-1] = out[-1] + lst[i0:]
            return out

        def cap(fn, *args):
            S.capture = []
            fn(*args)
            lst = S.capture
            S.capture = None
            return lst

        def rwkv_headpair(hp, ji, nxt):
            Rr, Rk, Rv, SZ = RAW[ji % 2]
            if STOP <= 1:
                return
            AR_.reset()
            A = AR_.a
            GT = 256
            NG = S_TOK // GT
            Sf = [A(128, F32) for _ in range(2)]
            Sbf = A(128)
            Ssc = A(128, F32)
            Stmp = A(128, F32)
            Zsb = A(128)
            Usb = A(128)
            tinyb_t = A(2, F32)
            YAg = [A(GT) for _ in range(2)]
            ARfs = [A(2 * GT, F32) for _ in range(2)]
            KTfs = [A(GT, F32) for _ in range(2)]
            BTfs = [A(GT, F32) for _ in range(2)]
            KTbs = [A(GT) for _ in range(2)]
            BTbs = [A(GT) for _ in range(2)]
            vgs = [A(GT) for _ in range(2)]
            WCs = [A(4, F32) for _ in range(4)]
            BONs = [A(GT) for _ in range(4)]
            ARb = [A(2 * GT) for _ in range(2)]
            TM = [A(3 * GT) for _ in range(2)]
            AKs = [A(2 * GT) for _ in range(2)]
            RKs = [A(2 * GT) for _ in range(2)]
            RBs = [A(2 * GT) for _ in range(2)]
            TTs = [A(2 * GT) for _ in range(2)]
            Ytms = [A(GT, F32) for _ in range(2)]
            mean = A(4, F32); var = A(4, F32)
            YC = A(GT, F32); YQ = A(GT, F32); YN = A(GT)
            Dm = A(GT); SQ = A(GT); PROD = A(GT)
            rg = A(GT, F32); kg = A(GT, F32); SG = A(GT, F32); CS = A(GT, F32); ag = A(GT, F32)
            E1 = A(GT, F32); E2 = A(GT, F32); E3 = A(GT, F32)
            KK = A(GT, F32); RI = A(GT, F32); Tt = A(GT, F32); K2 = A(GT, F32); Bb = A(GT, F32)
            N0s = A(2 * GT); X0s = A(2 * GT)
            Ns = [A(2 * GT) for _ in range(2)]
            Xs = [A(2 * GT) for _ in range(2)]
            Ps = [A(2 * GT) for _ in range(2)]
            MEMSET('pool', tinyb_t, 1e-12)
            tinyb = tinyb_t[:, 0:1]
            MEMSET('pool', Sf[0], 0.0)
            MEMSET('pool', Sbf, 0.0)
            NT = GT // 128
            NM = NT * 2
            PA2 = PSA[:, 1024:1536]
            PA3 = PSA[:, 1536:2048]

            def v3(ap):
                return ap.rearrange("p (j t) -> p j t", t=128)

            def emit_A(g):
                c0 = g * GT
                par = g % 2
                ARf = ARfs[par]; KTf = KTfs[par]; BTf = BTfs[par]; vg = vgs[par]
                ARfv = ARf.rearrange("p (j w t) -> p j w t", w=2, t=128)
                for (raw, mu, dst) in ((Rr, 'mu_r', rg), (Rk, 'mu_k', kg), (Rv, 'mu_v', vg)):
                    if g == 0:
                        TT('dve', Dm[:, 1:GT], raw[:, 0:GT - 1], raw[:, 1:GT], ALU.subtract)
                        TS('dve', Dm[:, 0:1], raw[:, 0:1], -1.0, None, ALU.mult)
                    else:
                        TT('dve', Dm, raw[:, c0 - 1:c0 + GT - 1], raw[:, c0:c0 + GT], ALU.subtract)
                    STT('dve', dst, Dm, ppc(mu, hp), raw[:, c0:c0 + GT], ALU.mult, ALU.add)
                pu = PA2[:, 0:GT]
                pa_ = PA2[:, GT:2 * GT]
                MM([(pu, loraD[:, hp * 128:(hp + 1) * 128], TL[:, c0:c0 + GT], True, True)],
                   reads=[loraD[:, hp * 128:(hp + 1) * 128], TL[:, c0:c0 + GT]], writes=[pu])
                MM([(pa_, loraI[:, hp * 128:(hp + 1) * 128], TL[:, c0:c0 + GT], True, True)],
                   reads=[loraI[:, hp * 128:(hp + 1) * 128], TL[:, c0:c0 + GT]], writes=[pa_])
                ACT(SG, pu, AF.Sigmoid, bias=ppc('w0', hp))
                ACT(ag, pa_, AF.Sigmoid, bias=ppc('a0', hp))
                S.add('dve', (lambda o, m, d1: (lambda e: e.tensor_tensor_scan(out=o, data0=m, data1=d1, initial=0.0,
                                                                                op0=ALU.mult, op1=ALU.add)))(CS, scanmask[:, 0:GT], SG),
                      reads=[scanmask[:, 0:GT], SG], writes=[CS], cost=0.1 + 2 * GT / 960.0)
                ACT(E1, CS, AF.Exp, scale=-CDEC)
                ACT(WCs[g % 4], CS.rearrange("p (c t) -> p c t", t=64)[:, :, 63], AF.Exp, scale=-CDEC)
                ACT(E3, CS, AF.Exp, scale=CDEC)
                TT('dve', SG, CS, SG, ALU.subtract)
                ACT(E2, SG, AF.Exp, scale=-CDEC)
                ACT(KK, kg, AF.Identity, scale=ppc('k_k', hp))
                ACT(SQ, kg, AF.Square, scale=ppc('k_k', hp))
                pk = PA2[:, 0:GT]
                MM([(pk, blk2, SQ, True, True)], reads=[blk2, SQ], writes=[pk])
                ACT(RI, pk, AF.Ln, bias=tinyb)
                ACT(RI, RI, AF.Exp, scale=-0.5)
                TT('dve', KK, KK, RI, ALU.mult)
                TS('dve', Tt, ag, -1.0, ppc('k_a', hp), ALU.add, ALU.mult)
                STT('dve', K2, Tt, 1.0, kg, ALU.add, ALU.mult)
                TT('dve', Bb, KK, ag, ALU.mult)
                STT('dve', PROD, rg, ppc('r_k', hp), K2, ALU.mult, ALU.mult)
                pb_ = PA2[:, GT:2 * GT]
                MM([(pb_, blk2, PROD, True, True)], reads=[blk2, PROD], writes=[pb_])
                TT('dve', BONs[g % 4], pb_, vg, ALU.mult)
                TT('dve', ARfv[:, :, 1, :], v3(rg), v3(E1), ALU.mult)
                STT('dve', ARfv[:, :, 0, :], v3(KK), -1.0, v3(E2), ALU.mult, ALU.mult)
                TT('dve', KTf, K2, E3, ALU.mult)
                TT('dve', BTf, Bb, E3, ALU.mult)
                CP('act', KTbs[par], KTf)
                CP('act', BTbs[par], BTf)

            def emit_B(g):
                par = g % 2
                ARf = ARfs[par]; KTf = KTfs[par]; BTf = BTfs[par]; vg = vgs[par]
                KTb = KTbs[par]; BTb = BTbs[par]
                CP('act', ARb[par], ARf)
                ptm = PA3.bitcast(BF16)[:, 0:3 * GT]
                items = []
                for q, src in enumerate((vg, KTb, BTb)):
                    for j in range(NT):
                        items.append((ptm[:, (q * NT + j) * 128:(q * NT + j + 1) * 128], src[:, j * 128:(j + 1) * 128], ident))
                TRS(items, reads=[vg, KTb, BTb, ident], writes=[ptm])
                CP('act', TM[par], ptm)
                for j in range(NT):
                    items = []
                    for h in range(2):
                        hr = slice(h * 64, h * 64 + 64)
                        rhs_ar = ARf[hr, j * 256:(j + 1) * 256]
                        bh = psb_f32(h)
                        items.append((bh[:, 0:256], KTf[hr, j * 128:(j + 1) * 128], rhs_ar, True, True))
                        items.append((bh[:, 256:512], BTf[hr, j * 128:(j + 1) * 128], rhs_ar, True, True))
                    MM(items, reads=[KTf, BTf, ARf], writes=[PSB[:, 0:1024]])
                    bxy = PSB[:, 0:1024].rearrange("p (h w t) -> p h w t", h=2, w=4)
                    msu_b = m_su.unsqueeze(1).to_broadcast([128, 2, 128])
                    miu_b = m_iu.unsqueeze(1).to_broadcast([128, 2, 128])
                    msl_b = m_sl.unsqueeze(1).to_broadcast([128, 2, 128])

                    def dst(t):
                        return t.rearrange("p (j h t) -> p j h t", h=2, t=128)[:, j, :, :]
                    TT('dve', dst(AKs[par]), bxy[:, :, 0, :], msu_b, ALU.mult)
                    TT('dve', dst(RKs[par]), bxy[:, :, 1, :], miu_b, ALU.mult)
                    TT('dve', dst(N0s), bxy[:, :, 2, :], msu_b, ALU.mult)
                    TT('dve', dst(RBs[par]), bxy[:, :, 3, :], miu_b, ALU.mult)

                pxt = PA3.bitcast(BF16)[:, 0:NM * 128]
                TRS([(pxt[:, m * 128:(m + 1) * 128], N0s[:, m * 128:(m + 1) * 128], ident) for m in range(NM)],
                    reads=[N0s, ident], writes=[pxt])
                CP('act', X0s, pxt)

                def m8(t):
                    return t.rearrange("p (m t) -> p m t", t=128)
                W = NM * 128
                TT('dve', m8(Ps[0]), m8(N0s), ident.unsqueeze(1).to_broadcast([128, NM, 128]), ALU.add)
                Ncur, Xcur, Pcur = N0s, X0s, Ps[0]
                for lvl in range(1, 6):
                    pX = PSB[:, 0:W]
                    MM([(pX[:, m * 128:(m + 1) * 128], Ncur[:, m * 128:(m + 1) * 128], Xcur[:, m * 128:(m + 1) * 128], True, True)
                        for m in range(NM)], reads=[Ncur, Xcur], writes=[pX])
                    Xn = Xs[lvl % 2]
                    if lvl < 5:
                        pN = PSB[:, 512:512 + W]
                        MM([(pN[:, m * 128:(m + 1) * 128], Xcur[:, m * 128:(m + 1) * 128], Ncur[:, m * 128:(m + 1) * 128], True, True)
                            for m in range(NM)], reads=[Ncur, Xcur], writes=[pN])
                    CP('act', Xn, pX)
                    if lvl < 5:
                        Nn = Ns[lvl % 2]
                        CP('dve', Nn, pN)
                    pP = PA3[:, 0:W]
                    MM([(pP[:, m * 128:(m + 1) * 128], Xn[:, m * 128:(m + 1) * 128], Pcur[:, m * 128:(m + 1) * 128], True, True)
                        for m in range(NM)], reads=[Xn, Pcur], writes=[pP])
                    Pn = TTs[par] if lvl == 5 else Ps[lvl % 2]
                    TT('dve', Pn, pP, Pcur, ALU.add)
                    Pcur = Pn
                    Xcur = Xn
                    if lvl < 5:
                        Ncur = Nn

            def emit_chain(g):
                par = g % 2
                ARbv = ARb[par].rearrange("p (j w t) -> p j w t", w=2, t=128)
                TMv = TM[par].rearrange("p (q j f) -> p q j f", q=3, f=128)
                TTg = TTs[par].rearrange("p (j h t) -> p j h t", h=2, t=128)
                AKv = AKs[par].rearrange("p (j h t) -> p j h t", h=2, t=128)
                RKv = RKs[par].rearrange("p (j h t) -> p j h t", h=2, t=128)
                RBv = RBs[par].rearrange("p (j h t) -> p j h t", h=2, t=128)
                Ytv = Ytms[par].rearrange("p (j f) -> p j f", f=128)
                for cl in range(2 * NT):
                    c = g * 2 * NT + cl
                    j = cl // 2
                    pr = slice((cl % 2) * 64, (cl % 2) * 64 + 64)
                    Scur = Sf[c % 2]
                    Snxt = Sf[(c + 1) % 2]
                    wc = WCs[g % 4][:, cl:cl + 1]
                    Zp = PSB[:, 1536:1664]
                    Up = PSB[:, 1664:1792]
                    Yp = PSB[:, 1792:1920]
                    Sp = PSB[:, 1920:2048]
                    ACT(Ssc, Scur, AF.Identity, scale=wc)
                    MM([(Zp[:, 0:64], AKv[pr, j, 0, :], TMv[pr, 0, j, 0:64], True, False),
                        (Zp[:, 64:128], AKv[pr, j, 1, :], TMv[pr, 0, j, 64:128], False, False),
                        (Zp[:, 0:128], ARbv[:, j, 0, :], Sbf, False, True)],
                       reads=[AKs[par], TM[par], ARb[par], Sbf], writes=[Zp])
                    CP('act', Zsb[pr, :], Zp[pr, :])
                    MM([(Up[:, 0:64], TTg[pr, j, 0, :], Zsb[pr, 0:64], True, False),
                        (Up[:, 64:128], TTg[pr, j, 1, :], Zsb[pr, 64:128], False, True)],
                       reads=[TTs[par], Zsb[pr, :]], writes=[Up])
                    CP('dve', Usb[pr, :], Up[pr, :])
                    MM([(Yp[:, 0:128], ARbv[:, j, 1, :], Sbf, True, False),
                        (Yp[:, 0:64], RBv[pr, j, 0, :], Usb[pr, 0:64], False, False),
                        (Yp[:, 0:64], RKv[pr, j, 0, :], TMv[pr, 0, j, 0:64], False, False),
                        (Yp[:, 64:128], RBv[pr, j, 1, :], Usb[pr, 64:128], False, False),
                        (Yp[:, 64:128], RKv[pr, j, 1, :], TMv[pr, 0, j, 64:128], False, True)],
                       reads=[ARb[par], Sbf, RBs[par], RKs[par], Usb[pr, :], TM[par]], writes=[Yp])
                    CP('act', Ytv[pr, j, :], Yp[pr, :])
                    MM([(Sp, TMv[pr, 2, j, :], Usb[pr, :], True, False),
                        (Sp, TMv[pr, 1, j, :], TMv[pr, 0, j, :], False, True)],
                       reads=[TM[par], Usb[pr, :]], writes=[Sp])
                    TT('dve', Stmp, Sp, blk2, ALU.mult)
                    STT('dve', Sbf, Stmp, wc, Ssc, ALU.mult, ALU.add)
                    STT('dve', Snxt, Stmp, wc, Ssc, ALU.mult, ALU.add)

            def emit_post(g):
                par = g % 2
                c0 = g * GT
                Ytm = Ytms[par]
                Y4 = Ytm.rearrange("p (m f) -> p m f", f=64)
                YC4 = YC.rearrange("p (m f) -> p m f", f=64)
                YQ4 = YQ.rearrange("p (m f) -> p m f", f=64)
                nm_ = 2 * NT
                S.add('dve', (lambda o, i: (lambda e: e.tensor_reduce(out=o, in_=i, axis=AX.X, op=ALU.add)))(mean, Y4),
                      reads=[Ytm], writes=[mean])
                TS('dve', mean, mean, 1.0 / 64, None, ALU.mult)
                TT('dve', YC4, Y4, mean.unsqueeze(2).to_broadcast([128, nm_, 64]), ALU.subtract)
                ACT(YQ, YC, AF.Square)
                S.add('dve', (lambda o, i: (lambda e: e.tensor_reduce(out=o, in_=i, axis=AX.X, op=ALU.add)))(var, YQ4),
                      reads=[YQ], writes=[var])
                TS('dve', var, var, 1.0 / 64, 64e-5, ALU.mult, ALU.add)
                ACT(var, var, AF.Ln)
                ACT(var, var, AF.Exp, scale=-0.5)
                TT('dve', YN.rearrange("p (m f) -> p m f", f=64), YC4, var.unsqueeze(2).to_broadcast([128, nm_, 64]), ALU.mult)
                pyt = psb_bf(2, GT)
                TRS([(pyt[:, jj * 128:(jj + 1) * 128], YN[:, jj * 128:(jj + 1) * 128], ident) for jj in range(NT)],
                    reads=[YN, ident], writes=[pyt])
                YF = YC
                ACT(YF, pyt, AF.Identity, bias=ppc('gn_b', hp), scale=ppc('gn_w', hp))
                TT('dve', YF, YF, BONs[g % 4], ALU.add)
                TT('dve', YAg[par], YF, SZ[:, c0:c0 + GT], ALU.mult)
                DMA('sp', ya_d[hp, :, c0:c0 + GT], YAg[par], reads=[YAg[par]], writes=[('DR', 'ya', hp, hp + 1, 0, 1, False)])

            nparts = split_parts(nxt, [1.0] * (NG + 3))
            for it in range(-2, NG + 1):
                streams = []
                if 0 <= it < NG:
                    streams.append(cap(emit_chain, it))
                if 0 <= it + 1 < NG:
                    streams.append(cap(emit_B, it + 1))
                if 0 <= it - 1 < NG:
                    streams.append(cap(emit_post, it - 1))
                if 0 <= it + 2 < NG:
                    streams.append(cap(emit_A, it + 2))
                streams.append(nparts[it + 2])
                S.merge_streams(streams)

        def nxt_stream(ji):
            return cap(inproj_job, ji + 1) if ji + 1 < len(jobs_seq) else []

        if jobs_seq:
            inproj_job(0)
        if n_hp_r > 0:
            lora_prep()
        for hp in range(n_hp_r):
            rwkv_headpair(hp, hp, nxt_stream(hp))

        AR_.reset()
        if n_hp_m > 0:
            QA = AR_.a(2048); QB = AR_.a(2048); KA = AR_.a(2048); KB = AR_.a(2048)
            VA = AR_.a(2048); VB = AR_.a(2048)
            for t in (QA, QB, KA, KB):
                MEMSET('pool', t, 0.0)
            MEMSET('pool', VA, 1.0)
            MEMSET('pool', VB, 1.0)
            DMA('sp', KA[64:72, :], ind_d, writes=[KA[64:72, :]])
            DMA('sp', KB[0:8, :], ind_d, writes=[KB[0:8, :]])
        m_base = AR_.off

        def moba_headpair(hp, ji, nxt):
            Rq, Rkq, Rvq, SZ = RAW[ji % 2]
            AR_.off = m_base
            A = AR_.a
            nparts = split_parts(nxt, [2.0, 1.0, 2.0, 3.0, 4.0])
            Qf = A(2048, F32); Kf = A(2048, F32)

            def norm_stream(raw, wname, dA, dB, Ff, SQm, RIm, bank):
                for g in range(4):
                    c0 = g * 512
                    ACT(SQm, raw[:, c0:c0 + 512], AF.Square)
                    pk = psb_f32(bank)
                    MM([(pk, blk2, SQm, True, True)], reads=[blk2, SQm], writes=[pk])
                    ACT(RIm, pk, AF.Ln, bias=ppc_eps, scale=1.0 / 64)
                    ACT(RIm, RIm, AF.Exp, scale=-0.5)
                    STT('dve', Ff[:, c0:c0 + 512], raw[:, c0:c0 + 512], ppc(wname), RIm, ALU.mult, ALU.mult)
                    CP('act', dA[0:64, c0:c0 + 512], Ff[0:64, c0:c0 + 512])
                    CP('dve', dB[64:128, c0:c0 + 512], Ff[64:128, c0:c0 + 512])

            def v_stream():
                pv = psb2_bf(2)
                TRS([(pv[:, t * 128:(t + 1) * 128], Rvq[:, t * 128:(t + 1) * 128], ident) for t in range(16)],
                    reads=[Rvq, ident], writes=[pv])
                pv3 = pv.rearrange("p (t f) -> p t f", f=128)
                CP('act', VA.rearrange("p (t f) -> p t f", f=128)[:, :, 0:64], pv3[:, :, 0:64])
                CP('dve', VB.rearrange("p (t f) -> p t f", f=128)[:, :, 64:128], pv3[:, :, 64:128])

            SQq = A(512); RIq = A(512, F32); SQk = A(512); RIk = A(512, F32)
            st_q = cap(norm_stream, Rq, 'qnw', QA, QB, Qf, SQq, RIq, 0)
            st_k = cap(norm_stream, Rkq, 'knw', KA, KB, Kf, SQk, RIk, 1)
            st_v = cap(v_stream)
            S.merge_streams([st_k, st_q, st_v, nparts[0]])
            nparts[0] = []
            S.capture = []
            kmp = A(16, F32)
            MEMSET('pool', kmp, 0.0)
            S.add('dve', (lambda o, i: (lambda e: e.tensor_reduce(out=o, in_=i, axis=AX.X, op=ALU.add)))(
                kmp[0:64, 0:8], Kf[0:64, :].rearrange("p (n t) -> p n t", t=256)), reads=[Kf[0:64, :]], writes=[kmp[0:64, 0:8]])
            S.add('dve', (lambda o, i: (lambda e: e.tensor_reduce(out=o, in_=i, axis=AX.X, op=ALU.add)))(
                kmp[64:128, 8:16], Kf[64:128, :].rearrange("p (n t) -> p n t", t=256)), reads=[Kf[64:128, :]], writes=[kmp[64:128, 8:16]])
            pg = psb_f32(0, 256)
            MM([(pg[:, qt * 16:(qt + 1) * 16], Qf[:, qt * 128:(qt + 1) * 128], kmp, True, True) for qt in range(16)],
               reads=[Qf, kmp], writes=[pg])
            GM = A(256, F32); G2 = A(256, F32); EQ = A(256, F32); mx = A(32, F32); BI = A(256)
            g3 = lambda t: t.rearrange("p (m n) -> p m n", n=8)
            mxb = mx.unsqueeze(2).to_broadcast([128, 32, 8])

            def rmax(o, i):
                S.add('dve', (lambda o_, i_: (lambda e: e.tensor_reduce(out=o_, in_=i_, axis=AX.X, op=ALU.max)))(o, g3(i)),
                      reads=[i], writes=[o])
            TT('dve', GM, pg, pastneg, ALU.add)
            rmax(mx, GM)
            TT('dve', g3(EQ), g3(GM), mxb, ALU.is_ge)
            STT('dve', G2, EQ, NEG, GM, ALU.mult, ALU.add)
            rmax(mx, G2)
            TT('dve', g3(EQ), g3(G2), mxb, ALU.is_ge)
            STT('dve', G2, EQ, NEG, G2, ALU.mult, ALU.add)
            rmax(mx, G2)
            TT('dve', g3(EQ), g3(GM), mxb, ALU.is_ge)
            TT('dve', EQ, EQ, ownpos, ALU.max)
            TS('dve', BI, EQ, -1.0, -NEG, ALU.add, ALU.mult)
            pbt = psb_bf(1, 1024)
            pbt2 = psb_bf(2, 1024)
            TRS([((pbt if qt < 8 else pbt2)[0:16, (qt % 8) * 128:(qt % 8 + 1) * 128], BI[:, qt * 16:(qt + 1) * 16], ident)
                 for qt in range(16)], reads=[BI, ident], writes=[pbt, pbt2])
            BT_ = A(2048)
            CP('act', BT_[0:16, 0:1024], pbt[0:16, :])
            CP('act', BT_[0:16, 1024:2048], pbt2[0:16, :])
            DMA('sp', QA[64:72, :], BT_[0:8, :], reads=[BT_[0:8, :]], writes=[QA[64:72, :]])
            DMA('sp', QB[0:8, :], BT_[8:16, :], reads=[BT_[8:16, :]], writes=[QB[0:8, :]])
            pro = S.capture
            S.capture = None
            S.merge_streams([pro, nparts[0]])
            PT = [A(512) for _ in range(3)]
            RS = A(512, F32)
            RW = A(512, F32)
            YO = A(512, F32)
            YB = A(2048)
            VA3 = VA.rearrange("p (t f) -> p t f", f=128)
            VB3 = VB.rearrange("p (t f) -> p t f", f=128)
            OpA = psb_f32(2)
            OpB = psb_f32(3)
            for QT in range(4):
                S.capture = []
                q0 = QT * 512
                units = [(h, kt) for kt in range(4 * QT + 4) for h in range(2)]
                nkt = 4 * QT + 4

                def qk(ui):
                    h, kt = units[ui]
                    Kh = KA if h == 0 else KB
                    Qh = QA if h == 0 else QB
                    sp_ = psb_f32(ui % 2)
                    diag = kt >= 4 * QT
                    items = [(sp_, Kh[:, kt * 128:(kt + 1) * 128], Qh[:, q0:q0 + 512], True, not diag)]
                    rds = [Kh[:, kt * 128:(kt + 1) * 128], Qh[:, q0:q0 + 512]]
                    if diag:
                        items.append((sp_, ident, cmask[:, kt - 4 * QT, :], False, True))
                        rds += [ident, cmask[:, kt - 4 * QT, :]]
                    MM(items, reads=rds, writes=[sp_])
                qk(0)
                for ui, (h, kt) in enumerate(units):
                    if ui + 1 < len(units):
                        qk(ui + 1)
                    sp_ = psb_f32(ui % 2)
                    pt = PT[ui % 3]
                    ACT(pt, sp_, AF.Exp, scale=0.125)
                    Vh = VA3 if h == 0 else VB3
                    Oh = OpA if h == 0 else OpB
                    MM([(Oh, Vh[:, kt, :], pt, kt == 0, kt == nkt - 1)], reads=[Vh[:, kt, :], pt], writes=[Oh])
                ACT(RS[64:128, :], OpA[64:128, :], AF.Ln)
                ACT(RS[0:64, :], OpB[0:64, :], AF.Ln)
                ACT(RS, RS, AF.Exp, scale=-1.0)
                pw = psb_f32(0)
                MM([(pw, swapP, RS, True, True)], reads=[swapP, RS], writes=[pw])
                CP('act', RW, pw)
                TT('dve', YO[0:64, :], OpA[0:64, :], RW[0:64, :], ALU.mult)
                TT('dve', YO[64:128, :], OpB[64:128, :], RW[64:128, :], ALU.mult)
                TT('dve', YB[:, q0:q0 + 512], YO, SZ[:, q0:q0 + 512], ALU.mult)
                att = S.capture
                S.capture = None
                S.merge_streams([att, nparts[QT + 1]])
            out_dmas.append(DMA('sp', yb_d[hp], YB, reads=[YB], writes=[('DR', 'yb', hp, hp + 1, 0, 1, False)]))

        if n_hp_m > 0:
            epsb = AR_.a(2, F32)
            MEMSET('pool', epsb, 1e-6)
            ppc_eps = epsb[:, 0:1]
            m_base = AR_.off
        for hp in range(n_hp_m):
            moba_headpair(hp, n_hp_r + hp, nxt_stream(n_hp_r + hp))

        if do_final:
            for th in range(2):
                AR_.reset()
                t0 = th * 1024
                YAh = RAWT[0][:, :]; YBh = RAWT[1][:, :]
                YA3 = YAh.rearrange("p (k t) -> p k t", t=1024)
                YB3 = YBh.rearrange("p (k t) -> p k t", t=1024)
                for k in range(8):
                    DMA('sp', YA3[:, k, :], ya_d[k, :, t0:t0 + 1024], reads=[('DR', 'ya', k, k + 1, 0, 1, False)], writes=[YA3[:, k, :]])
                    DMA('sp', YB3[:, k, :], yb_d[k, :, t0:t0 + 1024], reads=[('DR', 'yb', k, k + 1, 0, 1, False)], writes=[YB3[:, k, :]])
                MG = AR_.a(16 * 1024)
                MG3 = MG.rearrange("p (k t) -> p k t", t=1024)
                sga = AR_.a(1024); sgb = AR_.a(1024); m1 = AR_.a(1024, F32)
                fj = []
                for c in range(16):
                    fj.append(('in', COL['ga'] + c * 128))
                    fj.append(('pa', c * 128))
                    fj.append(('in', COL['gb'] + c * 128))
                    fj.append(('pb', c * 128))
                fq = []
                fst = {'i': 0}

                def next_fw():
                    while fst['i'] < len(fj) and len(fq) < NW:
                        kind, col = fj[fst['i']]
                        fst['i'] += 1
                        if kind == 'in':
                            fq.append(wload_in(col))
                        elif kind == 'pa':
                            fq.append(wload_proj(wpa_d, col))
                        else:
                            fq.append(wload_proj(wpb_d, col))
                    return fq.pop(0)

                def run_half_proj(wbuf, nk, src3, evac):
                    acc = PSA[:, (run_half_proj.n % 2) * 1024:(run_half_proj.n % 2 + 1) * 1024]
                    run_half_proj.n += 1
                    items = []
                    for k in range(nk):
                        for tt in range(2):
                            items.append((acc[:, tt * 512:(tt + 1) * 512], wbuf[:, k, :], src3(k, tt), k == 0, k == nk - 1))
                    MM(items, reads=[wbuf[:, 0:nk, :]] + run_half_proj.rd, writes=[acc])
                    evac(acc)
                run_half_proj.n = 0
                for c in range(16):
                    run_half_proj.rd = [hT[:, :, t0:t0 + 1024]]
                    run_half_proj(next_fw(), KC, lambda k, tt: hT[:, k, t0 + tt * 512:t0 + (tt + 1) * 512],
                                  lambda acc: ACT(sga, acc, AF.Sigmoid))
                    run_half_proj.rd = [YAh]
                    run_half_proj(next_fw(), 8, lambda k, tt: YA3[:, k, tt * 512:(tt + 1) * 512],
                                  lambda acc: TT('dve', m1, acc, sga, ALU.mult))
                    run_half_proj.rd = [hT[:, :, t0:t0 + 1024]]
                    run_half_proj(next_fw(), KC, lambda k, tt: hT[:, k, t0 + tt * 512:t0 + (tt + 1) * 512],
                                  lambda acc: ACT(sgb, acc, AF.Sigmoid))
                    run_half_proj.rd = [YBh]

                    def ev_b(acc, c=c):
                        TT('dve', sgb, acc, sgb, ALU.mult)
                        TT('dve', MG3[:, c, :], m1, sgb, ALU.add)
                    run_half_proj(next_fw(), 8, lambda k, tt: YB3[:, k, tt * 512:(tt + 1) * 512], ev_b)
                WO = [AR_.a(16 * 512) for _ in range(2)]
                XR = [AR_.a(512, F32) for _ in range(1)]
                OT = [AR_.a(512, F32) for _ in range(1)]
                n_o = 0
                for c4 in range(4):
                    wo = WO[c4 % 2]
                    wo3 = wo.rearrange("p (k m) -> p k m", m=512)
                    DMA('pool', wo3, wo_d[:, c4 * 512:(c4 + 1) * 512].rearrange("(kc p) m -> p kc m", p=128), writes=[wo])
                    for tl in range(8):
                        tok0 = t0 + tl * 128
                        xr = XR[0]
                        ot = OT[0]
                        acc = PSB[:, (n_o % 4) * 512:(n_o % 4 + 1) * 512]
                        n_o += 1
                        DMA('sp', xr, x_d[tok0:tok0 + 128, c4 * 512:(c4 + 1) * 512], writes=[xr])
                        MM([(acc, MG3[:, k, tl * 128:(tl + 1) * 128], wo3[:, k, :], k == 0, k == 15) for k in range(16)],
                           reads=[MG, wo], writes=[acc])
                        TT('dve', ot, acc, xr, ALU.add)
                        out_dmas.append(DMA('sp', out_d[tok0:tok0 + 128, c4 * 512:(c4 + 1) * 512], ot, reads=[ot]))

        cnt = S.emit(final_wait_ops=out_dmas)
    return nc, cnt, len(S.ops)


def _consts():
    bf = ml_dtypes.bfloat16
    cbv = np.zeros((128, NCB), np.float32)
    idx = np.arange(128)
    cbv[:, CB['ident']:CB['ident'] + 128] = np.eye(128)
    same = (idx[:, None] // 64) == (idx[None, :] // 64)
    cbv[:, CB['blk2']:CB['blk2'] + 128] = same
    cbv[:, CB['m_su']:CB['m_su'] + 128] = same & (idx[:, None] < idx[None, :])
    cbv[:, CB['m_iu']:CB['m_iu'] + 128] = same & (idx[:, None] <= idx[None, :])
    cbv[:, CB['m_sl']:CB['m_sl'] + 128] = same & (idx[:, None] > idx[None, :])
    cbv[:, CB['onesA']:CB['onesA'] + 64] = 1.0
    cbv[:, CB['onesB'] + 64:CB['onesB'] + 128] = 1.0
    cm = np.zeros((128, 4, 512), np.float32)
    for ktl in range(4):
        kpos = ktl * 128 + idx[:, None]
        qpos = np.arange(512)[None, :]
        kb = kpos // 256
        qb = qpos // 256
        ok = np.where(kb == qb, kpos <= qpos, kb < qb)
        cm[:, ktl, :] = np.where(ok, 0.0, NEG)
    cbv[:, CB['cmask']:CB['cmask'] + 2048] = cm.reshape(128, 2048)
    cfv = np.zeros((128, NCF), np.float32)
    sm = np.ones(512, np.float32)
    sm[::64] = 0.0
    cfv[:, CF['scanmask']:CF['scanmask'] + 512] = sm[None, :]
    pn = np.zeros((16, 2, 8), np.float32)
    op = np.zeros((16, 2, 8), np.float32)
    for qt in range(16):
        qb = qt // 2
        for n in range(8):
            pn[qt, :, n] = 0.0 if n < qb else NEG
            op[qt, :, n] = 1.0 if n >= qb else 0.0
    cfv[:, CF['pastneg']:CF['pastneg'] + 256] = pn.reshape(1, 256)
    cfv[:, CF['ownpos']:CF['ownpos'] + 256] = op.reshape(1, 256)
    cfv[:, CF['swapP']:CF['swapP'] + 128] = np.roll(np.eye(128, dtype=np.float32), 64, axis=1)
    ind = np.zeros((8, S_TOK), np.float32)
    for n in range(8):
        ind[n, n * 256:(n + 1) * 256] = 1.0
    return cbv.astype(bf), cfv, ind.astype(bf)


def _pack_params(i):
    ppv = np.zeros((128, NPP), np.float32)

    def fm(v):
        return np.ascontiguousarray(v.reshape(-1, 128).T)
    for nm in ('mu_r', 'mu_k', 'mu_v', 'w0', 'a0', 'k_k', 'k_a', 'r_k', 'gn_w', 'gn_b'):
        ppv[:, PP[nm]:PP[nm] + 8] = fm(i[nm][0])
    ppv[:, PP['norm_w']:PP['norm_w'] + 16] = fm(i['norm_w'][0])
    ppv[0:64, PP['mu_wa']] = i['mu_w'][0]
    ppv[64:128, PP['mu_wa']] = i['mu_a'][0]
    ppv[0:64, PP['qnw']] = i['q_norm_w'][0]
    ppv[64:128, PP['qnw']] = i['q_norm_w'][0]
    ppv[0:64, PP['knw']] = i['k_norm_w'][0]
    ppv[64:128, PP['knw']] = i['k_norm_w'][0]
    lora = np.concatenate([i['w_decay_up'][0], i['w_iclr_up'][0]], axis=0).astype(np.float32)
    return ppv, np.ascontiguousarray(lora)


_CACHE = {}


def make_in_maps(inputs, n_cores=8):
    i = {k: np.asarray(v) for k, v in inputs.items()}
    cbv, cfv, ind = _consts()
    ppv, lora = _pack_params(i)
    shared = dict(w_in=np.ascontiguousarray(i['w_in'][0]), w_pa=np.ascontiguousarray(i['w_proj_rwkv'][0]),
                  w_pb=np.ascontiguousarray(i['w_proj_moba'][0]), w_out=np.ascontiguousarray(i['w_out'][0]),
                  lora=lora, pp=ppv, cb=cbv, cf=cfv, ind=ind)
    maps = []
    for c in range(n_cores):
        m = dict(shared)
        m['x'] = np.ascontiguousarray(i['x'][c])
        maps.append(m)
    return maps


def kernel(**inputs):
    if 'nc' not in _CACHE:
        _CACHE['nc'] = build()[0]
    nc = _CACHE['nc']
    maps = make_in_maps(inputs, 8)
    res = run_bass_kernel_spmd(nc, maps, core_ids=list(range(8)))
    out = np.stack([np.asarray(r['out']) for r in res.results], axis=0)
    return out.astype(np.float32)
```

```python
import contextlib
import numpy as np
import ml_dtypes
import concourse.bass as bass
import concourse.mybir as mybir
from concourse.bass_utils import run_bass_kernel_spmd

F32 = mybir.dt.float32
BF16 = mybir.dt.bfloat16
AF = mybir.ActivationFunctionType
ALU = mybir.AluOpType
AX = mybir.AxisListType

S_TOK = 2048
D = 2048
KC = 16
IN_COLS = 12416
COL = dict(r=0, k=1024, v=2048, za=3072, wd=4096, q=4224, kq=5248, vq=6272, zb=7296, ga=8320, gb=10368)
CDEC = 0.6065306597126334
NEG = -1.0e30
STOP = 99
NOMERGE = False
STRICT = True

PP = dict(mu_r=0, mu_k=8, mu_v=16, w0=24, a0=32, k_k=40, k_a=48, r_k=56, gn_w=64, gn_b=72, norm_w=80,
          mu_wa=96, qnw=97, knw=98)
NPP = 100
CB = dict(ident=0, blk2=128, m_su=256, m_iu=384, m_sl=512, onesA=640, onesB=768, cmask=896)
NCB = 896 + 4 * 512
CF = dict(scanmask=0, pastneg=512, ownpos=768, swapP=1024)
NCF = 1152


_DTSZ = {}


def _dtsize(dt):
    s = _DTSZ.get(dt)
    if s is None:
        name = str(dt)
        s = 4 if '32' in name else 2 if '16' in name else 1 if '8' in name else 8
        _DTSZ[dt] = s
    return s


def box_of(ap):
    dims = ap.ap
    sz = _dtsize(ap.dtype)
    pstep, pcnt = dims[0]
    off = int(ap.offset)
    if pstep == 0:
        pstep = 1 << 40
    p0 = off // pstep
    f0 = off % pstep
    ext = 0
    for st, cn in dims[1:]:
        ext += abs(st) * (cn - 1)
    f1 = f0 + ext + 1
    if 'PSUM' in str(ap.space).upper():
        b0 = (f0 * sz) // 2048
        b1 = ((f1 * sz) - 1) // 2048
        return ('PS', ap.name, 0, 128, b0 * 2048, (b1 + 1) * 2048, True)
    return ('SB', ap.name, p0, p0 + pcnt, f0 * sz, f1 * sz, False)


class Op:
    __slots__ = ('idx', 'eng', 'fn', 'deps', 'signal', 'count', 'is_dma', 'dsem', 'dcount', 'prev_slot')

    def __init__(self, idx, eng, fn, is_dma):
        self.idx = idx
        self.eng = eng
        self.fn = fn
        self.deps = {}
        self.signal = False
        self.count = 0
        self.is_dma = is_dma
        self.dsem = None
        self.dcount = 0
        self.prev_slot = None


class Sched:
    ENGS = ('pe', 'act', 'dve', 'pool', 'sp')

    def __init__(self, nc, n_dma_slots=10):
        self.nc = nc
        self.ops = []
        self.recs = {}
        self.n_dma_slots = n_dma_slots

    def _touch(self, op, box, is_write):
        kind, name, p0, p1, f0, f1, excl = box
        lst = self.recs.setdefault(name, [])
        found = None
        for r in lst:
            if r[0] < p1 and p0 < r[1] and r[2] < f1 and f0 < r[3]:
                if r[4] is not None:
                    if (not is_write) or excl:
                        op.deps[r[4]] = True
                    else:
                        op.deps.setdefault(r[4], False)
                if is_write or excl:
                    for e, o in r[5].items():
                        op.deps.setdefault(o, False)
            if r[0] == p0 and r[1] == p1 and r[2] == f0 and r[3] == f1:
                found = r
        if found is None:
            found = [p0, p1, f0, f1, None, {}]
            lst.append(found)
        if is_write or excl:
            found[4] = op.idx
            found[5] = {}
        else:
            found[5][op.eng] = op.idx

    capture = None

    def add(self, eng, fn, reads=(), writes=(), dma=False, cost=0.5):
        if self.capture is not None:
            self.capture.append((eng, fn, list(reads), list(writes), dma, cost))
            return -1
        op = Op(len(self.ops), eng, fn, dma)
        self.ops.append(op)
        for ap in reads:
            if ap is not None and not isinstance(ap, (int, float)):
                self._touch(op, ap if isinstance(ap, tuple) else box_of(ap), False)
        for ap in writes:
            if ap is not None:
                self._touch(op, ap if isinstance(ap, tuple) else box_of(ap), True)
        op.deps.pop(op.idx, None)
        return op.idx

    def commit(self, lst):
        for it in lst:
            self.add(*it[:5])

    def merge_streams(self, streams):
        if NOMERGE:
            for st in streams:
                self.commit(st)
            return
        pos = [0] * len(streams)
        ready = [0.0] * len(streams)
        free = {e: 0.0 for e in self.ENGS}
        while True:
            best = None
            for si, st in enumerate(streams):
                if pos[si] >= len(st):
                    continue
                it = st[pos[si]]
                t = max(free[it[0]], ready[si])
                if best is None or t < best[0] - 1e-9:
                    best = (t, si)
            if best is None:
                break
            t, si = best
            it = streams[si][pos[si]]
            pos[si] += 1
            self.add(*it[:5])
            if it[4]:
                free[it[0]] = t + 0.06
                ready[si] = t + 0.06
            else:
                free[it[0]] = t + it[5]
                ready[si] = t + it[5] + 0.12

    def merge(self, main, bg):
        return self.merge_streams([main, bg])
        nb = len(bg)
        nm = max(len(main), 1)
        j = 0
        for i, it in enumerate(main):
            self.add(*it[:5])
            tgt = (i + 1) * nb // nm
            while j < tgt:
                self.add(*bg[j][:5])
                j += 1
        while j < nb:
            self.add(*bg[j])
            j += 1

    def emit(self, final_wait_ops=()):
        nc = self.nc
        ops = self.ops
        for op in ops:
            for d, raw in op.deps.items():
                dop = ops[d]
                if dop.is_dma:
                    continue
                if (not op.is_dma) and dop.eng == op.eng and (op.eng == 'pe' or (not raw and not STRICT)):
                    continue
                dop.signal = True
        for d in final_wait_ops:
            if not ops[d].is_dma:
                ops[d].signal = True
        cnt = {e: 0 for e in self.ENGS}
        for op in ops:
            if op.is_dma:
                continue
            if op.signal:
                cnt[op.eng] += 1
            op.count = cnt[op.eng]
        slot_state = {}
        for op in ops:
            if not op.is_dma:
                continue
            st = slot_state.setdefault(op.eng, {'next': 0, 'counts': [0] * self.n_dma_slots,
                                                'last': [None] * self.n_dma_slots})
            s = st['next']
            st['next'] = (s + 1) % self.n_dma_slots
            op.prev_slot = st['last'][s]
            st['counts'][s] += 16
            op.dsem = (op.eng, s)
            op.dcount = st['counts'][s]
            st['last'][s] = op.idx
        used = [e for e in self.ENGS if any(o.eng == e for o in ops)]
        if 'sp' not in used:
            used.append('sp')
        with contextlib.ExitStack() as es:
            sems = {e: es.enter_context(nc.semaphore('s_' + e)) for e in used}
            dsems = {}
            for e in slot_state:
                for s in range(self.n_dma_slots):
                    dsems[(e, s)] = es.enter_context(nc.semaphore('d_%s_%d' % (e, s)))
            block = es.enter_context(nc.Block())

            def run_engine(ename, eng):
                waited = {e: 0 for e in self.ENGS}
                dwaited = {}

                def wait_on(dop):
                    if dop.is_dma:
                        if dwaited.get(dop.dsem, 0) < dop.dcount:
                            eng.wait_ge(dsems[dop.dsem], dop.dcount)
                            dwaited[dop.dsem] = dop.dcount
                    elif dop.count > waited[dop.eng]:
                        eng.wait_ge(sems[dop.eng], dop.count)
                        waited[dop.eng] = dop.count

                for op in ops:
                    if op.eng != ename:
                        continue
                    for d in sorted(op.deps):
                        dop = ops[d]
                        raw = op.deps[d]
                        if (not dop.is_dma) and (not op.is_dma) and dop.eng == ename and (ename == 'pe' or (not raw and not STRICT)):
                            continue
                        wait_on(dop)
                    if op.is_dma and op.prev_slot is not None:
                        wait_on(ops[op.prev_slot])
                    ins = op.fn(eng)
                    if op.is_dma:
                        ins.then_inc(dsems[op.dsem], 16)
                    elif op.signal:
                        ins.then_inc(sems[ename], 1)
                if ename == 'sp':
                    for d in final_wait_ops:
                        wait_on(ops[d])

            @block.tensor
            def _(eng):
                run_engine('pe', eng)

            @block.scalar
            def _(eng):
                run_engine('act', eng)

            @block.vector
            def _(eng):
                run_engine('dve', eng)

            @block.gpsimd
            def _(eng):
                run_engine('pool', eng)

            @block.sync
            def _(eng):
                run_engine('sp', eng)
        return cnt


def build(n_hp_r=8, n_hp_m=8, do_final=True, dbg=False):
    nc = bass.Bass("TRN2", target_bir_lowering=False)
    x_d = nc.dram_tensor("x", [S_TOK, D], F32, kind="ExternalInput").ap()
    win_d = nc.dram_tensor("w_in", [D, IN_COLS], F32, kind="ExternalInput").ap()
    wpa_d = nc.dram_tensor("w_pa", [1024, D], F32, kind="ExternalInput").ap()
    wpb_d = nc.dram_tensor("w_pb", [1024, D], F32, kind="ExternalInput").ap()
    wo_d = nc.dram_tensor("w_out", [D, D], F32, kind="ExternalInput").ap()
    lora_d = nc.dram_tensor("lora", [128, 1024], F32, kind="ExternalInput").ap()
    pp_d = nc.dram_tensor("pp", [128, NPP], F32, kind="ExternalInput").ap()
    cb_d = nc.dram_tensor("cb", [128, NCB], BF16, kind="ExternalInput").ap()
    cf_d = nc.dram_tensor("cf", [128, NCF], F32, kind="ExternalInput").ap()
    ind_d = nc.dram_tensor("ind", [8, S_TOK], BF16, kind="ExternalInput").ap()
    out_d = nc.dram_tensor("out", [S_TOK, D], F32, kind="ExternalOutput").ap()
    scr_kind = "ExternalOutput" if dbg else "Internal"
    ya_d = nc.dram_tensor("ya_scr", [8, 128, S_TOK], BF16, kind=scr_kind).ap()
    yb_d = nc.dram_tensor("yb_scr", [8, 128, S_TOK], BF16, kind=scr_kind).ap()

    with contextlib.ExitStack() as es:
        def sb(name, shape, dt):
            return es.enter_context(nc.sbuf_tensor(name, shape, dt))

        hT = sb("hT", [128, KC, S_TOK], BF16)
        NW = 4
        wpool = [sb("wp%d" % i, [128, KC, 128], BF16) for i in range(NW)]
        cb = sb("cb_s", [128, NCB], BF16)
        cf = sb("cf_s", [128, NCF], F32)
        pp = sb("pp_s", [128, NPP], F32)
        loraD = sb("loraD_s", [128, 1024], BF16)
        loraI = sb("loraI_s", [128, 1024], BF16)
        TL = sb("TL", [128, S_TOK], BF16)
        RAWT = [sb("rawt%d" % s, [128, 4 * S_TOK], BF16) for s in range(2)]
        RAW = [[RAWT[s][:, j * S_TOK:(j + 1) * S_TOK] for j in range(4)] for s in range(2)]
        ARENA_N = 39424
        arena_t = sb("arena", [128, ARENA_N], BF16)
        PSA = es.enter_context(nc.psum_tensor("PSA", [128, 2048], F32))
        PSB = es.enter_context(nc.psum_tensor("PSB", [128, 2048], F32))

        S = Sched(nc)
        out_dmas = []

        class Arena:
            def __init__(self):
                self.off = 0

            def reset(self):
                self.off = 0

            def a(self, n, dt=BF16):
                nb = n * (2 if dt == F32 else 1)
                nb = (nb + 1) // 2 * 2
                assert self.off + nb <= ARENA_N, (self.off, nb)
                v = arena_t[:, self.off:self.off + nb]
                self.off += nb
                if dt == F32:
                    v = v.bitcast(F32)
                return v

        AR_ = Arena()

        def rd(*aps):
            return [a for a in aps if a is not None and not isinstance(a, (int, float))]

        def fsz(ap):
            n = 1
            for st, cn in ap.ap[1:]:
                n *= cn
            return n

        def ACT(out, in_, func, bias=None, scale=None, accum=None):
            kw = {}
            if bias is not None:
                kw['bias'] = bias
            if scale is not None:
                kw['scale'] = scale
            if accum is not None:
                kw['accum_out'] = accum
            return S.add('act', lambda e: e.activation(out=out, in_=in_, func=func, **kw),
                         reads=rd(in_, bias, scale), writes=[out, accum], cost=0.22 + fsz(out) / 1200.0)

        def TT(eng, out, in0, in1, op):
            return S.add(eng, lambda e: e.tensor_tensor(out=out, in0=in0, in1=in1, op=op),
                         reads=rd(in0, in1), writes=[out], cost=0.1 + fsz(out) / 960.0)

        def TS(eng, out, in0, s1, s2, op0, op1=None):
            if op1 is None:
                return S.add(eng, lambda e: e.tensor_scalar(out=out, in0=in0, scalar1=s1, scalar2=None, op0=op0),
                             reads=rd(in0, s1), writes=[out], cost=0.1 + fsz(out) / 960.0)
            return S.add(eng, lambda e: e.tensor_scalar(out=out, in0=in0, scalar1=s1, scalar2=s2, op0=op0, op1=op1),
                         reads=rd(in0, s1, s2), writes=[out], cost=0.1 + fsz(out) / 960.0)

        def STT(eng, out, in0, scalar, in1, op0, op1):
            eng = 'dve'
            return S.add(eng, lambda e: e.scalar_tensor_tensor(out=out, in0=in0, scalar=scalar, in1=in1,
                                                                op0=op0, op1=op1),
                         reads=rd(in0, scalar, in1), writes=[out], cost=0.1 + fsz(out) / 960.0)

        def CP(eng, out, in_):
            if eng == 'act':
                return S.add('act', lambda e: e.copy(out=out, in_=in_), reads=[in_], writes=[out],
                             cost=0.22 + fsz(out) / 1200.0)
            return S.add(eng, lambda e: e.tensor_copy(out=out, in_=in_), reads=[in_], writes=[out],
                         cost=0.1 + fsz(out) / 960.0)

        def MEMSET(eng, out, val):
            return S.add(eng, lambda e: e.memset(out, val), writes=[out], cost=1.0)

        def MM(items, reads, writes):
            def fn(e):
                ins = None
                for (o, l, r, st, sp) in items:
                    ins = e.matmul(o, lhsT=l, rhs=r, start=st, stop=sp)
                return ins
            c = 0.05
            for (o, l, r, st, sp) in items:
                c += max(0.065, fsz(o) / (600.0 if l.dtype == F32 else 2400.0))
            return S.add('pe', fn, reads=reads, writes=writes, cost=c)

        def TRS(items, reads, writes):
            def fn(e):
                ins = None
                for (o, i, idn) in items:
                    ins = e.transpose(o, i, idn)
                return ins
            return S.add('pe', fn, reads=reads, writes=writes, cost=0.05 + 0.11 * len(items))

        def DMA(q, out, in_, reads=(), writes=()):
            return S.add(q, lambda e: e.dma_start(out=out, in_=in_), reads=reads, writes=writes, dma=True)

        def ppc(name, j=0):
            c = PP[name] + j
            return pp[:, c:c + 1]

        ident = cb[:, CB['ident']:CB['ident'] + 128]
        blk2 = cb[:, CB['blk2']:CB['blk2'] + 128]
        m_su = cb[:, CB['m_su']:CB['m_su'] + 128]
        m_iu = cb[:, CB['m_iu']:CB['m_iu'] + 128]
        m_sl = cb[:, CB['m_sl']:CB['m_sl'] + 128]
        onesA = cb[:, CB['onesA']:CB['onesA'] + 128]
        onesB = cb[:, CB['onesB']:CB['onesB'] + 128]
        cmask = cb[:, CB['cmask']:CB['cmask'] + 2048].rearrange("p (a b) -> p a b", b=512)
        scanmask = cf[:, CF['scanmask']:CF['scanmask'] + 512]
        pastneg = cf[:, CF['pastneg']:CF['pastneg'] + 256]
        ownpos = cf[:, CF['ownpos']:CF['ownpos'] + 256]
        swapP = cf[:, CF['swapP']:CF['swapP'] + 128]

        def psb_f32(bank, n=512, off=0):
            return PSB[:, bank * 512 + off: bank * 512 + off + n]

        def psb_bf(bank, n=1024, off=0):
            v = PSB[:, bank * 512:(bank + 1) * 512].bitcast(BF16)
            return v[:, off:off + n]

        def psb2_bf(bank):
            return PSB[:, bank * 512:(bank + 2) * 512].bitcast(BF16)

        DMA('sp', cb[:], cb_d, writes=[cb[:]])
        DMA('sp', cf[:], cf_d, writes=[cf[:]])
        DMA('sp', pp[:], pp_d, writes=[pp[:]])
        S.add('pool', lambda e: e.memset(loraD[:], 0.0), writes=[loraD[:]])
        S.add('pool', lambda e: e.memset(loraI[:], 0.0), writes=[loraI[:]])
        DMA('pool', loraD[0:64, :], lora_d[0:64, :], writes=[loraD[0:64, :]])
        DMA('pool', loraI[64:128, :], lora_d[64:128, :], writes=[loraI[64:128, :]])

        wstate = {'n': 0}

        def wload_in(col):
            b = wpool[wstate['n'] % NW]
            wstate['n'] += 1
            src = win_d[:, col:col + 128].rearrange("(kc p) m -> p kc m", p=128)
            DMA('pool', b[:], src, writes=[b[:]])
            return b

        def wload_proj(wd, col):
            b = wpool[wstate['n'] % NW]
            wstate['n'] += 1
            src = wd[:, col:col + 128].rearrange("(kc p) m -> p kc m", p=128)
            DMA('pool', b[:, 0:8, :], src, writes=[b[:, 0:8, :]])
            return b

        def inproj(wbuf, nk, rhs_src, evac):
            for half in range(2):
                acc = PSA[:, 0:1024]
                items = []
                for k in range(nk):
                    for tt in range(2):
                        t0 = half * 1024 + tt * 512
                        items.append((acc[:, tt * 512:(tt + 1) * 512], wbuf[:, k, :], rhs_src[:, k, t0:t0 + 512],
                                      k == 0, k == nk - 1))
                for i0 in range(0, len(items), 8):
                    MM(items[i0:i0 + 8], reads=[wbuf[:, 0:nk, :], rhs_src[:, 0:nk, half * 1024:(half + 1) * 1024]], writes=[acc])
                evac(acc, half)

        AR_.reset()
        xt = [AR_.a(2048, F32) for _ in range(2)]
        hb = [AR_.a(2048) for _ in range(2)]
        junk = AR_.a(2048)
        ssb = AR_.a(16, F32)
        rsb = AR_.a(16, F32)
        nwb = pp[:, PP['norm_w']:PP['norm_w'] + 16].unsqueeze(2).to_broadcast([128, 16, 128])
        for tt in range(16):
            xtile = xt[tt % 2]
            DMA('sp', xtile, x_d[tt * 128:(tt + 1) * 128, :], writes=[xtile])
            ACT(junk, xtile, AF.Square, accum=ssb[:, tt:tt + 1])
            TS('dve', rsb[:, tt:tt + 1], ssb[:, tt:tt + 1], 1.0 / D, 1e-6, ALU.mult, ALU.add)
            ACT(rsb[:, tt:tt + 1], rsb[:, tt:tt + 1], AF.Sqrt)
            S.add('dve', (lambda o: (lambda e: e.reciprocal(out=o, in_=o)))(rsb[:, tt:tt + 1]),
                  reads=[rsb[:, tt:tt + 1]], writes=[rsb[:, tt:tt + 1]])
            if tt % 2:
                TS('dve', hb[tt % 2], xtile, rsb[:, tt:tt + 1], None, ALU.mult)
            else:
                ACT(hb[tt % 2], xtile, AF.Identity, scale=rsb[:, tt:tt + 1])
            pst = psb2_bf((tt % 2) * 2)
            TRS([(pst[:, k * 128:(k + 1) * 128], hb[tt % 2][:, k * 128:(k + 1) * 128], ident) for k in range(16)],
                reads=[hb[tt % 2], ident], writes=[pst])
            TT('dve', hT[:, :, tt * 128:(tt + 1) * 128], pst.rearrange("p (a b) -> p a b", b=128), nwb, ALU.mult)

        jobs = []
        jobs_seq = [('R', hp) for hp in range(n_hp_r)] + [('M', hp) for hp in range(n_hp_m)]
        for hp in range(n_hp_r):
            for nm in ('r', 'k', 'v', 'za'):
                jobs.append(('R', hp, nm, COL[nm] + hp * 128))
            if hp == 0:
                jobs.append(('R', 0, 'wd', COL['wd']))
        for hp in range(n_hp_m):
            for nm in ('q', 'kq', 'vq', 'zb'):
                jobs.append(('M', hp, nm, COL[nm] + hp * 128))
        PREF = NW - 1
        wq = []
        jstate = {'issued': 0}

        def next_w():
            while jstate['issued'] < len(jobs) and len(wq) < PREF + 1:
                wq.append(wload_in(jobs[jstate['issued']][3]))
                jstate['issued'] += 1
            return wq.pop(0)

        def ev_copy(dst, eng):
            def f(acc, half):
                CP(eng, dst[:, half * 1024:(half + 1) * 1024], acc)
            return f

        def ev_silu(dst):
            def f(acc, half):
                ACT(dst[:, half * 1024:(half + 1) * 1024], acc, AF.Silu)
            return f

        def inproj_job(ji):
            kind, hp = jobs_seq[ji]
            R0, R1, R2, R3 = RAW[ji % 2]
            inproj(next_w(), KC, hT, ev_copy(R0, 'act'))
            inproj(next_w(), KC, hT, ev_copy(R1, 'dve'))
            inproj(next_w(), KC, hT, ev_copy(R2, 'act'))
            inproj(next_w(), KC, hT, ev_silu(R3))

        def lora_prep():
            AR_.reset()
            lraw = AR_.a(2048)
            ld = AR_.a(2048)
            inproj(next_w(), KC, hT, ev_copy(lraw, 'dve'))
            TT('dve', ld[:, 1:2048], lraw[:, 0:2047], lraw[:, 1:2048], ALU.subtract)
            TS('dve', ld[:, 0:1], lraw[:, 0:1], -1.0, None, ALU.mult)
            STT('dve', TL[:], ld, ppc('mu_wa'), lraw, ALU.mult, ALU.add)
            ACT(TL[0:64, :], TL[0:64, :], AF.Tanh)

        def split_parts(lst, fracs):
            tot = float(sum(fracs))
            out = []
            acc = 0.0
            i0 = 0
            for f in fracs:
                acc += f
                i1 = int(round(len(lst) * acc / tot))
                out.append(lst[i0:i1])
                i0 = i1
            out[-1] = out[-1] + lst[i0:]
            return out

        def cap(fn, *args):
            S.capture = []
            fn(*args)
            lst = S.capture
            S.capture = None
            return lst

        def rwkv_headpair(hp, ji, nxt):
            Rr, Rk, Rv, SZ = RAW[ji % 2]
            if STOP <= 1:
                return
            AR_.reset()
            A = AR_.a
            GT = 256
            NG = S_TOK // GT
            Sf = [A(128, F32) for _ in range(2)]
            Sbf = A(128)
            Ssc = A(128, F32)
            Stmp = A(128, F32)
            Zsb = A(128)
            Usb = A(128)
            tinyb_t = A(2, F32)
            YAg = [A(GT) for _ in range(2)]
            ARfs = [A(2 * GT, F32) for _ in range(2)]
            KTfs = [A(GT, F32) for _ in range(2)]
            BTfs = [A(GT, F32) for _ in range(2)]
            KTbs = [A(GT) for _ in range(2)]
            BTbs = [A(GT) for _ in range(2)]
            vgs = [A(GT) for _ in range(2)]
            WCs = [A(4, F32) for _ in range(4)]
            BONs = [A(GT) for _ in range(4)]
            ARb = [A(2 * GT) for _ in range(2)]
            TM = [A(3 * GT) for _ in range(2)]
            AKs = [A(2 * GT) for _ in range(2)]
            RKs = [A(2 * GT) for _ in range(2)]
            RBs = [A(2 * GT) for _ in range(2)]
            TTs = [A(2 * GT) for _ in range(2)]
            Ytms = [A(GT, F32) for _ in range(2)]
            mean = A(4, F32); var = A(4, F32)
            YC = A(GT, F32); YQ = A(GT, F32); YN = A(GT)
            Dm = A(GT, F32); SQ = A(GT); PROD = A(GT)
            omu = A(4, F32)
            rg = A(GT, F32); kg = A(GT, F32); SG = A(GT, F32); CS = A(GT, F32); ag = A(GT, F32)
            E1 = A(GT, F32); E2 = A(GT, F32); E3 = A(GT, F32)
            KK = A(GT, F32); RI = A(GT, F32); Tt = A(GT, F32); K2 = A(GT, F32); Bb = A(GT, F32)
            N0s = A(2 * GT); X0s = A(2 * GT)
            Ns = [A(2 * GT) for _ in range(2)]
            Xs = [A(2 * GT) for _ in range(2)]
            Ps = [A(2 * GT) for _ in range(2)]
            MEMSET('pool', tinyb_t, 1e-12)
            tinyb = tinyb_t[:, 0:1]
            for mi, mu in enumerate(('mu_r', 'mu_k', 'mu_v')):
                TS('dve', omu[:, mi:mi + 1], ppc(mu, hp), -1.0, 1.0, ALU.mult, ALU.add)
            MEMSET('pool', Sf[0], 0.0)
            MEMSET('pool', Sbf, 0.0)
            NT = GT // 128
            NM = NT * 2
            PA2 = PSA[:, 1024:1536]
            PA3 = PSA[:, 1536:2048]

            def v3(ap):
                return ap.rearrange("p (j t) -> p j t", t=128)

            def emit_A(g):
                c0 = g * GT
                par = g % 2
                ARf = ARfs[par]; KTf = KTfs[par]; BTf = BTfs[par]; vg = vgs[par]
                ARfv = ARf.rearrange("p (j w t) -> p j w t", w=2, t=128)
                for mi, (raw, mu, dst) in enumerate(((Rr, 'mu_r', rg), (Rk, 'mu_k', kg), (Rv, 'mu_v', vg))):
                    if g == 0:
                        ACT(Dm[:, 1:GT], raw[:, 0:GT - 1], AF.Identity, scale=ppc(mu, hp))
                        TS('dve', Dm[:, 0:1], raw[:, 0:1], 0.0, None, ALU.mult)
                    else:
                        ACT(Dm, raw[:, c0 - 1:c0 + GT - 1], AF.Identity, scale=ppc(mu, hp))
                    STT('dve', dst, raw[:, c0:c0 + GT], omu[:, mi:mi + 1], Dm, ALU.mult, ALU.add)
                pu = PA2[:, 0:GT]
                pa_ = PA2[:, GT:2 * GT]
                MM([(pu, loraD[:, hp * 128:(hp + 1) * 128], TL[:, c0:c0 + GT], True, True)],
                   reads=[loraD[:, hp * 128:(hp + 1) * 128], TL[:, c0:c0 + GT]], writes=[pu])
                MM([(pa_, loraI[:, hp * 128:(hp + 1) * 128], TL[:, c0:c0 + GT], True, True)],
                   reads=[loraI[:, hp * 128:(hp + 1) * 128], TL[:, c0:c0 + GT]], writes=[pa_])
                ACT(SG, pu, AF.Sigmoid, bias=ppc('w0', hp))
                ACT(ag, pa_, AF.Sigmoid, bias=ppc('a0', hp))
                S.add('dve', (lambda o, m, d1: (lambda e: e.tensor_tensor_scan(out=o, data0=m, data1=d1, initial=0.0,
                                                                                op0=ALU.mult, op1=ALU.add)))(CS, scanmask[:, 0:GT], SG),
                      reads=[scanmask[:, 0:GT], SG], writes=[CS], cost=0.1 + 2 * GT / 960.0)
                ACT(E1, CS, AF.Exp, scale=-CDEC)
                ACT(WCs[g % 4], CS.rearrange("p (c t) -> p c t", t=64)[:, :, 63], AF.Exp, scale=-CDEC)
                ACT(E3, CS, AF.Exp, scale=CDEC)
                TT('dve', SG, CS, SG, ALU.subtract)
                ACT(E2, SG, AF.Exp, scale=-CDEC)
                ACT(KK, kg, AF.Identity, scale=ppc('k_k', hp))
                ACT(SQ, kg, AF.Square, scale=ppc('k_k', hp))
                pk = PA2[:, 0:GT]
                MM([(pk, blk2, SQ, True, True)], reads=[blk2, SQ], writes=[pk])
                ACT(RI, pk, AF.Ln, bias=tinyb)
                ACT(RI, RI, AF.Exp, scale=-0.5)
                TT('dve', KK, KK, RI, ALU.mult)
                TS('dve', Tt, ag, -1.0, ppc('k_a', hp), ALU.add, ALU.mult)
                STT('dve', K2, Tt, 1.0, kg, ALU.add, ALU.mult)
                TT('dve', Bb, KK, ag, ALU.mult)
                STT('dve', PROD, rg, ppc('r_k', hp), K2, ALU.mult, ALU.mult)
                pb_ = PA2[:, GT:2 * GT]
                MM([(pb_, blk2, PROD, True, True)], reads=[blk2, PROD], writes=[pb_])
                TT('dve', BONs[g % 4], pb_, vg, ALU.mult)
                TT('dve', ARfv[:, :, 1, :], v3(rg), v3(E1), ALU.mult)
                STT('dve', ARfv[:, :, 0, :], v3(KK), -1.0, v3(E2), ALU.mult, ALU.mult)
                TT('dve', KTf, K2, E3, ALU.mult)
                TT('dve', BTf, Bb, E3, ALU.mult)
                CP('act', KTbs[par], KTf)
                CP('act', BTbs[par], BTf)

            def emit_B(g):
                par = g % 2
                ARf = ARfs[par]; KTf = KTfs[par]; BTf = BTfs[par]; vg = vgs[par]
                KTb = KTbs[par]; BTb = BTbs[par]
                CP('act', ARb[par], ARf)
                ptm = PA3.bitcast(BF16)[:, 0:3 * GT]
                items = []
                for q, src in enumerate((vg, KTb, BTb)):
                    for j in range(NT):
                        items.append((ptm[:, (q * NT + j) * 128:(q * NT + j + 1) * 128], src[:, j * 128:(j + 1) * 128], ident))
                TRS(items, reads=[vg, KTb, BTb, ident], writes=[ptm])
                CP('act', TM[par], ptm)
                for j in range(NT):
                    items = []
                    for h in range(2):
                        hr = slice(h * 64, h * 64 + 64)
                        rhs_ar = ARf[hr, j * 256:(j + 1) * 256]
                        bh = psb_f32(h)
                        items.append((bh[:, 0:256], KTf[hr, j * 128:(j + 1) * 128], rhs_ar, True, True))
                        items.append((bh[:, 256:512], BTf[hr, j * 128:(j + 1) * 128], rhs_ar, True, True))
                    MM(items, reads=[KTf, BTf, ARf], writes=[PSB[:, 0:1024]])
                    bxy = PSB[:, 0:1024].rearrange("p (h w t) -> p h w t", h=2, w=4)
                    msu_b = m_su.unsqueeze(1).to_broadcast([128, 2, 128])
                    miu_b = m_iu.unsqueeze(1).to_broadcast([128, 2, 128])
                    msl_b = m_sl.unsqueeze(1).to_broadcast([128, 2, 128])

                    def dst(t):
                        return t.rearrange("p (j h t) -> p j h t", h=2, t=128)[:, j, :, :]
                    TT('dve', dst(AKs[par]), bxy[:, :, 0, :], msu_b, ALU.mult)
                    TT('dve', dst(RKs[par]), bxy[:, :, 1, :], miu_b, ALU.mult)
                    TT('dve', dst(N0s), bxy[:, :, 2, :], msu_b, ALU.mult)
                    TT('dve', dst(RBs[par]), bxy[:, :, 3, :], miu_b, ALU.mult)

                pxt = PA3.bitcast(BF16)[:, 0:NM * 128]
                TRS([(pxt[:, m * 128:(m + 1) * 128], N0s[:, m * 128:(m + 1) * 128], ident) for m in range(NM)],
                    reads=[N0s, ident], writes=[pxt])
                CP('act', X0s, pxt)

                def m8(t):
                    return t.rearrange("p (m t) -> p m t", t=128)
                W = NM * 128
                TT('dve', m8(Ps[0]), m8(N0s), ident.unsqueeze(1).to_broadcast([128, NM, 128]), ALU.add)
                Ncur, Xcur, Pcur = N0s, X0s, Ps[0]
                for lvl in range(1, 6):
                    pX = PSB[:, 0:W]
                    MM([(pX[:, m * 128:(m + 1) * 128], Ncur[:, m * 128:(m + 1) * 128], Xcur[:, m * 128:(m + 1) * 128], True, True)
                        for m in range(NM)], reads=[Ncur, Xcur], writes=[pX])
                    Xn = Xs[lvl % 2]
                    if lvl < 5:
                        pN = PSB[:, 512:512 + W]
                        MM([(pN[:, m * 128:(m + 1) * 128], Xcur[:, m * 128:(m + 1) * 128], Ncur[:, m * 128:(m + 1) * 128], True, True)
                            for m in range(NM)], reads=[Ncur, Xcur], writes=[pN])
                    CP('act', Xn, pX)
                    if lvl < 5:
                        Nn = Ns[lvl % 2]
                        CP('act', Nn, pN)
                    pP = PA3[:, 0:W]
                    pitems = []
                    for m in range(NM):
                        pitems.append((pP[:, m * 128:(m + 1) * 128], Xn[:, m * 128:(m + 1) * 128], Pcur[:, m * 128:(m + 1) * 128], m == 0, False))
                        pitems.append((pP[:, m * 128:(m + 1) * 128], ident, Pcur[:, m * 128:(m + 1) * 128], False, m == NM - 1))
                    MM(pitems, reads=[Xn, Pcur, ident], writes=[pP])
                    Pn = TTs[par] if lvl == 5 else Ps[lvl % 2]
                    CP('act', Pn, pP)
                    Pcur = Pn
                    Xcur = Xn
                    if lvl < 5:
                        Ncur = Nn

            def emit_chain(g):
                par = g % 2
                ARbv = ARb[par].rearrange("p (j w t) -> p j w t", w=2, t=128)
                TMv = TM[par].rearrange("p (q j f) -> p q j f", q=3, f=128)
                TTg = TTs[par].rearrange("p (j h t) -> p j h t", h=2, t=128)
                AKv = AKs[par].rearrange("p (j h t) -> p j h t", h=2, t=128)
                RKv = RKs[par].rearrange("p (j h t) -> p j h t", h=2, t=128)
                RBv = RBs[par].rearrange("p (j h t) -> p j h t", h=2, t=128)
                Ytv = Ytms[par].rearrange("p (j f) -> p j f", f=128)
                for cl in range(2 * NT):
                    c = g * 2 * NT + cl
                    j = cl // 2
                    pr = slice((cl % 2) * 64, (cl % 2) * 64 + 64)
                    Scur = Sf[c % 2]
                    Snxt = Sf[(c + 1) % 2]
                    wc = WCs[g % 4][:, cl:cl + 1]
                    Zp = PSB[:, 1536:1664]
                    Up = PSB[:, 1664:1792]
                    Yp = PSB[:, 1792:1920]
                    Sp = PSB[:, 1920:2048]
                    ACT(Ssc, Scur, AF.Identity, scale=wc)
                    MM([(Zp[:, 0:64], AKv[pr, j, 0, :], TMv[pr, 0, j, 0:64], True, False),
                        (Zp[:, 64:128], AKv[pr, j, 1, :], TMv[pr, 0, j, 64:128], False, False),
                        (Zp[:, 0:128], ARbv[:, j, 0, :], Sbf, False, True)],
                       reads=[AKs[par], TM[par], ARb[par], Sbf], writes=[Zp])
                    CP('act', Zsb[pr, :], Zp[pr, :])
                    MM([(Up[:, 0:64], TTg[pr, j, 0, :], Zsb[pr, 0:64], True, False),
                        (Up[:, 64:128], TTg[pr, j, 1, :], Zsb[pr, 64:128], False, True)],
                       reads=[TTs[par], Zsb[pr, :]], writes=[Up])
                    CP('dve', Usb[pr, :], Up[pr, :])
                    MM([(Yp[:, 0:128], ARbv[:, j, 1, :], Sbf, True, False),
                        (Yp[:, 0:64], RBv[pr, j, 0, :], Usb[pr, 0:64], False, False),
                        (Yp[:, 0:64], RKv[pr, j, 0, :], TMv[pr, 0, j, 0:64], False, False),
                        (Yp[:, 64:128], RBv[pr, j, 1, :], Usb[pr, 64:128], False, False),
                        (Yp[:, 64:128], RKv[pr, j, 1, :], TMv[pr, 0, j, 64:128], False, True)],
                       reads=[ARb[par], Sbf, RBs[par], RKs[par], Usb[pr, :], TM[par]], writes=[Yp])
                    CP('act', Ytv[pr, j, :], Yp[pr, :])
                    MM([(Sp, TMv[pr, 2, j, :], Usb[pr, :], True, False),
                        (Sp, TMv[pr, 1, j, :], TMv[pr, 0, j, :], False, True)],
                       reads=[TM[par], Usb[pr, :]], writes=[Sp])
                    TT('dve', Stmp, Sp, blk2, ALU.mult)
                    STT('dve', Sbf, Stmp, wc, Ssc, ALU.mult, ALU.add)
                    STT('dve', Snxt, Stmp, wc, Ssc, ALU.mult, ALU.add)

            def emit_post(g):
                par = g % 2
                c0 = g * GT
                Ytm = Ytms[par]
                Y4 = Ytm.rearrange("p (m f) -> p m f", f=64)
                YC4 = YC.rearrange("p (m f) -> p m f", f=64)
                YQ4 = YQ.rearrange("p (m f) -> p m f", f=64)
                nm_ = 2 * NT
                S.add('dve', (lambda o, i: (lambda e: e.tensor_reduce(out=o, in_=i, axis=AX.X, op=ALU.add)))(mean, Y4),
                      reads=[Ytm], writes=[mean])
                TS('dve', mean, mean, 1.0 / 64, None, ALU.mult)
                TT('dve', YC4, Y4, mean.unsqueeze(2).to_broadcast([128, nm_, 64]), ALU.subtract)
                ACT(YQ, YC, AF.Square)
                S.add('dve', (lambda o, i: (lambda e: e.tensor_reduce(out=o, in_=i, axis=AX.X, op=ALU.add)))(var, YQ4),
                      reads=[YQ], writes=[var])
                TS('dve', var, var, 1.0 / 64, 64e-5, ALU.mult, ALU.add)
                ACT(var, var, AF.Ln)
                ACT(var, var, AF.Exp, scale=-0.5)
                TT('dve', YN.rearrange("p (m f) -> p m f", f=64), YC4, var.unsqueeze(2).to_broadcast([128, nm_, 64]), ALU.mult)
                pyt = psb_bf(2, GT)
                TRS([(pyt[:, jj * 128:(jj + 1) * 128], YN[:, jj * 128:(jj + 1) * 128], ident) for jj in range(NT)],
                    reads=[YN, ident], writes=[pyt])
                YF = YC
                ACT(YF, pyt, AF.Identity, bias=ppc('gn_b', hp), scale=ppc('gn_w', hp))
                TT('dve', YF, YF, BONs[g % 4], ALU.add)
                TT('dve', YAg[par], YF, SZ[:, c0:c0 + GT], ALU.mult)
                DMA('sp', ya_d[hp, :, c0:c0 + GT], YAg[par], reads=[YAg[par]], writes=[('DR', 'ya', hp, hp + 1, 0, 1, False)])

            nparts = split_parts(nxt, [1.0] * (NG + 3))
            for it in range(-2, NG + 1):
                streams = []
                if 0 <= it < NG:
                    streams.append(cap(emit_chain, it))
                if 0 <= it + 1 < NG:
                    streams.append(cap(emit_B, it + 1))
                if 0 <= it - 1 < NG:
                    streams.append(cap(emit_post, it - 1))
                if 0 <= it + 2 < NG:
                    streams.append(cap(emit_A, it + 2))
                streams.append(nparts[it + 2])
                S.merge_streams(streams)

        def nxt_stream(ji):
            return cap(inproj_job, ji + 1) if ji + 1 < len(jobs_seq) else []

        if jobs_seq:
            inproj_job(0)
        if n_hp_r > 0:
            lora_prep()
        for hp in range(n_hp_r):
            rwkv_headpair(hp, hp, nxt_stream(hp))

        AR_.reset()
        if n_hp_m > 0:
            QA = AR_.a(2048); QB = AR_.a(2048); KA = AR_.a(2048); KB = AR_.a(2048)
            VA = AR_.a(2048); VB = AR_.a(2048)
            for t in (QA, QB, KA, KB):
                MEMSET('pool', t, 0.0)
            MEMSET('pool', VA, 1.0)
            MEMSET('pool', VB, 1.0)
            DMA('sp', KA[64:72, :], ind_d, writes=[KA[64:72, :]])
            DMA('sp', KB[0:8, :], ind_d, writes=[KB[0:8, :]])
        m_base = AR_.off

        def moba_headpair(hp, ji, nxt):
            Rq, Rkq, Rvq, SZ = RAW[ji % 2]
            AR_.off = m_base
            A = AR_.a
            nparts = split_parts(nxt, [2.0, 1.0, 2.0, 3.0, 4.0])
            Qf = A(2048, F32); Kf = A(2048, F32)

            def norm_stream(raw, wname, dA, dB, Ff, SQm, RIm, bank):
                for g in range(4):
                    c0 = g * 512
                    ACT(SQm, raw[:, c0:c0 + 512], AF.Square)
                    pk = psb_f32(bank)
                    MM([(pk, blk2, SQm, True, True)], reads=[blk2, SQm], writes=[pk])
                    ACT(RIm, pk, AF.Ln, bias=ppc_eps, scale=1.0 / 64)
                    ACT(RIm, RIm, AF.Exp, scale=-0.5)
                    STT('dve', Ff[:, c0:c0 + 512], raw[:, c0:c0 + 512], ppc(wname), RIm, ALU.mult, ALU.mult)
                    CP('act', dA[0:64, c0:c0 + 512], Ff[0:64, c0:c0 + 512])
                    CP('dve', dB[64:128, c0:c0 + 512], Ff[64:128, c0:c0 + 512])

            def v_stream():
                pv = psb2_bf(2)
                TRS([(pv[:, t * 128:(t + 1) * 128], Rvq[:, t * 128:(t + 1) * 128], ident) for t in range(16)],
                    reads=[Rvq, ident], writes=[pv])
                pv3 = pv.rearrange("p (t f) -> p t f", f=128)
                CP('act', VA.rearrange("p (t f) -> p t f", f=128)[:, :, 0:64], pv3[:, :, 0:64])
                CP('dve', VB.rearrange("p (t f) -> p t f", f=128)[:, :, 64:128], pv3[:, :, 64:128])

            SQq = A(512); RIq = A(512, F32); SQk = A(512); RIk = A(512, F32)
            st_q = cap(norm_stream, Rq, 'qnw', QA, QB, Qf, SQq, RIq, 0)
            st_k = cap(norm_stream, Rkq, 'knw', KA, KB, Kf, SQk, RIk, 1)
            st_v = cap(v_stream)
            S.merge_streams([st_k, st_q, st_v, nparts[0]])
            nparts[0] = []
            S.capture = []
            kmp = A(16, F32)
            MEMSET('pool', kmp, 0.0)
            S.add('dve', (lambda o, i: (lambda e: e.tensor_reduce(out=o, in_=i, axis=AX.X, op=ALU.add)))(
                kmp[0:64, 0:8], Kf[0:64, :].rearrange("p (n t) -> p n t", t=256)), reads=[Kf[0:64, :]], writes=[kmp[0:64, 0:8]])
            S.add('dve', (lambda o, i: (lambda e: e.tensor_reduce(out=o, in_=i, axis=AX.X, op=ALU.add)))(
                kmp[64:128, 8:16], Kf[64:128, :].rearrange("p (n t) -> p n t", t=256)), reads=[Kf[64:128, :]], writes=[kmp[64:128, 8:16]])
            pg = psb_f32(0, 256)
            MM([(pg[:, qt * 16:(qt + 1) * 16], Qf[:, qt * 128:(qt + 1) * 128], kmp, True, True) for qt in range(16)],
               reads=[Qf, kmp], writes=[pg])
            GM = A(256, F32); G2 = A(256, F32); EQ = A(256, F32); mx = A(32, F32); BI = A(256)
            g3 = lambda t: t.rearrange("p (m n) -> p m n", n=8)
            mxb = mx.unsqueeze(2).to_broadcast([128, 32, 8])

            def rmax(o, i):
                S.add('dve', (lambda o_, i_: (lambda e: e.tensor_reduce(out=o_, in_=i_, axis=AX.X, op=ALU.max)))(o, g3(i)),
                      reads=[i], writes=[o])
            TT('dve', GM, pg, pastneg, ALU.add)
            rmax(mx, GM)
            TT('dve', g3(EQ), g3(GM), mxb, ALU.is_ge)
            STT('dve', G2, EQ, NEG, GM, ALU.mult, ALU.add)
            rmax(mx, G2)
            TT('dve', g3(EQ), g3(G2), mxb, ALU.is_ge)
            STT('dve', G2, EQ, NEG, G2, ALU.mult, ALU.add)
            rmax(mx, G2)
            TT('dve', g3(EQ), g3(GM), mxb, ALU.is_ge)
            TT('dve', EQ, EQ, ownpos, ALU.max)
            TS('dve', BI, EQ, -1.0, -NEG, ALU.add, ALU.mult)
            pbt = psb_bf(1, 1024)
            pbt2 = psb_bf(2, 1024)
            TRS([((pbt if qt < 8 else pbt2)[0:16, (qt % 8) * 128:(qt % 8 + 1) * 128], BI[:, qt * 16:(qt + 1) * 16], ident)
                 for qt in range(16)], reads=[BI, ident], writes=[pbt, pbt2])
            BT_ = A(2048)
            CP('act', BT_[0:16, 0:1024], pbt[0:16, :])
            CP('act', BT_[0:16, 1024:2048], pbt2[0:16, :])
            DMA('sp', QA[64:72, :], BT_[0:8, :], reads=[BT_[0:8, :]], writes=[QA[64:72, :]])
            DMA('sp', QB[0:8, :], BT_[8:16, :], reads=[BT_[8:16, :]], writes=[QB[0:8, :]])
            pro = S.capture
            S.capture = None
            S.merge_streams([pro, nparts[0]])
            PT = [A(512) for _ in range(3)]
            RS = A(512, F32)
            RW = A(512, F32)
            YO = A(512, F32)
            YB = A(2048)
            VA3 = VA.rearrange("p (t f) -> p t f", f=128)
            VB3 = VB.rearrange("p (t f) -> p t f", f=128)
            OpA = psb_f32(2)
            OpB = psb_f32(3)
            for QT in range(4):
                S.capture = []
                q0 = QT * 512
                units = [(h, kt) for kt in range(4 * QT + 4) for h in range(2)]
                nkt = 4 * QT + 4

                def qk(ui):
                    h, kt = units[ui]
                    Kh = KA if h == 0 else KB
                    Qh = QA if h == 0 else QB
                    sp_ = psb_f32(ui % 2)
                    diag = kt >= 4 * QT
                    items = [(sp_, Kh[:, kt * 128:(kt + 1) * 128], Qh[:, q0:q0 + 512], True, not diag)]
                    rds = [Kh[:, kt * 128:(kt + 1) * 128], Qh[:, q0:q0 + 512]]
                    if diag:
                        items.append((sp_, ident, cmask[:, kt - 4 * QT, :], False, True))
                        rds += [ident, cmask[:, kt - 4 * QT, :]]
                    MM(items, reads=rds, writes=[sp_])
                qk(0)
                for ui, (h, kt) in enumerate(units):
                    if ui + 1 < len(units):
                        qk(ui + 1)
                    sp_ = psb_f32(ui % 2)
                    pt = PT[ui % 3]
                    ACT(pt, sp_, AF.Exp, scale=0.125)
                    Vh = VA3 if h == 0 else VB3
                    Oh = OpA if h == 0 else OpB
                    MM([(Oh, Vh[:, kt, :], pt, kt == 0, kt == nkt - 1)], reads=[Vh[:, kt, :], pt], writes=[Oh])
                ACT(RS[64:128, :], OpA[64:128, :], AF.Ln)
                ACT(RS[0:64, :], OpB[0:64, :], AF.Ln)
                ACT(RS, RS, AF.Exp, scale=-1.0)
                pw = psb_f32(0)
                MM([(pw, swapP, RS, True, True)], reads=[swapP, RS], writes=[pw])
                CP('act', RW, pw)
                TT('dve', YO[0:64, :], OpA[0:64, :], RW[0:64, :], ALU.mult)
                TT('dve', YO[64:128, :], OpB[64:128, :], RW[64:128, :], ALU.mult)
                TT('dve', YB[:, q0:q0 + 512], YO, SZ[:, q0:q0 + 512], ALU.mult)
                att = S.capture
                S.capture = None
                S.merge_streams([att, nparts[QT + 1]])
            out_dmas.append(DMA('sp', yb_d[hp], YB, reads=[YB], writes=[('DR', 'yb', hp, hp + 1, 0, 1, False)]))

        if n_hp_m > 0:
            epsb = AR_.a(2, F32)
            MEMSET('pool', epsb, 1e-6)
            ppc_eps = epsb[:, 0:1]
            m_base = AR_.off
        for hp in range(n_hp_m):
            moba_headpair(hp, n_hp_r + hp, nxt_stream(n_hp_r + hp))

        if do_final:
            for th in range(2):
                AR_.reset()
                t0 = th * 1024
                YAh = RAWT[0][:, :]; YBh = RAWT[1][:, :]
                YA3 = YAh.rearrange("p (k t) -> p k t", t=1024)
                YB3 = YBh.rearrange("p (k t) -> p k t", t=1024)
                for k in range(8):
                    DMA('sp', YA3[:, k, :], ya_d[k, :, t0:t0 + 1024], reads=[('DR', 'ya', k, k + 1, 0, 1, False)], writes=[YA3[:, k, :]])
                    DMA('sp', YB3[:, k, :], yb_d[k, :, t0:t0 + 1024], reads=[('DR', 'yb', k, k + 1, 0, 1, False)], writes=[YB3[:, k, :]])
                MG = AR_.a(16 * 1024)
                MG3 = MG.rearrange("p (k t) -> p k t", t=1024)
                sga = AR_.a(1024); sgb = AR_.a(1024); m1 = AR_.a(1024, F32)
                fj = []
                for c in range(16):
                    fj.append(('in', COL['ga'] + c * 128))
                    fj.append(('pa', c * 128))
                    fj.append(('in', COL['gb'] + c * 128))
                    fj.append(('pb', c * 128))
                fq = []
                fst = {'i': 0}

                def next_fw():
                    while fst['i'] < len(fj) and len(fq) < NW:
                        kind, col = fj[fst['i']]
                        fst['i'] += 1
                        if kind == 'in':
                            fq.append(wload_in(col))
                        elif kind == 'pa':
                            fq.append(wload_proj(wpa_d, col))
                        else:
                            fq.append(wload_proj(wpb_d, col))
                    return fq.pop(0)

                def run_half_proj(wbuf, nk, src3, evac):
                    acc = PSA[:, (run_half_proj.n % 2) * 1024:(run_half_proj.n % 2 + 1) * 1024]
                    run_half_proj.n += 1
                    items = []
                    for k in range(nk):
                        for tt in range(2):
                            items.append((acc[:, tt * 512:(tt + 1) * 512], wbuf[:, k, :], src3(k, tt), k == 0, k == nk - 1))
                    MM(items, reads=[wbuf[:, 0:nk, :]] + run_half_proj.rd, writes=[acc])
                    evac(acc)
                run_half_proj.n = 0
                for c in range(16):
                    run_half_proj.rd = [hT[:, :, t0:t0 + 1024]]
                    run_half_proj(next_fw(), KC, lambda k, tt: hT[:, k, t0 + tt * 512:t0 + (tt + 1) * 512],
                                  lambda acc: ACT(sga, acc, AF.Sigmoid))
                    run_half_proj.rd = [YAh]
                    run_half_proj(next_fw(), 8, lambda k, tt: YA3[:, k, tt * 512:(tt + 1) * 512],
                                  lambda acc: TT('dve', m1, acc, sga, ALU.mult))
                    run_half_proj.rd = [hT[:, :, t0:t0 + 1024]]
                    run_half_proj(next_fw(), KC, lambda k, tt: hT[:, k, t0 + tt * 512:t0 + (tt + 1) * 512],
                                  lambda acc: ACT(sgb, acc, AF.Sigmoid))
                    run_half_proj.rd = [YBh]

                    def ev_b(acc, c=c):
                        TT('dve', sgb, acc, sgb, ALU.mult)
                        TT('dve', MG3[:, c, :], m1, sgb, ALU.add)
                    run_half_proj(next_fw(), 8, lambda k, tt: YB3[:, k, tt * 512:(tt + 1) * 512], ev_b)
                WO = [AR_.a(16 * 512) for _ in range(2)]
                XR = [AR_.a(512, F32) for _ in range(1)]
                OT = [AR_.a(512, F32) for _ in range(1)]
                n_o = 0
                for c4 in range(4):
                    wo = WO[c4 % 2]
                    wo3 = wo.rearrange("p (k m) -> p k m", m=512)
                    DMA('pool', wo3, wo_d[:, c4 * 512:(c4 + 1) * 512].rearrange("(kc p) m -> p kc m", p=128), writes=[wo])
                    for tl in range(8):
                        tok0 = t0 + tl * 128
                        xr = XR[0]
                        ot = OT[0]
                        acc = PSB[:, (n_o % 4) * 512:(n_o % 4 + 1) * 512]
                        n_o += 1
                        DMA('sp', xr, x_d[tok0:tok0 + 128, c4 * 512:(c4 + 1) * 512], writes=[xr])
                        MM([(acc, MG3[:, k, tl * 128:(tl + 1) * 128], wo3[:, k, :], k == 0, k == 15) for k in range(16)],
                           reads=[MG, wo], writes=[acc])
                        TT('dve', ot, acc, xr, ALU.add)
                        out_dmas.append(DMA('sp', out_d[tok0:tok0 + 128, c4 * 512:(c4 + 1) * 512], ot, reads=[ot]))

        cnt = S.emit(final_wait_ops=out_dmas)
    return nc, cnt, len(S.ops)


def _consts():
    bf = ml_dtypes.bfloat16
    cbv = np.zeros((128, NCB), np.float32)
    idx = np.arange(128)
    cbv[:, CB['ident']:CB['ident'] + 128] = np.eye(128)
    same = (idx[:, None] // 64) == (idx[None, :] // 64)
    cbv[:, CB['blk2']:CB['blk2'] + 128] = same
    cbv[:, CB['m_su']:CB['m_su'] + 128] = same & (idx[:, None] < idx[None, :])
    cbv[:, CB['m_iu']:CB['m_iu'] + 128] = same & (idx[:, None] <= idx[None, :])
    cbv[:, CB['m_sl']:CB['m_sl'] + 128] = same & (idx[:, None] > idx[None, :])
    cbv[:, CB['onesA']:CB['onesA'] + 64] = 1.0
    cbv[:, CB['onesB'] + 64:CB['onesB'] + 128] = 1.0
    cm = np.zeros((128, 4, 512), np.float32)
    for ktl in range(4):
        kpos = ktl * 128 + idx[:, None]
        qpos = np.arange(512)[None, :]
        kb = kpos // 256
        qb = qpos // 256
        ok = np.where(kb == qb, kpos <= qpos, kb < qb)
        cm[:, ktl, :] = np.where(ok, 0.0, NEG)
    cbv[:, CB['cmask']:CB['cmask'] + 2048] = cm.reshape(128, 2048)
    cfv = np.zeros((128, NCF), np.float32)
    sm = np.ones(512, np.float32)
    sm[::64] = 0.0
    cfv[:, CF['scanmask']:CF['scanmask'] + 512] = sm[None, :]
    pn = np.zeros((16, 2, 8), np.float32)
    op = np.zeros((16, 2, 8), np.float32)
    for qt in range(16):
        qb = qt // 2
        for n in range(8):
            pn[qt, :, n] = 0.0 if n < qb else NEG
            op[qt, :, n] = 1.0 if n >= qb else 0.0
    cfv[:, CF['pastneg']:CF['pastneg'] + 256] = pn.reshape(1, 256)
    cfv[:, CF['ownpos']:CF['ownpos'] + 256] = op.reshape(1, 256)
    cfv[:, CF['swapP']:CF['swapP'] + 128] = np.roll(np.eye(128, dtype=np.float32), 64, axis=1)
    ind = np.zeros((8, S_TOK), np.float32)
    for n in range(8):
        ind[n, n * 256:(n + 1) * 256] = 1.0
    return cbv.astype(bf), cfv, ind.astype(bf)


def _pack_params(i):
    ppv = np.zeros((128, NPP), np.float32)

    def fm(v):
        return np.ascontiguousarray(v.reshape(-1, 128).T)
    for nm in ('mu_r', 'mu_k', 'mu_v', 'w0', 'a0', 'k_k', 'k_a', 'r_k', 'gn_w', 'gn_b'):
        ppv[:, PP[nm]:PP[nm] + 8] = fm(i[nm][0])
    ppv[:, PP['norm_w']:PP['norm_w'] + 16] = fm(i['norm_w'][0])
    ppv[0:64, PP['mu_wa']] = i['mu_w'][0]
    ppv[64:128, PP['mu_wa']] = i['mu_a'][0]
    ppv[0:64, PP['qnw']] = i['q_norm_w'][0]
    ppv[64:128, PP['qnw']] = i['q_norm_w'][0]
    ppv[0:64, PP['knw']] = i['k_norm_w'][0]
    ppv[64:128, PP['knw']] = i['k_norm_w'][0]
    lora = np.concatenate([i['w_decay_up'][0], i['w_iclr_up'][0]], axis=0).astype(np.float32)
    return ppv, np.ascontiguousarray(lora)


_CACHE = {}


def make_in_maps(inputs, n_cores=8):
    i = {k: np.asarray(v) for k, v in inputs.items()}
    cbv, cfv, ind = _consts()
    ppv, lora = _pack_params(i)
    shared = dict(w_in=np.ascontiguousarray(i['w_in'][0]), w_pa=np.ascontiguousarray(i['w_proj_rwkv'][0]),
                  w_pb=np.ascontiguousarray(i['w_proj_moba'][0]), w_out=np.ascontiguousarray(i['w_out'][0]),
                  lora=lora, pp=ppv, cb=cbv, cf=cfv, ind=ind)
    maps = []
    for c in range(n_cores):
        m = dict(shared)
        m['x'] = np.ascontiguousarray(i['x'][c])
        maps.append(m)
    return maps


def kernel(**inputs):
    if 'nc' not in _CACHE:
        _CACHE['nc'] = build()[0]
    nc = _CACHE['nc']
    maps = make_in_maps(inputs, 8)
    res = run_bass_kernel_spmd(nc, maps, core_ids=list(range(8)))
    out = np.stack([np.asarray(r['out']) for r in res.results], axis=0)
    return out.astype(np.float32)
```

```python
import contextlib
import numpy as np
import ml_dtypes
import concourse.bass as bass
import concourse.mybir as mybir
from concourse.bass_utils import run_bass_kernel_spmd

F32 = mybir.dt.float32
BF16 = mybir.dt.bfloat16
AF = mybir.ActivationFunctionType
ALU = mybir.AluOpType
AX = mybir.AxisListType

S_TOK = 2048
D = 2048
KC = 16
IN_COLS = 12416
COL = dict(r=0, k=1024, v=2048, za=3072, wd=4096, q=4224, kq=5248, vq=6272, zb=7296, ga=8320, gb=10368)
CDEC = 0.6065306597126334
NEG = -1.0e30
STOP = 99
NOMERGE = False
STRICT = True

PP = dict(mu_r=0, mu_k=8, mu_v=16, w0=24, a0=32, k_k=40, k_a=48, r_k=56, gn_w=64, gn_b=72, norm_w=80,
          mu_wa=96, qnw=97, knw=98)
NPP = 100
CB = dict(ident=0, blk2=128, m_su=256, m_iu=384, m_sl=512, onesA=640, onesB=768, cmask=896)
NCB = 896 + 4 * 512
CF = dict(scanmask=0, pastneg=512, ownpos=768, swapP=1024)
NCF = 1152


_DTSZ = {}


def _dtsize(dt):
    s = _DTSZ.get(dt)
    if s is None:
        name = str(dt)
        s = 4 if '32' in name else 2 if '16' in name else 1 if '8' in name else 8
        _DTSZ[dt] = s
    return s


def box_of(ap):
    dims = ap.ap
    sz = _dtsize(ap.dtype)
    pstep, pcnt = dims[0]
    off = int(ap.offset)
    if pstep == 0:
        pstep = 1 << 40
    p0 = off // pstep
    f0 = off % pstep
    ext = 0
    for st, cn in dims[1:]:
        ext += abs(st) * (cn - 1)
    f1 = f0 + ext + 1
    if 'PSUM' in str(ap.space).upper():
        b0 = (f0 * sz) // 2048
        b1 = ((f1 * sz) - 1) // 2048
        return ('PS', ap.name, 0, 128, b0 * 2048, (b1 + 1) * 2048, True)
    return ('SB', ap.name, p0, p0 + pcnt, f0 * sz, f1 * sz, False)


class Op:
    __slots__ = ('idx', 'eng', 'fn', 'deps', 'signal', 'count', 'is_dma', 'dsem', 'dcount', 'prev_slot')

    def __init__(self, idx, eng, fn, is_dma):
        self.idx = idx
        self.eng = eng
        self.fn = fn
        self.deps = {}
        self.signal = False
        self.count = 0
        self.is_dma = is_dma
        self.dsem = None
        self.dcount = 0
        self.prev_slot = None


class Sched:
    ENGS = ('pe', 'act', 'dve', 'pool', 'sp')

    def __init__(self, nc, n_dma_slots=10):
        self.nc = nc
        self.ops = []
        self.recs = {}
        self.n_dma_slots = n_dma_slots

    def _touch(self, op, box, is_write):
        kind, name, p0, p1, f0, f1, excl = box
        lst = self.recs.setdefault(name, [])
        found = None
        for r in lst:
            if r[0] < p1 and p0 < r[1] and r[2] < f1 and f0 < r[3]:
                if r[4] is not None:
                    if (not is_write) or excl:
                        op.deps[r[4]] = True
                    else:
                        op.deps.setdefault(r[4], False)
                if is_write or excl:
                    for e, o in r[5].items():
                        op.deps.setdefault(o, False)
            if r[0] == p0 and r[1] == p1 and r[2] == f0 and r[3] == f1:
                found = r
        if found is None:
            found = [p0, p1, f0, f1, None, {}]
            lst.append(found)
        if is_write or excl:
            found[4] = op.idx
            found[5] = {}
        else:
            found[5][op.eng] = op.idx

    capture = None

    def add(self, eng, fn, reads=(), writes=(), dma=False, cost=0.5):
        if self.capture is not None:
            self.capture.append((eng, fn, list(reads), list(writes), dma, cost))
            return -1
        op = Op(len(self.ops), eng, fn, dma)
        self.ops.append(op)
        for ap in reads:
            if ap is not None and not isinstance(ap, (int, float)):
                self._touch(op, ap if isinstance(ap, tuple) else box_of(ap), False)
        for ap in writes:
            if ap is not None:
                self._touch(op, ap if isinstance(ap, tuple) else box_of(ap), True)
        op.deps.pop(op.idx, None)
        return op.idx

    def commit(self, lst):
        for it in lst:
            self.add(*it[:5])

    def merge_streams(self, streams):
        if NOMERGE:
            for st in streams:
                self.commit(st)
            return
        pos = [0] * len(streams)
        ready = [0.0] * len(streams)
        free = {e: 0.0 for e in self.ENGS}
        while True:
            best = None
            for si, st in enumerate(streams):
                if pos[si] >= len(st):
                    continue
                it = st[pos[si]]
                t = max(free[it[0]], ready[si])
                if best is None or t < best[0] - 1e-9:
                    best = (t, si)
            if best is None:
                break
            t, si = best
            it = streams[si][pos[si]]
            pos[si] += 1
            self.add(*it[:5])
            if it[4]:
                free[it[0]] = t + 0.06
                ready[si] = t + 0.06
            else:
                free[it[0]] = t + it[5]
                ready[si] = t + it[5] + 0.12

    def merge(self, main, bg):
        return self.merge_streams([main, bg])
        nb = len(bg)
        nm = max(len(main), 1)
        j = 0
        for i, it in enumerate(main):
            self.add(*it[:5])
            tgt = (i + 1) * nb // nm
            while j < tgt:
                self.add(*bg[j][:5])
                j += 1
        while j < nb:
            self.add(*bg[j])
            j += 1

    def emit(self, final_wait_ops=()):
        nc = self.nc
        ops = self.ops
        for op in ops:
            for d, raw in op.deps.items():
                dop = ops[d]
                if dop.is_dma:
                    continue
                if (not op.is_dma) and dop.eng == op.eng and (op.eng == 'pe' or (not raw and not STRICT)):
                    continue
                dop.signal = True
        for d in final_wait_ops:
            if not ops[d].is_dma:
                ops[d].signal = True
        cnt = {e: 0 for e in self.ENGS}
        for op in ops:
            if op.is_dma:
                continue
            if op.signal:
                cnt[op.eng] += 1
            op.count = cnt[op.eng]
        slot_state = {}
        for op in ops:
            if not op.is_dma:
                continue
            st = slot_state.setdefault(op.eng, {'next': 0, 'counts': [0] * self.n_dma_slots,
                                                'last': [None] * self.n_dma_slots})
            s = st['next']
            st['next'] = (s + 1) % self.n_dma_slots
            op.prev_slot = st['last'][s]
            st['counts'][s] += 16
            op.dsem = (op.eng, s)
            op.dcount = st['counts'][s]
            st['last'][s] = op.idx
        used = [e for e in self.ENGS if any(o.eng == e for o in ops)]
        if 'sp' not in used:
            used.append('sp')
        with contextlib.ExitStack() as es:
            sems = {e: es.enter_context(nc.semaphore('s_' + e)) for e in used}
            dsems = {}
            for e in slot_state:
                for s in range(self.n_dma_slots):
                    dsems[(e, s)] = es.enter_context(nc.semaphore('d_%s_%d' % (e, s)))
            block = es.enter_context(nc.Block())

            def run_engine(ename, eng):
                waited = {e: 0 for e in self.ENGS}
                dwaited = {}

                def wait_on(dop):
                    if dop.is_dma:
                        if dwaited.get(dop.dsem, 0) < dop.dcount:
                            eng.wait_ge(dsems[dop.dsem], dop.dcount)
                            dwaited[dop.dsem] = dop.dcount
                    elif dop.count > waited[dop.eng]:
                        eng.wait_ge(sems[dop.eng], dop.count)
                        waited[dop.eng] = dop.count

                for op in ops:
                    if op.eng != ename:
                        continue
                    for d in sorted(op.deps):
                        dop = ops[d]
                        raw = op.deps[d]
                        if (not dop.is_dma) and (not op.is_dma) and dop.eng == ename and (ename == 'pe' or (not raw and not STRICT)):
                            continue
                        wait_on(dop)
                    if op.is_dma and op.prev_slot is not None:
                        wait_on(ops[op.prev_slot])
                    ins = op.fn(eng)
                    if op.is_dma:
                        ins.then_inc(dsems[op.dsem], 16)
                    elif op.signal:
                        ins.then_inc(sems[ename], 1)
                if ename == 'sp':
                    for d in final_wait_ops:
                        wait_on(ops[d])

            @block.tensor
            def _(eng):
                run_engine('pe', eng)

            @block.scalar
            def _(eng):
                run_engine('act', eng)

            @block.vector
            def _(eng):
                run_engine('dve', eng)

            @block.gpsimd
            def _(eng):
                run_engine('pool', eng)

            @block.sync
            def _(eng):
                run_engine('sp', eng)
        return cnt


def build(n_hp_r=8, n_hp_m=8, do_final=True, dbg=False):
    nc = bass.Bass("TRN2", target_bir_lowering=False)
    x_d = nc.dram_tensor("x", [S_TOK, D], F32, kind="ExternalInput").ap()
    win_d = nc.dram_tensor("w_in", [D, IN_COLS], F32, kind="ExternalInput").ap()
    wpa_d = nc.dram_tensor("w_pa", [1024, D], F32, kind="ExternalInput").ap()
    wpb_d = nc.dram_tensor("w_pb", [1024, D], F32, kind="ExternalInput").ap()
    wo_d = nc.dram_tensor("w_out", [D, D], F32, kind="ExternalInput").ap()
    lora_d = nc.dram_tensor("lora", [128, 1024], F32, kind="ExternalInput").ap()
    pp_d = nc.dram_tensor("pp", [128, NPP], F32, kind="ExternalInput").ap()
    cb_d = nc.dram_tensor("cb", [128, NCB], BF16, kind="ExternalInput").ap()
    cf_d = nc.dram_tensor("cf", [128, NCF], F32, kind="ExternalInput").ap()
    ind_d = nc.dram_tensor("ind", [8, S_TOK], BF16, kind="ExternalInput").ap()
    out_d = nc.dram_tensor("out", [S_TOK, D], F32, kind="ExternalOutput").ap()
    scr_kind = "ExternalOutput" if dbg else "Internal"
    ya_d = nc.dram_tensor("ya_scr", [8, 128, S_TOK], BF16, kind=scr_kind).ap()
    yb_d = nc.dram_tensor("yb_scr", [8, 128, S_TOK], BF16, kind=scr_kind).ap()

    with contextlib.ExitStack() as es:
        def sb(name, shape, dt):
            return es.enter_context(nc.sbuf_tensor(name, shape, dt))

        hT = sb("hT", [128, KC, S_TOK], BF16)
        NW = 4
        wpool = [sb("wp%d" % i, [128, KC, 128], BF16) for i in range(NW)]
        cb = sb("cb_s", [128, NCB], BF16)
        cf = sb("cf_s", [128, NCF], F32)
        pp = sb("pp_s", [128, NPP], F32)
        loraD = sb("loraD_s", [128, 1024], BF16)
        loraI = sb("loraI_s", [128, 1024], BF16)
        TL = sb("TL", [128, S_TOK], BF16)
        RAWT = [sb("rawt%d" % s, [128, 4 * S_TOK], BF16) for s in range(2)]
        RAW = [[RAWT[s][:, j * S_TOK:(j + 1) * S_TOK] for j in range(4)] for s in range(2)]
        ARENA_N = 39424
        arena_t = sb("arena", [128, ARENA_N], BF16)
        PSA = es.enter_context(nc.psum_tensor("PSA", [128, 2048], F32))
        PSB = es.enter_context(nc.psum_tensor("PSB", [128, 2048], F32))

        S = Sched(nc)
        out_dmas = []

        class Arena:
            def __init__(self):
                self.off = 0

            def reset(self):
                self.off = 0

            def a(self, n, dt=BF16):
                nb = n * (2 if dt == F32 else 1)
                nb = (nb + 1) // 2 * 2
                assert self.off + nb <= ARENA_N, (self.off, nb)
                v = arena_t[:, self.off:self.off + nb]
                self.off += nb
                if dt == F32:
                    v = v.bitcast(F32)
                return v

        AR_ = Arena()

        def rd(*aps):
            return [a for a in aps if a is not None and not isinstance(a, (int, float))]

        def fsz(ap):
            n = 1
            for st, cn in ap.ap[1:]:
                n *= cn
            return n

        def ACT(out, in_, func, bias=None, scale=None, accum=None):
            kw = {}
            if bias is not None:
                kw['bias'] = bias
            if scale is not None:
                kw['scale'] = scale
            if accum is not None:
                kw['accum_out'] = accum
            return S.add('act', lambda e: e.activation(out=out, in_=in_, func=func, **kw),
                         reads=rd(in_, bias, scale), writes=[out, accum], cost=0.22 + fsz(out) / 1200.0)

        def TT(eng, out, in0, in1, op):
            return S.add(eng, lambda e: e.tensor_tensor(out=out, in0=in0, in1=in1, op=op),
                         reads=rd(in0, in1), writes=[out], cost=0.1 + fsz(out) / 960.0)

        def TS(eng, out, in0, s1, s2, op0, op1=None):
            if op1 is None:
                return S.add(eng, lambda e: e.tensor_scalar(out=out, in0=in0, scalar1=s1, scalar2=None, op0=op0),
                             reads=rd(in0, s1), writes=[out], cost=0.1 + fsz(out) / 960.0)
            return S.add(eng, lambda e: e.tensor_scalar(out=out, in0=in0, scalar1=s1, scalar2=s2, op0=op0, op1=op1),
                         reads=rd(in0, s1, s2), writes=[out], cost=0.1 + fsz(out) / 960.0)

        def STT(eng, out, in0, scalar, in1, op0, op1):
            eng = 'dve'
            return S.add(eng, lambda e: e.scalar_tensor_tensor(out=out, in0=in0, scalar=scalar, in1=in1,
                                                                op0=op0, op1=op1),
                         reads=rd(in0, scalar, in1), writes=[out], cost=0.1 + fsz(out) / 960.0)

        def CP(eng, out, in_):
            if eng == 'act':
                return S.add('act', lambda e: e.copy(out=out, in_=in_), reads=[in_], writes=[out],
                             cost=0.22 + fsz(out) / 1200.0)
            return S.add(eng, lambda e: e.tensor_copy(out=out, in_=in_), reads=[in_], writes=[out],
                         cost=0.1 + fsz(out) / 960.0)

        def MEMSET(eng, out, val):
            return S.add(eng, lambda e: e.memset(out, val), writes=[out], cost=1.0)

        def MM(items, reads, writes):
            def fn(e):
                ins = None
                for (o, l, r, st, sp) in items:
                    ins = e.matmul(o, lhsT=l, rhs=r, start=st, stop=sp)
                return ins
            c = 0.05
            for (o, l, r, st, sp) in items:
                c += max(0.065, fsz(o) / (600.0 if l.dtype == F32 else 2400.0))
            return S.add('pe', fn, reads=reads, writes=writes, cost=c)

        def TRS(items, reads, writes):
            def fn(e):
                ins = None
                for (o, i, idn) in items:
                    ins = e.transpose(o, i, idn)
                return ins
            return S.add('pe', fn, reads=reads, writes=writes, cost=0.05 + 0.11 * len(items))

        def DMA(q, out, in_, reads=(), writes=()):
            return S.add(q, lambda e: e.dma_start(out=out, in_=in_), reads=reads, writes=writes, dma=True)

        def ppc(name, j=0):
            c = PP[name] + j
            return pp[:, c:c + 1]

        ident = cb[:, CB['ident']:CB['ident'] + 128]
        blk2 = cb[:, CB['blk2']:CB['blk2'] + 128]
        m_su = cb[:, CB['m_su']:CB['m_su'] + 128]
        m_iu = cb[:, CB['m_iu']:CB['m_iu'] + 128]
        m_sl = cb[:, CB['m_sl']:CB['m_sl'] + 128]
        onesA = cb[:, CB['onesA']:CB['onesA'] + 128]
        onesB = cb[:, CB['onesB']:CB['onesB'] + 128]
        cmask = cb[:, CB['cmask']:CB['cmask'] + 2048].rearrange("p (a b) -> p a b", b=512)
        scanmask = cf[:, CF['scanmask']:CF['scanmask'] + 512]
        pastneg = cf[:, CF['pastneg']:CF['pastneg'] + 256]
        ownpos = cf[:, CF['ownpos']:CF['ownpos'] + 256]
        swapP = cf[:, CF['swapP']:CF['swapP'] + 128]

        def psb_f32(bank, n=512, off=0):
            return PSB[:, bank * 512 + off: bank * 512 + off + n]

        def psb_bf(bank, n=1024, off=0):
            v = PSB[:, bank * 512:(bank + 1) * 512].bitcast(BF16)
            return v[:, off:off + n]

        def psb2_bf(bank):
            return PSB[:, bank * 512:(bank + 2) * 512].bitcast(BF16)

        DMA('sp', cb[:], cb_d, writes=[cb[:]])
        DMA('sp', cf[:], cf_d, writes=[cf[:]])
        DMA('sp', pp[:], pp_d, writes=[pp[:]])
        S.add('pool', lambda e: e.memset(loraD[:], 0.0), writes=[loraD[:]])
        S.add('pool', lambda e: e.memset(loraI[:], 0.0), writes=[loraI[:]])
        DMA('pool', loraD[0:64, :], lora_d[0:64, :], writes=[loraD[0:64, :]])
        DMA('pool', loraI[64:128, :], lora_d[64:128, :], writes=[loraI[64:128, :]])

        wstate = {'n': 0}

        def wload_in(col):
            b = wpool[wstate['n'] % NW]
            wstate['n'] += 1
            src = win_d[:, col:col + 128].rearrange("(kc p) m -> p kc m", p=128)
            DMA('pool', b[:], src, writes=[b[:]])
            return b

        def wload_proj(wd, col):
            b = wpool[wstate['n'] % NW]
            wstate['n'] += 1
            src = wd[:, col:col + 128].rearrange("(kc p) m -> p kc m", p=128)
            DMA('pool', b[:, 0:8, :], src, writes=[b[:, 0:8, :]])
            return b

        def inproj(wbuf, nk, rhs_src, evac):
            for half in range(2):
                acc = PSA[:, 0:1024]
                items = []
                for k in range(nk):
                    for tt in range(2):
                        t0 = half * 1024 + tt * 512
                        items.append((acc[:, tt * 512:(tt + 1) * 512], wbuf[:, k, :], rhs_src[:, k, t0:t0 + 512],
                                      k == 0, k == nk - 1))
                for i0 in range(0, len(items), 8):
                    MM(items[i0:i0 + 8], reads=[wbuf[:, 0:nk, :], rhs_src[:, 0:nk, half * 1024:(half + 1) * 1024]], writes=[acc])
                evac(acc, half)

        AR_.reset()
        xt = [AR_.a(2048, F32) for _ in range(2)]
        hb = [AR_.a(2048) for _ in range(2)]
        junk = AR_.a(2048)
        ssb = AR_.a(16, F32)
        rsb = AR_.a(16, F32)
        nwb = pp[:, PP['norm_w']:PP['norm_w'] + 16].unsqueeze(2).to_broadcast([128, 16, 128])
        for tt in range(16):
            xtile = xt[tt % 2]
            DMA('sp', xtile, x_d[tt * 128:(tt + 1) * 128, :], writes=[xtile])
            ACT(junk, xtile, AF.Square, accum=ssb[:, tt:tt + 1])
            TS('dve', rsb[:, tt:tt + 1], ssb[:, tt:tt + 1], 1.0 / D, 1e-6, ALU.mult, ALU.add)
            ACT(rsb[:, tt:tt + 1], rsb[:, tt:tt + 1], AF.Sqrt)
            S.add('dve', (lambda o: (lambda e: e.reciprocal(out=o, in_=o)))(rsb[:, tt:tt + 1]),
                  reads=[rsb[:, tt:tt + 1]], writes=[rsb[:, tt:tt + 1]])
            if tt % 2:
                TS('dve', hb[tt % 2], xtile, rsb[:, tt:tt + 1], None, ALU.mult)
            else:
                ACT(hb[tt % 2], xtile, AF.Identity, scale=rsb[:, tt:tt + 1])
            pst = psb2_bf((tt % 2) * 2)
            TRS([(pst[:, k * 128:(k + 1) * 128], hb[tt % 2][:, k * 128:(k + 1) * 128], ident) for k in range(16)],
                reads=[hb[tt % 2], ident], writes=[pst])
            TT('dve', hT[:, :, tt * 128:(tt + 1) * 128], pst.rearrange("p (a b) -> p a b", b=128), nwb, ALU.mult)

        jobs = []
        jobs_seq = [('R', hp) for hp in range(n_hp_r)] + [('M', hp) for hp in range(n_hp_m)]
        for hp in range(n_hp_r):
            for nm in ('r', 'k', 'v', 'za'):
                jobs.append(('R', hp, nm, COL[nm] + hp * 128))
            if hp == 0:
                jobs.append(('R', 0, 'wd', COL['wd']))
        for hp in range(n_hp_m):
            for nm in ('q', 'kq', 'vq', 'zb'):
                jobs.append(('M', hp, nm, COL[nm] + hp * 128))
        PREF = NW - 1
        wq = []
        jstate = {'issued': 0}

        def next_w():
            while jstate['issued'] < len(jobs) and len(wq) < PREF + 1:
                wq.append(wload_in(jobs[jstate['issued']][3]))
                jstate['issued'] += 1
            return wq.pop(0)

        def ev_copy(dst, eng):
            def f(acc, half):
                CP(eng, dst[:, half * 1024:(half + 1) * 1024], acc)
            return f

        def ev_silu(dst):
            def f(acc, half):
                ACT(dst[:, half * 1024:(half + 1) * 1024], acc, AF.Silu)
            return f

        def inproj_job(ji):
            kind, hp = jobs_seq[ji]
            R0, R1, R2, R3 = RAW[ji % 2]
            inproj(next_w(), KC, hT, ev_copy(R0, 'act'))
            inproj(next_w(), KC, hT, ev_copy(R1, 'dve'))
            inproj(next_w(), KC, hT, ev_copy(R2, 'act'))
            inproj(next_w(), KC, hT, ev_silu(R3))

        def lora_prep():
            AR_.reset()
            lraw = AR_.a(2048)
            ld = AR_.a(2048)
            inproj(next_w(), KC, hT, ev_copy(lraw, 'dve'))
            TT('dve', ld[:, 1:2048], lraw[:, 0:2047], lraw[:, 1:2048], ALU.subtract)
            TS('dve', ld[:, 0:1], lraw[:, 0:1], -1.0, None, ALU.mult)
            STT('dve', TL[:], ld, ppc('mu_wa'), lraw, ALU.mult, ALU.add)
            ACT(TL[0:64, :], TL[0:64, :], AF.Tanh)

        def split_parts(lst, fracs):
            tot = float(sum(fracs))
            out = []
            acc = 0.0
            i0 = 0
            for f in fracs:
                acc += f
                i1 = int(round(len(lst) * acc / tot))
                out.append(lst[i0:i1])
                i0 = i1
            out[-1] = out[-1] + lst[i0:]
            return out

        def cap(fn, *args):
            S.capture = []
            fn(*args)
            lst = S.capture
            S.capture = None
            return lst

        def rwkv_all():
            AR_.reset()
            A = AR_.a
            GT = 256
            NG = S_TOK // GT
            Sf = [A(128, F32) for _ in range(2)]
            Sbf = A(128)
            Ssc = A(128, F32)
            Stmp = A(128, F32)
            Zsb = A(128)
            Usb = A(128)
            tinyb_t = A(2, F32)
            YAg = [A(GT) for _ in range(2)]
            ARfs = [A(2 * GT, F32) for _ in range(2)]
            KTfs = [A(GT, F32) for _ in range(2)]
            BTfs = [A(GT, F32) for _ in range(2)]
            KTbs = [A(GT) for _ in range(2)]
            BTbs = [A(GT) for _ in range(2)]
            vgs = [A(GT) for _ in range(2)]
            WCs = [A(4, F32) for _ in range(4)]
            BONs = [A(GT) for _ in range(4)]
            ARb = [A(2 * GT) for _ in range(2)]
            TM = [A(3 * GT) for _ in range(2)]
            AKs = [A(2 * GT) for _ in range(2)]
            RKs = [A(2 * GT) for _ in range(2)]
            RBs = [A(2 * GT) for _ in range(2)]
            TTs = [A(2 * GT) for _ in range(2)]
            Ytms = [A(GT, F32) for _ in range(2)]
            mean = A(4, F32); var = A(4, F32)
            YC = A(GT, F32); YQ = A(GT, F32); YN = A(GT)
            Dm = A(GT, F32); SQ = A(GT); PROD = A(GT)
            omu = A(4, F32)
            rg = A(GT, F32); kg = A(GT, F32); SG = A(GT, F32); CS = A(GT, F32); ag = A(GT, F32)
            E1 = A(GT, F32); E2 = A(GT, F32); E3 = A(GT, F32)
            KK = A(GT, F32); RI = A(GT, F32); Tt = A(GT, F32); K2 = A(GT, F32); Bb = A(GT, F32)
            N0s = A(2 * GT); X0s = A(2 * GT)
            Ns = [A(2 * GT) for _ in range(2)]
            Xs = [A(2 * GT) for _ in range(2)]
            Ps = [A(2 * GT) for _ in range(2)]
            MEMSET('pool', tinyb_t, 1e-12)
            tinyb = tinyb_t[:, 0:1]
            NT = GT // 128
            NM = NT * 2
            PA2 = PSA[:, 1024:1536]
            PA3 = PSA[:, 1536:2048]

            def v3(ap):
                return ap.rearrange("p (j t) -> p j t", t=128)

            def emit_A(hp, g):
                Rr, Rk, Rv, SZ = RAW[hp % 2]
                if g == 0:
                    for mi, mu in enumerate(('mu_r', 'mu_k', 'mu_v')):
                        TS('dve', omu[:, mi:mi + 1], ppc(mu, hp), -1.0, 1.0, ALU.mult, ALU.add)
                c0 = g * GT
                par = g % 2
                ARf = ARfs[par]; KTf = KTfs[par]; BTf = BTfs[par]; vg = vgs[par]
                ARfv = ARf.rearrange("p (j w t) -> p j w t", w=2, t=128)
                for mi, (raw, mu, dst) in enumerate(((Rr, 'mu_r', rg), (Rk, 'mu_k', kg), (Rv, 'mu_v', vg))):
                    if g == 0:
                        ACT(Dm[:, 1:GT], raw[:, 0:GT - 1], AF.Identity, scale=ppc(mu, hp))
                        TS('dve', Dm[:, 0:1], raw[:, 0:1], 0.0, None, ALU.mult)
                    else:
                        ACT(Dm, raw[:, c0 - 1:c0 + GT - 1], AF.Identity, scale=ppc(mu, hp))
                    STT('dve', dst, raw[:, c0:c0 + GT], omu[:, mi:mi + 1], Dm, ALU.mult, ALU.add)
                pu = PA2[:, 0:GT]
                pa_ = PA2[:, GT:2 * GT]
                MM([(pu, loraD[:, hp * 128:(hp + 1) * 128], TL[:, c0:c0 + GT], True, True)],
                   reads=[loraD[:, hp * 128:(hp + 1) * 128], TL[:, c0:c0 + GT]], writes=[pu])
                MM([(pa_, loraI[:, hp * 128:(hp + 1) * 128], TL[:, c0:c0 + GT], True, True)],
                   reads=[loraI[:, hp * 128:(hp + 1) * 128], TL[:, c0:c0 + GT]], writes=[pa_])
                ACT(SG, pu, AF.Sigmoid, bias=ppc('w0', hp))
                ACT(ag, pa_, AF.Sigmoid, bias=ppc('a0', hp))
                S.add('dve', (lambda o, m, d1: (lambda e: e.tensor_tensor_scan(out=o, data0=m, data1=d1, initial=0.0,
                                                                                op0=ALU.mult, op1=ALU.add)))(CS, scanmask[:, 0:GT], SG),
                      reads=[scanmask[:, 0:GT], SG], writes=[CS], cost=0.1 + 2 * GT / 960.0)
                ACT(E1, CS, AF.Exp, scale=-CDEC)
                ACT(WCs[g % 4], CS.rearrange("p (c t) -> p c t", t=64)[:, :, 63], AF.Exp, scale=-CDEC)
                ACT(E3, CS, AF.Exp, scale=CDEC)
                TT('dve', SG, CS, SG, ALU.subtract)
                ACT(E2, SG, AF.Exp, scale=-CDEC)
                ACT(KK, kg, AF.Identity, scale=ppc('k_k', hp))
                ACT(SQ, kg, AF.Square, scale=ppc('k_k', hp))
                pk = PA2[:, 0:GT]
                MM([(pk, blk2, SQ, True, True)], reads=[blk2, SQ], writes=[pk])
                ACT(RI, pk, AF.Ln, bias=tinyb)
                ACT(RI, RI, AF.Exp, scale=-0.5)
                TT('dve', KK, KK, RI, ALU.mult)
                TS('dve', Tt, ag, -1.0, ppc('k_a', hp), ALU.add, ALU.mult)
                STT('dve', K2, Tt, 1.0, kg, ALU.add, ALU.mult)
                TT('dve', Bb, KK, ag, ALU.mult)
                STT('dve', PROD, rg, ppc('r_k', hp), K2, ALU.mult, ALU.mult)
                pb_ = PA2[:, GT:2 * GT]
                MM([(pb_, blk2, PROD, True, True)], reads=[blk2, PROD], writes=[pb_])
                TT('dve', BONs[g % 4], pb_, vg, ALU.mult)
                TT('dve', ARfv[:, :, 1, :], v3(rg), v3(E1), ALU.mult)
                STT('dve', ARfv[:, :, 0, :], v3(KK), -1.0, v3(E2), ALU.mult, ALU.mult)
                TT('dve', KTf, K2, E3, ALU.mult)
                TT('dve', BTf, Bb, E3, ALU.mult)
                CP('act', KTbs[par], KTf)
                CP('act', BTbs[par], BTf)

            def emit_B(hp, g):
                par = g % 2
                ARf = ARfs[par]; KTf = KTfs[par]; BTf = BTfs[par]; vg = vgs[par]
                KTb = KTbs[par]; BTb = BTbs[par]
                CP('act', ARb[par], ARf)
                ptm = PA3.bitcast(BF16)[:, 0:3 * GT]
                items = []
                for q, src in enumerate((vg, KTb, BTb)):
                    for j in range(NT):
                        items.append((ptm[:, (q * NT + j) * 128:(q * NT + j + 1) * 128], src[:, j * 128:(j + 1) * 128], ident))
                TRS(items, reads=[vg, KTb, BTb, ident], writes=[ptm])
                CP('act', TM[par], ptm)
                for j in range(NT):
                    items = []
                    for h in range(2):
                        hr = slice(h * 64, h * 64 + 64)
                        rhs_ar = ARf[hr, j * 256:(j + 1) * 256]
                        bh = psb_f32(h)
                        items.append((bh[:, 0:256], KTf[hr, j * 128:(j + 1) * 128], rhs_ar, True, True))
                        items.append((bh[:, 256:512], BTf[hr, j * 128:(j + 1) * 128], rhs_ar, True, True))
                    MM(items, reads=[KTf, BTf, ARf], writes=[PSB[:, 0:1024]])
                    bxy = PSB[:, 0:1024].rearrange("p (h w t) -> p h w t", h=2, w=4)
                    msu_b = m_su.unsqueeze(1).to_broadcast([128, 2, 128])
                    miu_b = m_iu.unsqueeze(1).to_broadcast([128, 2, 128])
                    msl_b = m_sl.unsqueeze(1).to_broadcast([128, 2, 128])

                    def dst(t):
                        return t.rearrange("p (j h t) -> p j h t", h=2, t=128)[:, j, :, :]
                    TT('dve', dst(AKs[par]), bxy[:, :, 0, :], msu_b, ALU.mult)
                    TT('dve', dst(RKs[par]), bxy[:, :, 1, :], miu_b, ALU.mult)
                    TT('dve', dst(N0s), bxy[:, :, 2, :], msu_b, ALU.mult)
                    TT('dve', dst(RBs[par]), bxy[:, :, 3, :], miu_b, ALU.mult)

                pxt = PA3.bitcast(BF16)[:, 0:NM * 128]
                TRS([(pxt[:, m * 128:(m + 1) * 128], N0s[:, m * 128:(m + 1) * 128], ident) for m in range(NM)],
                    reads=[N0s, ident], writes=[pxt])
                CP('act', X0s, pxt)

                def m8(t):
                    return t.rearrange("p (m t) -> p m t", t=128)
                W = NM * 128
                TT('dve', m8(Ps[0]), m8(N0s), ident.unsqueeze(1).to_broadcast([128, NM, 128]), ALU.add)
                Ncur, Xcur, Pcur = N0s, X0s, Ps[0]
                for lvl in range(1, 6):
                    pX = PSB[:, 0:W]
                    MM([(pX[:, m * 128:(m + 1) * 128], Ncur[:, m * 128:(m + 1) * 128], Xcur[:, m * 128:(m + 1) * 128], True, True)
                        for m in range(NM)], reads=[Ncur, Xcur], writes=[pX])
                    Xn = Xs[lvl % 2]
                    if lvl < 5:
                        pN = PSB[:, 512:512 + W]
                        MM([(pN[:, m * 128:(m + 1) * 128], Xcur[:, m * 128:(m + 1) * 128], Ncur[:, m * 128:(m + 1) * 128], True, True)
                            for m in range(NM)], reads=[Ncur, Xcur], writes=[pN])
                    CP('act', Xn, pX)
                    if lvl < 5:
                        Nn = Ns[lvl % 2]
                        CP('act', Nn, pN)
                    pP = PA3[:, 0:W]
                    pitems = []
                    for m in range(NM):
                        pitems.append((pP[:, m * 128:(m + 1) * 128], Xn[:, m * 128:(m + 1) * 128], Pcur[:, m * 128:(m + 1) * 128], m == 0, False))
                        pitems.append((pP[:, m * 128:(m + 1) * 128], ident, Pcur[:, m * 128:(m + 1) * 128], False, m == NM - 1))
                    MM(pitems, reads=[Xn, Pcur, ident], writes=[pP])
                    Pn = TTs[par] if lvl == 5 else Ps[lvl % 2]
                    CP('act', Pn, pP)
                    Pcur = Pn
                    Xcur = Xn
                    if lvl < 5:
                        Ncur = Nn

            def emit_chain(hp, g):
                par = g % 2
                if g == 0:
                    TS('dve', Sf[0], Sf[0], 0.0, None, ALU.mult)
                    TS('dve', Sbf, Sbf, 0.0, None, ALU.mult)
                ARbv = ARb[par].rearrange("p (j w t) -> p j w t", w=2, t=128)
                TMv = TM[par].rearrange("p (q j f) -> p q j f", q=3, f=128)
                TTg = TTs[par].rearrange("p (j h t) -> p j h t", h=2, t=128)
                AKv = AKs[par].rearrange("p (j h t) -> p j h t", h=2, t=128)
                RKv = RKs[par].rearrange("p (j h t) -> p j h t", h=2, t=128)
                RBv = RBs[par].rearrange("p (j h t) -> p j h t", h=2, t=128)
                Ytv = Ytms[par].rearrange("p (j f) -> p j f", f=128)
                for cl in range(2 * NT):
                    c = g * 2 * NT + cl
                    j = cl // 2
                    pr = slice((cl % 2) * 64, (cl % 2) * 64 + 64)
                    Scur = Sf[c % 2]
                    Snxt = Sf[(c + 1) % 2]
                    wc = WCs[g % 4][:, cl:cl + 1]
                    Zp = PSB[:, 1536:1664]
                    Up = PSB[:, 1664:1792]
                    Yp = PSB[:, 1792:1920]
                    Sp = PSB[:, 1920:2048]
                    ACT(Ssc, Scur, AF.Identity, scale=wc)
                    MM([(Zp[:, 0:64], AKv[pr, j, 0, :], TMv[pr, 0, j, 0:64], True, False),
                        (Zp[:, 64:128], AKv[pr, j, 1, :], TMv[pr, 0, j, 64:128], False, False),
                        (Zp[:, 0:128], ARbv[:, j, 0, :], Sbf, False, True)],
                       reads=[AKs[par], TM[par], ARb[par], Sbf], writes=[Zp])
                    CP('act', Zsb[pr, :], Zp[pr, :])
                    MM([(Up[:, 0:64], TTg[pr, j, 0, :], Zsb[pr, 0:64], True, False),
                        (Up[:, 64:128], TTg[pr, j, 1, :], Zsb[pr, 64:128], False, True)],
                       reads=[TTs[par], Zsb[pr, :]], writes=[Up])
                    CP('dve', Usb[pr, :], Up[pr, :])
                    MM([(Yp[:, 0:128], ARbv[:, j, 1, :], Sbf, True, False),
                        (Yp[:, 0:64], RBv[pr, j, 0, :], Usb[pr, 0:64], False, False),
                        (Yp[:, 0:64], RKv[pr, j, 0, :], TMv[pr, 0, j, 0:64], False, False),
                        (Yp[:, 64:128], RBv[pr, j, 1, :], Usb[pr, 64:128], False, False),
                        (Yp[:, 64:128], RKv[pr, j, 1, :], TMv[pr, 0, j, 64:128], False, True)],
                       reads=[ARb[par], Sbf, RBs[par], RKs[par], Usb[pr, :], TM[par]], writes=[Yp])
                    CP('act', Ytv[pr, j, :], Yp[pr, :])
                    MM([(Sp, TMv[pr, 2, j, :], Usb[pr, :], True, False),
                        (Sp, TMv[pr, 1, j, :], TMv[pr, 0, j, :], False, True)],
                       reads=[TM[par], Usb[pr, :]], writes=[Sp])
                    TT('dve', Stmp, Sp, blk2, ALU.mult)
                    STT('dve', Sbf, Stmp, wc, Ssc, ALU.mult, ALU.add)
                    STT('dve', Snxt, Stmp, wc, Ssc, ALU.mult, ALU.add)

            def emit_post(hp, g):
                Rr, Rk, Rv, SZ = RAW[hp % 2]
                par = g % 2
                c0 = g * GT
                Ytm = Ytms[par]
                Y4 = Ytm.rearrange("p (m f) -> p m f", f=64)
                YC4 = YC.rearrange("p (m f) -> p m f", f=64)
                YQ4 = YQ.rearrange("p (m f) -> p m f", f=64)
                nm_ = 2 * NT
                S.add('dve', (lambda o, i: (lambda e: e.tensor_reduce(out=o, in_=i, axis=AX.X, op=ALU.add)))(mean, Y4),
                      reads=[Ytm], writes=[mean])
                TS('dve', mean, mean, 1.0 / 64, None, ALU.mult)
                TT('dve', YC4, Y4, mean.unsqueeze(2).to_broadcast([128, nm_, 64]), ALU.subtract)
                ACT(YQ, YC, AF.Square)
                S.add('dve', (lambda o, i: (lambda e: e.tensor_reduce(out=o, in_=i, axis=AX.X, op=ALU.add)))(var, YQ4),
                      reads=[YQ], writes=[var])
                TS('dve', var, var, 1.0 / 64, 64e-5, ALU.mult, ALU.add)
                ACT(var, var, AF.Ln)
                ACT(var, var, AF.Exp, scale=-0.5)
                TT('dve', YN.rearrange("p (m f) -> p m f", f=64), YC4, var.unsqueeze(2).to_broadcast([128, nm_, 64]), ALU.mult)
                pyt = psb_bf(2, GT)
                TRS([(pyt[:, jj * 128:(jj + 1) * 128], YN[:, jj * 128:(jj + 1) * 128], ident) for jj in range(NT)],
                    reads=[YN, ident], writes=[pyt])
                YF = YC
                ACT(YF, pyt, AF.Identity, bias=ppc('gn_b', hp), scale=ppc('gn_w', hp))
                TT('dve', YF, YF, BONs[g % 4], ALU.add)
                TT('dve', YAg[par], YF, SZ[:, c0:c0 + GT], ALU.mult)
                DMA('sp', ya_d[hp, :, c0:c0 + GT], YAg[par], reads=[YAg[par]], writes=[('DR', 'ya', hp, hp + 1, 0, 1, False)])

            MEMSET('pool', Sf[0], 0.0)
            MEMSET('pool', Sbf, 0.0)
            NGT = n_hp_r * NG
            nx = [split_parts(nxt_stream(hp), [1.0] * NG) for hp in range(n_hp_r)]

            def hg(G):
                return (G // NG, G % NG)
            for it in range(-2, NGT + 1):
                streams = []
                if 0 <= it < NGT:
                    streams.append(cap(emit_chain, *hg(it)))
                if 0 <= it + 1 < NGT:
                    streams.append(cap(emit_B, *hg(it + 1)))
                if 0 <= it - 1 < NGT:
                    streams.append(cap(emit_post, *hg(it - 1)))
                if 0 <= it + 2 < NGT:
                    streams.append(cap(emit_A, *hg(it + 2)))
                if 0 <= it < NGT:
                    streams.append(nx[it // NG][it % NG])
                S.merge_streams(streams)

        def nxt_stream(ji):
            return cap(inproj_job, ji + 1) if ji + 1 < len(jobs_seq) else []

        if jobs_seq:
            inproj_job(0)
        if n_hp_r > 0:
            lora_prep()
        if n_hp_r > 0:
            rwkv_all()

        AR_.reset()
        if n_hp_m > 0:
            QA = AR_.a(2048); QB = AR_.a(2048); KA = AR_.a(2048); KB = AR_.a(2048)
            VA = AR_.a(2048); VB = AR_.a(2048)
            for t in (QA, QB, KA, KB):
                MEMSET('pool', t, 0.0)
            MEMSET('pool', VA, 1.0)
            MEMSET('pool', VB, 1.0)
            DMA('sp', KA[64:72, :], ind_d, writes=[KA[64:72, :]])
            DMA('sp', KB[0:8, :], ind_d, writes=[KB[0:8, :]])
        m_base = AR_.off

        def moba_headpair(hp, ji, nxt):
            Rq, Rkq, Rvq, SZ = RAW[ji % 2]
            AR_.off = m_base
            A = AR_.a
            nparts = split_parts(nxt, [2.0, 1.0, 2.0, 3.0, 4.0])
            Qf = A(2048, F32); Kf = A(2048, F32)

            def norm_stream(raw, wname, dA, dB, Ff, SQm, RIm, bank):
                for g in range(4):
                    c0 = g * 512
                    ACT(SQm, raw[:, c0:c0 + 512], AF.Square)
                    pk = psb_f32(bank)
                    MM([(pk, blk2, SQm, True, True)], reads=[blk2, SQm], writes=[pk])
                    ACT(RIm, pk, AF.Ln, bias=ppc_eps, scale=1.0 / 64)
                    ACT(RIm, RIm, AF.Exp, scale=-0.5)
                    STT('dve', Ff[:, c0:c0 + 512], raw[:, c0:c0 + 512], ppc(wname), RIm, ALU.mult, ALU.mult)
                    CP('act', dA[0:64, c0:c0 + 512], Ff[0:64, c0:c0 + 512])
                    CP('dve', dB[64:128, c0:c0 + 512], Ff[64:128, c0:c0 + 512])

            def v_stream():
                pv = psb2_bf(2)
                TRS([(pv[:, t * 128:(t + 1) * 128], Rvq[:, t * 128:(t + 1) * 128], ident) for t in range(16)],
                    reads=[Rvq, ident], writes=[pv])
                pv3 = pv.rearrange("p (t f) -> p t f", f=128)
                CP('act', VA.rearrange("p (t f) -> p t f", f=128)[:, :, 0:64], pv3[:, :, 0:64])
                CP('dve', VB.rearrange("p (t f) -> p t f", f=128)[:, :, 64:128], pv3[:, :, 64:128])

            SQq = A(512); RIq = A(512, F32); SQk = A(512); RIk = A(512, F32)
            st_q = cap(norm_stream, Rq, 'qnw', QA, QB, Qf, SQq, RIq, 0)
            st_k = cap(norm_stream, Rkq, 'knw', KA, KB, Kf, SQk, RIk, 1)
            st_v = cap(v_stream)
            S.merge_streams([st_k, st_q, st_v, nparts[0]])
            nparts[0] = []
            S.capture = []
            kmp = A(16, F32)
            MEMSET('pool', kmp, 0.0)
            S.add('dve', (lambda o, i: (lambda e: e.tensor_reduce(out=o, in_=i, axis=AX.X, op=ALU.add)))(
                kmp[0:64, 0:8], Kf[0:64, :].rearrange("p (n t) -> p n t", t=256)), reads=[Kf[0:64, :]], writes=[kmp[0:64, 0:8]])
            S.add('dve', (lambda o, i: (lambda e: e.tensor_reduce(out=o, in_=i, axis=AX.X, op=ALU.add)))(
                kmp[64:128, 8:16], Kf[64:128, :].rearrange("p (n t) -> p n t", t=256)), reads=[Kf[64:128, :]], writes=[kmp[64:128, 8:16]])
            pg = psb_f32(0, 256)
            MM([(pg[:, qt * 16:(qt + 1) * 16], Qf[:, qt * 128:(qt + 1) * 128], kmp, True, True) for qt in range(16)],
               reads=[Qf, kmp], writes=[pg])
            GM = A(256, F32); G2 = A(256, F32); EQ = A(256, F32); mx = A(32, F32); BI = A(256)
            g3 = lambda t: t.rearrange("p (m n) -> p m n", n=8)
            mxb = mx.unsqueeze(2).to_broadcast([128, 32, 8])

            def rmax(o, i):
                S.add('dve', (lambda o_, i_: (lambda e: e.tensor_reduce(out=o_, in_=i_, axis=AX.X, op=ALU.max)))(o, g3(i)),
                      reads=[i], writes=[o])
            TT('dve', GM, pg, pastneg, ALU.add)
            rmax(mx, GM)
            TT('dve', g3(EQ), g3(GM), mxb, ALU.is_ge)
            STT('dve', G2, EQ, NEG, GM, ALU.mult, ALU.add)
            rmax(mx, G2)
            TT('dve', g3(EQ), g3(G2), mxb, ALU.is_ge)
            STT('dve', G2, EQ, NEG, G2, ALU.mult, ALU.add)
            rmax(mx, G2)
            TT('dve', g3(EQ), g3(GM), mxb, ALU.is_ge)
            TT('dve', EQ, EQ, ownpos, ALU.max)
            TS('dve', BI, EQ, -1.0, -NEG, ALU.add, ALU.mult)
            pbt = psb_bf(1, 1024)
            pbt2 = psb_bf(2, 1024)
            TRS([((pbt if qt < 8 else pbt2)[0:16, (qt % 8) * 128:(qt % 8 + 1) * 128], BI[:, qt * 16:(qt + 1) * 16], ident)
                 for qt in range(16)], reads=[BI, ident], writes=[pbt, pbt2])
            BT_ = A(2048)
            CP('act', BT_[0:16, 0:1024], pbt[0:16, :])
            CP('act', BT_[0:16, 1024:2048], pbt2[0:16, :])
            DMA('sp', QA[64:72, :], BT_[0:8, :], reads=[BT_[0:8, :]], writes=[QA[64:72, :]])
            DMA('sp', QB[0:8, :], BT_[8:16, :], reads=[BT_[8:16, :]], writes=[QB[0:8, :]])
            pro = S.capture
            S.capture = None
            S.merge_streams([pro, nparts[0]])
            PT = [A(512) for _ in range(3)]
            RS = A(512, F32)
            RW = A(512, F32)
            YO = A(512, F32)
            YB = A(2048)
            VA3 = VA.rearrange("p (t f) -> p t f", f=128)
            VB3 = VB.rearrange("p (t f) -> p t f", f=128)
            OpA = psb_f32(2)
            OpB = psb_f32(3)
            for QT in range(4):
                S.capture = []
                q0 = QT * 512
                units = [(h, kt) for kt in range(4 * QT + 4) for h in range(2)]
                nkt = 4 * QT + 4

                def qk(ui):
                    h, kt = units[ui]
                    Kh = KA if h == 0 else KB
                    Qh = QA if h == 0 else QB
                    sp_ = psb_f32(ui % 2)
                    diag = kt >= 4 * QT
                    items = [(sp_, Kh[:, kt * 128:(kt + 1) * 128], Qh[:, q0:q0 + 512], True, not diag)]
                    rds = [Kh[:, kt * 128:(kt + 1) * 128], Qh[:, q0:q0 + 512]]
                    if diag:
                        items.append((sp_, ident, cmask[:, kt - 4 * QT, :], False, True))
                        rds += [ident, cmask[:, kt - 4 * QT, :]]
                    MM(items, reads=rds, writes=[sp_])
                qk(0)
                for ui, (h, kt) in enumerate(units):
                    if ui + 1 < len(units):
                        qk(ui + 1)
                    sp_ = psb_f32(ui % 2)
                    pt = PT[ui % 3]
                    ACT(pt, sp_, AF.Exp, scale=0.125)
                    Vh = VA3 if h == 0 else VB3
                    Oh = OpA if h == 0 else OpB
                    MM([(Oh, Vh[:, kt, :], pt, kt == 0, kt == nkt - 1)], reads=[Vh[:, kt, :], pt], writes=[Oh])
                ACT(RS[64:128, :], OpA[64:128, :], AF.Ln)
                ACT(RS[0:64, :], OpB[0:64, :], AF.Ln)
                ACT(RS, RS, AF.Exp, scale=-1.0)
                pw = psb_f32(0)
                MM([(pw, swapP, RS, True, True)], reads=[swapP, RS], writes=[pw])
                CP('act', RW, pw)
                TT('dve', YO[0:64, :], OpA[0:64, :], RW[0:64, :], ALU.mult)
                TT('dve', YO[64:128, :], OpB[64:128, :], RW[64:128, :], ALU.mult)
                TT('dve', YB[:, q0:q0 + 512], YO, SZ[:, q0:q0 + 512], ALU.mult)
                att = S.capture
                S.capture = None
                S.merge_streams([att, nparts[QT + 1]])
            out_dmas.append(DMA('sp', yb_d[hp], YB, reads=[YB], writes=[('DR', 'yb', hp, hp + 1, 0, 1, False)]))

        if n_hp_m > 0:
            epsb = AR_.a(2, F32)
            MEMSET('pool', epsb, 1e-6)
            ppc_eps = epsb[:, 0:1]
            m_base = AR_.off
        for hp in range(n_hp_m):
            moba_headpair(hp, n_hp_r + hp, nxt_stream(n_hp_r + hp))

        if do_final:
            for th in range(2):
                AR_.reset()
                t0 = th * 1024
                YAh = RAWT[0][:, :]; YBh = RAWT[1][:, :]
                YA3 = YAh.rearrange("p (k t) -> p k t", t=1024)
                YB3 = YBh.rearrange("p (k t) -> p k t", t=1024)
                for k in range(8):
                    DMA('sp', YA3[:, k, :], ya_d[k, :, t0:t0 + 1024], reads=[('DR', 'ya', k, k + 1, 0, 1, False)], writes=[YA3[:, k, :]])
                    DMA('sp', YB3[:, k, :], yb_d[k, :, t0:t0 + 1024], reads=[('DR', 'yb', k, k + 1, 0, 1, False)], writes=[YB3[:, k, :]])
                MG = AR_.a(16 * 1024)
                MG3 = MG.rearrange("p (k t) -> p k t", t=1024)
                sga = AR_.a(1024); sgb = AR_.a(1024); m1 = AR_.a(1024, F32)
                fj = []
                for c in range(16):
                    fj.append(('in', COL['ga'] + c * 128))
                    fj.append(('pa', c * 128))
                    fj.append(('in', COL['gb'] + c * 128))
                    fj.append(('pb', c * 128))
                fq = []
                fst = {'i': 0}

                def next_fw():
                    while fst['i'] < len(fj) and len(fq) < NW:
                        kind, col = fj[fst['i']]
                        fst['i'] += 1
                        if kind == 'in':
                            fq.append(wload_in(col))
                        elif kind == 'pa':
                            fq.append(wload_proj(wpa_d, col))
                        else:
                            fq.append(wload_proj(wpb_d, col))
                    return fq.pop(0)

                def run_half_proj(wbuf, nk, src3, evac):
                    acc = PSA[:, (run_half_proj.n % 2) * 1024:(run_half_proj.n % 2 + 1) * 1024]
                    run_half_proj.n += 1
                    items = []
                    for k in range(nk):
                        for tt in range(2):
                            items.append((acc[:, tt * 512:(tt + 1) * 512], wbuf[:, k, :], src3(k, tt), k == 0, k == nk - 1))
                    MM(items, reads=[wbuf[:, 0:nk, :]] + run_half_proj.rd, writes=[acc])
                    evac(acc)
                run_half_proj.n = 0
                for c in range(16):
                    run_half_proj.rd = [hT[:, :, t0:t0 + 1024]]
                    run_half_proj(next_fw(), KC, lambda k, tt: hT[:, k, t0 + tt * 512:t0 + (tt + 1) * 512],
                                  lambda acc: ACT(sga, acc, AF.Sigmoid))
                    run_half_proj.rd = [YAh]
                    run_half_proj(next_fw(), 8, lambda k, tt: YA3[:, k, tt * 512:(tt + 1) * 512],
                                  lambda acc: TT('dve', m1, acc, sga, ALU.mult))
                    run_half_proj.rd = [hT[:, :, t0:t0 + 1024]]
                    run_half_proj(next_fw(), KC, lambda k, tt: hT[:, k, t0 + tt * 512:t0 + (tt + 1) * 512],
                                  lambda acc: ACT(sgb, acc, AF.Sigmoid))
                    run_half_proj.rd = [YBh]

                    def ev_b(acc, c=c):
                        TT('dve', sgb, acc, sgb, ALU.mult)
                        TT('dve', MG3[:, c, :], m1, sgb, ALU.add)
                    run_half_proj(next_fw(), 8, lambda k, tt: YB3[:, k, tt * 512:(tt + 1) * 512], ev_b)
                WO = [AR_.a(16 * 512) for _ in range(2)]
                XR = [AR_.a(512, F32) for _ in range(1)]
                OT = [AR_.a(512, F32) for _ in range(1)]
                n_o = 0
                for c4 in range(4):
                    wo = WO[c4 % 2]
                    wo3 = wo.rearrange("p (k m) -> p k m", m=512)
                    DMA('pool', wo3, wo_d[:, c4 * 512:(c4 + 1) * 512].rearrange("(kc p) m -> p kc m", p=128), writes=[wo])
                    for tl in range(8):
                        tok0 = t0 + tl * 128
                        xr = XR[0]
                        ot = OT[0]
                        acc = PSB[:, (n_o % 4) * 512:(n_o % 4 + 1) * 512]
                        n_o += 1
                        DMA('sp', xr, x_d[tok0:tok0 + 128, c4 * 512:(c4 + 1) * 512], writes=[xr])
                        MM([(acc, MG3[:, k, tl * 128:(tl + 1) * 128], wo3[:, k, :], k == 0, k == 15) for k in range(16)],
                           reads=[MG, wo], writes=[acc])
                        TT('dve', ot, acc, xr, ALU.add)
                        out_dmas.append(DMA('sp', out_d[tok0:tok0 + 128, c4 * 512:(c4 + 1) * 512], ot, reads=[ot]))

        cnt = S.emit(final_wait_ops=out_dmas)
    return nc, cnt, len(S.ops)


def _consts():
    bf = ml_dtypes.bfloat16
    cbv = np.zeros((128, NCB), np.float32)
    idx = np.arange(128)
    cbv[:, CB['ident']:CB['ident'] + 128] = np.eye(128)
    same = (idx[:, None] // 64) == (idx[None, :] // 64)
    cbv[:, CB['blk2']:CB['blk2'] + 128] = same
    cbv[:, CB['m_su']:CB['m_su'] + 128] = same & (idx[:, None] < idx[None, :])
    cbv[:, CB['m_iu']:CB['m_iu'] + 128] = same & (idx[:, None] <= idx[None, :])
    cbv[:, CB['m_sl']:CB['m_sl'] + 128] = same & (idx[:, None] > idx[None, :])
    cbv[:, CB['onesA']:CB['onesA'] + 64] = 1.0
    cbv[:, CB['onesB'] + 64:CB['onesB'] + 128] = 1.0
    cm = np.zeros((128, 4, 512), np.float32)
    for ktl in range(4):
        kpos = ktl * 128 + idx[:, None]
        qpos = np.arange(512)[None, :]
        kb = kpos // 256
        qb = qpos // 256
        ok = np.where(kb == qb, kpos <= qpos, kb < qb)
        cm[:, ktl, :] = np.where(ok, 0.0, NEG)
    cbv[:, CB['cmask']:CB['cmask'] + 2048] = cm.reshape(128, 2048)
    cfv = np.zeros((128, NCF), np.float32)
    sm = np.ones(512, np.float32)
    sm[::64] = 0.0
    cfv[:, CF['scanmask']:CF['scanmask'] + 512] = sm[None, :]
    pn = np.zeros((16, 2, 8), np.float32)
    op = np.zeros((16, 2, 8), np.float32)
    for qt in range(16):
        qb = qt // 2
        for n in range(8):
            pn[qt, :, n] = 0.0 if n < qb else NEG
            op[qt, :, n] = 1.0 if n >= qb else 0.0
    cfv[:, CF['pastneg']:CF['pastneg'] + 256] = pn.reshape(1, 256)
    cfv[:, CF['ownpos']:CF['ownpos'] + 256] = op.reshape(1, 256)
    cfv[:, CF['swapP']:CF['swapP'] + 128] = np.roll(np.eye(128, dtype=np.float32), 64, axis=1)
    ind = np.zeros((8, S_TOK), np.float32)
    for n in range(8):
        ind[n, n * 256:(n + 1) * 256] = 1.0
    return cbv.astype(bf), cfv, ind.astype(bf)


def _pack_params(i):
    ppv = np.zeros((128, NPP), np.float32)

    def fm(v):
        return np.ascontiguousarray(v.reshape(-1, 128).T)
    for nm in ('mu_r', 'mu_k', 'mu_v', 'w0', 'a0', 'k_k', 'k_a', 'r_k', 'gn_w', 'gn_b'):
        ppv[:, PP[nm]:PP[nm] + 8] = fm(i[nm][0])
    ppv[:, PP['norm_w']:PP['norm_w'] + 16] = fm(i['norm_w'][0])
    ppv[0:64, PP['mu_wa']] = i['mu_w'][0]
    ppv[64:128, PP['mu_wa']] = i['mu_a'][0]
    ppv[0:64, PP['qnw']] = i['q_norm_w'][0]
    ppv[64:128, PP['qnw']] = i['q_norm_w'][0]
    ppv[0:64, PP['knw']] = i['k_norm_w'][0]
    ppv[64:128, PP['knw']] = i['k_norm_w'][0]
    lora = np.concatenate([i['w_decay_up'][0], i['w_iclr_up'][0]], axis=0).astype(np.float32)
    return ppv, np.ascontiguousarray(lora)


_CACHE = {}


def make_in_maps(inputs, n_cores=8):
    i = {k: np.asarray(v) for k, v in inputs.items()}
    cbv, cfv, ind = _consts()
    ppv, lora = _pack_params(i)
    shared = dict(w_in=np.ascontiguousarray(i['w_in'][0]), w_pa=np.ascontiguousarray(i['w_proj_rwkv'][0]),
                  w_pb=np.ascontiguousarray(i['w_proj_moba'][0]), w_out=np.ascontiguousarray(i['w_out'][0]),
                  lora=lora, pp=ppv, cb=cbv, cf=cfv, ind=ind)
    maps = []
    for c in range(n_cores):
        m = dict(shared)
        m['x'] = np.ascontiguousarray(i['x'][c])
        maps.append(m)
    return maps


def kernel(**inputs):
    if 'nc' not in _CACHE:
        _CACHE['nc'] = build()[0]
    nc = _CACHE['nc']
    maps = make_in_maps(inputs, 8)
    res = run_bass_kernel_spmd(nc, maps, core_ids=list(range(8)))
    out = np.stack([np.asarray(r['out']) for r in res.results], axis=0)
    return out.astype(np.float32)
```

```python
import contextlib
import numpy as np
import ml_dtypes
import concourse.bass as bass
import concourse.mybir as mybir
from concourse.bass_utils import run_bass_kernel_spmd

F32 = mybir.dt.float32
BF16 = mybir.dt.bfloat16
AF = mybir.ActivationFunctionType
ALU = mybir.AluOpType
AX = mybir.AxisListType

S_TOK = 2048
D = 2048
KC = 16
IN_COLS = 12416
COL = dict(r=0, k=1024, v=2048, za=3072, wd=4096, q=4224, kq=5248, vq=6272, zb=7296, ga=8320, gb=10368)
CDEC = 0.6065306597126334
NEG = -1.0e30
STOP = 99
NOMERGE = False
STRICT = True

PP = dict(mu_r=0, mu_k=8, mu_v=16, w0=24, a0=32, k_k=40, k_a=48, r_k=56, gn_w=64, gn_b=72, norm_w=80,
          mu_wa=96, qnw=97, knw=98)
NPP = 100
CB = dict(ident=0, blk2=128, m_su=256, m_iu=384, m_sl=512, onesA=640, onesB=768, cmask=896)
NCB = 896 + 4 * 512
CF = dict(scanmask=0, pastneg=512, ownpos=768, swapP=1024)
NCF = 1152


_DTSZ = {}


def _dtsize(dt):
    s = _DTSZ.get(dt)
    if s is None:
        name = str(dt)
        s = 4 if '32' in name else 2 if '16' in name else 1 if '8' in name else 8
        _DTSZ[dt] = s
    return s


def box_of(ap):
    dims = ap.ap
    sz = _dtsize(ap.dtype)
    pstep, pcnt = dims[0]
    off = int(ap.offset)
    if pstep == 0:
        pstep = 1 << 40
    p0 = off // pstep
    f0 = off % pstep
    ext = 0
    for st, cn in dims[1:]:
        ext += abs(st) * (cn - 1)
    f1 = f0 + ext + 1
    if 'PSUM' in str(ap.space).upper():
        b0 = (f0 * sz) // 2048
        b1 = ((f1 * sz) - 1) // 2048
        return ('PS', ap.name, 0, 128, b0 * 2048, (b1 + 1) * 2048, True)
    return ('SB', ap.name, p0, p0 + pcnt, f0 * sz, f1 * sz, False)


class Op:
    __slots__ = ('idx', 'eng', 'fn', 'deps', 'signal', 'count', 'is_dma', 'dsem', 'dcount', 'prev_slot')

    def __init__(self, idx, eng, fn, is_dma):
        self.idx = idx
        self.eng = eng
        self.fn = fn
        self.deps = {}
        self.signal = False
        self.count = 0
        self.is_dma = is_dma
        self.dsem = None
        self.dcount = 0
        self.prev_slot = None


class Sched:
    ENGS = ('pe', 'act', 'dve', 'pool', 'sp')

    def __init__(self, nc, n_dma_slots=10):
        self.nc = nc
        self.ops = []
        self.recs = {}
        self.n_dma_slots = n_dma_slots

    def _touch(self, op, box, is_write):
        kind, name, p0, p1, f0, f1, excl = box
        lst = self.recs.setdefault(name, [])
        found = None
        for r in lst:
            if r[0] < p1 and p0 < r[1] and r[2] < f1 and f0 < r[3]:
                if r[4] is not None:
                    if (not is_write) or excl:
                        op.deps[r[4]] = True
                    else:
                        op.deps.setdefault(r[4], False)
                if is_write or excl:
                    for e, o in r[5].items():
                        op.deps.setdefault(o, False)
            if r[0] == p0 and r[1] == p1 and r[2] == f0 and r[3] == f1:
                found = r
        if found is None:
            found = [p0, p1, f0, f1, None, {}]
            lst.append(found)
        if is_write or excl:
            found[4] = op.idx
            found[5] = {}
        else:
            found[5][op.eng] = op.idx

    capture = None

    def add(self, eng, fn, reads=(), writes=(), dma=False, cost=0.5):
        if self.capture is not None:
            self.capture.append((eng, fn, list(reads), list(writes), dma, cost))
            return -1
        op = Op(len(self.ops), eng, fn, dma)
        self.ops.append(op)
        for ap in reads:
            if ap is not None and not isinstance(ap, (int, float)):
                self._touch(op, ap if isinstance(ap, tuple) else box_of(ap), False)
        for ap in writes:
            if ap is not None:
                self._touch(op, ap if isinstance(ap, tuple) else box_of(ap), True)
        op.deps.pop(op.idx, None)
        return op.idx

    def commit(self, lst):
        for it in lst:
            self.add(*it[:5])

    def merge_streams(self, streams):
        if NOMERGE:
            for st in streams:
                self.commit(st)
            return
        pos = [0] * len(streams)
        ready = [0.0] * len(streams)
        free = {e: 0.0 for e in self.ENGS}
        while True:
            best = None
            for si, st in enumerate(streams):
                if pos[si] >= len(st):
                    continue
                it = st[pos[si]]
                t = max(free[it[0]], ready[si])
                if best is None or t < best[0] - 1e-9:
                    best = (t, si)
            if best is None:
                break
            t, si = best
            it = streams[si][pos[si]]
            pos[si] += 1
            self.add(*it[:5])
            if it[4]:
                free[it[0]] = t + 0.06
                ready[si] = t + 0.06
            else:
                free[it[0]] = t + it[5]
                ready[si] = t + it[5] + 0.12

    def merge(self, main, bg):
        return self.merge_streams([main, bg])
        nb = len(bg)
        nm = max(len(main), 1)
        j = 0
        for i, it in enumerate(main):
            self.add(*it[:5])
            tgt = (i + 1) * nb // nm
            while j < tgt:
                self.add(*bg[j][:5])
                j += 1
        while j < nb:
            self.add(*bg[j])
            j += 1

    def emit(self, final_wait_ops=()):
        nc = self.nc
        ops = self.ops
        for op in ops:
            for d, raw in op.deps.items():
                dop = ops[d]
                if dop.is_dma:
                    continue
                if (not op.is_dma) and dop.eng == op.eng and (op.eng == 'pe' or (not raw and not STRICT)):
                    continue
                dop.signal = True
        for d in final_wait_ops:
            if not ops[d].is_dma:
                ops[d].signal = True
        cnt = {e: 0 for e in self.ENGS}
        for op in ops:
            if op.is_dma:
                continue
            if op.signal:
                cnt[op.eng] += 1
            op.count = cnt[op.eng]
        slot_state = {}
        for op in ops:
            if not op.is_dma:
                continue
            st = slot_state.setdefault(op.eng, {'next': 0, 'counts': [0] * self.n_dma_slots,
                                                'last': [None] * self.n_dma_slots})
            s = st['next']
            st['next'] = (s + 1) % self.n_dma_slots
            op.prev_slot = st['last'][s]
            st['counts'][s] += 16
            op.dsem = (op.eng, s)
            op.dcount = st['counts'][s]
            st['last'][s] = op.idx
        used = [e for e in self.ENGS if any(o.eng == e for o in ops)]
        if 'sp' not in used:
            used.append('sp')
        with contextlib.ExitStack() as es:
            sems = {e: es.enter_context(nc.semaphore('s_' + e)) for e in used}
            dsems = {}
            for e in slot_state:
                for s in range(self.n_dma_slots):
                    dsems[(e, s)] = es.enter_context(nc.semaphore('d_%s_%d' % (e, s)))
            block = es.enter_context(nc.Block())

            def run_engine(ename, eng):
                waited = {e: 0 for e in self.ENGS}
                dwaited = {}

                def wait_on(dop):
                    if dop.is_dma:
                        if dwaited.get(dop.dsem, 0) < dop.dcount:
                            eng.wait_ge(dsems[dop.dsem], dop.dcount)
                            dwaited[dop.dsem] = dop.dcount
                    elif dop.count > waited[dop.eng]:
                        eng.wait_ge(sems[dop.eng], dop.count)
                        waited[dop.eng] = dop.count

                for op in ops:
                    if op.eng != ename:
                        continue
                    for d in sorted(op.deps):
                        dop = ops[d]
                        raw = op.deps[d]
                        if (not dop.is_dma) and (not op.is_dma) and dop.eng == ename and (ename == 'pe' or (not raw and not STRICT)):
                            continue
                        wait_on(dop)
                    if op.is_dma and op.prev_slot is not None:
                        wait_on(ops[op.prev_slot])
                    ins = op.fn(eng)
                    if op.is_dma:
                        ins.then_inc(dsems[op.dsem], 16)
                    elif op.signal:
                        ins.then_inc(sems[ename], 1)
                if ename == 'sp':
                    for d in final_wait_ops:
                        wait_on(ops[d])

            @block.tensor
            def _(eng):
                run_engine('pe', eng)

            @block.scalar
            def _(eng):
                run_engine('act', eng)

            @block.vector
            def _(eng):
                run_engine('dve', eng)

            @block.gpsimd
            def _(eng):
                run_engine('pool', eng)

            @block.sync
            def _(eng):
                run_engine('sp', eng)
        return cnt


def build(n_hp_r=8, n_hp_m=8, do_final=True, dbg=False):
    nc = bass.Bass("TRN2", target_bir_lowering=False)
    x_d = nc.dram_tensor("x", [S_TOK, D], F32, kind="ExternalInput").ap()
    win_d = nc.dram_tensor("w_in", [D, IN_COLS], F32, kind="ExternalInput").ap()
    wpa_d = nc.dram_tensor("w_pa", [1024, D], F32, kind="ExternalInput").ap()
    wpb_d = nc.dram_tensor("w_pb", [1024, D], F32, kind="ExternalInput").ap()
    wo_d = nc.dram_tensor("w_out", [D, D], F32, kind="ExternalInput").ap()
    lora_d = nc.dram_tensor("lora", [128, 1024], F32, kind="ExternalInput").ap()
    pp_d = nc.dram_tensor("pp", [128, NPP], F32, kind="ExternalInput").ap()
    cb_d = nc.dram_tensor("cb", [128, NCB], BF16, kind="ExternalInput").ap()
    cf_d = nc.dram_tensor("cf", [128, NCF], F32, kind="ExternalInput").ap()
    ind_d = nc.dram_tensor("ind", [8, S_TOK], BF16, kind="ExternalInput").ap()
    out_d = nc.dram_tensor("out", [S_TOK, D], F32, kind="ExternalOutput").ap()
    scr_kind = "ExternalOutput" if dbg else "Internal"
    ya_d = nc.dram_tensor("ya_scr", [8, 128, S_TOK], BF16, kind=scr_kind).ap()
    yb_d = nc.dram_tensor("yb_scr", [8, 128, S_TOK], BF16, kind=scr_kind).ap()

    with contextlib.ExitStack() as es:
        def sb(name, shape, dt):
            return es.enter_context(nc.sbuf_tensor(name, shape, dt))

        hT = sb("hT", [128, KC, S_TOK], BF16)
        NW = 4
        wpool = [sb("wp%d" % i, [128, KC, 128], BF16) for i in range(NW)]
        cb = sb("cb_s", [128, NCB], BF16)
        cf = sb("cf_s", [128, NCF], F32)
        pp = sb("pp_s", [128, NPP], F32)
        loraD = sb("loraD_s", [128, 1024], BF16)
        loraI = sb("loraI_s", [128, 1024], BF16)
        TL = sb("TL", [128, S_TOK], BF16)
        RAWT = [sb("rawt%d" % s, [128, 4 * S_TOK], BF16) for s in range(2)]
        RAW = [[RAWT[s][:, j * S_TOK:(j + 1) * S_TOK] for j in range(4)] for s in range(2)]
        ARENA_N = 39424
        arena_t = sb("arena", [128, ARENA_N], BF16)
        PSA = es.enter_context(nc.psum_tensor("PSA", [128, 2048], F32))
        PSB = es.enter_context(nc.psum_tensor("PSB", [128, 2048], F32))

        S = Sched(nc)
        out_dmas = []

        class Arena:
            def __init__(self):
                self.off = 0

            def reset(self):
                self.off = 0

            def a(self, n, dt=BF16):
                nb = n * (2 if dt == F32 else 1)
                nb = (nb + 1) // 2 * 2
                assert self.off + nb <= ARENA_N, (self.off, nb)
                v = arena_t[:, self.off:self.off + nb]
                self.off += nb
                if dt == F32:
                    v = v.bitcast(F32)
                return v

        AR_ = Arena()

        def rd(*aps):
            return [a for a in aps if a is not None and not isinstance(a, (int, float))]

        def fsz(ap):
            n = 1
            for st, cn in ap.ap[1:]:
                n *= cn
            return n

        def ACT(out, in_, func, bias=None, scale=None, accum=None):
            kw = {}
            if bias is not None:
                kw['bias'] = bias
            if scale is not None:
                kw['scale'] = scale
            if accum is not None:
                kw['accum_out'] = accum
            return S.add('act', lambda e: e.activation(out=out, in_=in_, func=func, **kw),
                         reads=rd(in_, bias, scale), writes=[out, accum], cost=0.22 + fsz(out) / 1200.0)

        def TT(eng, out, in0, in1, op):
            return S.add(eng, lambda e: e.tensor_tensor(out=out, in0=in0, in1=in1, op=op),
                         reads=rd(in0, in1), writes=[out], cost=0.1 + fsz(out) / 960.0)

        def TS(eng, out, in0, s1, s2, op0, op1=None):
            if op1 is None:
                return S.add(eng, lambda e: e.tensor_scalar(out=out, in0=in0, scalar1=s1, scalar2=None, op0=op0),
                             reads=rd(in0, s1), writes=[out], cost=0.1 + fsz(out) / 960.0)
            return S.add(eng, lambda e: e.tensor_scalar(out=out, in0=in0, scalar1=s1, scalar2=s2, op0=op0, op1=op1),
                         reads=rd(in0, s1, s2), writes=[out], cost=0.1 + fsz(out) / 960.0)

        def STT(eng, out, in0, scalar, in1, op0, op1):
            eng = 'dve'
            return S.add(eng, lambda e: e.scalar_tensor_tensor(out=out, in0=in0, scalar=scalar, in1=in1,
                                                                op0=op0, op1=op1),
                         reads=rd(in0, scalar, in1), writes=[out], cost=0.1 + fsz(out) / 960.0)

        def CP(eng, out, in_):
            if eng == 'act':
                return S.add('act', lambda e: e.copy(out=out, in_=in_), reads=[in_], writes=[out],
                             cost=0.22 + fsz(out) / 1200.0)
            return S.add(eng, lambda e: e.tensor_copy(out=out, in_=in_), reads=[in_], writes=[out],
                         cost=0.1 + fsz(out) / 960.0)

        def MEMSET(eng, out, val):
            return S.add(eng, lambda e: e.memset(out, val), writes=[out], cost=1.0)

        def MM(items, reads, writes):
            def fn(e):
                ins = None
                for (o, l, r, st, sp) in items:
                    ins = e.matmul(o, lhsT=l, rhs=r, start=st, stop=sp)
                return ins
            c = 0.05
            for (o, l, r, st, sp) in items:
                c += max(0.065, fsz(o) / (600.0 if l.dtype == F32 else 2400.0))
            return S.add('pe', fn, reads=reads, writes=writes, cost=c)

        def TRS(items, reads, writes):
            def fn(e):
                ins = None
                for (o, i, idn) in items:
                    ins = e.transpose(o, i, idn)
                return ins
            return S.add('pe', fn, reads=reads, writes=writes, cost=0.05 + 0.11 * len(items))

        def DMA(q, out, in_, reads=(), writes=()):
            return S.add(q, lambda e: e.dma_start(out=out, in_=in_), reads=reads, writes=writes, dma=True)

        def ppc(name, j=0):
            c = PP[name] + j
            return pp[:, c:c + 1]

        ident = cb[:, CB['ident']:CB['ident'] + 128]
        blk2 = cb[:, CB['blk2']:CB['blk2'] + 128]
        m_su = cb[:, CB['m_su']:CB['m_su'] + 128]
        m_iu = cb[:, CB['m_iu']:CB['m_iu'] + 128]
        m_sl = cb[:, CB['m_sl']:CB['m_sl'] + 128]
        onesA = cb[:, CB['onesA']:CB['onesA'] + 128]
        onesB = cb[:, CB['onesB']:CB['onesB'] + 128]
        cmask = cb[:, CB['cmask']:CB['cmask'] + 2048].rearrange("p (a b) -> p a b", b=512)
        scanmask = cf[:, CF['scanmask']:CF['scanmask'] + 512]
        pastneg = cf[:, CF['pastneg']:CF['pastneg'] + 256]
        ownpos = cf[:, CF['ownpos']:CF['ownpos'] + 256]
        swapP = cf[:, CF['swapP']:CF['swapP'] + 128]

        def psb_f32(bank, n=512, off=0):
            return PSB[:, bank * 512 + off: bank * 512 + off + n]

        def psb_bf(bank, n=1024, off=0):
            v = PSB[:, bank * 512:(bank + 1) * 512].bitcast(BF16)
            return v[:, off:off + n]

        def psb2_bf(bank):
            return PSB[:, bank * 512:(bank + 2) * 512].bitcast(BF16)

        DMA('sp', cb[:], cb_d, writes=[cb[:]])
        DMA('sp', cf[:], cf_d, writes=[cf[:]])
        DMA('sp', pp[:], pp_d, writes=[pp[:]])
        S.add('pool', lambda e: e.memset(loraD[:], 0.0), writes=[loraD[:]])
        S.add('pool', lambda e: e.memset(loraI[:], 0.0), writes=[loraI[:]])
        DMA('pool', loraD[0:64, :], lora_d[0:64, :], writes=[loraD[0:64, :]])
        DMA('pool', loraI[64:128, :], lora_d[64:128, :], writes=[loraI[64:128, :]])

        wstate = {'n': 0}

        def wload_in(col):
            b = wpool[wstate['n'] % NW]
            wstate['n'] += 1
            src = win_d[:, col:col + 128].rearrange("(kc p) m -> p kc m", p=128)
            DMA('pool', b[:], src, writes=[b[:]])
            return b

        def wload_proj(wd, col):
            b = wpool[wstate['n'] % NW]
            wstate['n'] += 1
            src = wd[:, col:col + 128].rearrange("(kc p) m -> p kc m", p=128)
            DMA('pool', b[:, 0:8, :], src, writes=[b[:, 0:8, :]])
            return b

        def inproj(wbuf, nk, rhs_src, evac):
            for half in range(2):
                acc = PSA[:, 0:1024]
                items = []
                for k in range(nk):
                    for tt in range(2):
                        t0 = half * 1024 + tt * 512
                        items.append((acc[:, tt * 512:(tt + 1) * 512], wbuf[:, k, :], rhs_src[:, k, t0:t0 + 512],
                                      k == 0, k == nk - 1))
                for i0 in range(0, len(items), 8):
                    MM(items[i0:i0 + 8], reads=[wbuf[:, 0:nk, :], rhs_src[:, 0:nk, half * 1024:(half + 1) * 1024]], writes=[acc])
                evac(acc, half)

        AR_.reset()
        xt = [AR_.a(2048, F32) for _ in range(4)]
        hb = [AR_.a(2048) for _ in range(2)]
        junk = AR_.a(2048)
        ssb = AR_.a(16, F32)
        rsb = AR_.a(16, F32)
        nwb = pp[:, PP['norm_w']:PP['norm_w'] + 16].unsqueeze(2).to_broadcast([128, 16, 128])
        eps0 = AR_.a(2, F32)
        MEMSET('pool', eps0, 1e-6)

        def p0_stage1(tt):
            xtile = xt[tt % 4]
            if tt + 2 < 16:
                DMA('sp', xt[(tt + 2) % 4], x_d[(tt + 2) * 128:(tt + 3) * 128, :], writes=[xt[(tt + 2) % 4]])
            ACT(junk, xtile, AF.Square, accum=ssb[:, tt:tt + 1])
            ACT(rsb[:, tt:tt + 1], ssb[:, tt:tt + 1], AF.Ln, bias=eps0[:, 0:1], scale=1.0 / D)
            ACT(rsb[:, tt:tt + 1], rsb[:, tt:tt + 1], AF.Exp, scale=-0.5)
            TS('dve', hb[tt % 2], xtile, rsb[:, tt:tt + 1], None, ALU.mult)

        def p0_stage2(tt):
            pst = psb2_bf((tt % 2) * 2)
            TRS([(pst[:, k * 128:(k + 1) * 128], hb[tt % 2][:, k * 128:(k + 1) * 128], ident) for k in range(16)],
                reads=[hb[tt % 2], ident], writes=[pst])
            TT('dve', hT[:, :, tt * 128:(tt + 1) * 128], pst.rearrange("p (a b) -> p a b", b=128), nwb, ALU.mult)

        def cap0(fn, *args):
            S.capture = []
            fn(*args)
            lst = S.capture
            S.capture = None
            return lst

        for t_ in range(2):
            DMA('sp', xt[t_], x_d[t_ * 128:(t_ + 1) * 128, :], writes=[xt[t_]])
        for tt in range(17):
            streams = []
            if tt >= 1:
                streams.append(cap0(p0_stage2, tt - 1))
            if tt < 16:
                streams.append(cap0(p0_stage1, tt))
            S.merge_streams(streams)

        jobs = []
        jobs_seq = [('R', hp) for hp in range(n_hp_r)] + [('M', hp) for hp in range(n_hp_m)]
        for hp in range(n_hp_r):
            for nm in ('r', 'k', 'v', 'za'):
                jobs.append(('R', hp, nm, COL[nm] + hp * 128))
            if hp == 0:
                jobs.append(('R', 0, 'wd', COL['wd']))
        for hp in range(n_hp_m):
            for nm in ('q', 'kq', 'vq', 'zb'):
                jobs.append(('M', hp, nm, COL[nm] + hp * 128))
        PREF = NW - 1
        wq = []
        jstate = {'issued': 0}

        def next_w():
            while jstate['issued'] < len(jobs) and len(wq) < PREF + 1:
                wq.append(wload_in(jobs[jstate['issued']][3]))
                jstate['issued'] += 1
            return wq.pop(0)

        def ev_copy(dst, eng):
            def f(acc, half):
                CP(eng, dst[:, half * 1024:(half + 1) * 1024], acc)
            return f

        def ev_silu(dst):
            def f(acc, half):
                ACT(dst[:, half * 1024:(half + 1) * 1024], acc, AF.Silu)
            return f

        def inproj_job(ji):
            kind, hp = jobs_seq[ji]
            R0, R1, R2, R3 = RAW[ji % 2]
            inproj(next_w(), KC, hT, ev_copy(R0, 'act'))
            inproj(next_w(), KC, hT, ev_copy(R1, 'dve'))
            inproj(next_w(), KC, hT, ev_copy(R2, 'act'))
            inproj(next_w(), KC, hT, ev_silu(R3))

        def lora_prep():
            AR_.reset()
            lraw = AR_.a(2048)
            ld = AR_.a(2048)
            inproj(next_w(), KC, hT, ev_copy(lraw, 'dve'))
            TT('dve', ld[:, 1:2048], lraw[:, 0:2047], lraw[:, 1:2048], ALU.subtract)
            TS('dve', ld[:, 0:1], lraw[:, 0:1], -1.0, None, ALU.mult)
            STT('dve', TL[:], ld, ppc('mu_wa'), lraw, ALU.mult, ALU.add)
            ACT(TL[0:64, :], TL[0:64, :], AF.Tanh)

        def split_parts(lst, fracs):
            tot = float(sum(fracs))
            out = []
            acc = 0.0
            i0 = 0
            for f in fracs:
                acc += f
                i1 = int(round(len(lst) * acc / tot))
                out.append(lst[i0:i1])
                i0 = i1
            out[-1] = out[-1] + lst[i0:]
            return out

        def cap(fn, *args):
            S.capture = []
            fn(*args)
            lst = S.capture
            S.capture = None
            return lst

        def rwkv_all():
            AR_.reset()
            A = AR_.a
            GT = 256
            NG = S_TOK // GT
            Sf = [A(128, F32) for _ in range(2)]
            Sbf = A(128)
            Ssc = A(128, F32)
            Stmp = A(128, F32)
            Zsb = A(128)
            Usb = A(128)
            tinyb_t = A(2, F32)
            YAg = [A(GT) for _ in range(2)]
            ARfs = [A(2 * GT, F32) for _ in range(2)]
            KTfs = [A(GT, F32) for _ in range(2)]
            BTfs = [A(GT, F32) for _ in range(2)]
            KTbs = [A(GT) for _ in range(2)]
            BTbs = [A(GT) for _ in range(2)]
            vgs = [A(GT) for _ in range(2)]
            WCs = [A(4, F32) for _ in range(4)]
            BONs = [A(GT) for _ in range(4)]
            ARb = [A(2 * GT) for _ in range(2)]
            TM = [A(3 * GT) for _ in range(2)]
            AKs = [A(2 * GT) for _ in range(2)]
            RKs = [A(2 * GT) for _ in range(2)]
            RBs = [A(2 * GT) for _ in range(2)]
            TTs = [A(2 * GT) for _ in range(2)]
            Ytms = [A(GT, F32) for _ in range(2)]
            mean = A(4, F32); var = A(4, F32)
            YC = A(GT, F32); YQ = A(GT, F32); YN = A(GT)
            Dm = A(GT, F32); SQ = A(GT); PROD = A(GT)
            omu = A(4, F32)
            rg = A(GT, F32); kg = A(GT, F32); SG = A(GT, F32); CS = A(GT, F32); ag = A(GT, F32)
            E1 = A(GT, F32); E2 = A(GT, F32); E3 = A(GT, F32)
            KK = A(GT, F32); RI = A(GT, F32); Tt = A(GT, F32); K2 = A(GT, F32); Bb = A(GT, F32)
            N0s = A(2 * GT); X0s = A(2 * GT)
            Ns = [A(2 * GT) for _ in range(2)]
            Xs = [A(2 * GT) for _ in range(2)]
            Ps = [A(2 * GT) for _ in range(2)]
            MEMSET('pool', tinyb_t, 1e-12)
            tinyb = tinyb_t[:, 0:1]
            NT = GT // 128
            NM = NT * 2
            PA2 = PSA[:, 1024:1536]
            PA3 = PSA[:, 1536:2048]

            def v3(ap):
                return ap.rearrange("p (j t) -> p j t", t=128)

            def emit_A(hp, g):
                Rr, Rk, Rv, SZ = RAW[hp % 2]
                if g == 0:
                    for mi, mu in enumerate(('mu_r', 'mu_k', 'mu_v')):
                        TS('dve', omu[:, mi:mi + 1], ppc(mu, hp), -1.0, 1.0, ALU.mult, ALU.add)
                c0 = g * GT
                par = g % 2
                ARf = ARfs[par]; KTf = KTfs[par]; BTf = BTfs[par]; vg = vgs[par]
                ARfv = ARf.rearrange("p (j w t) -> p j w t", w=2, t=128)
                for mi, (raw, mu, dst) in enumerate(((Rr, 'mu_r', rg), (Rk, 'mu_k', kg), (Rv, 'mu_v', vg))):
                    if g == 0:
                        ACT(Dm[:, 1:GT], raw[:, 0:GT - 1], AF.Identity, scale=ppc(mu, hp))
                        TS('dve', Dm[:, 0:1], raw[:, 0:1], 0.0, None, ALU.mult)
                    else:
                        ACT(Dm, raw[:, c0 - 1:c0 + GT - 1], AF.Identity, scale=ppc(mu, hp))
                    STT('dve', dst, raw[:, c0:c0 + GT], omu[:, mi:mi + 1], Dm, ALU.mult, ALU.add)
                pu = PA2[:, 0:GT]
                pa_ = PA2[:, GT:2 * GT]
                MM([(pu, loraD[:, hp * 128:(hp + 1) * 128], TL[:, c0:c0 + GT], True, True)],
                   reads=[loraD[:, hp * 128:(hp + 1) * 128], TL[:, c0:c0 + GT]], writes=[pu])
                MM([(pa_, loraI[:, hp * 128:(hp + 1) * 128], TL[:, c0:c0 + GT], True, True)],
                   reads=[loraI[:, hp * 128:(hp + 1) * 128], TL[:, c0:c0 + GT]], writes=[pa_])
                ACT(SG, pu, AF.Sigmoid, bias=ppc('w0', hp))
                ACT(ag, pa_, AF.Sigmoid, bias=ppc('a0', hp))
                S.add('dve', (lambda o, m, d1: (lambda e: e.tensor_tensor_scan(out=o, data0=m, data1=d1, initial=0.0,
                                                                                op0=ALU.mult, op1=ALU.add)))(CS, scanmask[:, 0:GT], SG),
                      reads=[scanmask[:, 0:GT], SG], writes=[CS], cost=0.1 + 2 * GT / 960.0)
                ACT(E1, CS, AF.Exp, scale=-CDEC)
                ACT(WCs[g % 4], CS.rearrange("p (c t) -> p c t", t=64)[:, :, 63], AF.Exp, scale=-CDEC)
                ACT(E3, CS, AF.Exp, scale=CDEC)
                TT('dve', SG, CS, SG, ALU.subtract)
                ACT(E2, SG, AF.Exp, scale=-CDEC)
                ACT(KK, kg, AF.Identity, scale=ppc('k_k', hp))
                ACT(SQ, kg, AF.Square, scale=ppc('k_k', hp))
                pk = PA2[:, 0:GT]
                MM([(pk, blk2, SQ, True, True)], reads=[blk2, SQ], writes=[pk])
                ACT(RI, pk, AF.Ln, bias=tinyb)
                ACT(RI, RI, AF.Exp, scale=-0.5)
                TT('dve', KK, KK, RI, ALU.mult)
                TS('dve', Tt, ag, -1.0, ppc('k_a', hp), ALU.add, ALU.mult)
                STT('dve', K2, Tt, 1.0, kg, ALU.add, ALU.mult)
                TT('dve', Bb, KK, ag, ALU.mult)
                STT('dve', PROD, rg, ppc('r_k', hp), K2, ALU.mult, ALU.mult)
                pb_ = PA2[:, GT:2 * GT]
                MM([(pb_, blk2, PROD, True, True)], reads=[blk2, PROD], writes=[pb_])
                TT('dve', BONs[g % 4], pb_, vg, ALU.mult)
                TT('dve', ARfv[:, :, 1, :], v3(rg), v3(E1), ALU.mult)
                STT('dve', ARfv[:, :, 0, :], v3(KK), -1.0, v3(E2), ALU.mult, ALU.mult)
                TT('dve', KTf, K2, E3, ALU.mult)
                TT('dve', BTf, Bb, E3, ALU.mult)
                CP('act', KTbs[par], KTf)
                CP('act', BTbs[par], BTf)

            def emit_B(hp, g):
                par = g % 2
                ARf = ARfs[par]; KTf = KTfs[par]; BTf = BTfs[par]; vg = vgs[par]
                KTb = KTbs[par]; BTb = BTbs[par]
                CP('act', ARb[par], ARf)
                ptm = PA3.bitcast(BF16)[:, 0:3 * GT]
                items = []
                for q, src in enumerate((vg, KTb, BTb)):
                    for j in range(NT):
                        items.append((ptm[:, (q * NT + j) * 128:(q * NT + j + 1) * 128], src[:, j * 128:(j + 1) * 128], ident))
                TRS(items, reads=[vg, KTb, BTb, ident], writes=[ptm])
                CP('act', TM[par], ptm)
                for j in range(NT):
                    items = []
                    for h in range(2):
                        hr = slice(h * 64, h * 64 + 64)
                        rhs_ar = ARf[hr, j * 256:(j + 1) * 256]
                        bh = psb_f32(h)
                        items.append((bh[:, 0:256], KTf[hr, j * 128:(j + 1) * 128], rhs_ar, True, True))
                        items.append((bh[:, 256:512], BTf[hr, j * 128:(j + 1) * 128], rhs_ar, True, True))
                    MM(items, reads=[KTf, BTf, ARf], writes=[PSB[:, 0:1024]])
                    bxy = PSB[:, 0:1024].rearrange("p (h w t) -> p h w t", h=2, w=4)
                    msu_b = m_su.unsqueeze(1).to_broadcast([128, 2, 128])
                    miu_b = m_iu.unsqueeze(1).to_broadcast([128, 2, 128])
                    msl_b = m_sl.unsqueeze(1).to_broadcast([128, 2, 128])

                    def dst(t):
                        return t.rearrange("p (j h t) -> p j h t", h=2, t=128)[:, j, :, :]
                    TT('dve', dst(AKs[par]), bxy[:, :, 0, :], msu_b, ALU.mult)
                    TT('dve', dst(RKs[par]), bxy[:, :, 1, :], miu_b, ALU.mult)
                    TT('dve', dst(N0s), bxy[:, :, 2, :], msu_b, ALU.mult)
                    TT('dve', dst(RBs[par]), bxy[:, :, 3, :], miu_b, ALU.mult)

                pxt = PA3.bitcast(BF16)[:, 0:NM * 128]
                TRS([(pxt[:, m * 128:(m + 1) * 128], N0s[:, m * 128:(m + 1) * 128], ident) for m in range(NM)],
                    reads=[N0s, ident], writes=[pxt])
                CP('act', X0s, pxt)

                def m8(t):
                    return t.rearrange("p (m t) -> p m t", t=128)
                W = NM * 128
                TT('dve', m8(Ps[0]), m8(N0s), ident.unsqueeze(1).to_broadcast([128, NM, 128]), ALU.add)
                Ncur, Xcur, Pcur = N0s, X0s, Ps[0]
                for lvl in range(1, 6):
                    pX = PSB[:, 0:W]
                    MM([(pX[:, m * 128:(m + 1) * 128], Ncur[:, m * 128:(m + 1) * 128], Xcur[:, m * 128:(m + 1) * 128], True, True)
                        for m in range(NM)], reads=[Ncur, Xcur], writes=[pX])
                    Xn = Xs[lvl % 2]
                    if lvl < 5:
                        pN = PSB[:, 512:512 + W]
                        MM([(pN[:, m * 128:(m + 1) * 128], Xcur[:, m * 128:(m + 1) * 128], Ncur[:, m * 128:(m + 1) * 128], True, True)
                            for m in range(NM)], reads=[Ncur, Xcur], writes=[pN])
                    CP('act', Xn, pX)
                    if lvl < 5:
                        Nn = Ns[lvl % 2]
                        CP('act', Nn, pN)
                    pP = PA3[:, 0:W]
                    pitems = []
                    for m in range(NM):
                        pitems.append((pP[:, m * 128:(m + 1) * 128], Xn[:, m * 128:(m + 1) * 128], Pcur[:, m * 128:(m + 1) * 128], m == 0, False))
                        pitems.append((pP[:, m * 128:(m + 1) * 128], ident, Pcur[:, m * 128:(m + 1) * 128], False, m == NM - 1))
                    MM(pitems, reads=[Xn, Pcur, ident], writes=[pP])
                    Pn = TTs[par] if lvl == 5 else Ps[lvl % 2]
                    CP('act', Pn, pP)
                    Pcur = Pn
                    Xcur = Xn
                    if lvl < 5:
                        Ncur = Nn

            def emit_chain(hp, g):
                par = g % 2
                if g == 0:
                    TS('dve', Sf[0], Sf[0], 0.0, None, ALU.mult)
                    TS('dve', Sbf, Sbf, 0.0, None, ALU.mult)
                ARbv = ARb[par].rearrange("p (j w t) -> p j w t", w=2, t=128)
                TMv = TM[par].rearrange("p (q j f) -> p q j f", q=3, f=128)
                TTg = TTs[par].rearrange("p (j h t) -> p j h t", h=2, t=128)
                AKv = AKs[par].rearrange("p (j h t) -> p j h t", h=2, t=128)
                RKv = RKs[par].rearrange("p (j h t) -> p j h t", h=2, t=128)
                RBv = RBs[par].rearrange("p (j h t) -> p j h t", h=2, t=128)
                Ytv = Ytms[par].rearrange("p (j f) -> p j f", f=128)
                for cl in range(2 * NT):
                    c = g * 2 * NT + cl
                    j = cl // 2
                    pr = slice((cl % 2) * 64, (cl % 2) * 64 + 64)
                    Scur = Sf[c % 2]
                    Snxt = Sf[(c + 1) % 2]
                    wc = WCs[g % 4][:, cl:cl + 1]
                    Zp = PSB[:, 1536:1664]
                    Up = PSB[:, 1664:1792]
                    Yp = PSB[:, 1792:1920]
                    Sp = PSB[:, 1920:2048]
                    ACT(Ssc, Scur, AF.Identity, scale=wc)
                    MM([(Zp[:, 0:64], AKv[pr, j, 0, :], TMv[pr, 0, j, 0:64], True, False),
                        (Zp[:, 64:128], AKv[pr, j, 1, :], TMv[pr, 0, j, 64:128], False, False),
                        (Zp[:, 0:128], ARbv[:, j, 0, :], Sbf, False, True)],
                       reads=[AKs[par], TM[par], ARb[par], Sbf], writes=[Zp])
                    CP('act', Zsb[pr, :], Zp[pr, :])
                    MM([(Up[:, 0:64], TTg[pr, j, 0, :], Zsb[pr, 0:64], True, False),
                        (Up[:, 64:128], TTg[pr, j, 1, :], Zsb[pr, 64:128], False, True)],
                       reads=[TTs[par], Zsb[pr, :]], writes=[Up])
                    CP('dve', Usb[pr, :], Up[pr, :])
                    MM([(Yp[:, 0:128], ARbv[:, j, 1, :], Sbf, True, False),
                        (Yp[:, 0:64], RBv[pr, j, 0, :], Usb[pr, 0:64], False, False),
                        (Yp[:, 0:64], RKv[pr, j, 0, :], TMv[pr, 0, j, 0:64], False, False),
                        (Yp[:, 64:128], RBv[pr, j, 1, :], Usb[pr, 64:128], False, False),
                        (Yp[:, 64:128], RKv[pr, j, 1, :], TMv[pr, 0, j, 64:128], False, True)],
                       reads=[ARb[par], Sbf, RBs[par], RKs[par], Usb[pr, :], TM[par]], writes=[Yp])
                    CP('act', Ytv[pr, j, :], Yp[pr, :])
                    MM([(Sp, TMv[pr, 2, j, :], Usb[pr, :], True, False),
                        (Sp, TMv[pr, 1, j, :], TMv[pr, 0, j, :], False, True)],
                       reads=[TM[par], Usb[pr, :]], writes=[Sp])
                    TT('dve', Stmp, Sp, blk2, ALU.mult)
                    STT('dve', Sbf, Stmp, wc, Ssc, ALU.mult, ALU.add)
                    STT('dve', Snxt, Stmp, wc, Ssc, ALU.mult, ALU.add)

            def emit_post(hp, g):
                Rr, Rk, Rv, SZ = RAW[hp % 2]
                par = g % 2
                c0 = g * GT
                Ytm = Ytms[par]
                Y4 = Ytm.rearrange("p (m f) -> p m f", f=64)
                YC4 = YC.rearrange("p (m f) -> p m f", f=64)
                YQ4 = YQ.rearrange("p (m f) -> p m f", f=64)
                nm_ = 2 * NT
                S.add('dve', (lambda o, i: (lambda e: e.tensor_reduce(out=o, in_=i, axis=AX.X, op=ALU.add)))(mean, Y4),
                      reads=[Ytm], writes=[mean])
                TS('dve', mean, mean, 1.0 / 64, None, ALU.mult)
                TT('dve', YC4, Y4, mean.unsqueeze(2).to_broadcast([128, nm_, 64]), ALU.subtract)
                ACT(YQ, YC, AF.Square)
                S.add('dve', (lambda o, i: (lambda e: e.tensor_reduce(out=o, in_=i, axis=AX.X, op=ALU.add)))(var, YQ4),
                      reads=[YQ], writes=[var])
                TS('dve', var, var, 1.0 / 64, 64e-5, ALU.mult, ALU.add)
                ACT(var, var, AF.Ln)
                ACT(var, var, AF.Exp, scale=-0.5)
                TT('dve', YN.rearrange("p (m f) -> p m f", f=64), YC4, var.unsqueeze(2).to_broadcast([128, nm_, 64]), ALU.mult)
                pyt = psb_bf(2, GT)
                TRS([(pyt[:, jj * 128:(jj + 1) * 128], YN[:, jj * 128:(jj + 1) * 128], ident) for jj in range(NT)],
                    reads=[YN, ident], writes=[pyt])
                YF = YC
                ACT(YF, pyt, AF.Identity, bias=ppc('gn_b', hp), scale=ppc('gn_w', hp))
                TT('dve', YF, YF, BONs[g % 4], ALU.add)
                TT('dve', YAg[par], YF, SZ[:, c0:c0 + GT], ALU.mult)
                DMA('sp', ya_d[hp, :, c0:c0 + GT], YAg[par], reads=[YAg[par]], writes=[('DR', 'ya', hp, hp + 1, 0, 1, False)])

            MEMSET('pool', Sf[0], 0.0)
            MEMSET('pool', Sbf, 0.0)
            NGT = n_hp_r * NG
            nx = [split_parts(nxt_stream(hp), [1.0] * NG) for hp in range(n_hp_r)]

            def hg(G):
                return (G // NG, G % NG)
            for it in range(-2, NGT + 1):
                streams = []
                if 0 <= it < NGT:
                    streams.append(cap(emit_chain, *hg(it)))
                if 0 <= it + 1 < NGT:
                    streams.append(cap(emit_B, *hg(it + 1)))
                if 0 <= it - 1 < NGT:
                    streams.append(cap(emit_post, *hg(it - 1)))
                if 0 <= it + 2 < NGT:
                    streams.append(cap(emit_A, *hg(it + 2)))
                if 0 <= it < NGT:
                    streams.append(nx[it // NG][it % NG])
                S.merge_streams(streams)

        def nxt_stream(ji):
            return cap(inproj_job, ji + 1) if ji + 1 < len(jobs_seq) else []

        if jobs_seq:
            inproj_job(0)
        if n_hp_r > 0:
            lora_prep()
        if n_hp_r > 0:
            rwkv_all()

        AR_.reset()
        if n_hp_m > 0:
            QA = AR_.a(2048); QB = AR_.a(2048); KA = AR_.a(2048); KB = AR_.a(2048)
            VA = AR_.a(2048); VB = AR_.a(2048)
            for t in (QA, QB, KA, KB):
                MEMSET('pool', t, 0.0)
            MEMSET('pool', VA, 1.0)
            MEMSET('pool', VB, 1.0)
            DMA('sp', KA[64:72, :], ind_d, writes=[KA[64:72, :]])
            DMA('sp', KB[0:8, :], ind_d, writes=[KB[0:8, :]])
        m_base = AR_.off

        def moba_headpair(hp, ji, nxt):
            Rq, Rkq, Rvq, SZ = RAW[ji % 2]
            AR_.off = m_base
            A = AR_.a
            nparts = split_parts(nxt, [2.0, 1.0, 2.0, 3.0, 4.0])
            Qf = A(2048, F32); Kf = A(2048, F32)

            def norm_stream(raw, wname, dA, dB, Ff, SQm, RIm, bank):
                for g in range(4):
                    c0 = g * 512
                    ACT(SQm, raw[:, c0:c0 + 512], AF.Square)
                    pk = psb_f32(bank)
                    MM([(pk, blk2, SQm, True, True)], reads=[blk2, SQm], writes=[pk])
                    ACT(RIm, pk, AF.Ln, bias=ppc_eps, scale=1.0 / 64)
                    ACT(RIm, RIm, AF.Exp, scale=-0.5)
                    STT('dve', Ff[:, c0:c0 + 512], raw[:, c0:c0 + 512], ppc(wname), RIm, ALU.mult, ALU.mult)
                    CP('act', dA[0:64, c0:c0 + 512], Ff[0:64, c0:c0 + 512])
                    CP('dve', dB[64:128, c0:c0 + 512], Ff[64:128, c0:c0 + 512])

            def v_stream():
                pv = psb2_bf(2)
                TRS([(pv[:, t * 128:(t + 1) * 128], Rvq[:, t * 128:(t + 1) * 128], ident) for t in range(16)],
                    reads=[Rvq, ident], writes=[pv])
                pv3 = pv.rearrange("p (t f) -> p t f", f=128)
                CP('act', VA.rearrange("p (t f) -> p t f", f=128)[:, :, 0:64], pv3[:, :, 0:64])
                CP('dve', VB.rearrange("p (t f) -> p t f", f=128)[:, :, 64:128], pv3[:, :, 64:128])

            SQq = A(512); RIq = A(512, F32); SQk = A(512); RIk = A(512, F32)
            st_q = cap(norm_stream, Rq, 'qnw', QA, QB, Qf, SQq, RIq, 0)
            st_k = cap(norm_stream, Rkq, 'knw', KA, KB, Kf, SQk, RIk, 1)
            st_v = cap(v_stream)
            S.merge_streams([st_k, st_q, st_v, nparts[0]])
            nparts[0] = []
            S.capture = []
            kmp = A(16, F32)
            MEMSET('pool', kmp, 0.0)
            S.add('dve', (lambda o, i: (lambda e: e.tensor_reduce(out=o, in_=i, axis=AX.X, op=ALU.add)))(
                kmp[0:64, 0:8], Kf[0:64, :].rearrange("p (n t) -> p n t", t=256)), reads=[Kf[0:64, :]], writes=[kmp[0:64, 0:8]])
            S.add('dve', (lambda o, i: (lambda e: e.tensor_reduce(out=o, in_=i, axis=AX.X, op=ALU.add)))(
                kmp[64:128, 8:16], Kf[64:128, :].rearrange("p (n t) -> p n t", t=256)), reads=[Kf[64:128, :]], writes=[kmp[64:128, 8:16]])
            pg = psb_f32(0, 256)
            MM([(pg[:, qt * 16:(qt + 1) * 16], Qf[:, qt * 128:(qt + 1) * 128], kmp, True, True) for qt in range(16)],
               reads=[Qf, kmp], writes=[pg])
            GM = A(256, F32); G2 = A(256, F32); EQ = A(256, F32); mx = A(32, F32); BI = A(256)
            g3 = lambda t: t.rearrange("p (m n) -> p m n", n=8)
            mxb = mx.unsqueeze(2).to_broadcast([128, 32, 8])

            def rmax(o, i):
                S.add('dve', (lambda o_, i_: (lambda e: e.tensor_reduce(out=o_, in_=i_, axis=AX.X, op=ALU.max)))(o, g3(i)),
                      reads=[i], writes=[o])
            TT('dve', GM, pg, pastneg, ALU.add)
            rmax(mx, GM)
            TT('dve', g3(EQ), g3(GM), mxb, ALU.is_ge)
            STT('dve', G2, EQ, NEG, GM, ALU.mult, ALU.add)
            rmax(mx, G2)
            TT('dve', g3(EQ), g3(G2), mxb, ALU.is_ge)
            STT('dve', G2, EQ, NEG, G2, ALU.mult, ALU.add)
            rmax(mx, G2)
            TT('dve', g3(EQ), g3(GM), mxb, ALU.is_ge)
            TT('dve', EQ, EQ, ownpos, ALU.max)
            TS('dve', BI, EQ, -1.0, -NEG, ALU.add, ALU.mult)
            pbt = psb_bf(1, 1024)
            pbt2 = psb_bf(2, 1024)
            TRS([((pbt if qt < 8 else pbt2)[0:16, (qt % 8) * 128:(qt % 8 + 1) * 128], BI[:, qt * 16:(qt + 1) * 16], ident)
                 for qt in range(16)], reads=[BI, ident], writes=[pbt, pbt2])
            BT_ = A(2048)
            CP('act', BT_[0:16, 0:1024], pbt[0:16, :])
            CP('act', BT_[0:16, 1024:2048], pbt2[0:16, :])
            DMA('sp', QA[64:72, :], BT_[0:8, :], reads=[BT_[0:8, :]], writes=[QA[64:72, :]])
            DMA('sp', QB[0:8, :], BT_[8:16, :], reads=[BT_[8:16, :]], writes=[QB[0:8, :]])
            pro = S.capture
            S.capture = None
            S.merge_streams([pro, nparts[0]])
            PT = [A(512) for _ in range(3)]
            RS = A(512, F32)
            RW = A(512, F32)
            YO = A(512, F32)
            YB = A(2048)
            VA3 = VA.rearrange("p (t f) -> p t f", f=128)
            VB3 = VB.rearrange("p (t f) -> p t f", f=128)
            OpA = psb_f32(2)
            OpB = psb_f32(3)
            for QT in range(4):
                S.capture = []
                q0 = QT * 512
                units = [(h, kt) for kt in range(4 * QT + 4) for h in range(2)]
                nkt = 4 * QT + 4

                def qk(ui):
                    h, kt = units[ui]
                    Kh = KA if h == 0 else KB
                    Qh = QA if h == 0 else QB
                    sp_ = psb_f32(ui % 2)
                    diag = kt >= 4 * QT
                    items = [(sp_, Kh[:, kt * 128:(kt + 1) * 128], Qh[:, q0:q0 + 512], True, not diag)]
                    rds = [Kh[:, kt * 128:(kt + 1) * 128], Qh[:, q0:q0 + 512]]
                    if diag:
                        items.append((sp_, ident, cmask[:, kt - 4 * QT, :], False, True))
                        rds += [ident, cmask[:, kt - 4 * QT, :]]
                    MM(items, reads=rds, writes=[sp_])
                qk(0)
                for ui, (h, kt) in enumerate(units):
                    if ui + 1 < len(units):
                        qk(ui + 1)
                    sp_ = psb_f32(ui % 2)
                    pt = PT[ui % 3]
                    ACT(pt, sp_, AF.Exp, scale=0.125)
                    Vh = VA3 if h == 0 else VB3
                    Oh = OpA if h == 0 else OpB
                    MM([(Oh, Vh[:, kt, :], pt, kt == 0, kt == nkt - 1)], reads=[Vh[:, kt, :], pt], writes=[Oh])
                ACT(RS[64:128, :], OpA[64:128, :], AF.Ln)
                ACT(RS[0:64, :], OpB[0:64, :], AF.Ln)
                ACT(RS, RS, AF.Exp, scale=-1.0)
                pw = psb_f32(0)
                MM([(pw, swapP, RS, True, True)], reads=[swapP, RS], writes=[pw])
                CP('act', RW, pw)
                TT('dve', YO[0:64, :], OpA[0:64, :], RW[0:64, :], ALU.mult)
                TT('dve', YO[64:128, :], OpB[64:128, :], RW[64:128, :], ALU.mult)
                TT('dve', YB[:, q0:q0 + 512], YO, SZ[:, q0:q0 + 512], ALU.mult)
                att = S.capture
                S.capture = None
                S.merge_streams([att, nparts[QT + 1]])
            out_dmas.append(DMA('sp', yb_d[hp], YB, reads=[YB], writes=[('DR', 'yb', hp, hp + 1, 0, 1, False)]))

        if n_hp_m > 0:
            epsb = AR_.a(2, F32)
            MEMSET('pool', epsb, 1e-6)
            ppc_eps = epsb[:, 0:1]
            m_base = AR_.off
        for hp in range(n_hp_m):
            moba_headpair(hp, n_hp_r + hp, nxt_stream(n_hp_r + hp))

        if do_final:
            for th in range(2):
                AR_.reset()
                t0 = th * 1024
                YAh = RAWT[0][:, :]; YBh = RAWT[1][:, :]
                YA3 = YAh.rearrange("p (k t) -> p k t", t=1024)
                YB3 = YBh.rearrange("p (k t) -> p k t", t=1024)
                for k in range(8):
                    DMA('sp', YA3[:, k, :], ya_d[k, :, t0:t0 + 1024], reads=[('DR', 'ya', k, k + 1, 0, 1, False)], writes=[YA3[:, k, :]])
                    DMA('sp', YB3[:, k, :], yb_d[k, :, t0:t0 + 1024], reads=[('DR', 'yb', k, k + 1, 0, 1, False)], writes=[YB3[:, k, :]])
                MG = AR_.a(16 * 1024)
                MG3 = MG.rearrange("p (k t) -> p k t", t=1024)
                sga = AR_.a(1024); sgb = AR_.a(1024); m1 = AR_.a(1024, F32)
                fj = []
                for c in range(16):
                    fj.append(('in', COL['ga'] + c * 128))
                    fj.append(('pa', c * 128))
                    fj.append(('in', COL['gb'] + c * 128))
                    fj.append(('pb', c * 128))
                fq = []
                fst = {'i': 0}

                def next_fw():
                    while fst['i'] < len(fj) and len(fq) < NW:
                        kind, col = fj[fst['i']]
                        fst['i'] += 1
                        if kind == 'in':
                            fq.append(wload_in(col))
                        elif kind == 'pa':
                            fq.append(wload_proj(wpa_d, col))
                        else:
                            fq.append(wload_proj(wpb_d, col))
                    return fq.pop(0)

                def run_half_proj(wbuf, nk, src3, evac):
                    acc = PSA[:, (run_half_proj.n % 2) * 1024:(run_half_proj.n % 2 + 1) * 1024]
                    run_half_proj.n += 1
                    items = []
                    for k in range(nk):
                        for tt in range(2):
                            items.append((acc[:, tt * 512:(tt + 1) * 512], wbuf[:, k, :], src3(k, tt), k == 0, k == nk - 1))
                    MM(items, reads=[wbuf[:, 0:nk, :]] + run_half_proj.rd, writes=[acc])
                    evac(acc)
                run_half_proj.n = 0
                for c in range(16):
                    run_half_proj.rd = [hT[:, :, t0:t0 + 1024]]
                    run_half_proj(next_fw(), KC, lambda k, tt: hT[:, k, t0 + tt * 512:t0 + (tt + 1) * 512],
                                  lambda acc: ACT(sga, acc, AF.Sigmoid))
                    run_half_proj.rd = [YAh]
                    run_half_proj(next_fw(), 8, lambda k, tt: YA3[:, k, tt * 512:(tt + 1) * 512],
                                  lambda acc: TT('dve', m1, acc, sga, ALU.mult))
                    run_half_proj.rd = [hT[:, :, t0:t0 + 1024]]
                    run_half_proj(next_fw(), KC, lambda k, tt: hT[:, k, t0 + tt * 512:t0 + (tt + 1) * 512],
                                  lambda acc: ACT(sgb, acc, AF.Sigmoid))
                    run_half_proj.rd = [YBh]

                    def ev_b(acc, c=c):
                        TT('dve', sgb, acc, sgb, ALU.mult)
                        TT('dve', MG3[:, c, :], m1, sgb, ALU.add)
                    run_half_proj(next_fw(), 8, lambda k, tt: YB3[:, k, tt * 512:(tt + 1) * 512], ev_b)
                WO = [AR_.a(16 * 512) for _ in range(2)]
                XR = [AR_.a(512, F32) for _ in range(1)]
                OT = [AR_.a(512, F32) for _ in range(1)]
                n_o = 0
                for c4 in range(4):
                    wo = WO[c4 % 2]
                    wo3 = wo.rearrange("p (k m) -> p k m", m=512)
                    DMA('pool', wo3, wo_d[:, c4 * 512:(c4 + 1) * 512].rearrange("(kc p) m -> p kc m", p=128), writes=[wo])
                    for tl in range(8):
                        tok0 = t0 + tl * 128
                        xr = XR[0]
                        ot = OT[0]
                        acc = PSB[:, (n_o % 4) * 512:(n_o % 4 + 1) * 512]
                        n_o += 1
                        DMA('sp', xr, x_d[tok0:tok0 + 128, c4 * 512:(c4 + 1) * 512], writes=[xr])
                        MM([(acc, MG3[:, k, tl * 128:(tl + 1) * 128], wo3[:, k, :], k == 0, k == 15) for k in range(16)],
                           reads=[MG, wo], writes=[acc])
                        TT('dve', ot, acc, xr, ALU.add)
                        out_dmas.append(DMA('sp', out_d[tok0:tok0 + 128, c4 * 512:(c4 + 1) * 512], ot, reads=[ot]))

        cnt = S.emit(final_wait_ops=out_dmas)
    return nc, cnt, len(S.ops)


def _consts():
    bf = ml_dtypes.bfloat16
    cbv = np.zeros((128, NCB), np.float32)
    idx = np.arange(128)
    cbv[:, CB['ident']:CB['ident'] + 128] = np.eye(128)
    same = (idx[:, None] // 64) == (idx[None, :] // 64)
    cbv[:, CB['blk2']:CB['blk2'] + 128] = same
    cbv[:, CB['m_su']:CB['m_su'] + 128] = same & (idx[:, None] < idx[None, :])
    cbv[:, CB['m_iu']:CB['m_iu'] + 128] = same & (idx[:, None] <= idx[None, :])
    cbv[:, CB['m_sl']:CB['m_sl'] + 128] = same & (idx[:, None] > idx[None, :])
    cbv[:, CB['onesA']:CB['onesA'] + 64] = 1.0
    cbv[:, CB['onesB'] + 64:CB['onesB'] + 128] = 1.0
    cm = np.zeros((128, 4, 512), np.float32)
    for ktl in range(4):
        kpos = ktl * 128 + idx[:, None]
        qpos = np.arange(512)[None, :]
        kb = kpos // 256
        qb = qpos // 256
        ok = np.where(kb == qb, kpos <= qpos, kb < qb)
        cm[:, ktl, :] = np.where(ok, 0.0, NEG)
    cbv[:, CB['cmask']:CB['cmask'] + 2048] = cm.reshape(128, 2048)
    cfv = np.zeros((128, NCF), np.float32)
    sm = np.ones(512, np.float32)
    sm[::64] = 0.0
    cfv[:, CF['scanmask']:CF['scanmask'] + 512] = sm[None, :]
    pn = np.zeros((16, 2, 8), np.float32)
    op = np.zeros((16, 2, 8), np.float32)
    for qt in range(16):
        qb = qt // 2
        for n in range(8):
            pn[qt, :, n] = 0.0 if n < qb else NEG
            op[qt, :, n] = 1.0 if n >= qb else 0.0
    cfv[:, CF['pastneg']:CF['pastneg'] + 256] = pn.reshape(1, 256)
    cfv[:, CF['ownpos']:CF['ownpos'] + 256] = op.reshape(1, 256)
    cfv[:, CF['swapP']:CF['swapP'] + 128] = np.roll(np.eye(128, dtype=np.float32), 64, axis=1)
    ind = np.zeros((8, S_TOK), np.float32)
    for n in range(8):
        ind[n, n * 256:(n + 1) * 256] = 1.0
    return cbv.astype(bf), cfv, ind.astype(bf)


def _pack_params(i):
    ppv = np.zeros((128, NPP), np.float32)

    def fm(v):
        return np.ascontiguousarray(v.reshape(-1, 128).T)
    for nm in ('mu_r', 'mu_k', 'mu_v', 'w0', 'a0', 'k_k', 'k_a', 'r_k', 'gn_w', 'gn_b'):
        ppv[:, PP[nm]:PP[nm] + 8] = fm(i[nm][0])
    ppv[:, PP['norm_w']:PP['norm_w'] + 16] = fm(i['norm_w'][0])
    ppv[0:64, PP['mu_wa']] = i['mu_w'][0]
    ppv[64:128, PP['mu_wa']] = i['mu_a'][0]
    ppv[0:64, PP['qnw']] = i['q_norm_w'][0]
    ppv[64:128, PP['qnw']] = i['q_norm_w'][0]
    ppv[0:64, PP['knw']] = i['k_norm_w'][0]
    ppv[64:128, PP['knw']] = i['k_norm_w'][0]
    lora = np.concatenate([i['w_decay_up'][0], i['w_iclr_up'][0]], axis=0).astype(np.float32)
    return ppv, np.ascontiguousarray(lora)


_CACHE = {}


def make_in_maps(inputs, n_cores=8):
    i = {k: np.asarray(v) for k, v in inputs.items()}
    cbv, cfv, ind = _consts()
    ppv, lora = _pack_params(i)
    shared = dict(w_in=np.ascontiguousarray(i['w_in'][0]), w_pa=np.ascontiguousarray(i['w_proj_rwkv'][0]),
                  w_pb=np.ascontiguousarray(i['w_proj_moba'][0]), w_out=np.ascontiguousarray(i['w_out'][0]),
                  lora=lora, pp=ppv, cb=cbv, cf=cfv, ind=ind)
    maps = []
    for c in range(n_cores):
        m = dict(shared)
        m['x'] = np.ascontiguousarray(i['x'][c])
        maps.append(m)
    return maps


def kernel(**inputs):
    if 'nc' not in _CACHE:
        _CACHE['nc'] = build()[0]
    nc = _CACHE['nc']
    maps = make_in_maps(inputs, 8)
    res = run_bass_kernel_spmd(nc, maps, core_ids=list(range(8)))
    out = np.stack([np.asarray(r['out']) for r in res.results], axis=0)
    return out.astype(np.float32)
```

```python
import contextlib
import numpy as np
import ml_dtypes
import concourse.bass as bass
import concourse.mybir as mybir
from concourse.bass_utils import run_bass_kernel_spmd

F32 = mybir.dt.float32
BF16 = mybir.dt.bfloat16
AF = mybir.ActivationFunctionType
ALU = mybir.AluOpType
AX = mybir.AxisListType

S_TOK = 2048
D = 2048
KC = 16
IN_COLS = 12416
COL = dict(r=0, k=1024, v=2048, za=3072, wd=4096, q=4224, kq=5248, vq=6272, zb=7296, ga=8320, gb=10368)
CDEC = 0.6065306597126334
NEG = -1.0e30
STOP = 99
NOMERGE = False
STRICT = True

PP = dict(mu_r=0, mu_k=8, mu_v=16, w0=24, a0=32, k_k=40, k_a=48, r_k=56, gn_w=64, gn_b=72, norm_w=80,
          mu_wa=96, qnw=97, knw=98)
NPP = 100
CB = dict(ident=0, blk2=128, m_su=256, m_iu=384, m_sl=512, onesA=640, onesB=768, cmask=896)
NCB = 896 + 4 * 512
CF = dict(scanmask=0, pastneg=512, ownpos=768, swapP=1024)
NCF = 1152


_DTSZ = {}


def _dtsize(dt):
    s = _DTSZ.get(dt)
    if s is None:
        name = str(dt)
        s = 4 if '32' in name else 2 if '16' in name else 1 if '8' in name else 8
        _DTSZ[dt] = s
    return s


def box_of(ap):
    dims = ap.ap
    sz = _dtsize(ap.dtype)
    pstep, pcnt = dims[0]
    off = int(ap.offset)
    if pstep == 0:
        pstep = 1 << 40
    p0 = off // pstep
    f0 = off % pstep
    ext = 0
    for st, cn in dims[1:]:
        ext += abs(st) * (cn - 1)
    f1 = f0 + ext + 1
    if 'PSUM' in str(ap.space).upper():
        b0 = (f0 * sz) // 2048
        b1 = ((f1 * sz) - 1) // 2048
        return ('PS', ap.name, 0, 128, b0 * 2048, (b1 + 1) * 2048, True)
    return ('SB', ap.name, p0, p0 + pcnt, f0 * sz, f1 * sz, False)


class Op:
    __slots__ = ('idx', 'eng', 'fn', 'deps', 'signal', 'count', 'is_dma', 'dsem', 'dcount', 'prev_slot')

    def __init__(self, idx, eng, fn, is_dma):
        self.idx = idx
        self.eng = eng
        self.fn = fn
        self.deps = {}
        self.signal = False
        self.count = 0
        self.is_dma = is_dma
        self.dsem = None
        self.dcount = 0
        self.prev_slot = None


class Sched:
    ENGS = ('pe', 'act', 'dve', 'pool', 'sp')

    def __init__(self, nc, n_dma_slots=10):
        self.nc = nc
        self.ops = []
        self.recs = {}
        self.n_dma_slots = n_dma_slots

    def _touch(self, op, box, is_write):
        kind, name, p0, p1, f0, f1, excl = box
        lst = self.recs.setdefault(name, [])
        found = None
        for r in lst:
            if r[0] < p1 and p0 < r[1] and r[2] < f1 and f0 < r[3]:
                if r[4] is not None:
                    if (not is_write) or excl:
                        op.deps[r[4]] = True
                    else:
                        op.deps.setdefault(r[4], False)
                if is_write or excl:
                    for e, o in r[5].items():
                        op.deps.setdefault(o, False)
            if r[0] == p0 and r[1] == p1 and r[2] == f0 and r[3] == f1:
                found = r
        if found is None:
            found = [p0, p1, f0, f1, None, {}]
            lst.append(found)
        if is_write or excl:
            found[4] = op.idx
            found[5] = {}
        else:
            found[5][op.eng] = op.idx

    capture = None

    def add(self, eng, fn, reads=(), writes=(), dma=False, cost=0.5):
        if self.capture is not None:
            self.capture.append((eng, fn, list(reads), list(writes), dma, cost))
            return -1
        op = Op(len(self.ops), eng, fn, dma)
        self.ops.append(op)
        for ap in reads:
            if ap is not None and not isinstance(ap, (int, float)):
                self._touch(op, ap if isinstance(ap, tuple) else box_of(ap), False)
        for ap in writes:
            if ap is not None:
                self._touch(op, ap if isinstance(ap, tuple) else box_of(ap), True)
        op.deps.pop(op.idx, None)
        return op.idx

    def commit(self, lst):
        for it in lst:
            self.add(*it[:5])

    def merge_streams(self, streams):
        if NOMERGE:
            for st in streams:
                self.commit(st)
            return
        pos = [0] * len(streams)
        ready = [0.0] * len(streams)
        free = {e: 0.0 for e in self.ENGS}
        while True:
            best = None
            for si, st in enumerate(streams):
                if pos[si] >= len(st):
                    continue
                it = st[pos[si]]
                t = max(free[it[0]], ready[si])
                if best is None or t < best[0] - 1e-9:
                    best = (t, si)
            if best is None:
                break
            t, si = best
            it = streams[si][pos[si]]
            pos[si] += 1
            self.add(*it[:5])
            if it[4]:
                free[it[0]] = t + 0.06
                ready[si] = t + 0.06
            else:
                free[it[0]] = t + it[5]
                ready[si] = t + it[5] + 0.12

    def merge(self, main, bg):
        return self.merge_streams([main, bg])
        nb = len(bg)
        nm = max(len(main), 1)
        j = 0
        for i, it in enumerate(main):
            self.add(*it[:5])
            tgt = (i + 1) * nb // nm
            while j < tgt:
                self.add(*bg[j][:5])
                j += 1
        while j < nb:
            self.add(*bg[j])
            j += 1

    def emit(self, final_wait_ops=()):
        nc = self.nc
        ops = self.ops
        for op in ops:
            for d, raw in op.deps.items():
                dop = ops[d]
                if dop.is_dma:
                    continue
                if (not op.is_dma) and dop.eng == op.eng and (op.eng == 'pe' or (not raw and not STRICT)):
                    continue
                dop.signal = True
        for d in final_wait_ops:
            if not ops[d].is_dma:
                ops[d].signal = True
        cnt = {e: 0 for e in self.ENGS}
        for op in ops:
            if op.is_dma:
                continue
            if op.signal:
                cnt[op.eng] += 1
            op.count = cnt[op.eng]
        slot_state = {}
        for op in ops:
            if not op.is_dma:
                continue
            st = slot_state.setdefault(op.eng, {'next': 0, 'counts': [0] * self.n_dma_slots,
                                                'last': [None] * self.n_dma_slots})
            s = st['next']
            st['next'] = (s + 1) % self.n_dma_slots
            op.prev_slot = st['last'][s]
            st['counts'][s] += 16
            op.dsem = (op.eng, s)
            op.dcount = st['counts'][s]
            st['last'][s] = op.idx
        used = [e for e in self.ENGS if any(o.eng == e for o in ops)]
        if 'sp' not in used:
            used.append('sp')
        with contextlib.ExitStack() as es:
            sems = {e: es.enter_context(nc.semaphore('s_' + e)) for e in used}
            dsems = {}
            for e in slot_state:
                for s in range(self.n_dma_slots):
                    dsems[(e, s)] = es.enter_context(nc.semaphore('d_%s_%d' % (e, s)))
            block = es.enter_context(nc.Block())

            def run_engine(ename, eng):
                waited = {e: 0 for e in self.ENGS}
                dwaited = {}

                def wait_on(dop):
                    if dop.is_dma:
                        if dwaited.get(dop.dsem, 0) < dop.dcount:
                            eng.wait_ge(dsems[dop.dsem], dop.dcount)
                            dwaited[dop.dsem] = dop.dcount
                    elif dop.count > waited[dop.eng]:
                        eng.wait_ge(sems[dop.eng], dop.count)
                        waited[dop.eng] = dop.count

                for op in ops:
                    if op.eng != ename:
                        continue
                    for d in sorted(op.deps):
                        dop = ops[d]
                        raw = op.deps[d]
                        if (not dop.is_dma) and (not op.is_dma) and dop.eng == ename and (ename == 'pe' or (not raw and not STRICT)):
                            continue
                        wait_on(dop)
                    if op.is_dma and op.prev_slot is not None:
                        wait_on(ops[op.prev_slot])
                    ins = op.fn(eng)
                    if op.is_dma:
                        ins.then_inc(dsems[op.dsem], 16)
                    elif op.signal:
                        ins.then_inc(sems[ename], 1)
                if ename == 'sp':
                    for d in final_wait_ops:
                        wait_on(ops[d])

            @block.tensor
            def _(eng):
                run_engine('pe', eng)

            @block.scalar
            def _(eng):
                run_engine('act', eng)

            @block.vector
            def _(eng):
                run_engine('dve', eng)

            @block.gpsimd
            def _(eng):
                run_engine('pool', eng)

            @block.sync
            def _(eng):
                run_engine('sp', eng)
        return cnt


def build(n_hp_r=8, n_hp_m=8, do_final=True, dbg=False):
    nc = bass.Bass("TRN2", target_bir_lowering=False)
    x_d = nc.dram_tensor("x", [S_TOK, D], F32, kind="ExternalInput").ap()
    win_d = nc.dram_tensor("w_in", [D, IN_COLS], F32, kind="ExternalInput").ap()
    wpa_d = nc.dram_tensor("w_pa", [1024, D], F32, kind="ExternalInput").ap()
    wpb_d = nc.dram_tensor("w_pb", [1024, D], F32, kind="ExternalInput").ap()
    wo_d = nc.dram_tensor("w_out", [D, D], F32, kind="ExternalInput").ap()
    lora_d = nc.dram_tensor("lora", [128, 1024], F32, kind="ExternalInput").ap()
    pp_d = nc.dram_tensor("pp", [128, NPP], F32, kind="ExternalInput").ap()
    cb_d = nc.dram_tensor("cb", [128, NCB], BF16, kind="ExternalInput").ap()
    cf_d = nc.dram_tensor("cf", [128, NCF], F32, kind="ExternalInput").ap()
    ind_d = nc.dram_tensor("ind", [8, S_TOK], BF16, kind="ExternalInput").ap()
    out_d = nc.dram_tensor("out", [S_TOK, D], F32, kind="ExternalOutput").ap()
    scr_kind = "ExternalOutput" if dbg else "Internal"
    ya_d = nc.dram_tensor("ya_scr", [8, 128, S_TOK], BF16, kind=scr_kind).ap()
    yb_d = nc.dram_tensor("yb_scr", [8, 128, S_TOK], BF16, kind=scr_kind).ap()

    with contextlib.ExitStack() as es:
        def sb(name, shape, dt):
            return es.enter_context(nc.sbuf_tensor(name, shape, dt))

        hT = sb("hT", [128, KC, S_TOK], BF16)
        NW = 4
        wpool = [sb("wp%d" % i, [128, KC, 128], BF16) for i in range(NW)]
        cb = sb("cb_s", [128, NCB], BF16)
        cf = sb("cf_s", [128, NCF], F32)
        pp = sb("pp_s", [128, NPP], F32)
        loraD = sb("loraD_s", [128, 1024], BF16)
        loraI = sb("loraI_s", [128, 1024], BF16)
        TL = sb("TL", [128, S_TOK], BF16)
        RAWT = [sb("rawt%d" % s, [128, 4 * S_TOK], BF16) for s in range(2)]
        RAW = [[RAWT[s][:, j * S_TOK:(j + 1) * S_TOK] for j in range(4)] for s in range(2)]
        ARENA_N = 39424
        arena_t = sb("arena", [128, ARENA_N], BF16)
        PSA = es.enter_context(nc.psum_tensor("PSA", [128, 2048], F32))
        PSB = es.enter_context(nc.psum_tensor("PSB", [128, 2048], F32))

        S = Sched(nc)
        out_dmas = []

        class Arena:
            def __init__(self):
                self.off = 0

            def reset(self):
                self.off = 0

            def a(self, n, dt=BF16):
                nb = n * (2 if dt == F32 else 1)
                nb = (nb + 1) // 2 * 2
                assert self.off + nb <= ARENA_N, (self.off, nb)
                v = arena_t[:, self.off:self.off + nb]
                self.off += nb
                if dt == F32:
                    v = v.bitcast(F32)
                return v

        AR_ = Arena()

        def rd(*aps):
            return [a for a in aps if a is not None and not isinstance(a, (int, float))]

        def fsz(ap):
            n = 1
            for st, cn in ap.ap[1:]:
                n *= cn
            return n

        def ACT(out, in_, func, bias=None, scale=None, accum=None):
            kw = {}
            if bias is not None:
                kw['bias'] = bias
            if scale is not None:
                kw['scale'] = scale
            if accum is not None:
                kw['accum_out'] = accum
            return S.add('act', lambda e: e.activation(out=out, in_=in_, func=func, **kw),
                         reads=rd(in_, bias, scale), writes=[out, accum], cost=0.22 + fsz(out) / 1200.0)

        def TT(eng, out, in0, in1, op):
            return S.add(eng, lambda e: e.tensor_tensor(out=out, in0=in0, in1=in1, op=op),
                         reads=rd(in0, in1), writes=[out], cost=0.1 + fsz(out) / 960.0)

        def TS(eng, out, in0, s1, s2, op0, op1=None):
            if op1 is None:
                return S.add(eng, lambda e: e.tensor_scalar(out=out, in0=in0, scalar1=s1, scalar2=None, op0=op0),
                             reads=rd(in0, s1), writes=[out], cost=0.1 + fsz(out) / 960.0)
            return S.add(eng, lambda e: e.tensor_scalar(out=out, in0=in0, scalar1=s1, scalar2=s2, op0=op0, op1=op1),
                         reads=rd(in0, s1, s2), writes=[out], cost=0.1 + fsz(out) / 960.0)

        def STT(eng, out, in0, scalar, in1, op0, op1):
            eng = 'dve'
            return S.add(eng, lambda e: e.scalar_tensor_tensor(out=out, in0=in0, scalar=scalar, in1=in1,
                                                                op0=op0, op1=op1),
                         reads=rd(in0, scalar, in1), writes=[out], cost=0.1 + fsz(out) / 960.0)

        def CP(eng, out, in_):
            if eng == 'act':
                return S.add('act', lambda e: e.copy(out=out, in_=in_), reads=[in_], writes=[out],
                             cost=0.22 + fsz(out) / 1200.0)
            return S.add(eng, lambda e: e.tensor_copy(out=out, in_=in_), reads=[in_], writes=[out],
                         cost=0.1 + fsz(out) / 960.0)

        def MEMSET(eng, out, val):
            return S.add(eng, lambda e: e.memset(out, val), writes=[out], cost=1.0)

        def MM(items, reads, writes):
            def fn(e):
                ins = None
                for (o, l, r, st, sp) in items:
                    ins = e.matmul(o, lhsT=l, rhs=r, start=st, stop=sp)
                return ins
            c = 0.05
            for (o, l, r, st, sp) in items:
                c += max(0.065, fsz(o) / (600.0 if l.dtype == F32 else 2400.0))
            return S.add('pe', fn, reads=reads, writes=writes, cost=c)

        def TRS(items, reads, writes):
            def fn(e):
                ins = None
                for (o, i, idn) in items:
                    ins = e.transpose(o, i, idn)
                return ins
            return S.add('pe', fn, reads=reads, writes=writes, cost=0.05 + 0.11 * len(items))

        def DMA(q, out, in_, reads=(), writes=()):
            return S.add(q, lambda e: e.dma_start(out=out, in_=in_), reads=reads, writes=writes, dma=True)

        def ppc(name, j=0):
            c = PP[name] + j
            return pp[:, c:c + 1]

        ident = cb[:, CB['ident']:CB['ident'] + 128]
        blk2 = cb[:, CB['blk2']:CB['blk2'] + 128]
        m_su = cb[:, CB['m_su']:CB['m_su'] + 128]
        m_iu = cb[:, CB['m_iu']:CB['m_iu'] + 128]
        m_sl = cb[:, CB['m_sl']:CB['m_sl'] + 128]
        onesA = cb[:, CB['onesA']:CB['onesA'] + 128]
        onesB = cb[:, CB['onesB']:CB['onesB'] + 128]
        cmask = cb[:, CB['cmask']:CB['cmask'] + 2048].rearrange("p (a b) -> p a b", b=512)
        scanmask = cf[:, CF['scanmask']:CF['scanmask'] + 512]
        pastneg = cf[:, CF['pastneg']:CF['pastneg'] + 256]
        ownpos = cf[:, CF['ownpos']:CF['ownpos'] + 256]
        swapP = cf[:, CF['swapP']:CF['swapP'] + 128]

        def psb_f32(bank, n=512, off=0):
            return PSB[:, bank * 512 + off: bank * 512 + off + n]

        def psb_bf(bank, n=1024, off=0):
            v = PSB[:, bank * 512:(bank + 1) * 512].bitcast(BF16)
            return v[:, off:off + n]

        def psb2_bf(bank):
            return PSB[:, bank * 512:(bank + 2) * 512].bitcast(BF16)

        DMA('sp', cb[:], cb_d, writes=[cb[:]])
        DMA('sp', cf[:], cf_d, writes=[cf[:]])
        DMA('sp', pp[:], pp_d, writes=[pp[:]])
        S.add('pool', lambda e: e.memset(loraD[:], 0.0), writes=[loraD[:]])
        S.add('pool', lambda e: e.memset(loraI[:], 0.0), writes=[loraI[:]])
        DMA('pool', loraD[0:64, :], lora_d[0:64, :], writes=[loraD[0:64, :]])
        DMA('pool', loraI[64:128, :], lora_d[64:128, :], writes=[loraI[64:128, :]])

        wstate = {'n': 0}

        def wload_in(col):
            b = wpool[wstate['n'] % NW]
            wstate['n'] += 1
            src = win_d[:, col:col + 128].rearrange("(kc p) m -> p kc m", p=128)
            DMA('pool', b[:], src, writes=[b[:]])
            return b

        def wload_proj(wd, col):
            b = wpool[wstate['n'] % NW]
            wstate['n'] += 1
            src = wd[:, col:col + 128].rearrange("(kc p) m -> p kc m", p=128)
            DMA('pool', b[:, 0:8, :], src, writes=[b[:, 0:8, :]])
            return b

        def inproj(wbuf, nk, rhs_src, evac):
            for half in range(2):
                acc = PSA[:, 0:1024]
                items = []
                for k in range(nk):
                    for tt in range(2):
                        t0 = half * 1024 + tt * 512
                        items.append((acc[:, tt * 512:(tt + 1) * 512], wbuf[:, k, :], rhs_src[:, k, t0:t0 + 512],
                                      k == 0, k == nk - 1))
                for i0 in range(0, len(items), 8):
                    MM(items[i0:i0 + 8], reads=[wbuf[:, 0:nk, :], rhs_src[:, 0:nk, half * 1024:(half + 1) * 1024]], writes=[acc])
                evac(acc, half)

        AR_.reset()
        xt = [AR_.a(2048, F32) for _ in range(4)]
        hb = [AR_.a(2048) for _ in range(2)]
        junk = AR_.a(2048)
        ssb = AR_.a(16, F32)
        rsb = AR_.a(16, F32)
        nwb = pp[:, PP['norm_w']:PP['norm_w'] + 16].unsqueeze(2).to_broadcast([128, 16, 128])
        eps0 = AR_.a(2, F32)
        MEMSET('pool', eps0, 1e-6)

        def p0_stage1(tt):
            xtile = xt[tt % 4]
            if tt + 2 < 16:
                DMA('sp', xt[(tt + 2) % 4], x_d[(tt + 2) * 128:(tt + 3) * 128, :], writes=[xt[(tt + 2) % 4]])
            ACT(junk, xtile, AF.Square, accum=ssb[:, tt:tt + 1])
            ACT(rsb[:, tt:tt + 1], ssb[:, tt:tt + 1], AF.Ln, bias=eps0[:, 0:1], scale=1.0 / D)
            ACT(rsb[:, tt:tt + 1], rsb[:, tt:tt + 1], AF.Exp, scale=-0.5)
            TS('dve', hb[tt % 2], xtile, rsb[:, tt:tt + 1], None, ALU.mult)

        def p0_stage2(tt):
            pst = psb2_bf((tt % 2) * 2)
            TRS([(pst[:, k * 128:(k + 1) * 128], hb[tt % 2][:, k * 128:(k + 1) * 128], ident) for k in range(16)],
                reads=[hb[tt % 2], ident], writes=[pst])
            TT('dve', hT[:, :, tt * 128:(tt + 1) * 128], pst.rearrange("p (a b) -> p a b", b=128), nwb, ALU.mult)

        def cap0(fn, *args):
            S.capture = []
            fn(*args)
            lst = S.capture
            S.capture = None
            return lst

        for t_ in range(2):
            DMA('sp', xt[t_], x_d[t_ * 128:(t_ + 1) * 128, :], writes=[xt[t_]])
        for tt in range(17):
            streams = []
            if tt >= 1:
                streams.append(cap0(p0_stage2, tt - 1))
            if tt < 16:
                streams.append(cap0(p0_stage1, tt))
            S.merge_streams(streams)

        jobs = []
        jobs_seq = [('R', hp) for hp in range(n_hp_r)] + [('M', hp) for hp in range(n_hp_m)]
        for hp in range(n_hp_r):
            for nm in ('r', 'k', 'v', 'za'):
                jobs.append(('R', hp, nm, COL[nm] + hp * 128))
            if hp == 0:
                jobs.append(('R', 0, 'wd', COL['wd']))
        for hp in range(n_hp_m):
            for nm in ('q', 'kq', 'vq', 'zb'):
                jobs.append(('M', hp, nm, COL[nm] + hp * 128))
        PREF = NW - 1
        wq = []
        jstate = {'issued': 0}

        def next_w():
            while jstate['issued'] < len(jobs) and len(wq) < PREF + 1:
                wq.append(wload_in(jobs[jstate['issued']][3]))
                jstate['issued'] += 1
            return wq.pop(0)

        def ev_copy(dst, eng):
            def f(acc, half):
                CP(eng, dst[:, half * 1024:(half + 1) * 1024], acc)
            return f

        def ev_silu(dst):
            def f(acc, half):
                ACT(dst[:, half * 1024:(half + 1) * 1024], acc, AF.Silu)
            return f

        def inproj_job(ji):
            kind, hp = jobs_seq[ji]
            R0, R1, R2, R3 = RAW[ji % 2]
            inproj(next_w(), KC, hT, ev_copy(R0, 'act'))
            inproj(next_w(), KC, hT, ev_copy(R1, 'dve'))
            inproj(next_w(), KC, hT, ev_copy(R2, 'act'))
            inproj(next_w(), KC, hT, ev_silu(R3))

        def lora_prep():
            AR_.reset()
            lraw = AR_.a(2048)
            ld = AR_.a(2048)
            inproj(next_w(), KC, hT, ev_copy(lraw, 'dve'))
            TT('dve', ld[:, 1:2048], lraw[:, 0:2047], lraw[:, 1:2048], ALU.subtract)
            TS('dve', ld[:, 0:1], lraw[:, 0:1], -1.0, None, ALU.mult)
            STT('dve', TL[:], ld, ppc('mu_wa'), lraw, ALU.mult, ALU.add)
            ACT(TL[0:64, :], TL[0:64, :], AF.Tanh)

        def split_parts(lst, fracs):
            tot = float(sum(fracs))
            out = []
            acc = 0.0
            i0 = 0
            for f in fracs:
                acc += f
                i1 = int(round(len(lst) * acc / tot))
                out.append(lst[i0:i1])
                i0 = i1
            out[-1] = out[-1] + lst[i0:]
            return out

        def cap(fn, *args):
            S.capture = []
            fn(*args)
            lst = S.capture
            S.capture = None
            return lst

        def rwkv_all():
            AR_.reset()
            A = AR_.a
            GT = 256
            NG = S_TOK // GT
            Sf = [A(128, F32) for _ in range(2)]
            Sbf = A(128)
            Ssc = A(128, F32)
            Stmp = A(128, F32)
            Zsb = A(128)
            Usb = A(128)
            tinyb_t = A(2, F32)
            YAg = [A(GT) for _ in range(2)]
            ARfs = [A(2 * GT, F32) for _ in range(2)]
            KTfs = [A(GT, F32) for _ in range(2)]
            BTfs = [A(GT, F32) for _ in range(2)]
            KTbs = [A(GT) for _ in range(2)]
            BTbs = [A(GT) for _ in range(2)]
            vgs = [A(GT) for _ in range(2)]
            WCs = [A(4, F32) for _ in range(4)]
            BONs = [A(GT) for _ in range(4)]
            ARb = [A(2 * GT) for _ in range(2)]
            TM = [A(3 * GT) for _ in range(2)]
            AKs = [A(2 * GT) for _ in range(2)]
            RKs = [A(2 * GT) for _ in range(2)]
            RBs = [A(2 * GT) for _ in range(2)]
            TTs = [A(2 * GT) for _ in range(2)]
            Ytms = [A(GT, F32) for _ in range(2)]
            mean = A(4, F32); var = A(4, F32)
            YC = A(GT, F32); YQ = A(GT, F32); YN = A(GT)
            Dm = A(GT, F32); SQ = A(GT); PROD = A(GT)
            omu = A(4, F32)
            rg = A(GT, F32); kg = A(GT, F32); SG = A(GT, F32); CS = A(GT, F32); ag = A(GT, F32)
            E1 = A(GT, F32); E2 = A(GT, F32); E3 = A(GT, F32)
            KK = A(GT, F32); RI = A(GT, F32); Tt = A(GT, F32); K2 = A(GT, F32); Bb = A(GT, F32)
            N0s = A(2 * GT); X0s = A(2 * GT)
            Ns = [A(2 * GT) for _ in range(2)]
            Xs = [A(2 * GT) for _ in range(2)]
            Ps = [A(2 * GT) for _ in range(2)]
            MEMSET('pool', tinyb_t, 1e-12)
            tinyb = tinyb_t[:, 0:1]
            NT = GT // 128
            NM = NT * 2
            PA2 = PSA[:, 1024:1536]
            PA3 = PSA[:, 1536:2048]

            def v3(ap):
                return ap.rearrange("p (j t) -> p j t", t=128)

            def emit_A(hp, g):
                Rr, Rk, Rv, SZ = RAW[hp % 2]
                if g == 0:
                    for mi, mu in enumerate(('mu_r', 'mu_k', 'mu_v')):
                        TS('dve', omu[:, mi:mi + 1], ppc(mu, hp), -1.0, 1.0, ALU.mult, ALU.add)
                c0 = g * GT
                par = g % 2
                ARf = ARfs[par]; KTf = KTfs[par]; BTf = BTfs[par]; vg = vgs[par]
                ARfv = ARf.rearrange("p (j w t) -> p j w t", w=2, t=128)
                for mi, (raw, mu, dst) in enumerate(((Rr, 'mu_r', rg), (Rk, 'mu_k', kg), (Rv, 'mu_v', vg))):
                    if g == 0:
                        ACT(Dm[:, 1:GT], raw[:, 0:GT - 1], AF.Identity, scale=ppc(mu, hp))
                        TS('dve', Dm[:, 0:1], raw[:, 0:1], 0.0, None, ALU.mult)
                    else:
                        ACT(Dm, raw[:, c0 - 1:c0 + GT - 1], AF.Identity, scale=ppc(mu, hp))
                    STT('dve', dst, raw[:, c0:c0 + GT], omu[:, mi:mi + 1], Dm, ALU.mult, ALU.add)
                pu = PA2[:, 0:GT]
                pa_ = PA2[:, GT:2 * GT]
                MM([(pu, loraD[:, hp * 128:(hp + 1) * 128], TL[:, c0:c0 + GT], True, True)],
                   reads=[loraD[:, hp * 128:(hp + 1) * 128], TL[:, c0:c0 + GT]], writes=[pu])
                MM([(pa_, loraI[:, hp * 128:(hp + 1) * 128], TL[:, c0:c0 + GT], True, True)],
                   reads=[loraI[:, hp * 128:(hp + 1) * 128], TL[:, c0:c0 + GT]], writes=[pa_])
                ACT(SG, pu, AF.Sigmoid, bias=ppc('w0', hp))
                ACT(ag, pa_, AF.Sigmoid, bias=ppc('a0', hp))
                S.add('dve', (lambda o, m, d1: (lambda e: e.tensor_tensor_scan(out=o, data0=m, data1=d1, initial=0.0,
                                                                                op0=ALU.mult, op1=ALU.add)))(CS, scanmask[:, 0:GT], SG),
                      reads=[scanmask[:, 0:GT], SG], writes=[CS], cost=0.1 + 2 * GT / 960.0)
                ACT(E1, CS, AF.Exp, scale=-CDEC)
                ACT(WCs[g % 4], CS.rearrange("p (c t) -> p c t", t=64)[:, :, 63], AF.Exp, scale=-CDEC)
                ACT(E3, CS, AF.Exp, scale=CDEC)
                TT('dve', SG, CS, SG, ALU.subtract)
                ACT(E2, SG, AF.Exp, scale=-CDEC)
                ACT(KK, kg, AF.Identity, scale=ppc('k_k', hp))
                ACT(SQ, kg, AF.Square, scale=ppc('k_k', hp))
                pk = PA2[:, 0:GT]
                MM([(pk, blk2, SQ, True, True)], reads=[blk2, SQ], writes=[pk])
                ACT(RI, pk, AF.Ln, bias=tinyb)
                ACT(RI, RI, AF.Exp, scale=-0.5)
                TT('dve', KK, KK, RI, ALU.mult)
                TS('dve', Tt, ag, -1.0, ppc('k_a', hp), ALU.add, ALU.mult)
                STT('dve', K2, Tt, 1.0, kg, ALU.add, ALU.mult)
                TT('dve', Bb, KK, ag, ALU.mult)
                STT('dve', PROD, rg, ppc('r_k', hp), K2, ALU.mult, ALU.mult)
                pb_ = PA2[:, GT:2 * GT]
                MM([(pb_, blk2, PROD, True, True)], reads=[blk2, PROD], writes=[pb_])
                TT('dve', BONs[g % 4], pb_, vg, ALU.mult)
                TT('dve', ARfv[:, :, 1, :], v3(rg), v3(E1), ALU.mult)
                STT('dve', ARfv[:, :, 0, :], v3(KK), -1.0, v3(E2), ALU.mult, ALU.mult)
                TT('dve', KTf, K2, E3, ALU.mult)
                TT('dve', BTf, Bb, E3, ALU.mult)
                CP('act', KTbs[par], KTf)
                CP('act', BTbs[par], BTf)

            def emit_B(hp, g):
                par = g % 2
                ARf = ARfs[par]; KTf = KTfs[par]; BTf = BTfs[par]; vg = vgs[par]
                KTb = KTbs[par]; BTb = BTbs[par]
                CP('act', ARb[par], ARf)
                ptm = PA3.bitcast(BF16)[:, 0:3 * GT]
                items = []
                for q, src in enumerate((vg, KTb, BTb)):
                    for j in range(NT):
                        items.append((ptm[:, (q * NT + j) * 128:(q * NT + j + 1) * 128], src[:, j * 128:(j + 1) * 128], ident))
                TRS(items, reads=[vg, KTb, BTb, ident], writes=[ptm])
                CP('act', TM[par], ptm)
                for j in range(NT):
                    items = []
                    for h in range(2):
                        hr = slice(h * 64, h * 64 + 64)
                        rhs_ar = ARf[hr, j * 256:(j + 1) * 256]
                        bh = psb_f32(h)
                        items.append((bh[:, 0:256], KTf[hr, j * 128:(j + 1) * 128], rhs_ar, True, True))
                        items.append((bh[:, 256:512], BTf[hr, j * 128:(j + 1) * 128], rhs_ar, True, True))
                    MM(items, reads=[KTf, BTf, ARf], writes=[PSB[:, 0:1024]])
                    bxy = PSB[:, 0:1024].rearrange("p (h w t) -> p h w t", h=2, w=4)
                    msu_b = m_su.unsqueeze(1).to_broadcast([128, 2, 128])
                    miu_b = m_iu.unsqueeze(1).to_broadcast([128, 2, 128])
                    msl_b = m_sl.unsqueeze(1).to_broadcast([128, 2, 128])

                    def dst(t):
                        return t.rearrange("p (j h t) -> p j h t", h=2, t=128)[:, j, :, :]
                    TT('dve', dst(AKs[par]), bxy[:, :, 0, :], msu_b, ALU.mult)
                    TT('dve', dst(RKs[par]), bxy[:, :, 1, :], miu_b, ALU.mult)
                    TT('dve', dst(N0s), bxy[:, :, 2, :], msu_b, ALU.mult)
                    TT('dve', dst(RBs[par]), bxy[:, :, 3, :], miu_b, ALU.mult)

                pxt = PA3.bitcast(BF16)[:, 0:NM * 128]
                TRS([(pxt[:, m * 128:(m + 1) * 128], N0s[:, m * 128:(m + 1) * 128], ident) for m in range(NM)],
                    reads=[N0s, ident], writes=[pxt])
                CP('act', X0s, pxt)

                def m8(t):
                    return t.rearrange("p (m t) -> p m t", t=128)
                W = NM * 128
                TT('dve', m8(Ps[0]), m8(N0s), ident.unsqueeze(1).to_broadcast([128, NM, 128]), ALU.add)
                Ncur, Xcur, Pcur = N0s, X0s, Ps[0]
                for lvl in range(1, 6):
                    pX = PSB[:, 0:W]
                    MM([(pX[:, m * 128:(m + 1) * 128], Ncur[:, m * 128:(m + 1) * 128], Xcur[:, m * 128:(m + 1) * 128], True, True)
                        for m in range(NM)], reads=[Ncur, Xcur], writes=[pX])
                    Xn = Xs[lvl % 2]
                    if lvl < 5:
                        pN = PSB[:, 512:512 + W]
                        MM([(pN[:, m * 128:(m + 1) * 128], Xcur[:, m * 128:(m + 1) * 128], Ncur[:, m * 128:(m + 1) * 128], True, True)
                            for m in range(NM)], reads=[Ncur, Xcur], writes=[pN])
                    CP('act', Xn, pX)
                    if lvl < 5:
                        Nn = Ns[lvl % 2]
                        CP('act', Nn, pN)
                    pP = PA3[:, 0:W]
                    pitems = []
                    for m in range(NM):
                        pitems.append((pP[:, m * 128:(m + 1) * 128], Xn[:, m * 128:(m + 1) * 128], Pcur[:, m * 128:(m + 1) * 128], m == 0, False))
                        pitems.append((pP[:, m * 128:(m + 1) * 128], ident, Pcur[:, m * 128:(m + 1) * 128], False, m == NM - 1))
                    MM(pitems, reads=[Xn, Pcur, ident], writes=[pP])
                    Pn = TTs[par] if lvl == 5 else Ps[lvl % 2]
                    CP('act', Pn, pP)
                    Pcur = Pn
                    Xcur = Xn
                    if lvl < 5:
                        Ncur = Nn

            def emit_chain(hp, g):
                par = g % 2
                if g == 0:
                    TS('dve', Sf[0], Sf[0], 0.0, None, ALU.mult)
                    TS('dve', Sbf, Sbf, 0.0, None, ALU.mult)
                ARbv = ARb[par].rearrange("p (j w t) -> p j w t", w=2, t=128)
                TMv = TM[par].rearrange("p (q j f) -> p q j f", q=3, f=128)
                TTg = TTs[par].rearrange("p (j h t) -> p j h t", h=2, t=128)
                AKv = AKs[par].rearrange("p (j h t) -> p j h t", h=2, t=128)
                RKv = RKs[par].rearrange("p (j h t) -> p j h t", h=2, t=128)
                RBv = RBs[par].rearrange("p (j h t) -> p j h t", h=2, t=128)
                Ytv = Ytms[par].rearrange("p (j f) -> p j f", f=128)
                for cl in range(2 * NT):
                    c = g * 2 * NT + cl
                    j = cl // 2
                    pr = slice((cl % 2) * 64, (cl % 2) * 64 + 64)
                    Scur = Sf[c % 2]
                    Snxt = Sf[(c + 1) % 2]
                    wc = WCs[g % 4][:, cl:cl + 1]
                    Zp = PSB[:, 1536:1664]
                    Up = PSB[:, 1664:1792]
                    Yp = PSB[:, 1792:1920]
                    Sp = PSB[:, 1920:2048]
                    ACT(Ssc, Scur, AF.Identity, scale=wc)
                    MM([(Zp[:, 0:64], AKv[pr, j, 0, :], TMv[pr, 0, j, 0:64], True, False),
                        (Zp[:, 64:128], AKv[pr, j, 1, :], TMv[pr, 0, j, 64:128], False, False),
                        (Zp[:, 0:128], ARbv[:, j, 0, :], Sbf, False, True)],
                       reads=[AKs[par], TM[par], ARb[par], Sbf], writes=[Zp])
                    CP('act', Zsb[pr, :], Zp[pr, :])
                    MM([(Up[:, 0:64], TTg[pr, j, 0, :], Zsb[pr, 0:64], True, False),
                        (Up[:, 64:128], TTg[pr, j, 1, :], Zsb[pr, 64:128], False, True)],
                       reads=[TTs[par], Zsb[pr, :]], writes=[Up])
                    CP('dve', Usb[pr, :], Up[pr, :])
                    MM([(Yp[:, 0:128], ARbv[:, j, 1, :], Sbf, True, False),
                        (Yp[:, 0:64], RBv[pr, j, 0, :], Usb[pr, 0:64], False, False),
                        (Yp[:, 0:64], RKv[pr, j, 0, :], TMv[pr, 0, j, 0:64], False, False),
                        (Yp[:, 64:128], RBv[pr, j, 1, :], Usb[pr, 64:128], False, False),
                        (Yp[:, 64:128], RKv[pr, j, 1, :], TMv[pr, 0, j, 64:128], False, True)],
                       reads=[ARb[par], Sbf, RBs[par], RKs[par], Usb[pr, :], TM[par]], writes=[Yp])
                    CP('act', Ytv[pr, j, :], Yp[pr, :])
                    MM([(Sp, TMv[pr, 2, j, :], Usb[pr, :], True, False),
                        (Sp, TMv[pr, 1, j, :], TMv[pr, 0, j, :], False, True)],
                       reads=[TM[par], Usb[pr, :]], writes=[Sp])
                    TT('dve', Stmp, Sp, blk2, ALU.mult)
                    STT('dve', Sbf, Stmp, wc, Ssc, ALU.mult, ALU.add)
                    STT('dve', Snxt, Stmp, wc, Ssc, ALU.mult, ALU.add)

            def emit_post(hp, g):
                Rr, Rk, Rv, SZ = RAW[hp % 2]
                par = g % 2
                c0 = g * GT
                Ytm = Ytms[par]
                Y4 = Ytm.rearrange("p (m f) -> p m f", f=64)
                YC4 = YC.rearrange("p (m f) -> p m f", f=64)
                YQ4 = YQ.rearrange("p (m f) -> p m f", f=64)
                nm_ = 2 * NT
                S.add('dve', (lambda o, i: (lambda e: e.tensor_reduce(out=o, in_=i, axis=AX.X, op=ALU.add)))(mean, Y4),
                      reads=[Ytm], writes=[mean])
                TS('dve', mean, mean, 1.0 / 64, None, ALU.mult)
                TT('dve', YC4, Y4, mean.unsqueeze(2).to_broadcast([128, nm_, 64]), ALU.subtract)
                ACT(YQ, YC, AF.Square)
                S.add('dve', (lambda o, i: (lambda e: e.tensor_reduce(out=o, in_=i, axis=AX.X, op=ALU.add)))(var, YQ4),
                      reads=[YQ], writes=[var])
                TS('dve', var, var, 1.0 / 64, 64e-5, ALU.mult, ALU.add)
                ACT(var, var, AF.Ln)
                ACT(var, var, AF.Exp, scale=-0.5)
                TT('dve', YN.rearrange("p (m f) -> p m f", f=64), YC4, var.unsqueeze(2).to_broadcast([128, nm_, 64]), ALU.mult)
                pyt = psb_bf(2, GT)
                TRS([(pyt[:, jj * 128:(jj + 1) * 128], YN[:, jj * 128:(jj + 1) * 128], ident) for jj in range(NT)],
                    reads=[YN, ident], writes=[pyt])
                YF = YC
                ACT(YF, pyt, AF.Identity, bias=ppc('gn_b', hp), scale=ppc('gn_w', hp))
                TT('dve', YF, YF, BONs[g % 4], ALU.add)
                TT('dve', YAg[par], YF, SZ[:, c0:c0 + GT], ALU.mult)
                DMA('sp', ya_d[hp, :, c0:c0 + GT], YAg[par], reads=[YAg[par]], writes=[('DR', 'ya', hp, hp + 1, 0, 1, False)])

            MEMSET('pool', Sf[0], 0.0)
            MEMSET('pool', Sbf, 0.0)
            NGT = n_hp_r * NG
            nx = [split_parts(nxt_stream(hp), [1.0] * NG) for hp in range(n_hp_r)]

            def hg(G):
                return (G // NG, G % NG)
            for it in range(-2, NGT + 1):
                streams = []
                if 0 <= it < NGT:
                    streams.append(cap(emit_chain, *hg(it)))
                if 0 <= it + 1 < NGT:
                    streams.append(cap(emit_B, *hg(it + 1)))
                if 0 <= it - 1 < NGT:
                    streams.append(cap(emit_post, *hg(it - 1)))
                if 0 <= it + 2 < NGT:
                    streams.append(cap(emit_A, *hg(it + 2)))
                if 0 <= it < NGT:
                    streams.append(nx[it // NG][it % NG])
                S.merge_streams(streams)

        def nxt_stream(ji):
            return cap(inproj_job, ji + 1) if ji + 1 < len(jobs_seq) else []

        if jobs_seq:
            inproj_job(0)
        if n_hp_r > 0:
            lora_prep()
        if n_hp_r > 0:
            rwkv_all()

        AR_.reset()
        if n_hp_m > 0:
            QA = AR_.a(2048); QB = AR_.a(2048); KA = AR_.a(2048); KB = AR_.a(2048)
            VA = AR_.a(2048); VB = AR_.a(2048)
            for t in (QA, QB, KA, KB):
                MEMSET('pool', t, 0.0)
            MEMSET('pool', VA, 1.0)
            MEMSET('pool', VB, 1.0)
            DMA('sp', KA[64:72, :], ind_d, writes=[KA[64:72, :]])
            DMA('sp', KB[0:8, :], ind_d, writes=[KB[0:8, :]])
        m_base = AR_.off

        def moba_headpair(hp, ji, nxt):
            Rq, Rkq, Rvq, SZ = RAW[ji % 2]
            AR_.off = m_base
            A = AR_.a
            nparts = split_parts(nxt, [1.0, 0.0, 0.0, 0.0, 0.0])
            Qf = A(2048, F32); Kf = A(2048, F32)

            def norm_stream(raw, wname, dA, dB, Ff, SQm, RIm, bank):
                for g in range(4):
                    c0 = g * 512
                    ACT(SQm, raw[:, c0:c0 + 512], AF.Square)
                    pk = psb_f32(bank)
                    MM([(pk, blk2, SQm, True, True)], reads=[blk2, SQm], writes=[pk])
                    ACT(RIm, pk, AF.Ln, bias=ppc_eps, scale=1.0 / 64)
                    ACT(RIm, RIm, AF.Exp, scale=-0.5)
                    STT('dve', Ff[:, c0:c0 + 512], raw[:, c0:c0 + 512], ppc(wname), RIm, ALU.mult, ALU.mult)
                    CP('act', dA[0:64, c0:c0 + 512], Ff[0:64, c0:c0 + 512])
                    CP('dve', dB[64:128, c0:c0 + 512], Ff[64:128, c0:c0 + 512])

            def v_stream():
                pv = psb2_bf(2)
                TRS([(pv[:, t * 128:(t + 1) * 128], Rvq[:, t * 128:(t + 1) * 128], ident) for t in range(16)],
                    reads=[Rvq, ident], writes=[pv])
                pv3 = pv.rearrange("p (t f) -> p t f", f=128)
                CP('act', VA.rearrange("p (t f) -> p t f", f=128)[:, :, 0:64], pv3[:, :, 0:64])
                CP('dve', VB.rearrange("p (t f) -> p t f", f=128)[:, :, 64:128], pv3[:, :, 64:128])

            SQq = A(512); RIq = A(512, F32); SQk = A(512); RIk = A(512, F32)
            st_q = cap(norm_stream, Rq, 'qnw', QA, QB, Qf, SQq, RIq, 0)
            st_k = cap(norm_stream, Rkq, 'knw', KA, KB, Kf, SQk, RIk, 1)
            st_v = cap(v_stream)
            n0 = int(len(nparts[0]) * 0.72)
            S.merge_streams([st_k, st_q, st_v, nparts[0][:n0]])
            nparts[0] = nparts[0][n0:]
            S.capture = []
            kmp = A(16, F32)
            MEMSET('pool', kmp, 0.0)
            S.add('dve', (lambda o, i: (lambda e: e.tensor_reduce(out=o, in_=i, axis=AX.X, op=ALU.add)))(
                kmp[0:64, 0:8], Kf[0:64, :].rearrange("p (n t) -> p n t", t=256)), reads=[Kf[0:64, :]], writes=[kmp[0:64, 0:8]])
            S.add('dve', (lambda o, i: (lambda e: e.tensor_reduce(out=o, in_=i, axis=AX.X, op=ALU.add)))(
                kmp[64:128, 8:16], Kf[64:128, :].rearrange("p (n t) -> p n t", t=256)), reads=[Kf[64:128, :]], writes=[kmp[64:128, 8:16]])
            pg = psb_f32(0, 256)
            MM([(pg[:, qt * 16:(qt + 1) * 16], Qf[:, qt * 128:(qt + 1) * 128], kmp, True, True) for qt in range(16)],
               reads=[Qf, kmp], writes=[pg])
            GM = A(256, F32); G2 = A(256, F32); EQ = A(256, F32); mx = A(32, F32); BI = A(256)
            g3 = lambda t: t.rearrange("p (m n) -> p m n", n=8)
            mxb = mx.unsqueeze(2).to_broadcast([128, 32, 8])

            def rmax(o, i):
                S.add('dve', (lambda o_, i_: (lambda e: e.tensor_reduce(out=o_, in_=i_, axis=AX.X, op=ALU.max)))(o, g3(i)),
                      reads=[i], writes=[o])
            TT('dve', GM, pg, pastneg, ALU.add)
            rmax(mx, GM)
            TT('dve', g3(EQ), g3(GM), mxb, ALU.is_ge)
            STT('dve', G2, EQ, NEG, GM, ALU.mult, ALU.add)
            rmax(mx, G2)
            TT('dve', g3(EQ), g3(G2), mxb, ALU.is_ge)
            STT('dve', G2, EQ, NEG, G2, ALU.mult, ALU.add)
            rmax(mx, G2)
            TT('dve', g3(EQ), g3(GM), mxb, ALU.is_ge)
            TT('dve', EQ, EQ, ownpos, ALU.max)
            TS('dve', BI, EQ, -1.0, -NEG, ALU.add, ALU.mult)
            pbt = psb_bf(1, 1024)
            pbt2 = psb_bf(2, 1024)
            TRS([((pbt if qt < 8 else pbt2)[0:16, (qt % 8) * 128:(qt % 8 + 1) * 128], BI[:, qt * 16:(qt + 1) * 16], ident)
                 for qt in range(16)], reads=[BI, ident], writes=[pbt, pbt2])
            BT_ = A(2048)
            CP('act', BT_[0:16, 0:1024], pbt[0:16, :])
            CP('act', BT_[0:16, 1024:2048], pbt2[0:16, :])
            DMA('sp', QA[64:72, :], BT_[0:8, :], reads=[BT_[0:8, :]], writes=[QA[64:72, :]])
            DMA('sp', QB[0:8, :], BT_[8:16, :], reads=[BT_[8:16, :]], writes=[QB[0:8, :]])
            pro = S.capture
            S.capture = None
            S.merge_streams([pro, nparts[0]])
            PT = [A(512) for _ in range(4)]
            spring = [psb_f32(0), psb_f32(1), PSA[:, 1024:1536], PSA[:, 1536:2048]]
            RS = A(512, F32)
            RW = A(512, F32)
            YO = A(512, F32)
            YB = A(2048)
            VA3 = VA.rearrange("p (t f) -> p t f", f=128)
            VB3 = VB.rearrange("p (t f) -> p t f", f=128)
            OpA = psb_f32(2)
            OpB = psb_f32(3)
            allu = [(QT, h, kt) for QT in range(4) for kt in range(4 * QT + 4) for h in range(2)]
            pw = PSA[:, 0:512]

            def qk(gi):
                QT, h, kt = allu[gi]
                q0 = QT * 512
                Kh = KA if h == 0 else KB
                Qh = QA if h == 0 else QB
                sp_ = spring[gi % 4]
                diag = kt >= 4 * QT
                items = [(sp_, Kh[:, kt * 128:(kt + 1) * 128], Qh[:, q0:q0 + 512], True, not diag)]
                rds = [Kh[:, kt * 128:(kt + 1) * 128], Qh[:, q0:q0 + 512]]
                if diag:
                    items.append((sp_, ident, cmask[:, kt - 4 * QT, :], False, True))
                    rds += [ident, cmask[:, kt - 4 * QT, :]]
                MM(items, reads=rds, writes=[sp_])
            qk(0)
            qk(1)
            for gi, (QT, h, kt) in enumerate(allu):
                q0 = QT * 512
                nkt = 4 * QT + 4
                if gi + 2 < len(allu):
                    qk(gi + 2)
                sp_ = spring[gi % 4]
                pt = PT[gi % 4]
                ACT(pt, sp_, AF.Exp, scale=0.125)
                Vh = VA3 if h == 0 else VB3
                Oh = OpA if h == 0 else OpB
                MM([(Oh, Vh[:, kt, :], pt, kt == 0, kt == nkt - 1)], reads=[Vh[:, kt, :], pt], writes=[Oh])
                if kt == nkt - 1 and h == 1:
                    ACT(RS[64:128, :], OpA[64:128, :], AF.Ln)
                    ACT(RS[0:64, :], OpB[0:64, :], AF.Ln)
                    ACT(RS, RS, AF.Exp, scale=-1.0)
                    MM([(pw, swapP, RS, True, True)], reads=[swapP, RS], writes=[pw])
                    CP('act', RW, pw)
                    TT('dve', YO[0:64, :], OpA[0:64, :], RW[0:64, :], ALU.mult)
                    TT('dve', YO[64:128, :], OpB[64:128, :], RW[64:128, :], ALU.mult)
                    TT('dve', YB[:, q0:q0 + 512], YO, SZ[:, q0:q0 + 512], ALU.mult)
            for QT in range(4):
                S.commit(nparts[QT + 1])
            out_dmas.append(DMA('sp', yb_d[hp], YB, reads=[YB], writes=[('DR', 'yb', hp, hp + 1, 0, 1, False)]))

        if n_hp_m > 0:
            epsb = AR_.a(2, F32)
            MEMSET('pool', epsb, 1e-6)
            ppc_eps = epsb[:, 0:1]
            m_base = AR_.off
        for hp in range(n_hp_m):
            moba_headpair(hp, n_hp_r + hp, nxt_stream(n_hp_r + hp))

        if do_final:
            for th in range(2):
                AR_.reset()
                t0 = th * 1024
                YAh = RAWT[0][:, :]; YBh = RAWT[1][:, :]
                YA3 = YAh.rearrange("p (k t) -> p k t", t=1024)
                YB3 = YBh.rearrange("p (k t) -> p k t", t=1024)
                for k in range(8):
                    DMA('sp', YA3[:, k, :], ya_d[k, :, t0:t0 + 1024], reads=[('DR', 'ya', k, k + 1, 0, 1, False)], writes=[YA3[:, k, :]])
                    DMA('sp', YB3[:, k, :], yb_d[k, :, t0:t0 + 1024], reads=[('DR', 'yb', k, k + 1, 0, 1, False)], writes=[YB3[:, k, :]])
                MG = AR_.a(16 * 1024)
                MG3 = MG.rearrange("p (k t) -> p k t", t=1024)
                sga = AR_.a(1024); sgb = AR_.a(1024); m1 = AR_.a(1024, F32)
                fj = []
                for c in range(16):
                    fj.append(('in', COL['ga'] + c * 128))
                    fj.append(('pa', c * 128))
                    fj.append(('in', COL['gb'] + c * 128))
                    fj.append(('pb', c * 128))
                fq = []
                fst = {'i': 0}

                def next_fw():
                    while fst['i'] < len(fj) and len(fq) < NW:
                        kind, col = fj[fst['i']]
                        fst['i'] += 1
                        if kind == 'in':
                            fq.append(wload_in(col))
                        elif kind == 'pa':
                            fq.append(wload_proj(wpa_d, col))
                        else:
                            fq.append(wload_proj(wpb_d, col))
                    return fq.pop(0)

                def run_half_proj(wbuf, nk, src3, evac):
                    acc = PSA[:, (run_half_proj.n % 2) * 1024:(run_half_proj.n % 2 + 1) * 1024]
                    run_half_proj.n += 1
                    items = []
                    for k in range(nk):
                        for tt in range(2):
                            items.append((acc[:, tt * 512:(tt + 1) * 512], wbuf[:, k, :], src3(k, tt), k == 0, k == nk - 1))
                    MM(items, reads=[wbuf[:, 0:nk, :]] + run_half_proj.rd, writes=[acc])
                    evac(acc)
                run_half_proj.n = 0
                for c in range(16):
                    run_half_proj.rd = [hT[:, :, t0:t0 + 1024]]
                    run_half_proj(next_fw(), KC, lambda k, tt: hT[:, k, t0 + tt * 512:t0 + (tt + 1) * 512],
                                  lambda acc: ACT(sga, acc, AF.Sigmoid))
                    run_half_proj.rd = [YAh]
                    run_half_proj(next_fw(), 8, lambda k, tt: YA3[:, k, tt * 512:(tt + 1) * 512],
                                  lambda acc: TT('dve', m1, acc, sga, ALU.mult))
                    run_half_proj.rd = [hT[:, :, t0:t0 + 1024]]
                    run_half_proj(next_fw(), KC, lambda k, tt: hT[:, k, t0 + tt * 512:t0 + (tt + 1) * 512],
                                  lambda acc: ACT(sgb, acc, AF.Sigmoid))
                    run_half_proj.rd = [YBh]

                    def ev_b(acc, c=c):
                        TT('dve', sgb, acc, sgb, ALU.mult)
                        TT('dve', MG3[:, c, :], m1, sgb, ALU.add)
                    run_half_proj(next_fw(), 8, lambda k, tt: YB3[:, k, tt * 512:(tt + 1) * 512], ev_b)
                WO = [AR_.a(16 * 512) for _ in range(2)]
                XR = [AR_.a(512, F32) for _ in range(1)]
                OT = [AR_.a(512, F32) for _ in range(1)]
                n_o = 0
                for c4 in range(4):
                    wo = WO[c4 % 2]
                    wo3 = wo.rearrange("p (k m) -> p k m", m=512)
                    DMA('pool', wo3, wo_d[:, c4 * 512:(c4 + 1) * 512].rearrange("(kc p) m -> p kc m", p=128), writes=[wo])
                    for tl in range(8):
                        tok0 = t0 + tl * 128
                        xr = XR[0]
                        ot = OT[0]
                        acc = PSB[:, (n_o % 4) * 512:(n_o % 4 + 1) * 512]
                        n_o += 1
                        DMA('sp', xr, x_d[tok0:tok0 + 128, c4 * 512:(c4 + 1) * 512], writes=[xr])
                        MM([(acc, MG3[:, k, tl * 128:(tl + 1) * 128], wo3[:, k, :], k == 0, k == 15) for k in range(16)],
                           reads=[MG, wo], writes=[acc])
                        TT('dve', ot, acc, xr, ALU.add)
                        out_dmas.append(DMA('sp', out_d[tok0:tok0 + 128, c4 * 512:(c4 + 1) * 512], ot, reads=[ot]))

        cnt = S.emit(final_wait_ops=out_dmas)
    return nc, cnt, len(S.ops)


def _consts():
    bf = ml_dtypes.bfloat16
    cbv = np.zeros((128, NCB), np.float32)
    idx = np.arange(128)
    cbv[:, CB['ident']:CB['ident'] + 128] = np.eye(128)
    same = (idx[:, None] // 64) == (idx[None, :] // 64)
    cbv[:, CB['blk2']:CB['blk2'] + 128] = same
    cbv[:, CB['m_su']:CB['m_su'] + 128] = same & (idx[:, None] < idx[None, :])
    cbv[:, CB['m_iu']:CB['m_iu'] + 128] = same & (idx[:, None] <= idx[None, :])
    cbv[:, CB['m_sl']:CB['m_sl'] + 128] = same & (idx[:, None] > idx[None, :])
    cbv[:, CB['onesA']:CB['onesA'] + 64] = 1.0
    cbv[:, CB['onesB'] + 64:CB['onesB'] + 128] = 1.0
    cm = np.zeros((128, 4, 512), np.float32)
    for ktl in range(4):
        kpos = ktl * 128 + idx[:, None]
        qpos = np.arange(512)[None, :]
        kb = kpos // 256
        qb = qpos // 256
        ok = np.where(kb == qb, kpos <= qpos, kb < qb)
        cm[:, ktl, :] = np.where(ok, 0.0, NEG)
    cbv[:, CB['cmask']:CB['cmask'] + 2048] = cm.reshape(128, 2048)
    cfv = np.zeros((128, NCF), np.float32)
    sm = np.ones(512, np.float32)
    sm[::64] = 0.0
    cfv[:, CF['scanmask']:CF['scanmask'] + 512] = sm[None, :]
    pn = np.zeros((16, 2, 8), np.float32)
    op = np.zeros((16, 2, 8), np.float32)
    for qt in range(16):
        qb = qt // 2
        for n in range(8):
            pn[qt, :, n] = 0.0 if n < qb else NEG
            op[qt, :, n] = 1.0 if n >= qb else 0.0
    cfv[:, CF['pastneg']:CF['pastneg'] + 256] = pn.reshape(1, 256)
    cfv[:, CF['ownpos']:CF['ownpos'] + 256] = op.reshape(1, 256)
    cfv[:, CF['swapP']:CF['swapP'] + 128] = np.roll(np.eye(128, dtype=np.float32), 64, axis=1)
    ind = np.zeros((8, S_TOK), np.float32)
    for n in range(8):
        ind[n, n * 256:(n + 1) * 256] = 1.0
    return cbv.astype(bf), cfv, ind.astype(bf)


def _pack_params(i):
    ppv = np.zeros((128, NPP), np.float32)

    def fm(v):
        return np.ascontiguousarray(v.reshape(-1, 128).T)
    for nm in ('mu_r', 'mu_k', 'mu_v', 'w0', 'a0', 'k_k', 'k_a', 'r_k', 'gn_w', 'gn_b'):
        ppv[:, PP[nm]:PP[nm] + 8] = fm(i[nm][0])
    ppv[:, PP['norm_w']:PP['norm_w'] + 16] = fm(i['norm_w'][0])
    ppv[0:64, PP['mu_wa']] = i['mu_w'][0]
    ppv[64:128, PP['mu_wa']] = i['mu_a'][0]
    ppv[0:64, PP['qnw']] = i['q_norm_w'][0]
    ppv[64:128, PP['qnw']] = i['q_norm_w'][0]
    ppv[0:64, PP['knw']] = i['k_norm_w'][0]
    ppv[64:128, PP['knw']] = i['k_norm_w'][0]
    lora = np.concatenate([i['w_decay_up'][0], i['w_iclr_up'][0]], axis=0).astype(np.float32)
    return ppv, np.ascontiguousarray(lora)


_CACHE = {}


def make_in_maps(inputs, n_cores=8):
    i = {k: np.asarray(v) for k, v in inputs.items()}
    cbv, cfv, ind = _consts()
    ppv, lora = _pack_params(i)
    shared = dict(w_in=np.ascontiguousarray(i['w_in'][0]), w_pa=np.ascontiguousarray(i['w_proj_rwkv'][0]),
                  w_pb=np.ascontiguousarray(i['w_proj_moba'][0]), w_out=np.ascontiguousarray(i['w_out'][0]),
                  lora=lora, pp=ppv, cb=cbv, cf=cfv, ind=ind)
    maps = []
    for c in range(n_cores):
        m = dict(shared)
        m['x'] = np.ascontiguousarray(i['x'][c])
        maps.append(m)
    return maps


def kernel(**inputs):
    if 'nc' not in _CACHE:
        _CACHE['nc'] = build()[0]
    nc = _CACHE['nc']
    maps = make_in_maps(inputs, 8)
    res = run_bass_kernel_spmd(nc, maps, core_ids=list(range(8)))
    out = np.stack([np.asarray(r['out']) for r in res.results], axis=0)
    return out.astype(np.float32)
```

```python
import contextlib
import numpy as np
import ml_dtypes
import concourse.bass as bass
import concourse.mybir as mybir
from concourse.bass_utils import run_bass_kernel_spmd

F32 = mybir.dt.float32
BF16 = mybir.dt.bfloat16
AF = mybir.ActivationFunctionType
ALU = mybir.AluOpType
AX = mybir.AxisListType

S_TOK = 2048
D = 2048
KC = 16
IN_COLS = 12416
COL = dict(r=0, k=1024, v=2048, za=3072, wd=4096, q=4224, kq=5248, vq=6272, zb=7296, ga=8320, gb=10368)
CDEC = 0.6065306597126334
NEG = -1.0e30
STOP = 99
NOMERGE = False
STRICT = True

PP = dict(mu_r=0, mu_k=8, mu_v=16, w0=24, a0=32, k_k=40, k_a=48, r_k=56, gn_w=64, gn_b=72, norm_w=80,
          mu_wa=96, qnw=97, knw=98)
NPP = 100
CB = dict(ident=0, blk2=128, m_su=256, m_iu=384, m_sl=512, onesA=640, onesB=768, cmask=896)
NCB = 896 + 4 * 512
CF = dict(scanmask=0, pastneg=512, ownpos=768, swapP=1024)
NCF = 1152


_DTSZ = {}


def _dtsize(dt):
    s = _DTSZ.get(dt)
    if s is None:
        name = str(dt)
        s = 4 if '32' in name else 2 if '16' in name else 1 if '8' in name else 8
        _DTSZ[dt] = s
    return s


def box_of(ap):
    dims = ap.ap
    sz = _dtsize(ap.dtype)
    pstep, pcnt = dims[0]
    off = int(ap.offset)
    if pstep == 0:
        pstep = 1 << 40
    p0 = off // pstep
    f0 = off % pstep
    ext = 0
    for st, cn in dims[1:]:
        ext += abs(st) * (cn - 1)
    f1 = f0 + ext + 1
    if 'PSUM' in str(ap.space).upper():
        b0 = (f0 * sz) // 2048
        b1 = ((f1 * sz) - 1) // 2048
        return ('PS', ap.name, 0, 128, b0 * 2048, (b1 + 1) * 2048, True)
    return ('SB', ap.name, p0, p0 + pcnt, f0 * sz, f1 * sz, False)


class Op:
    __slots__ = ('idx', 'eng', 'fn', 'deps', 'signal', 'count', 'is_dma', 'dsem', 'dcount', 'prev_slot')

    def __init__(self, idx, eng, fn, is_dma):
        self.idx = idx
        self.eng = eng
        self.fn = fn
        self.deps = {}
        self.signal = False
        self.count = 0
        self.is_dma = is_dma
        self.dsem = None
        self.dcount = 0
        self.prev_slot = None


class Sched:
    ENGS = ('pe', 'act', 'dve', 'pool', 'sp')

    def __init__(self, nc, n_dma_slots=10):
        self.nc = nc
        self.ops = []
        self.recs = {}
        self.n_dma_slots = n_dma_slots

    def _touch(self, op, box, is_write):
        kind, name, p0, p1, f0, f1, excl = box
        lst = self.recs.setdefault(name, [])
        found = None
        for r in lst:
            if r[0] < p1 and p0 < r[1] and r[2] < f1 and f0 < r[3]:
                if r[4] is not None:
                    if (not is_write) or excl:
                        op.deps[r[4]] = True
                    else:
                        op.deps.setdefault(r[4], False)
                if is_write or excl:
                    for e, o in r[5].items():
                        op.deps.setdefault(o, False)
            if r[0] == p0 and r[1] == p1 and r[2] == f0 and r[3] == f1:
                found = r
        if found is None:
            found = [p0, p1, f0, f1, None, {}]
            lst.append(found)
        if is_write or excl:
            found[4] = op.idx
            found[5] = {}
        else:
            found[5][op.eng] = op.idx

    capture = None

    def add(self, eng, fn, reads=(), writes=(), dma=False, cost=0.5):
        if self.capture is not None:
            self.capture.append((eng, fn, list(reads), list(writes), dma, cost))
            return -1
        op = Op(len(self.ops), eng, fn, dma)
        self.ops.append(op)
        for ap in reads:
            if ap is not None and not isinstance(ap, (int, float)):
                self._touch(op, ap if isinstance(ap, tuple) else box_of(ap), False)
        for ap in writes:
            if ap is not None:
                self._touch(op, ap if isinstance(ap, tuple) else box_of(ap), True)
        op.deps.pop(op.idx, None)
        return op.idx

    def commit(self, lst):
        for it in lst:
            self.add(*it[:5])

    def merge_streams(self, streams):
        if NOMERGE:
            for st in streams:
                self.commit(st)
            return
        pos = [0] * len(streams)
        ready = [0.0] * len(streams)
        free = {e: 0.0 for e in self.ENGS}
        while True:
            best = None
            for si, st in enumerate(streams):
                if pos[si] >= len(st):
                    continue
                it = st[pos[si]]
                t = max(free[it[0]], ready[si])
                if best is None or t < best[0] - 1e-9:
                    best = (t, si)
            if best is None:
                break
            t, si = best
            it = streams[si][pos[si]]
            pos[si] += 1
            self.add(*it[:5])
            if it[4]:
                free[it[0]] = t + 0.06
                ready[si] = t + 0.06
            else:
                free[it[0]] = t + it[5]
                ready[si] = t + it[5] + 0.12

    def merge(self, main, bg):
        return self.merge_streams([main, bg])
        nb = len(bg)
        nm = max(len(main), 1)
        j = 0
        for i, it in enumerate(main):
            self.add(*it[:5])
            tgt = (i + 1) * nb // nm
            while j < tgt:
                self.add(*bg[j][:5])
                j += 1
        while j < nb:
            self.add(*bg[j])
            j += 1

    def emit(self, final_wait_ops=()):
        nc = self.nc
        ops = self.ops
        for op in ops:
            for d, raw in op.deps.items():
                dop = ops[d]
                if dop.is_dma:
                    continue
                if (not op.is_dma) and dop.eng == op.eng and (op.eng == 'pe' or (not raw and not STRICT)):
                    continue
                dop.signal = True
        for d in final_wait_ops:
            if not ops[d].is_dma:
                ops[d].signal = True
        cnt = {e: 0 for e in self.ENGS}
        for op in ops:
            if op.is_dma:
                continue
            if op.signal:
                cnt[op.eng] += 1
            op.count = cnt[op.eng]
        slot_state = {}
        for op in ops:
            if not op.is_dma:
                continue
            st = slot_state.setdefault(op.eng, {'next': 0, 'counts': [0] * self.n_dma_slots,
                                                'last': [None] * self.n_dma_slots})
            s = st['next']
            st['next'] = (s + 1) % self.n_dma_slots
            op.prev_slot = st['last'][s]
            st['counts'][s] += 16
            op.dsem = (op.eng, s)
            op.dcount = st['counts'][s]
            st['last'][s] = op.idx
        used = [e for e in self.ENGS if any(o.eng == e for o in ops)]
        if 'sp' not in used:
            used.append('sp')
        with contextlib.ExitStack() as es:
            sems = {e: es.enter_context(nc.semaphore('s_' + e)) for e in used}
            dsems = {}
            for e in slot_state:
                for s in range(self.n_dma_slots):
                    dsems[(e, s)] = es.enter_context(nc.semaphore('d_%s_%d' % (e, s)))
            block = es.enter_context(nc.Block())

            def run_engine(ename, eng):
                waited = {e: 0 for e in self.ENGS}
                dwaited = {}

                def wait_on(dop):
                    if dop.is_dma:
                        if dwaited.get(dop.dsem, 0) < dop.dcount:
                            eng.wait_ge(dsems[dop.dsem], dop.dcount)
                            dwaited[dop.dsem] = dop.dcount
                    elif dop.count > waited[dop.eng]:
                        eng.wait_ge(sems[dop.eng], dop.count)
                        waited[dop.eng] = dop.count

                for op in ops:
                    if op.eng != ename:
                        continue
                    for d in sorted(op.deps):
                        dop = ops[d]
                        raw = op.deps[d]
                        if (not dop.is_dma) and (not op.is_dma) and dop.eng == ename and (ename == 'pe' or (not raw and not STRICT)):
                            continue
                        wait_on(dop)
                    if op.is_dma and op.prev_slot is not None:
                        wait_on(ops[op.prev_slot])
                    ins = op.fn(eng)
                    if op.is_dma:
                        ins.then_inc(dsems[op.dsem], 16)
                    elif op.signal:
                        ins.then_inc(sems[ename], 1)
                if ename == 'sp':
                    for d in final_wait_ops:
                        wait_on(ops[d])

            @block.tensor
            def _(eng):
                run_engine('pe', eng)

            @block.scalar
            def _(eng):
                run_engine('act', eng)

            @block.vector
            def _(eng):
                run_engine('dve', eng)

            @block.gpsimd
            def _(eng):
                run_engine('pool', eng)

            @block.sync
            def _(eng):
                run_engine('sp', eng)
        return cnt


def build(n_hp_r=8, n_hp_m=8, do_final=True, dbg=False):
    nc = bass.Bass("TRN2", target_bir_lowering=False)
    x_d = nc.dram_tensor("x", [S_TOK, D], F32, kind="ExternalInput").ap()
    win_d = nc.dram_tensor("w_in", [D, IN_COLS], F32, kind="ExternalInput").ap()
    wpa_d = nc.dram_tensor("w_pa", [1024, D], F32, kind="ExternalInput").ap()
    wpb_d = nc.dram_tensor("w_pb", [1024, D], F32, kind="ExternalInput").ap()
    wo_d = nc.dram_tensor("w_out", [D, D], F32, kind="ExternalInput").ap()
    lora_d = nc.dram_tensor("lora", [128, 1024], F32, kind="ExternalInput").ap()
    pp_d = nc.dram_tensor("pp", [128, NPP], F32, kind="ExternalInput").ap()
    cb_d = nc.dram_tensor("cb", [128, NCB], BF16, kind="ExternalInput").ap()
    cf_d = nc.dram_tensor("cf", [128, NCF], F32, kind="ExternalInput").ap()
    ind_d = nc.dram_tensor("ind", [8, S_TOK], BF16, kind="ExternalInput").ap()
    out_d = nc.dram_tensor("out", [S_TOK, D], F32, kind="ExternalOutput").ap()
    scr_kind = "ExternalOutput" if dbg else "Internal"
    ya_d = nc.dram_tensor("ya_scr", [8, 128, S_TOK], BF16, kind=scr_kind).ap()
    yb_d = nc.dram_tensor("yb_scr", [8, 128, S_TOK], BF16, kind=scr_kind).ap()

    with contextlib.ExitStack() as es:
        def sb(name, shape, dt):
            return es.enter_context(nc.sbuf_tensor(name, shape, dt))

        hT = sb("hT", [128, KC, S_TOK], BF16)
        NW = 4
        wpool = [sb("wp%d" % i, [128, KC, 128], BF16) for i in range(NW)]
        cb = sb("cb_s", [128, NCB], BF16)
        cf = sb("cf_s", [128, NCF], F32)
        pp = sb("pp_s", [128, NPP], F32)
        loraD = sb("loraD_s", [128, 1024], BF16)
        loraI = sb("loraI_s", [128, 1024], BF16)
        TL = sb("TL", [128, S_TOK], BF16)
        RAWT = [sb("rawt%d" % s, [128, 4 * S_TOK], BF16) for s in range(2)]
        RAW = [[RAWT[s][:, j * S_TOK:(j + 1) * S_TOK] for j in range(4)] for s in range(2)]
        ARENA_N = 39424
        arena_t = sb("arena", [128, ARENA_N], BF16)
        PSA = es.enter_context(nc.psum_tensor("PSA", [128, 2048], F32))
        PSB = es.enter_context(nc.psum_tensor("PSB", [128, 2048], F32))

        S = Sched(nc)
        out_dmas = []

        class Arena:
            def __init__(self):
                self.off = 0

            def reset(self):
                self.off = 0

            def a(self, n, dt=BF16):
                nb = n * (2 if dt == F32 else 1)
                nb = (nb + 1) // 2 * 2
                assert self.off + nb <= ARENA_N, (self.off, nb)
                v = arena_t[:, self.off:self.off + nb]
                self.off += nb
                if dt == F32:
                    v = v.bitcast(F32)
                return v

        AR_ = Arena()

        def rd(*aps):
            return [a for a in aps if a is not None and not isinstance(a, (int, float))]

        def fsz(ap):
            n = 1
            for st, cn in ap.ap[1:]:
                n *= cn
            return n

        def ACT(out, in_, func, bias=None, scale=None, accum=None):
            kw = {}
            if bias is not None:
                kw['bias'] = bias
            if scale is not None:
                kw['scale'] = scale
            if accum is not None:
                kw['accum_out'] = accum
            return S.add('act', lambda e: e.activation(out=out, in_=in_, func=func, **kw),
                         reads=rd(in_, bias, scale), writes=[out, accum], cost=0.22 + fsz(out) / 1200.0)

        def TT(eng, out, in0, in1, op):
            return S.add(eng, lambda e: e.tensor_tensor(out=out, in0=in0, in1=in1, op=op),
                         reads=rd(in0, in1), writes=[out], cost=0.1 + fsz(out) / 960.0)

        def TS(eng, out, in0, s1, s2, op0, op1=None):
            if op1 is None:
                return S.add(eng, lambda e: e.tensor_scalar(out=out, in0=in0, scalar1=s1, scalar2=None, op0=op0),
                             reads=rd(in0, s1), writes=[out], cost=0.1 + fsz(out) / 960.0)
            return S.add(eng, lambda e: e.tensor_scalar(out=out, in0=in0, scalar1=s1, scalar2=s2, op0=op0, op1=op1),
                         reads=rd(in0, s1, s2), writes=[out], cost=0.1 + fsz(out) / 960.0)

        def STT(eng, out, in0, scalar, in1, op0, op1):
            eng = 'dve'
            return S.add(eng, lambda e: e.scalar_tensor_tensor(out=out, in0=in0, scalar=scalar, in1=in1,
                                                                op0=op0, op1=op1),
                         reads=rd(in0, scalar, in1), writes=[out], cost=0.1 + fsz(out) / 960.0)

        def CP(eng, out, in_):
            if eng == 'act':
                return S.add('act', lambda e: e.copy(out=out, in_=in_), reads=[in_], writes=[out],
                             cost=0.22 + fsz(out) / 1200.0)
            return S.add(eng, lambda e: e.tensor_copy(out=out, in_=in_), reads=[in_], writes=[out],
                         cost=0.1 + fsz(out) / 960.0)

        def MEMSET(eng, out, val):
            return S.add(eng, lambda e: e.memset(out, val), writes=[out], cost=1.0)

        def MM(items, reads, writes):
            def fn(e):
                ins = None
                for (o, l, r, st, sp) in items:
                    ins = e.matmul(o, lhsT=l, rhs=r, start=st, stop=sp)
                return ins
            c = 0.05
            for (o, l, r, st, sp) in items:
                c += max(0.065, fsz(o) / (600.0 if l.dtype == F32 else 2400.0))
            return S.add('pe', fn, reads=reads, writes=writes, cost=c)

        def TRS(items, reads, writes):
            def fn(e):
                ins = None
                for (o, i, idn) in items:
                    ins = e.transpose(o, i, idn)
                return ins
            return S.add('pe', fn, reads=reads, writes=writes, cost=0.05 + 0.11 * len(items))

        def DMA(q, out, in_, reads=(), writes=()):
            return S.add(q, lambda e: e.dma_start(out=out, in_=in_), reads=reads, writes=writes, dma=True)

        def ppc(name, j=0):
            c = PP[name] + j
            return pp[:, c:c + 1]

        ident = cb[:, CB['ident']:CB['ident'] + 128]
        blk2 = cb[:, CB['blk2']:CB['blk2'] + 128]
        m_su = cb[:, CB['m_su']:CB['m_su'] + 128]
        m_iu = cb[:, CB['m_iu']:CB['m_iu'] + 128]
        m_sl = cb[:, CB['m_sl']:CB['m_sl'] + 128]
        onesA = cb[:, CB['onesA']:CB['onesA'] + 128]
        onesB = cb[:, CB['onesB']:CB['onesB'] + 128]
        cmask = cb[:, CB['cmask']:CB['cmask'] + 2048].rearrange("p (a b) -> p a b", b=512)
        scanmask = cf[:, CF['scanmask']:CF['scanmask'] + 512]
        pastneg = cf[:, CF['pastneg']:CF['pastneg'] + 256]
        ownpos = cf[:, CF['ownpos']:CF['ownpos'] + 256]
        swapP = cf[:, CF['swapP']:CF['swapP'] + 128]

        def psb_f32(bank, n=512, off=0):
            return PSB[:, bank * 512 + off: bank * 512 + off + n]

        def psb_bf(bank, n=1024, off=0):
            v = PSB[:, bank * 512:(bank + 1) * 512].bitcast(BF16)
            return v[:, off:off + n]

        def psb2_bf(bank):
            return PSB[:, bank * 512:(bank + 2) * 512].bitcast(BF16)

        DMA('sp', cb[:], cb_d, writes=[cb[:]])
        DMA('sp', cf[:], cf_d, writes=[cf[:]])
        DMA('sp', pp[:], pp_d, writes=[pp[:]])
        S.add('pool', lambda e: e.memset(loraD[:], 0.0), writes=[loraD[:]])
        S.add('pool', lambda e: e.memset(loraI[:], 0.0), writes=[loraI[:]])
        DMA('pool', loraD[0:64, :], lora_d[0:64, :], writes=[loraD[0:64, :]])
        DMA('pool', loraI[64:128, :], lora_d[64:128, :], writes=[loraI[64:128, :]])

        wstate = {'n': 0}

        def wload_in(col):
            b = wpool[wstate['n'] % NW]
            wstate['n'] += 1
            src = win_d[:, col:col + 128].rearrange("(kc p) m -> p kc m", p=128)
            DMA('pool', b[:], src, writes=[b[:]])
            return b

        def wload_proj(wd, col):
            b = wpool[wstate['n'] % NW]
            wstate['n'] += 1
            src = wd[:, col:col + 128].rearrange("(kc p) m -> p kc m", p=128)
            DMA('pool', b[:, 0:8, :], src, writes=[b[:, 0:8, :]])
            return b

        def inproj(wbuf, nk, rhs_src, evac):
            for half in range(2):
                acc = PSA[:, 0:1024]
                items = []
                for k in range(nk):
                    for tt in range(2):
                        t0 = half * 1024 + tt * 512
                        items.append((acc[:, tt * 512:(tt + 1) * 512], wbuf[:, k, :], rhs_src[:, k, t0:t0 + 512],
                                      k == 0, k == nk - 1))
                for i0 in range(0, len(items), 8):
                    MM(items[i0:i0 + 8], reads=[wbuf[:, 0:nk, :], rhs_src[:, 0:nk, half * 1024:(half + 1) * 1024]], writes=[acc])
                evac(acc, half)

        AR_.reset()
        xt = [AR_.a(2048, F32) for _ in range(4)]
        hb = [AR_.a(2048) for _ in range(2)]
        junk = AR_.a(2048)
        ssb = AR_.a(16, F32)
        rsb = AR_.a(16, F32)
        nwb = pp[:, PP['norm_w']:PP['norm_w'] + 16].unsqueeze(2).to_broadcast([128, 16, 128])
        eps0 = AR_.a(2, F32)
        MEMSET('pool', eps0, 1e-6)

        def p0_stage1(tt):
            xtile = xt[tt % 4]
            if tt + 2 < 16:
                DMA('sp', xt[(tt + 2) % 4], x_d[(tt + 2) * 128:(tt + 3) * 128, :], writes=[xt[(tt + 2) % 4]])
            ACT(junk, xtile, AF.Square, accum=ssb[:, tt:tt + 1])
            ACT(rsb[:, tt:tt + 1], ssb[:, tt:tt + 1], AF.Ln, bias=eps0[:, 0:1], scale=1.0 / D)
            ACT(rsb[:, tt:tt + 1], rsb[:, tt:tt + 1], AF.Exp, scale=-0.5)
            TS('dve', hb[tt % 2], xtile, rsb[:, tt:tt + 1], None, ALU.mult)

        def p0_stage2(tt):
            pst = psb2_bf((tt % 2) * 2)
            TRS([(pst[:, k * 128:(k + 1) * 128], hb[tt % 2][:, k * 128:(k + 1) * 128], ident) for k in range(16)],
                reads=[hb[tt % 2], ident], writes=[pst])
            TT('dve', hT[:, :, tt * 128:(tt + 1) * 128], pst.rearrange("p (a b) -> p a b", b=128), nwb, ALU.mult)

        def cap0(fn, *args):
            S.capture = []
            fn(*args)
            lst = S.capture
            S.capture = None
            return lst

        for t_ in range(2):
            DMA('sp', xt[t_], x_d[t_ * 128:(t_ + 1) * 128, :], writes=[xt[t_]])
        for tt in range(17):
            streams = []
            if tt >= 1:
                streams.append(cap0(p0_stage2, tt - 1))
            if tt < 16:
                streams.append(cap0(p0_stage1, tt))
            S.merge_streams(streams)

        jobs = []
        jobs_seq = [('R', hp) for hp in range(n_hp_r)] + [('M', hp) for hp in range(n_hp_m)]
        for hp in range(n_hp_r):
            for nm in ('r', 'k', 'v', 'za'):
                jobs.append(('R', hp, nm, COL[nm] + hp * 128))
            if hp == 0:
                jobs.append(('R', 0, 'wd', COL['wd']))
        for hp in range(n_hp_m):
            for nm in ('q', 'kq', 'vq', 'zb'):
                jobs.append(('M', hp, nm, COL[nm] + hp * 128))
        PREF = NW - 1
        wq = []
        jstate = {'issued': 0}

        def next_w():
            while jstate['issued'] < len(jobs) and len(wq) < PREF + 1:
                wq.append(wload_in(jobs[jstate['issued']][3]))
                jstate['issued'] += 1
            return wq.pop(0)

        def ev_copy(dst, eng):
            def f(acc, half):
                CP(eng, dst[:, half * 1024:(half + 1) * 1024], acc)
            return f

        def ev_silu(dst):
            def f(acc, half):
                ACT(dst[:, half * 1024:(half + 1) * 1024], acc, AF.Silu)
            return f

        def inproj_job(ji):
            kind, hp = jobs_seq[ji]
            R0, R1, R2, R3 = RAW[ji % 2]
            inproj(next_w(), KC, hT, ev_copy(R0, 'act'))
            inproj(next_w(), KC, hT, ev_copy(R1, 'dve'))
            inproj(next_w(), KC, hT, ev_copy(R2, 'act'))
            inproj(next_w(), KC, hT, ev_silu(R3))

        def lora_prep():
            AR_.reset()
            lraw = AR_.a(2048)
            ld = AR_.a(2048)
            inproj(next_w(), KC, hT, ev_copy(lraw, 'dve'))
            TT('dve', ld[:, 1:2048], lraw[:, 0:2047], lraw[:, 1:2048], ALU.subtract)
            TS('dve', ld[:, 0:1], lraw[:, 0:1], -1.0, None, ALU.mult)
            STT('dve', TL[:], ld, ppc('mu_wa'), lraw, ALU.mult, ALU.add)
            ACT(TL[0:64, :], TL[0:64, :], AF.Tanh)

        def split_parts(lst, fracs):
            tot = float(sum(fracs))
            out = []
            acc = 0.0
            i0 = 0
            for f in fracs:
                acc += f
                i1 = int(round(len(lst) * acc / tot))
                out.append(lst[i0:i1])
                i0 = i1
            out[-1] = out[-1] + lst[i0:]
            return out

        def cap(fn, *args):
            S.capture = []
            fn(*args)
            lst = S.capture
            S.capture = None
            return lst

        def rwkv_all():
            AR_.reset()
            A = AR_.a
            GT = 256
            NG = S_TOK // GT
            Sf = [A(128, F32) for _ in range(2)]
            Sbf = A(128)
            Ssc = A(128, F32)
            Stmp = A(128, F32)
            Zsb = A(128)
            Usb = A(128)
            tinyb_t = A(2, F32)
            YAg = [A(GT) for _ in range(2)]
            ARfs = [A(2 * GT, F32) for _ in range(2)]
            KTfs = [A(GT, F32) for _ in range(2)]
            BTfs = [A(GT, F32) for _ in range(2)]
            KTbs = [A(GT) for _ in range(2)]
            BTbs = [A(GT) for _ in range(2)]
            vgs = [A(GT) for _ in range(2)]
            WCs = [A(4, F32) for _ in range(4)]
            BONs = [A(GT) for _ in range(4)]
            ARb = [A(2 * GT) for _ in range(2)]
            TM = [A(3 * GT) for _ in range(2)]
            AKs = [A(2 * GT) for _ in range(2)]
            RKs = [A(2 * GT) for _ in range(2)]
            RBs = [A(2 * GT) for _ in range(2)]
            TTs = [A(2 * GT) for _ in range(2)]
            Ytms = [A(GT, F32) for _ in range(2)]
            mean = A(4, F32); var = A(4, F32)
            YC = A(GT, F32); YQ = A(GT, F32); YN = A(GT)
            Dm = A(GT, F32); SQ = A(GT); PROD = A(GT)
            omu = A(4, F32)
            rg = A(GT, F32); kg = A(GT, F32); SG = A(GT, F32); CS = A(GT, F32); ag = A(GT, F32)
            E1 = A(GT, F32); E2 = A(GT, F32); E3 = A(GT, F32)
            KK = A(GT, F32); RI = A(GT, F32); Tt = A(GT, F32); K2 = A(GT, F32); Bb = A(GT, F32)
            N0s = A(2 * GT); X0s = A(2 * GT)
            Ns = [A(2 * GT) for _ in range(2)]
            Xs = [A(2 * GT) for _ in range(2)]
            Ps = [A(2 * GT) for _ in range(2)]
            MEMSET('pool', tinyb_t, 1e-12)
            tinyb = tinyb_t[:, 0:1]
            NT = GT // 128
            NM = NT * 2
            PA2 = PSA[:, 1024:1536]
            PA3 = PSA[:, 1536:2048]

            def v3(ap):
                return ap.rearrange("p (j t) -> p j t", t=128)

            def emit_A(hp, g):
                Rr, Rk, Rv, SZ = RAW[hp % 2]
                if g == 0:
                    for mi, mu in enumerate(('mu_r', 'mu_k', 'mu_v')):
                        TS('dve', omu[:, mi:mi + 1], ppc(mu, hp), -1.0, 1.0, ALU.mult, ALU.add)
                c0 = g * GT
                par = g % 2
                ARf = ARfs[par]; KTf = KTfs[par]; BTf = BTfs[par]; vg = vgs[par]
                ARfv = ARf.rearrange("p (j w t) -> p j w t", w=2, t=128)
                for mi, (raw, mu, dst) in enumerate(((Rr, 'mu_r', rg), (Rk, 'mu_k', kg), (Rv, 'mu_v', vg))):
                    if g == 0:
                        ACT(Dm[:, 1:GT], raw[:, 0:GT - 1], AF.Identity, scale=ppc(mu, hp))
                        TS('dve', Dm[:, 0:1], raw[:, 0:1], 0.0, None, ALU.mult)
                    else:
                        ACT(Dm, raw[:, c0 - 1:c0 + GT - 1], AF.Identity, scale=ppc(mu, hp))
                    STT('dve', dst, raw[:, c0:c0 + GT], omu[:, mi:mi + 1], Dm, ALU.mult, ALU.add)
                pu = PA2[:, 0:GT]
                pa_ = PA2[:, GT:2 * GT]
                MM([(pu, loraD[:, hp * 128:(hp + 1) * 128], TL[:, c0:c0 + GT], True, True)],
                   reads=[loraD[:, hp * 128:(hp + 1) * 128], TL[:, c0:c0 + GT]], writes=[pu])
                MM([(pa_, loraI[:, hp * 128:(hp + 1) * 128], TL[:, c0:c0 + GT], True, True)],
                   reads=[loraI[:, hp * 128:(hp + 1) * 128], TL[:, c0:c0 + GT]], writes=[pa_])
                ACT(SG, pu, AF.Sigmoid, bias=ppc('w0', hp))
                ACT(ag, pa_, AF.Sigmoid, bias=ppc('a0', hp))
                S.add('dve', (lambda o, m, d1: (lambda e: e.tensor_tensor_scan(out=o, data0=m, data1=d1, initial=0.0,
                                                                                op0=ALU.mult, op1=ALU.add)))(CS, scanmask[:, 0:GT], SG),
                      reads=[scanmask[:, 0:GT], SG], writes=[CS], cost=0.1 + 2 * GT / 960.0)
                ACT(E1, CS, AF.Exp, scale=-CDEC)
                ACT(WCs[g % 4], CS.rearrange("p (c t) -> p c t", t=64)[:, :, 63], AF.Exp, scale=-CDEC)
                ACT(E3, CS, AF.Exp, scale=CDEC)
                TT('dve', SG, CS, SG, ALU.subtract)
                ACT(E2, SG, AF.Exp, scale=-CDEC)
                ACT(KK, kg, AF.Identity, scale=ppc('k_k', hp))
                ACT(SQ, kg, AF.Square, scale=ppc('k_k', hp))
                pk = PA2[:, 0:GT]
                MM([(pk, blk2, SQ, True, True)], reads=[blk2, SQ], writes=[pk])
                ACT(RI, pk, AF.Ln, bias=tinyb)
                ACT(RI, RI, AF.Exp, scale=-0.5)
                TT('dve', KK, KK, RI, ALU.mult)
                TS('dve', Tt, ag, -1.0, ppc('k_a', hp), ALU.add, ALU.mult)
                STT('dve', K2, Tt, 1.0, kg, ALU.add, ALU.mult)
                TT('dve', Bb, KK, ag, ALU.mult)
                STT('dve', PROD, rg, ppc('r_k', hp), K2, ALU.mult, ALU.mult)
                pb_ = PA2[:, GT:2 * GT]
                MM([(pb_, blk2, PROD, True, True)], reads=[blk2, PROD], writes=[pb_])
                TT('dve', BONs[g % 4], pb_, vg, ALU.mult)
                TT('dve', ARfv[:, :, 1, :], v3(rg), v3(E1), ALU.mult)
                STT('dve', ARfv[:, :, 0, :], v3(KK), -1.0, v3(E2), ALU.mult, ALU.mult)
                TT('dve', KTf, K2, E3, ALU.mult)
                TT('dve', BTf, Bb, E3, ALU.mult)
                CP('act', KTbs[par], KTf)
                CP('act', BTbs[par], BTf)

            def emit_B(hp, g):
                par = g % 2
                ARf = ARfs[par]; KTf = KTfs[par]; BTf = BTfs[par]; vg = vgs[par]
                KTb = KTbs[par]; BTb = BTbs[par]
                CP('act', ARb[par], ARf)
                ptm = PA3.bitcast(BF16)[:, 0:3 * GT]
                items = []
                for q, src in enumerate((vg, KTb, BTb)):
                    for j in range(NT):
                        items.append((ptm[:, (q * NT + j) * 128:(q * NT + j + 1) * 128], src[:, j * 128:(j + 1) * 128], ident))
                TRS(items, reads=[vg, KTb, BTb, ident], writes=[ptm])
                CP('act', TM[par], ptm)
                for j in range(NT):
                    items = []
                    for h in range(2):
                        hr = slice(h * 64, h * 64 + 64)
                        rhs_ar = ARf[hr, j * 256:(j + 1) * 256]
                        bh = psb_f32(h)
                        items.append((bh[:, 0:256], KTf[hr, j * 128:(j + 1) * 128], rhs_ar, True, True))
                        items.append((bh[:, 256:512], BTf[hr, j * 128:(j + 1) * 128], rhs_ar, True, True))
                    MM(items, reads=[KTf, BTf, ARf], writes=[PSB[:, 0:1024]])
                    bxy = PSB[:, 0:1024].rearrange("p (h w t) -> p h w t", h=2, w=4)
                    msu_b = m_su.unsqueeze(1).to_broadcast([128, 2, 128])
                    miu_b = m_iu.unsqueeze(1).to_broadcast([128, 2, 128])
                    msl_b = m_sl.unsqueeze(1).to_broadcast([128, 2, 128])

                    def dst(t):
                        return t.rearrange("p (j h t) -> p j h t", h=2, t=128)[:, j, :, :]
                    TT('dve', dst(AKs[par]), bxy[:, :, 0, :], msu_b, ALU.mult)
                    TT('dve', dst(RKs[par]), bxy[:, :, 1, :], miu_b, ALU.mult)
                    TT('dve', dst(N0s), bxy[:, :, 2, :], msu_b, ALU.mult)
                    TT('dve', dst(RBs[par]), bxy[:, :, 3, :], miu_b, ALU.mult)

                pxt = PA3.bitcast(BF16)[:, 0:NM * 128]
                TRS([(pxt[:, m * 128:(m + 1) * 128], N0s[:, m * 128:(m + 1) * 128], ident) for m in range(NM)],
                    reads=[N0s, ident], writes=[pxt])
                CP('act', X0s, pxt)

                def m8(t):
                    return t.rearrange("p (m t) -> p m t", t=128)
                W = NM * 128
                TT('dve', m8(Ps[0]), m8(N0s), ident.unsqueeze(1).to_broadcast([128, NM, 128]), ALU.add)
                Ncur, Xcur, Pcur = N0s, X0s, Ps[0]
                for lvl in range(1, 6):
                    pX = PSB[:, 0:W]
                    MM([(pX[:, m * 128:(m + 1) * 128], Ncur[:, m * 128:(m + 1) * 128], Xcur[:, m * 128:(m + 1) * 128], True, True)
                        for m in range(NM)], reads=[Ncur, Xcur], writes=[pX])
                    Xn = Xs[lvl % 2]
                    if lvl < 5:
                        pN = PSB[:, 512:512 + W]
                        MM([(pN[:, m * 128:(m + 1) * 128], Xcur[:, m * 128:(m + 1) * 128], Ncur[:, m * 128:(m + 1) * 128], True, True)
                            for m in range(NM)], reads=[Ncur, Xcur], writes=[pN])
                    CP('act', Xn, pX)
                    if lvl < 5:
                        Nn = Ns[lvl % 2]
                        CP('act', Nn, pN)
                    pP = PA3[:, 0:W]
                    pitems = []
                    for m in range(NM):
                        pitems.append((pP[:, m * 128:(m + 1) * 128], Xn[:, m * 128:(m + 1) * 128], Pcur[:, m * 128:(m + 1) * 128], m == 0, False))
                        pitems.append((pP[:, m * 128:(m + 1) * 128], ident, Pcur[:, m * 128:(m + 1) * 128], False, m == NM - 1))
                    MM(pitems, reads=[Xn, Pcur, ident], writes=[pP])
                    Pn = TTs[par] if lvl == 5 else Ps[lvl % 2]
                    CP('act', Pn, pP)
                    Pcur = Pn
                    Xcur = Xn
                    if lvl < 5:
                        Ncur = Nn

            def emit_chain(hp, g):
                par = g % 2
                if g == 0:
                    TS('dve', Sf[0], Sf[0], 0.0, None, ALU.mult)
                    TS('dve', Sbf, Sbf, 0.0, None, ALU.mult)
                ARbv = ARb[par].rearrange("p (j w t) -> p j w t", w=2, t=128)
                TMv = TM[par].rearrange("p (q j f) -> p q j f", q=3, f=128)
                TTg = TTs[par].rearrange("p (j h t) -> p j h t", h=2, t=128)
                AKv = AKs[par].rearrange("p (j h t) -> p j h t", h=2, t=128)
                RKv = RKs[par].rearrange("p (j h t) -> p j h t", h=2, t=128)
                RBv = RBs[par].rearrange("p (j h t) -> p j h t", h=2, t=128)
                Ytv = Ytms[par].rearrange("p (j f) -> p j f", f=128)
                for cl in range(2 * NT):
                    c = g * 2 * NT + cl
                    j = cl // 2
                    pr = slice((cl % 2) * 64, (cl % 2) * 64 + 64)
                    Scur = Sf[c % 2]
                    Snxt = Sf[(c + 1) % 2]
                    wc = WCs[g % 4][:, cl:cl + 1]
                    Zp = PSB[:, 1536:1664]
                    Up = PSB[:, 1664:1792]
                    Yp = PSB[:, 1792:1920]
                    Sp = PSB[:, 1920:2048]
                    ACT(Ssc, Scur, AF.Identity, scale=wc)
                    MM([(Zp[:, 0:64], AKv[pr, j, 0, :], TMv[pr, 0, j, 0:64], True, False),
                        (Zp[:, 64:128], AKv[pr, j, 1, :], TMv[pr, 0, j, 64:128], False, False),
                        (Zp[:, 0:128], ARbv[:, j, 0, :], Sbf, False, True)],
                       reads=[AKs[par], TM[par], ARb[par], Sbf], writes=[Zp])
                    CP('act', Zsb[pr, :], Zp[pr, :])
                    MM([(Up[:, 0:64], TTg[pr, j, 0, :], Zsb[pr, 0:64], True, False),
                        (Up[:, 64:128], TTg[pr, j, 1, :], Zsb[pr, 64:128], False, True)],
                       reads=[TTs[par], Zsb[pr, :]], writes=[Up])
                    CP('dve', Usb[pr, :], Up[pr, :])
                    MM([(Yp[:, 0:128], ARbv[:, j, 1, :], Sbf, True, False),
                        (Yp[:, 0:64], RBv[pr, j, 0, :], Usb[pr, 0:64], False, False),
                        (Yp[:, 0:64], RKv[pr, j, 0, :], TMv[pr, 0, j, 0:64], False, False),
                        (Yp[:, 64:128], RBv[pr, j, 1, :], Usb[pr, 64:128], False, False),
                        (Yp[:, 64:128], RKv[pr, j, 1, :], TMv[pr, 0, j, 64:128], False, True)],
                       reads=[ARb[par], Sbf, RBs[par], RKs[par], Usb[pr, :], TM[par]], writes=[Yp])
                    CP('act', Ytv[pr, j, :], Yp[pr, :])
                    MM([(Sp, TMv[pr, 2, j, :], Usb[pr, :], True, False),
                        (Sp, TMv[pr, 1, j, :], TMv[pr, 0, j, :], False, True)],
                       reads=[TM[par], Usb[pr, :]], writes=[Sp])
                    TT('dve', Stmp, Sp, blk2, ALU.mult)
                    STT('dve', Sbf, Stmp, wc, Ssc, ALU.mult, ALU.add)
                    STT('dve', Snxt, Stmp, wc, Ssc, ALU.mult, ALU.add)

            def emit_post(hp, g):
                Rr, Rk, Rv, SZ = RAW[hp % 2]
                par = g % 2
                c0 = g * GT
                Ytm = Ytms[par]
                Y4 = Ytm.rearrange("p (m f) -> p m f", f=64)
                YC4 = YC.rearrange("p (m f) -> p m f", f=64)
                YQ4 = YQ.rearrange("p (m f) -> p m f", f=64)
                nm_ = 2 * NT
                S.add('dve', (lambda o, i: (lambda e: e.tensor_reduce(out=o, in_=i, axis=AX.X, op=ALU.add)))(mean, Y4),
                      reads=[Ytm], writes=[mean])
                TS('dve', mean, mean, 1.0 / 64, None, ALU.mult)
                TT('dve', YC4, Y4, mean.unsqueeze(2).to_broadcast([128, nm_, 64]), ALU.subtract)
                ACT(YQ, YC, AF.Square)
                S.add('dve', (lambda o, i: (lambda e: e.tensor_reduce(out=o, in_=i, axis=AX.X, op=ALU.add)))(var, YQ4),
                      reads=[YQ], writes=[var])
                TS('dve', var, var, 1.0 / 64, 64e-5, ALU.mult, ALU.add)
                ACT(var, var, AF.Ln)
                ACT(var, var, AF.Exp, scale=-0.5)
                TT('dve', YN.rearrange("p (m f) -> p m f", f=64), YC4, var.unsqueeze(2).to_broadcast([128, nm_, 64]), ALU.mult)
                pyt = psb_bf(2, GT)
                TRS([(pyt[:, jj * 128:(jj + 1) * 128], YN[:, jj * 128:(jj + 1) * 128], ident) for jj in range(NT)],
                    reads=[YN, ident], writes=[pyt])
                YF = YC
                ACT(YF, pyt, AF.Identity, bias=ppc('gn_b', hp), scale=ppc('gn_w', hp))
                TT('dve', YF, YF, BONs[g % 4], ALU.add)
                TT('dve', YAg[par], YF, SZ[:, c0:c0 + GT], ALU.mult)
                DMA('sp', ya_d[hp, :, c0:c0 + GT], YAg[par], reads=[YAg[par]], writes=[('DR', 'ya', hp, hp + 1, 0, 1, False)])

            MEMSET('pool', Sf[0], 0.0)
            MEMSET('pool', Sbf, 0.0)
            NGT = n_hp_r * NG
            nx = [split_parts(nxt_stream(hp), [1.0] * NG) for hp in range(n_hp_r)]

            def hg(G):
                return (G // NG, G % NG)
            for it in range(-2, NGT + 1):
                streams = []
                if 0 <= it + 1 < NGT:
                    streams.append(cap(emit_B, *hg(it + 1)))
                if 0 <= it < NGT:
                    streams.append(cap(emit_chain, *hg(it)))
                if 0 <= it - 1 < NGT:
                    streams.append(cap(emit_post, *hg(it - 1)))
                if 0 <= it + 2 < NGT:
                    streams.append(cap(emit_A, *hg(it + 2)))
                if 0 <= it < NGT:
                    streams.append(nx[it // NG][it % NG])
                S.merge_streams(streams)

        def nxt_stream(ji):
            return cap(inproj_job, ji + 1) if ji + 1 < len(jobs_seq) else []

        if jobs_seq:
            inproj_job(0)
        if n_hp_r > 0:
            lora_prep()
        if n_hp_r > 0:
            rwkv_all()

        AR_.reset()
        if n_hp_m > 0:
            QA = AR_.a(2048); QB = AR_.a(2048); KA = AR_.a(2048); KB = AR_.a(2048)
            VA = AR_.a(2048); VB = AR_.a(2048)
            for t in (QA, QB, KA, KB):
                MEMSET('pool', t, 0.0)
            MEMSET('pool', VA, 1.0)
            MEMSET('pool', VB, 1.0)
            DMA('sp', KA[64:72, :], ind_d, writes=[KA[64:72, :]])
            DMA('sp', KB[0:8, :], ind_d, writes=[KB[0:8, :]])
        m_base = AR_.off

        def moba_headpair(hp, ji, nxt):
            Rq, Rkq, Rvq, SZ = RAW[ji % 2]
            AR_.off = m_base
            A = AR_.a
            nparts = split_parts(nxt, [1.0, 0.0, 0.0, 0.0, 0.0])
            Qf = A(2048, F32); Kf = A(2048, F32)

            def norm_stream(raw, wname, dA, dB, Ff, SQm, RIm, bank):
                for g in range(4):
                    c0 = g * 512
                    ACT(SQm, raw[:, c0:c0 + 512], AF.Square)
                    pk = psb_f32(bank)
                    MM([(pk, blk2, SQm, True, True)], reads=[blk2, SQm], writes=[pk])
                    ACT(RIm, pk, AF.Ln, bias=ppc_eps, scale=1.0 / 64)
                    ACT(RIm, RIm, AF.Exp, scale=-0.5)
                    STT('dve', Ff[:, c0:c0 + 512], raw[:, c0:c0 + 512], ppc(wname), RIm, ALU.mult, ALU.mult)
                    CP('act', dA[0:64, c0:c0 + 512], Ff[0:64, c0:c0 + 512])
                    CP('dve', dB[64:128, c0:c0 + 512], Ff[64:128, c0:c0 + 512])

            def v_stream():
                pv = psb2_bf(2)
                TRS([(pv[:, t * 128:(t + 1) * 128], Rvq[:, t * 128:(t + 1) * 128], ident) for t in range(16)],
                    reads=[Rvq, ident], writes=[pv])
                pv3 = pv.rearrange("p (t f) -> p t f", f=128)
                CP('act', VA.rearrange("p (t f) -> p t f", f=128)[:, :, 0:64], pv3[:, :, 0:64])
                CP('dve', VB.rearrange("p (t f) -> p t f", f=128)[:, :, 64:128], pv3[:, :, 64:128])

            SQq = A(512); RIq = A(512, F32); SQk = A(512); RIk = A(512, F32)
            st_q = cap(norm_stream, Rq, 'qnw', QA, QB, Qf, SQq, RIq, 0)
            st_k = cap(norm_stream, Rkq, 'knw', KA, KB, Kf, SQk, RIk, 1)
            st_v = cap(v_stream)
            n0 = int(len(nparts[0]) * 0.72)
            S.merge_streams([st_k, st_q, st_v, nparts[0][:n0]])
            nparts[0] = nparts[0][n0:]
            S.capture = []
            kmp = A(16, F32)
            MEMSET('pool', kmp, 0.0)
            S.add('dve', (lambda o, i: (lambda e: e.tensor_reduce(out=o, in_=i, axis=AX.X, op=ALU.add)))(
                kmp[0:64, 0:8], Kf[0:64, :].rearrange("p (n t) -> p n t", t=256)), reads=[Kf[0:64, :]], writes=[kmp[0:64, 0:8]])
            S.add('dve', (lambda o, i: (lambda e: e.tensor_reduce(out=o, in_=i, axis=AX.X, op=ALU.add)))(
                kmp[64:128, 8:16], Kf[64:128, :].rearrange("p (n t) -> p n t", t=256)), reads=[Kf[64:128, :]], writes=[kmp[64:128, 8:16]])
            pg = psb_f32(0, 256)
            MM([(pg[:, qt * 16:(qt + 1) * 16], Qf[:, qt * 128:(qt + 1) * 128], kmp, True, True) for qt in range(16)],
               reads=[Qf, kmp], writes=[pg])
            GM = A(256, F32); G2 = A(256, F32); EQ = A(256, F32); mx = A(32, F32); BI = A(256)
            g3 = lambda t: t.rearrange("p (m n) -> p m n", n=8)
            mxb = mx.unsqueeze(2).to_broadcast([128, 32, 8])

            def rmax(o, i):
                S.add('dve', (lambda o_, i_: (lambda e: e.tensor_reduce(out=o_, in_=i_, axis=AX.X, op=ALU.max)))(o, g3(i)),
                      reads=[i], writes=[o])
            TT('dve', GM, pg, pastneg, ALU.add)
            rmax(mx, GM)
            TT('dve', g3(EQ), g3(GM), mxb, ALU.is_ge)
            STT('dve', G2, EQ, NEG, GM, ALU.mult, ALU.add)
            rmax(mx, G2)
            TT('dve', g3(EQ), g3(G2), mxb, ALU.is_ge)
            STT('dve', G2, EQ, NEG, G2, ALU.mult, ALU.add)
            rmax(mx, G2)
            TT('dve', g3(EQ), g3(GM), mxb, ALU.is_ge)
            TT('dve', EQ, EQ, ownpos, ALU.max)
            TS('dve', BI, EQ, -1.0, -NEG, ALU.add, ALU.mult)
            pbt = psb_bf(1, 1024)
            pbt2 = psb_bf(2, 1024)
            TRS([((pbt if qt < 8 else pbt2)[0:16, (qt % 8) * 128:(qt % 8 + 1) * 128], BI[:, qt * 16:(qt + 1) * 16], ident)
                 for qt in range(16)], reads=[BI, ident], writes=[pbt, pbt2])
            BT_ = A(2048)
            CP('act', BT_[0:16, 0:1024], pbt[0:16, :])
            CP('act', BT_[0:16, 1024:2048], pbt2[0:16, :])
            DMA('sp', QA[64:72, :], BT_[0:8, :], reads=[BT_[0:8, :]], writes=[QA[64:72, :]])
            DMA('sp', QB[0:8, :], BT_[8:16, :], reads=[BT_[8:16, :]], writes=[QB[0:8, :]])
            pro = S.capture
            S.capture = None
            S.merge_streams([pro, nparts[0]])
            PT = [A(512) for _ in range(4)]
            spring = [psb_f32(0), psb_f32(1), PSA[:, 1024:1536], PSA[:, 1536:2048]]
            RS = A(512, F32)
            RW = A(512, F32)
            YO = A(512, F32)
            YB = A(2048)
            VA3 = VA.rearrange("p (t f) -> p t f", f=128)
            VB3 = VB.rearrange("p (t f) -> p t f", f=128)
            OpA = psb_f32(2)
            OpB = psb_f32(3)
            allu = [(QT, h, kt) for QT in range(4) for kt in range(4 * QT + 4) for h in range(2)]
            pw = PSA[:, 0:512]

            def qk(gi):
                QT, h, kt = allu[gi]
                q0 = QT * 512
                Kh = KA if h == 0 else KB
                Qh = QA if h == 0 else QB
                sp_ = spring[gi % 4]
                diag = kt >= 4 * QT
                items = [(sp_, Kh[:, kt * 128:(kt + 1) * 128], Qh[:, q0:q0 + 512], True, not diag)]
                rds = [Kh[:, kt * 128:(kt + 1) * 128], Qh[:, q0:q0 + 512]]
                if diag:
                    items.append((sp_, ident, cmask[:, kt - 4 * QT, :], False, True))
                    rds += [ident, cmask[:, kt - 4 * QT, :]]
                MM(items, reads=rds, writes=[sp_])
            qk(0)
            qk(1)
            for gi, (QT, h, kt) in enumerate(allu):
                q0 = QT * 512
                nkt = 4 * QT + 4
                if gi + 2 < len(allu):
                    qk(gi + 2)
                sp_ = spring[gi % 4]
                pt = PT[gi % 4]
                ACT(pt, sp_, AF.Exp, scale=0.125)
                Vh = VA3 if h == 0 else VB3
                Oh = OpA if h == 0 else OpB
                MM([(Oh, Vh[:, kt, :], pt, kt == 0, kt == nkt - 1)], reads=[Vh[:, kt, :], pt], writes=[Oh])
                if kt == nkt - 1 and h == 1:
                    ACT(RS[64:128, :], OpA[64:128, :], AF.Ln)
                    ACT(RS[0:64, :], OpB[0:64, :], AF.Ln)
                    ACT(RS, RS, AF.Exp, scale=-1.0)
                    MM([(pw, swapP, RS, True, True)], reads=[swapP, RS], writes=[pw])
                    CP('act', RW, pw)
                    TT('dve', YO[0:64, :], OpA[0:64, :], RW[0:64, :], ALU.mult)
                    TT('dve', YO[64:128, :], OpB[64:128, :], RW[64:128, :], ALU.mult)
                    TT('dve', YB[:, q0:q0 + 512], YO, SZ[:, q0:q0 + 512], ALU.mult)
            for QT in range(4):
                S.commit(nparts[QT + 1])
            out_dmas.append(DMA('sp', yb_d[hp], YB, reads=[YB], writes=[('DR', 'yb', hp, hp + 1, 0, 1, False)]))

        if n_hp_m > 0:
            epsb = AR_.a(2, F32)
            MEMSET('pool', epsb, 1e-6)
            ppc_eps = epsb[:, 0:1]
            m_base = AR_.off
        for hp in range(n_hp_m):
            moba_headpair(hp, n_hp_r + hp, nxt_stream(n_hp_r + hp))

        if do_final:
            for th in range(2):
                AR_.reset()
                t0 = th * 1024
                YAh = RAWT[0][:, :]; YBh = RAWT[1][:, :]
                YA3 = YAh.rearrange("p (k t) -> p k t", t=1024)
                YB3 = YBh.rearrange("p (k t) -> p k t", t=1024)
                for k in range(8):
                    DMA('sp', YA3[:, k, :], ya_d[k, :, t0:t0 + 1024], reads=[('DR', 'ya', k, k + 1, 0, 1, False)], writes=[YA3[:, k, :]])
                    DMA('sp', YB3[:, k, :], yb_d[k, :, t0:t0 + 1024], reads=[('DR', 'yb', k, k + 1, 0, 1, False)], writes=[YB3[:, k, :]])
                MG = AR_.a(16 * 1024)
                MG3 = MG.rearrange("p (k t) -> p k t", t=1024)
                sga = AR_.a(1024); sgb = AR_.a(1024); m1 = AR_.a(1024, F32)
                fj = []
                for c in range(16):
                    fj.append(('in', COL['ga'] + c * 128))
                    fj.append(('pa', c * 128))
                    fj.append(('in', COL['gb'] + c * 128))
                    fj.append(('pb', c * 128))
                fq = []
                fst = {'i': 0}

                def next_fw():
                    while fst['i'] < len(fj) and len(fq) < NW:
                        kind, col = fj[fst['i']]
                        fst['i'] += 1
                        if kind == 'in':
                            fq.append(wload_in(col))
                        elif kind == 'pa':
                            fq.append(wload_proj(wpa_d, col))
                        else:
                            fq.append(wload_proj(wpb_d, col))
                    return fq.pop(0)

                def run_half_proj(wbuf, nk, src3, evac):
                    acc = PSA[:, (run_half_proj.n % 2) * 1024:(run_half_proj.n % 2 + 1) * 1024]
                    run_half_proj.n += 1
                    items = []
                    for k in range(nk):
                        for tt in range(2):
                            items.append((acc[:, tt * 512:(tt + 1) * 512], wbuf[:, k, :], src3(k, tt), k == 0, k == nk - 1))
                    MM(items, reads=[wbuf[:, 0:nk, :]] + run_half_proj.rd, writes=[acc])
                    evac(acc)
                run_half_proj.n = 0
                for c in range(16):
                    run_half_proj.rd = [hT[:, :, t0:t0 + 1024]]
                    run_half_proj(next_fw(), KC, lambda k, tt: hT[:, k, t0 + tt * 512:t0 + (tt + 1) * 512],
                                  lambda acc: ACT(sga, acc, AF.Sigmoid))
                    run_half_proj.rd = [YAh]
                    run_half_proj(next_fw(), 8, lambda k, tt: YA3[:, k, tt * 512:(tt + 1) * 512],
                                  lambda acc: TT('dve', m1, acc, sga, ALU.mult))
                    run_half_proj.rd = [hT[:, :, t0:t0 + 1024]]
                    run_half_proj(next_fw(), KC, lambda k, tt: hT[:, k, t0 + tt * 512:t0 + (tt + 1) * 512],
                                  lambda acc: ACT(sgb, acc, AF.Sigmoid))
                    run_half_proj.rd = [YBh]

                    def ev_b(acc, c=c):
                        TT('dve', sgb, acc, sgb, ALU.mult)
                        TT('dve', MG3[:, c, :], m1, sgb, ALU.add)
                    run_half_proj(next_fw(), 8, lambda k, tt: YB3[:, k, tt * 512:(tt + 1) * 512], ev_b)
                WO = [AR_.a(16 * 512) for _ in range(2)]
                XR = [AR_.a(512, F32) for _ in range(1)]
                OT = [AR_.a(512, F32) for _ in range(1)]
                n_o = 0
                for c4 in range(4):
                    wo = WO[c4 % 2]
                    wo3 = wo.rearrange("p (k m) -> p k m", m=512)
                    DMA('pool', wo3, wo_d[:, c4 * 512:(c4 + 1) * 512].rearrange("(kc p) m -> p kc m", p=128), writes=[wo])
                    for tl in range(8):
                        tok0 = t0 + tl * 128
                        xr = XR[0]
                        ot = OT[0]
                        acc = PSB[:, (n_o % 4) * 512:(n_o % 4 + 1) * 512]
                        n_o += 1
                        DMA('sp', xr, x_d[tok0:tok0 + 128, c4 * 512:(c4 + 1) * 512], writes=[xr])
                        MM([(acc, MG3[:, k, tl * 128:(tl + 1) * 128], wo3[:, k, :], k == 0, k == 15) for k in range(16)],
                           reads=[MG, wo], writes=[acc])
                        TT('dve', ot, acc, xr, ALU.add)
                        out_dmas.append(DMA('sp', out_d[tok0:tok0 + 128, c4 * 512:(c4 + 1) * 512], ot, reads=[ot]))

        cnt = S.emit(final_wait_ops=out_dmas)
    return nc, cnt, len(S.ops)


def _consts():
    bf = ml_dtypes.bfloat16
    cbv = np.zeros((128, NCB), np.float32)
    idx = np.arange(128)
    cbv[:, CB['ident']:CB['ident'] + 128] = np.eye(128)
    same = (idx[:, None] // 64) == (idx[None, :] // 64)
    cbv[:, CB['blk2']:CB['blk2'] + 128] = same
    cbv[:, CB['m_su']:CB['m_su'] + 128] = same & (idx[:, None] < idx[None, :])
    cbv[:, CB['m_iu']:CB['m_iu'] + 128] = same & (idx[:, None] <= idx[None, :])
    cbv[:, CB['m_sl']:CB['m_sl'] + 128] = same & (idx[:, None] > idx[None, :])
    cbv[:, CB['onesA']:CB['onesA'] + 64] = 1.0
    cbv[:, CB['onesB'] + 64:CB['onesB'] + 128] = 1.0
    cm = np.zeros((128, 4, 512), np.float32)
    for ktl in range(4):
        kpos = ktl * 128 + idx[:, None]
        qpos = np.arange(512)[None, :]
        kb = kpos // 256
        qb = qpos // 256
        ok = np.where(kb == qb, kpos <= qpos, kb < qb)
        cm[:, ktl, :] = np.where(ok, 0.0, NEG)
    cbv[:, CB['cmask']:CB['cmask'] + 2048] = cm.reshape(128, 2048)
    cfv = np.zeros((128, NCF), np.float32)
    sm = np.ones(512, np.float32)
    sm[::64] = 0.0
    cfv[:, CF['scanmask']:CF['scanmask'] + 512] = sm[None, :]
    pn = np.zeros((16, 2, 8), np.float32)
    op = np.zeros((16, 2, 8), np.float32)
    for qt in range(16):
        qb = qt // 2
        for n in range(8):
            pn[qt, :, n] = 0.0 if n < qb else NEG
            op[qt, :, n] = 1.0 if n >= qb else 0.0
    cfv[:, CF['pastneg']:CF['pastneg'] + 256] = pn.reshape(1, 256)
    cfv[:, CF['ownpos']:CF['ownpos'] + 256] = op.reshape(1, 256)
    cfv[:, CF['swapP']:CF['swapP'] + 128] = np.roll(np.eye(128, dtype=np.float32), 64, axis=1)
    ind = np.zeros((8, S_TOK), np.float32)
    for n in range(8):
        ind[n, n * 256:(n + 1) * 256] = 1.0
    return cbv.astype(bf), cfv, ind.astype(bf)


def _pack_params(i):
    ppv = np.zeros((128, NPP), np.float32)

    def fm(v):
        return np.ascontiguousarray(v.reshape(-1, 128).T)
    for nm in ('mu_r', 'mu_k', 'mu_v', 'w0', 'a0', 'k_k', 'k_a', 'r_k', 'gn_w', 'gn_b'):
        ppv[:, PP[nm]:PP[nm] + 8] = fm(i[nm][0])
    ppv[:, PP['norm_w']:PP['norm_w'] + 16] = fm(i['norm_w'][0])
    ppv[0:64, PP['mu_wa']] = i['mu_w'][0]
    ppv[64:128, PP['mu_wa']] = i['mu_a'][0]
    ppv[0:64, PP['qnw']] = i['q_norm_w'][0]
    ppv[64:128, PP['qnw']] = i['q_norm_w'][0]
    ppv[0:64, PP['knw']] = i['k_norm_w'][0]
    ppv[64:128, PP['knw']] = i['k_norm_w'][0]
    lora = np.concatenate([i['w_decay_up'][0], i['w_iclr_up'][0]], axis=0).astype(np.float32)
    return ppv, np.ascontiguousarray(lora)


_CACHE = {}


def make_in_maps(inputs, n_cores=8):
    i = {k: np.asarray(v) for k, v in inputs.items()}
    cbv, cfv, ind = _consts()
    ppv, lora = _pack_params(i)
    shared = dict(w_in=np.ascontiguousarray(i['w_in'][0]), w_pa=np.ascontiguousarray(i['w_proj_rwkv'][0]),
                  w_pb=np.ascontiguousarray(i['w_proj_moba'][0]), w_out=np.ascontiguousarray(i['w_out'][0]),
                  lora=lora, pp=ppv, cb=cbv, cf=cfv, ind=ind)
    maps = []
    for c in range(n_cores):
        m = dict(shared)
        m['x'] = np.ascontiguousarray(i['x'][c])
        maps.append(m)
    return maps


def kernel(**inputs):
    if 'nc' not in _CACHE:
        _CACHE['nc'] = build()[0]
    nc = _CACHE['nc']
    maps = make_in_maps(inputs, 8)
    res = run_bass_kernel_spmd(nc, maps, core_ids=list(range(8)))
    out = np.stack([np.asarray(r['out']) for r in res.results], axis=0)
    return out.astype(np.float32)
```
